# Optimizing a Trainium2 kernel written in Bass

```python
import math
import jax, jax.numpy as jnp
from jax import lax
import numpy as np

D_MODEL = 1024
BATCH = 16
SEQ = 4096
DEPTH = 1
DEC_BATCH = 4
DEC_SEQ = 4096
PAST_LEN = 128

PLE_DIM = 256
M_HEADS = 4
M_QK = 128
M_V = 256
M_CHUNK = 64
M_CONV = 5
A_HEADS = 8
A_NOPE = 128
A_ROPE = 64
A_V = 128
Q_LORA = 256
KV_LORA = 256
ROPE_THETA = 10000.0
Q_BLOCK = 128
D_FF = -(-(8 * D_MODEL) // (3 * 256)) * 256
EPS = 1e-6

M_QK_W = 2 * M_HEADS * M_QK
M_V_W = M_HEADS * M_V
M_GATE_W = 4 * M_HEADS
MERGE_W = 2 * D_MODEL
IN_SIZES = (M_QK_W, M_V_W, M_V_W, M_GATE_W, Q_LORA, KV_LORA, A_ROPE, MERGE_W)
IN_W = M_QK_W + 2 * M_V_W + M_GATE_W + Q_LORA + KV_LORA + A_ROPE + MERGE_W

kernel_name = "hybrid_mlstm_mla_encoder"


def _offsets(sizes):
    out, acc = [], 0
    for s in sizes[:-1]:
        acc += s
        out.append(acc)
    return out


def rmsnorm(x, g):
    xf = x.astype(jnp.float32)
    y = xf * lax.rsqrt(jnp.mean(xf * xf, axis=-1, keepdims=True) + EPS)
    return (y * g.astype(jnp.float32)).astype(x.dtype)


def rope(x, cos, sin):
    x1, x2 = jnp.split(x, 2, axis=-1)
    cos = cos.astype(x.dtype)
    sin = sin.astype(x.dtype)
    return jnp.concatenate([x1 * cos - x2 * sin, x1 * sin + x2 * cos], axis=-1)


def centred_conv(u, w):
    S = u.shape[1]
    half = M_CONV // 2
    up = jnp.pad(u, ((0, 0), (half, half), (0, 0)))
    out = up[:, 0:S] * w[0]
    for j in range(1, M_CONV):
        out = out + up[:, j:j + S] * w[j]
    return out


def mlstm_chunkwise(q, k, v, log_i, log_f):
    q, k, v, log_i, log_f = (a.astype(jnp.float32) for a in (q, k, v, log_i, log_f))
    B, H, S, DK = q.shape
    DV = v.shape[-1]
    L = M_CHUNK
    NC = S // L

    def to_chunks(a):
        return jnp.moveaxis(a.reshape(B, H, NC, L, *a.shape[3:]), 2, 0)

    qc, kc, vc, ic, fc = (to_chunks(a) for a in (q, k, v, log_i, log_f))
    tri = jnp.tril(jnp.ones((L, L), dtype=bool))

    def step(carry, inp):
        C, n, m = carry
        qb, kb, vb, ib, fb = inp
        b = jnp.cumsum(fb, axis=-1)
        d_intra = jnp.where(tri, b[..., :, None] - b[..., None, :] + ib[..., None, :], -jnp.inf)
        d_inter = b + m[..., None]
        m_comb = jnp.maximum(d_inter, jnp.max(d_intra, axis=-1))
        w_intra = jnp.exp(d_intra - m_comb[..., None])
        w_inter = jnp.exp(d_inter - m_comb)
        s = jnp.einsum('bhtd,bhsd->bhts', qb, kb) * w_intra
        num = jnp.einsum('bhts,bhsv->bhtv', s, vb) + w_inter[..., None] * jnp.einsum('bhtd,bhdv->bhtv', qb, C)
        den = jnp.sum(s, axis=-1) + w_inter * jnp.einsum('bhtd,bhd->bht', qb, n)
        h = num / jnp.maximum(jnp.abs(den), jnp.exp(-m_comb))[..., None]
        b_last = b[..., -1]
        g = b_last[..., None] - b + ib
        m_new = jnp.maximum(b_last + m, jnp.max(g, axis=-1))
        a_prev = jnp.exp(b_last + m - m_new)
        ka = kb * jnp.exp(g - m_new[..., None])[..., None]
        C_new = a_prev[..., None, None] * C + jnp.einsum('bhsd,bhsv->bhdv', ka, vb)
        n_new = a_prev[..., None] * n + jnp.sum(ka, axis=2)
        return (C_new, n_new, m_new), h

    init = (jnp.zeros((B, H, DK, DV), jnp.float32), jnp.zeros((B, H, DK), jnp.float32),
            jnp.zeros((B, H), jnp.float32))
    _, hs = lax.scan(step, init, (qc, kc, vc, ic, fc))
    return jnp.moveaxis(hs, 0, 2).reshape(B, H, S, DV)


def mla_attention(q_nope, q_rope, k_nope, k_rope, v):
    B, S, H, _ = q_nope.shape
    nb = S // Q_BLOCK
    scale = (A_NOPE + A_ROPE) ** -0.5

    def blocks(a):
        return jnp.moveaxis(a.reshape(B, nb, Q_BLOCK, *a.shape[2:]), 1, 0)

    def one(args):
        qn, qr = args
        s = jnp.einsum('bqhd,bkhd->bhqk', qn, k_nope) + jnp.einsum('bqhr,bkr->bhqk', qr, k_rope)
        p = jax.nn.softmax(s.astype(jnp.float32) * scale, axis=-1).astype(v.dtype)
        return jnp.einsum('bhqk,bkhd->bqhd', p, v)

    o = lax.map(one, (blocks(q_nope), blocks(q_rope)))
    return jnp.moveaxis(o, 0, 1).reshape(B, S, H * A_V)


def encoder_layer(x, p_l, g_mix, w_in, b_gate, conv_w, g_mhead, w_mo, g_q, w_uq, g_kv, w_ukv,
                  w_ao, w_out, g_ffn, w_gate, w_up, w_down, w_ple, w_ple_gate):
    B, S, _ = x.shape
    pos = jnp.arange(S, dtype=jnp.float32)
    inv_freq = ROPE_THETA ** (-jnp.arange(0, A_ROPE, 2, dtype=jnp.float32) / A_ROPE)
    ang = pos[:, None] * inv_freq[None, :]
    cos, sin = jnp.cos(ang), jnp.sin(ang)

    h = rmsnorm(x, g_mix)
    u = h @ w_in
    qk, v, o, gates, c_q, c_kv, k_r, gm = jnp.split(u, _offsets(IN_SIZES), axis=-1)

    qk = jax.nn.silu(centred_conv(qk, conv_w))
    q, k = jnp.split(qk, 2, axis=-1)
    q = q.reshape(B, S, M_HEADS, M_QK).transpose(0, 2, 1, 3) * (M_QK ** -0.5)
    k = k.reshape(B, S, M_HEADS, M_QK).transpose(0, 2, 1, 3)
    vh = v.reshape(B, S, M_HEADS, M_V).transpose(0, 2, 1, 3)
    gates = (gates + b_gate).astype(jnp.float32).reshape(B, S, 4, M_HEADS).transpose(0, 2, 3, 1)
    log_i = jnp.concatenate([gates[:, 0], jnp.flip(gates[:, 1], axis=-1)], axis=1)
    log_f = jax.nn.log_sigmoid(jnp.concatenate([gates[:, 2], jnp.flip(gates[:, 3], axis=-1)], axis=1))
    both = lambda a: jnp.concatenate([a, jnp.flip(a, axis=2)], axis=1)
    hm = mlstm_chunkwise(both(q), both(k), both(vh), log_i, log_f)
    hm = hm[:, :M_HEADS] + jnp.flip(hm[:, M_HEADS:], axis=2)
    hm = rmsnorm(hm.transpose(0, 2, 1, 3), g_mhead.reshape(M_HEADS, M_V)).reshape(B, S, M_V_W)
    y_m = (jax.nn.sigmoid(o) * hm.astype(x.dtype)) @ w_mo

    qa = (rmsnorm(c_q, g_q) @ w_uq).reshape(B, S, A_HEADS, A_NOPE + A_ROPE)
    q_nope = qa[..., :A_NOPE]
    q_rope = rope(qa[..., A_NOPE:], cos[:, None, :], sin[:, None, :])
    kv = (rmsnorm(c_kv, g_kv) @ w_ukv).reshape(B, S, A_HEADS, A_NOPE + A_V)
    k_nope, va = kv[..., :A_NOPE], kv[..., A_NOPE:]
    k_rope = rope(k_r, cos, sin)
    y_a = mla_attention(q_nope, q_rope, k_nope, k_rope, va) @ w_ao

    g_m, g_a = jnp.split(jax.nn.sigmoid(gm), 2, axis=-1)
    x = x + (g_m * y_m + g_a * y_a) @ w_out

    h2 = rmsnorm(x, g_ffn)
    x = x + (jax.nn.silu(h2 @ w_gate) * (h2 @ w_up)) @ w_down

    x = x + jax.nn.sigmoid(x @ w_ple_gate) * (p_l @ w_ple)
    return x


def encoder(x, p, g_mix, w_in, b_gate, conv_w, g_mhead, w_mo, g_q, w_uq, g_kv, w_ukv, w_ao,
            w_out, g_ffn, w_gate, w_up, w_down, w_ple, w_ple_gate, g_final):
    for i in range(DEPTH):
        x = encoder_layer(x, p[i], g_mix[i], w_in[i], b_gate[i], conv_w[i], g_mhead[i], w_mo[i],
                          g_q[i], w_uq[i], g_kv[i], w_ukv[i], w_ao[i], w_out[i], g_ffn[i],
                          w_gate[i], w_up[i], w_down[i], w_ple[i], w_ple_gate[i])
    return rmsnorm(x, g_final)


def setup_inputs(seed: int = 0) -> dict:
    key = jax.random.key(seed)
    ks = jax.random.split(key, 32)

    def nrm(k, shape, scale):
        return jax.random.normal(k, shape, jnp.float32) * scale

    def gain(k, shape):
        return 1.0 + 0.05 * jax.random.normal(k, shape, jnp.float32)

    i_bias = nrm(ks[10], (DEPTH, 2 * M_HEADS), 0.1)
    f_bias = 3.0 + nrm(ks[11], (DEPTH, 2 * M_HEADS), 0.5)
    return {
        "x_prompt": nrm(ks[0], (BATCH, SEQ, D_MODEL), 1.0),
        "x_sample": nrm(ks[1], (DEC_BATCH, DEC_SEQ, D_MODEL), 1.0),
        "p_prompt": nrm(ks[2], (DEPTH, BATCH, SEQ, PLE_DIM), 1.0),
        "p_sample": nrm(ks[3], (DEPTH, DEC_BATCH, DEC_SEQ, PLE_DIM), 1.0),
        "g_mix": gain(ks[4], (DEPTH, D_MODEL)),
        "w_in": nrm(ks[5], (DEPTH, D_MODEL, IN_W), D_MODEL ** -0.5),
        "b_gate": jnp.concatenate([i_bias, f_bias], axis=-1),
        "conv_w": nrm(ks[6], (DEPTH, M_CONV, M_QK_W), M_CONV ** -0.5),
        "g_mhead": gain(ks[7], (DEPTH, M_V_W)),
        "w_mo": nrm(ks[8], (DEPTH, M_V_W, D_MODEL), M_V_W ** -0.5),
        "g_q": gain(ks[9], (DEPTH, Q_LORA)),
        "w_uq": nrm(ks[12], (DEPTH, Q_LORA, A_HEADS * (A_NOPE + A_ROPE)), Q_LORA ** -0.5),
        "g_kv": gain(ks[13], (DEPTH, KV_LORA)),
        "w_ukv": nrm(ks[14], (DEPTH, KV_LORA, A_HEADS * (A_NOPE + A_V)), KV_LORA ** -0.5),
        "w_ao": nrm(ks[15], (DEPTH, A_HEADS * A_V, D_MODEL), (A_HEADS * A_V) ** -0.5),
        "w_out": nrm(ks[16], (DEPTH, D_MODEL, D_MODEL), D_MODEL ** -0.5),
        "g_ffn": gain(ks[17], (DEPTH, D_MODEL)),
        "w_gate": nrm(ks[18], (DEPTH, D_MODEL, D_FF), D_MODEL ** -0.5),
        "w_up": nrm(ks[19], (DEPTH, D_MODEL, D_FF), D_MODEL ** -0.5),
        "w_down": nrm(ks[20], (DEPTH, D_FF, D_MODEL), D_FF ** -0.5),
        "w_ple": nrm(ks[21], (DEPTH, PLE_DIM, D_MODEL), PLE_DIM ** -0.5),
        "w_ple_gate": nrm(ks[22], (DEPTH, D_MODEL, D_MODEL), D_MODEL ** -0.5),
        "g_final": gain(ks[23], (D_MODEL,)),
    }


def reference(x_prompt, x_sample, p_prompt, p_sample, g_mix, w_in, b_gate, conv_w, g_mhead, w_mo,
              g_q, w_uq, g_kv, w_ukv, w_ao, w_out, g_ffn, w_gate, w_up, w_down, w_ple, w_ple_gate,
              g_final):
    y_prompt = encoder(x_prompt, p_prompt, g_mix, w_in, b_gate, conv_w, g_mhead, w_mo, g_q, w_uq,
                       g_kv, w_ukv, w_ao, w_out, g_ffn, w_gate, w_up, w_down, w_ple, w_ple_gate, g_final)
    y_sample = encoder(x_sample, p_sample, g_mix, w_in, b_gate, conv_w, g_mhead, w_mo, g_q, w_uq,
                       g_kv, w_ukv, w_ao, w_out, g_ffn, w_gate, w_up, w_down, w_ple, w_ple_gate, g_final)
    return (y_prompt, y_sample)
```

```python
import numpy as np
import concourse.bass as bass
import concourse.mybir as mybir
from concourse.bass_utils import run_bass_kernel_spmd

F32 = mybir.dt.float32
BF16 = mybir.dt.bfloat16
I32 = mybir.dt.int32
AF = mybir.ActivationFunctionType
ALU = mybir.AluOpType
AX = mybir.AxisListType


class Sem:
    def __init__(self, handle, name, dma=False, owner=None):
        self.h = handle
        self.name = name
        self.dma = dma
        self.owner = owner
        self.issued = 0


class Res:
    __slots__ = ("name", "w", "r", "dsem")

    def __init__(self, name):
        self.name = name
        self.w = None
        self.r = {}
        self.dsem = None


class Eng:
    def __init__(self, name, sem, is_pe=False):
        self.name = name
        self.sem = sem
        self.ops = []
        self.seen = {}
        self.is_pe = is_pe
        self.pending = False


class FW:
    def __init__(self, nc, stack):
        self.nc = nc
        self.stack = stack
        self.engs = {}
        self.nsem = 0
        self.all_sems = []
        for n in ("pe", "act", "dve", "pool", "sp"):
            self.engs[n] = Eng(n, None, is_pe=(n == "pe"))
            self.engs[n].sem = self._newsem("s_" + n, owner=self.engs[n])
        self.dma_res = []
        self.n_ins = 0

    SEM_LIMIT = 6000

    def _newsem(self, name, dma=False, owner=None):
        self.nsem += 1
        h = self.stack.enter_context(self.nc.semaphore("%s_%d" % (name, self.nsem)))
        sm = Sem(h, name, dma=dma, owner=owner)
        self.all_sems.append(sm)
        return sm

    def dsem(self, res, qclass):
        if res.dsem is None:
            res.dsem = {}
            self.dma_res.append(res)
        sm = res.dsem.get(qclass)
        if sm is None or sm.issued >= self.SEM_LIMIT:
            sm = res.dsem[qclass] = self._newsem("d%s_%s" % (qclass[0], res.name.replace("@", "h_")), dma=True)
        return sm

    def _collect(self, E, reads, writes):
        waits = {}

        def need(ev):
            if ev is None:
                return
            sem, val = ev
            if sem.owner is E:
                if E.is_pe:
                    return
            else:
                if sem.dma and sem.issued > val:
                    val = sem.issued
            if E.seen.get(sem, 0) >= val:
                return
            if waits.get(sem, 0) < val:
                waits[sem] = val

        for r in reads:
            need(r.w)
        for w in writes:
            need(w.w)
            for s, v in w.r.items():
                need((s, v))
        for s, v in waits.items():
            E.seen[s] = v
        return list(waits.items())

    def _commit(self, ev, reads, writes):
        for w in writes:
            w.w = ev
            w.r = {}
        for r in reads:
            if r in writes:
                continue
            s, v = ev
            if r.r.get(s, 0) < v:
                r.r[s] = v

    def op(self, eng, fn, reads=(), writes=(), signal=True):
        E = self.engs[eng]
        waits = self._collect(E, reads, writes)
        if signal:
            E.sem.issued += 1
        val = E.sem.issued if signal else E.sem.issued + 1
        sem_h = E.sem.h

        def emit(e, waits=waits, fn=fn, signal=signal):
            for s, v in waits:
                e.wait_ge(s.h, v)
            ins = fn(e)
            if signal:
                ins.then_inc(sem_h, 1)
        E.ops.append(emit)
        E.pending = not signal
        self._commit((E.sem, val), reads, writes)
        self.n_ins += 1
        if signal and E.sem.issued >= self.SEM_LIMIT:
            E.sem = self._newsem("s_" + E.name, owner=E)

    def dma(self, eng, out, in_, reads=(), writes=(), sem_res=None, **kw):
        E = self.engs[eng]
        waits = self._collect(E, reads, writes)
        if sem_res is None:
            sem_res = writes[0] if (writes and writes[0].name[0] != "@") else reads[0]
        ds = self.dsem(sem_res, "sw" if eng == "pool" else "hw")
        ds.issued += 16
        val = ds.issued

        def emit(e, waits=waits):
            for s, v in waits:
                e.wait_ge(s.h, v)
            e.dma_start(out=out, in_=in_, **kw).then_inc(ds.h, 16)
        E.ops.append(emit)
        self._commit((ds, val), reads, writes)
        self.n_ins += 1

    def barrier(self):
        for E in self.engs.values():
            assert not E.pending
        evs = [(sm, sm.issued) for sm in self.all_sems if sm.issued > 0]
        for E in self.engs.values():
            waits = []
            for s, v in evs:
                if s.owner is E and E.is_pe:
                    continue
                if E.seen.get(s, 0) < v:
                    waits.append((s, v))
                    E.seen[s] = v

            def emit(e, waits=waits):
                for s, v in waits:
                    e.wait_ge(s.h, v)
            E.ops.append(emit)

    def wait_all(self, eng, resources):
        E = self.engs[eng]
        waits = self._collect(E, resources, [])

        def emit(e, waits=waits):
            for s, v in waits:
                e.wait_ge(s.h, v)
        E.ops.append(emit)

    def emit_all(self):
        nc = self.nc
        with nc.Block() as block:
            @block.tensor
            def _(e):
                for f in self.engs["pe"].ops:
                    f(e)

            @block.scalar
            def _(e):
                for f in self.engs["act"].ops:
                    f(e)

            @block.vector
            def _(e):
                for f in self.engs["dve"].ops:
                    f(e)

            @block.gpsimd
            def _(e):
                for f in self.engs["pool"].ops:
                    f(e)

            @block.sync
            def _(e):
                for f in self.engs["sp"].ops:
                    f(e)


import math
from contextlib import ExitStack

S = 4096
D = 1024
NBLK = 8
NT = 32
DFF = 2816
V0, O0, G0, CQ0, CKV0, KR0, GM0, INW = 1024, 2048, 3072, 3088, 3344, 3600, 3664, 5712
EPS = 1e-6
LN_SQ = math.log(128 ** -0.5)
A_SCALE = 192 ** -0.5

WSHAPES = {
    "w_in": (1024, 5712), "w_mo": (1024, 1024), "w_uq": (256, 1536), "w_ukv": (256, 2048),
    "w_ao": (1024, 1024), "w_out": (1024, 1024), "w_gate": (1024, 2816), "w_up": (1024, 2816),
    "w_down": (2816, 1024), "w_ple": (256, 1024), "w_ple_gate": (1024, 1024),
}
GAINS = {"w_in": "g_mix", "w_mo": "g_mhead", "w_uq": "g_q", "w_ukv": "g_kv", "w_gate": "g_ffn", "w_up": "g_ffn"}
GSHAPES = {"g_mix": 1024, "g_mhead": 1024, "g_q": 256, "g_kv": 256, "g_ffn": 1024}


def build(NSEQ, dbg=False, phases=(1, 2, 3, 4)):
    nc = bass.Bass("TRN2", target_bir_lowering=False)
    din = lambda n, sh, dt=F32: nc.dram_tensor(n, list(sh), dt, kind="ExternalInput").ap()
    dscr = lambda n, sh, dt: nc.dram_tensor(n, list(sh), dt, kind="Internal").ap()
    xs = din("xs", [NSEQ, S, D])
    ps = din("ps", [NSEQ, S, 256])
    Wf = {k: din(k, sh) for k, sh in WSHAPES.items()}
    Gd = {k: din(k, [1, n]) for k, n in GSHAPES.items()}
    g_final = din("g_final", [1, D])
    b_gate = din("b_gate", [1, 16])
    conv_w = din("conv_w", [5, 1024])
    c_ident = din("c_ident", [128, 128])
    c_maskf = din("c_maskf", [128, 128])
    c_maskb = din("c_maskb", [128, 128])
    y = nc.dram_tensor("y", [NSEQ, S, D], F32, kind="ExternalOutput").ap()
    dbg_out = {}
    if dbg == "P1":
        dbg_out["d_colsT"] = nc.dram_tensor("d_colsT", [128, 512], F32, kind="ExternalOutput").ap()
    if dbg == "M1":
        dbg_out["d_oT"] = nc.dram_tensor("d_oT", [128, 512], BF16, kind="ExternalOutput").ap()
    if isinstance(dbg, str) and dbg.startswith("M:"):
        dbg_out["d_oTs0"] = nc.dram_tensor("d_oTs0", [128, 512], BF16, kind="ExternalOutput").ap()
        dbg_out["d_oTs1"] = nc.dram_tensor("d_oTs1", [128, 512], BF16, kind="ExternalOutput").ap()
    if dbg == "E":
        for n_, sh_ in (("d_cqnT", [128, 2, S]), ("d_ckvnT", [128, 2, S]), ("d_krT", [64, S]),
                        ("d_KhT", [128, S]), ("d_QhT", [128, S]), ("d_QrT", [64, S]), ("d_Vh", [128, S])):
            dbg_out[n_] = nc.dram_tensor(n_, sh_, BF16, kind="ExternalOutput").ap()
    Wb = {k: dscr(k + "_b", sh, BF16) for k, sh in WSHAPES.items()}
    w_kr_sw = dscr("w_kr_sw", [1024, 64], BF16)
    w_uq_sw = dscr("w_uq_sw", [256, 512], BF16)
    cs_scr = dscr("cs_scr", [2, 64, S], F32)
    qk_scr = (nc.dram_tensor("qk_scr", [8, 128, S], BF16, kind="ExternalOutput").ap() if dbg == "P1" else dscr("qk_scr", [8, 128, S], BF16))
    v_scr = (nc.dram_tensor("v_scr", [S, 1024], BF16, kind="ExternalOutput").ap() if dbg == "P1" else dscr("v_scr", [S, 1024], BF16))
    nb_scr = (nc.dram_tensor("nb_scr", [2, 4, S], F32, kind="ExternalOutput").ap() if dbg == "P1" else dscr("nb_scr", [2, 4, S], F32))
    hf_scr = dscr("hf_scr", [S, 256], F32)
    hmT_scr = (nc.dram_tensor("hmT_scr", [8, 128, S], BF16, kind="ExternalOutput").ap() if dbg == "L1" else dscr("hmT_scr", [8, 128, S], BF16))
    if isinstance(dbg, str) and dbg.startswith("M:"):
        oT_scr = nc.dram_tensor("oT_scr", [8, 128, S], BF16, kind="ExternalOutput").ap()
    else:
        oT_scr = dscr("oT_scr", [8, 128, S], BF16)
    cqnT_scr = dscr("cqnT_scr", [2, 128, S], BF16)
    ckvnT_scr = dscr("ckvnT_scr", [2, 128, S], BF16)
    krT_scr = dscr("krT_scr", [64, S], BF16)

    _res_registry = {}
    _ResClass = Res

    def RES(name):
        r = _res_registry.get(name)
        if r is None:
            r = _res_registry[name] = _ResClass(name)
        return r

    with ExitStack() as st:
        fw = FW(nc, st)
        T = lambda name, shape, dt: st.enter_context(nc.sbuf_tensor(name, shape, dt))
        identf = T("identf", [128, 128], F32)
        identb = T("identb", [128, 128], BF16)
        maskf = T("maskf", [128, 128], F32)
        maskb = T("maskb", [128, 128], F32)
        onesb = T("onesb", [128, 128], BF16)
        gfin = T("gfin", [128, D], F32)
        convw = T("convw", [128, 5, 8], F32)
        gcol = {k: T("gc_" + k, [128, n // 128], F32) for k, n in GSHAPES.items()}
        gneg = {k: T("gn_" + k, [128, GSHAPES[k] // 128], F32) for k in ("g_mix", "g_q")}
        bg = T("bg", [4, 4], F32)
        colsT = T("colsT", [128, 4, NT, 4], F32)
        wsl = [T("wsl%d" % i, [128, 4096], BF16) for i in range(6)]
        r_wsl = [RES("wsl%d" % i) for i in range(6)]
        AR = 34816
        arena = T("arena", [128, AR], F32)
        arena_b = arena.bitcast(BF16)
        arena_i = arena.bitcast(I32)
        banks = [st.enter_context(nc.psum_tensor("pb%d" % i, [128, 512], F32)) for i in range(8)]
        r_bank = [RES("pb%d" % i) for i in range(8)]
        r_const = RES("const")
        state = {"bank": 0, "wsl": 0, "ev": 0}

        def VF(off, n, parts=128):
            assert off % 4 == 0 and off // 4 + n <= AR
            return arena[0:parts, off // 4: off // 4 + n]

        def VB(off, n, parts=128):
            assert off % 2 == 0 and off // 2 + n <= 2 * AR
            return arena_b[0:parts, off // 2: off // 2 + n]

        def nbank():
            i = state["bank"]
            state["bank"] = (i + 1) % 8
            return banks[i], r_bank[i]

        def evac_eng():
            state["ev"] ^= 1
            return "act" if state["ev"] else "dve"

        def copy_op(eng, out, in_, reads, writes):
            if eng == "act":
                fw.op("act", lambda e: e.copy(out, in_), reads=reads, writes=writes)
            else:
                fw.op(eng, lambda e: e.tensor_copy(out, in_), reads=reads, writes=writes)

        def mm(out_ap, pairs, reads, wres):
            n = len(pairs)
            for i, (l, r) in enumerate(pairs):
                fw.op("pe", lambda e, l=l, r=r, i=i: e.matmul(out_ap, l, r, start=(i == 0), stop=(i == n - 1)),
                      reads=reads, writes=[wres], signal=(i == n - 1))

        def transp(out_ap, in_ap, ident, reads, wres, signal=True):
            fw.op("pe", lambda e: e.transpose(out_ap, in_ap, ident), reads=reads, writes=[wres], signal=signal)

        r_W = {k: RES("@" + k) for k in WSHAPES}
        r_W["w_kr_sw"] = RES("@w_kr_sw")
        r_W["w_uq_sw"] = RES("@w_uq_sw")

        def wload(name, W_ap, k0, nk, c0, ncols):
            i = state["wsl"]
            state["wsl"] = (i + 1) % 6
            assert nk * ncols <= 4096
            view = wsl[i][:, 0:nk * ncols].rearrange("p (k n) -> p k n", k=nk)
            src = W_ap[k0 * 128:(k0 + nk) * 128, c0:c0 + ncols].rearrange("(k p) n -> p k n", p=128)
            fw.dma("sp", view, src, reads=[r_W[name]], writes=[r_wsl[i]])
            return view, r_wsl[i]

        fw.dma("sp", identf[:], c_ident, writes=[r_const])
        fw.dma("sp", maskf[:], c_maskf, writes=[r_const])
        fw.dma("sp", maskb[:], c_maskb, writes=[r_const])
        fw.dma("sp", gfin[:], g_final.partition_broadcast(128).rearrange("p o n -> p (o n)"), writes=[r_const])
        for j in range(5):
            fw.dma("sp", convw[:, j, :], conv_w[j:j + 1, :].rearrange("o (c p) -> p (o c)", p=128), writes=[r_const], allow_slow_non_contiguous=True)
        for k in GSHAPES:
            fw.dma("sp", gcol[k][:], Gd[k].rearrange("o (k p) -> p (o k)", p=128), writes=[r_const], allow_slow_non_contiguous=True)
        fw.dma("sp", bg[:], b_gate.rearrange("o (g h) -> h (o g)", h=4), writes=[r_const], allow_slow_non_contiguous=True)
        fw.op("dve", lambda e: e.tensor_copy(identb[:], identf[:]), reads=[r_const], writes=[r_const])
        fw.op("dve", lambda e: e.memset(onesb[:], 1.0), writes=[r_const])
        for k in gneg:
            fw.op("dve", lambda e, k=k: e.tensor_scalar(gneg[k][:], gcol[k][:], -1.0, None, ALU.mult), reads=[r_const], writes=[r_const])
        fw.barrier()

        stg_in = [VF(i * 8192, 2048) for i in range(2)]
        stg_out = [VB(16384 + i * 4096, 2048) for i in range(2)]
        r_si = [RES("stg_in%d" % i) for i in range(2)]
        r_so = [RES("stg_out%d" % i) for i in range(2)]
        pw = {"i": 0}

        def prep(name):
            K, N = WSHAPES[name]
            g = gcol[GAINS[name]] if name in GAINS else None
            for kc in range(K // 128):
                for c0 in range(0, N, 2048):
                    n_ = min(2048, N - c0)
                    i = pw["i"] & 1
                    pw["i"] += 1
                    fw.dma("sp", stg_in[i][:, 0:n_], Wf[name][kc * 128:(kc + 1) * 128, c0:c0 + n_], writes=[r_si[i]])
                    eng = evac_eng()
                    if g is None:
                        copy_op(eng, stg_out[i][:, 0:n_], stg_in[i][:, 0:n_], [r_si[i]], [r_so[i]])
                    elif eng == "act":
                        fw.op("act", lambda e, i=i, kc=kc, n_=n_, g=g: e.activation(stg_out[i][:, 0:n_], stg_in[i][:, 0:n_], AF.Copy, scale=g[:, kc:kc + 1]),
                              reads=[r_si[i], r_const], writes=[r_so[i]])
                    else:
                        fw.op("dve", lambda e, i=i, kc=kc, n_=n_, g=g: e.tensor_scalar(stg_out[i][:, 0:n_], stg_in[i][:, 0:n_], g[:, kc:kc + 1], None, ALU.mult),
                              reads=[r_si[i], r_const], writes=[r_so[i]])
                    fw.dma("pool", Wb[name][kc * 128:(kc + 1) * 128, c0:c0 + n_], stg_out[i][:, 0:n_], reads=[r_so[i]], writes=[r_W[name]])

        def prep_sw(dst, dname, src, K, bases, g, gn):
            for kc in range(K // 128):
                i = pw["i"] & 1
                pw["i"] += 1
                nb_ = len(bases)
                for j, b in enumerate(bases):
                    fw.dma("sp", stg_in[i][:, j * 64:(j + 1) * 64], src[kc * 128:(kc + 1) * 128, b:b + 64], writes=[r_si[i]])
                for j in range(nb_):
                    fw.op("dve", lambda e, i=i, j=j, kc=kc: e.tensor_scalar(stg_out[i][:, j * 64:j * 64 + 32], stg_in[i][:, j * 64 + 32:j * 64 + 64], gn[:, kc:kc + 1], None, ALU.mult),
                          reads=[r_si[i], r_const], writes=[r_so[i]])
                    fw.op("dve", lambda e, i=i, j=j, kc=kc: e.tensor_scalar(stg_out[i][:, j * 64 + 32:j * 64 + 64], stg_in[i][:, j * 64:j * 64 + 32], g[:, kc:kc + 1], None, ALU.mult),
                          reads=[r_si[i], r_const], writes=[r_so[i]])
                fw.dma("pool", dst[kc * 128:(kc + 1) * 128, 0:64 * nb_], stg_out[i][:, 0:64 * nb_], reads=[r_so[i]], writes=[r_W[dname]])

        for name in WSHAPES:
            prep(name)
        prep_sw(w_kr_sw, "w_kr_sw", Wf["w_in"], 1024, [KR0], gcol["g_mix"], gneg["g_mix"])
        prep_sw(w_uq_sw, "w_uq_sw", Wf["w_uq"], 256, [h * 192 + 128 for h in range(8)], gcol["g_q"], gneg["g_q"])

        r_tr = RES("trig")
        TB0 = 24576
        posf = VF(TB0, S, 64)
        tq = VF(TB0 + 16384, S, 64)
        ti = arena_i[0:64, (TB0 + 32768) // 4:(TB0 + 32768) // 4 + S]
        tk = VF(TB0 + 49152, S, 64)
        tu = VF(TB0 + 65536, S, 64)
        pidx = T("pidx", [64, 1], F32)
        invf = T("invf", [64, 1], F32)
        pidx_i = T("pidx_i", [64, 1], I32)
        fw.op("pool", lambda e: e.iota(ti, [[1, S]], base=0, channel_multiplier=0), writes=[r_tr])
        fw.op("pool", lambda e: e.iota(pidx_i[0:32, :], [[0, 1]], base=0, channel_multiplier=1), writes=[r_tr])
        fw.op("pool", lambda e: e.iota(pidx_i[32:64, :], [[0, 1]], base=0, channel_multiplier=1), writes=[r_tr])
        fw.op("dve", lambda e: e.tensor_copy(posf, ti), reads=[r_tr], writes=[r_tr])
        fw.op("dve", lambda e: e.tensor_copy(pidx[:], pidx_i[:]), reads=[r_tr], writes=[r_tr])
        fw.op("act", lambda e: e.activation(invf[:], pidx[:], AF.Exp, scale=-math.log(10000.0) / 32.0), reads=[r_tr], writes=[r_tr])
        fw.op("dve", lambda e: e.tensor_scalar(tq, posf, invf[:, 0:1], 1.0 / (2 * math.pi), ALU.mult, ALU.mult), reads=[r_tr], writes=[r_tr])
        for which, shift in ((1, 0.0), (0, 0.25)):
            fw.op("dve", lambda e, shift=shift: e.tensor_scalar(tu, tq, shift, None, ALU.add), reads=[r_tr], writes=[r_tr])
            fw.op("dve", lambda e: e.tensor_copy(ti, tu), reads=[r_tr], writes=[r_tr])
            fw.op("dve", lambda e: e.tensor_copy(tk, ti), reads=[r_tr], writes=[r_tr])
            fw.op("dve", lambda e: e.tensor_tensor(tu, tu, tk, ALU.subtract), reads=[r_tr], writes=[r_tr])
            fw.op("dve", lambda e: e.tensor_scalar(tk, tu, 0.5, None, ALU.is_gt), reads=[r_tr], writes=[r_tr])
            fw.op("dve", lambda e: e.tensor_tensor(tu, tu, tk, ALU.subtract), reads=[r_tr], writes=[r_tr])
            fw.op("dve", lambda e: e.tensor_scalar(tk, tu, -0.5, None, ALU.is_lt), reads=[r_tr], writes=[r_tr])
            fw.op("dve", lambda e: e.tensor_tensor(tu, tu, tk, ALU.add), reads=[r_tr], writes=[r_tr])
            fw.op("act", lambda e: e.activation(tk, tu, AF.Sin, scale=2 * math.pi), reads=[r_tr], writes=[r_tr])
            fw.dma("pool", cs_scr[which], tk, reads=[r_tr], writes=[RES("@cs")], sem_res=r_tr)
        fw.barrier()

        st1 = [T("st1_%d" % i, [128, 8], F32) for i in range(2)]
        r_st1 = [RES("st1_%d" % i) for i in range(2)]
        cmask = T("cmask", [4, 512], F32)
        fw.op("pool", lambda e: e.memset(cmask[:], 1.0), writes=[r_const])
        for j in range(4):
            fw.op("pool", lambda e, j=j: e.memset(cmask[:, j * 128:j * 128 + 1], 0.0), writes=[r_const])
        fw.barrier()

        def rms_tile(x_ap, xn_ap, stt, r_x, r_xn, r_stt, n):
            fw.op("act", lambda e: e.activation(xn_ap, x_ap, AF.Square, accum_out=stt[:, 0:1]), reads=[r_x], writes=[r_xn, r_stt])
            fw.op("act", lambda e: e.activation(stt[:, 1:2], stt[:, 0:1], AF.Sqrt, bias=EPS, scale=1.0 / n), reads=[r_stt], writes=[r_stt])
            fw.op("dve", lambda e: e.reciprocal(stt[:, 2:3], stt[:, 1:2]), reads=[r_stt], writes=[r_stt])
            fw.op("dve", lambda e: e.tensor_scalar(xn_ap, x_ap, stt[:, 2:3], None, ALU.mult), reads=[r_x, r_stt], writes=[r_xn])

        def transpose_to(dst3, src_ap, nk, reads, r_dst):
            bk, rb = nbank()
            pbt = bk.bitcast(BF16)
            for k in range(nk):
                transp(pbt[:, k * 128:(k + 1) * 128], src_ap[:, k * 128:(k + 1) * 128], identb[:], reads + [r_const], rb, signal=(k == nk - 1))
            eng = evac_eng()
            copy_op(eng, dst3, pbt[:, 0:nk * 128].rearrange("p (k n) -> p k n", k=nk), [rb], [r_dst])

        def phase1(sq):
            hTb = [VB(i * 8192, 4096).rearrange("p (k n) -> p k n", k=8) for i in range(2)]
            r_hTb = [RES("hTb%d" % i) for i in range(2)]
            xa = [VF(16384 + i * 4096, 1024) for i in range(2)]
            r_xa = [RES("xa%d" % i) for i in range(2)]
            xnb = [VB(24576 + i * 2048, 1024) for i in range(2)]
            r_xnb = [RES("xnb%d" % i) for i in range(2)]
            U = [VF(28672 + ct * 2064, 516) for ct in range(8)]
            r_U = [RES("U%d" % ct) for ct in range(8)]
            acc = [VF(45184 + i * 2048, 512) for i in range(2)]
            r_acc = [RES("acc%d" % i) for i in range(2)]
            qkb = [VB(49280 + i * 1024, 512) for i in range(2)]
            r_qkb = [RES("qkb%d" % i) for i in range(2)]
            vst = [VB(51328 + i * 2048, 1024) for i in range(4)]
            r_vst = [RES("vst%d" % i) for i in range(4)]
            GB = 59520
            gt = [VF(GB + i * 2048, 512, 4) for i in range(16)]
            r_g = RES("gates")
            LB = GB + 32768
            lat = VB(LB, 2048).rearrange("p (k n) -> p k n", k=4)
            r_lat = RES("lat")
            cnb = [VB(LB + 4096 + i * 1024, 512) for i in range(2)]
            r_cnb = [RES("cnb%d" % i) for i in range(2)]
            cst = VF(LB + 6144, 1024, 64).rearrange("p (c n) -> p c n", c=2)
            r_cst = RES("cst")
            krt = [VF(LB + 10240 + i * 2048, 512, 64) for i in range(2)]
            r_krt = RES("krt")
            krb = VB(LB + 14336, 512, 64)
            r_krb = RES("krb")
            r_qk = RES("@qk_scr"); r_v = RES("@v_scr"); r_nb = RES("@nb_scr")
            r_cq = RES("@cq"); r_ckv = RES("@ckv"); r_kr = RES("@kr")
            xi = 0
            for b in range(NBLK):
                tok0 = b * 512
                j = b & 1
                for tt in range(4):
                    i = xi & 1
                    xi += 1
                    fw.dma("sp", xa[i], xs[sq, tok0 + tt * 128: tok0 + (tt + 1) * 128, :], writes=[r_xa[i]])
                    rms_tile(xa[i], xnb[i], st1[i], r_xa[i], r_xnb[i], r_st1[i], 1024)
                    transpose_to(hTb[j][:, :, tt * 128:(tt + 1) * 128], xnb[i], 8, [r_xnb[i]], r_hTb[j])
                for grp in range(2):
                    wv, rw = wload("w_in", Wb["w_in"], 0, 8, grp * 512, 512)
                    for cl in range(4):
                        ct = grp * 4 + cl
                        bk, rb = nbank()
                        mm(bk[:, :], [(wv[:, kc, cl * 128:(cl + 1) * 128], hTb[j][:, kc, :]) for kc in range(8)], [rw, r_hTb[j]], rb)
                        if b == 0:
                            fw.op("pool", lambda e, ct=ct: e.memset(U[ct][:, 0:4], 0.0), writes=[r_U[ct]])
                        fw.op("act", lambda e, ct=ct, bk=bk: e.copy(U[ct][:, 4:516], bk[:, :]), reads=[rb], writes=[r_U[ct]])
                        a = ct & 1

                        def conv(n, a=a, ct=ct):
                            fw.op("dve", lambda e: e.tensor_scalar(acc[a][:, 0:n], U[ct][:, 0:n], convw[:, 0, ct:ct + 1], None, ALU.mult),
                                  reads=[r_U[ct], r_const], writes=[r_acc[a]])
                            for jj in range(1, 5):
                                fw.op("dve", lambda e, jj=jj: e.scalar_tensor_tensor(acc[a][:, 0:n], U[ct][:, jj:jj + n], convw[:, jj, ct:ct + 1], acc[a][:, 0:n], ALU.mult, ALU.add),
                                      reads=[r_U[ct], r_const, r_acc[a]], writes=[r_acc[a]])
                            fw.op("act", lambda e: e.activation(qkb[a][:, 0:n], acc[a][:, 0:n], AF.Silu), reads=[r_acc[a]], writes=[r_qkb[a]])
                        conv(512)
                        if b == 0:
                            fw.dma("pool", qk_scr[ct, :, 0:510], qkb[a][:, 2:512], reads=[r_qkb[a]], writes=[r_qk])
                        else:
                            fw.dma("pool", qk_scr[ct, :, tok0 - 2:tok0 + 510], qkb[a][:, 0:512], reads=[r_qkb[a]], writes=[r_qk])
                        fw.op("pool", lambda e, ct=ct: e.tensor_copy(U[ct][:, 0:4], U[ct][:, 512:516]), reads=[r_U[ct]], writes=[r_U[ct]])
                        if b == NBLK - 1:
                            fw.op("pool", lambda e, ct=ct: e.memset(U[ct][:, 4:8], 0.0), writes=[r_U[ct]])
                            conv(2)
                            fw.dma("pool", qk_scr[ct, :, S - 2:S], qkb[a][:, 0:2], reads=[r_qkb[a]], writes=[r_qk])
                for half in range(2):
                    wv, rw = wload("w_in", Wb["w_in"], 0, 8, V0 + half * 512, 512)
                    for tt in range(4):
                        bk, rb = nbank()
                        mm(bk[:, :], [(hTb[j][:, kc, tt * 128:(tt + 1) * 128], wv[:, kc, :]) for kc in range(8)], [rw, r_hTb[j]], rb)
                        copy_op(evac_eng(), vst[tt][:, half * 512:(half + 1) * 512], bk[:, :], [rb], [r_vst[tt]])
                for tt in range(4):
                    fw.dma("pool", v_scr[tok0 + tt * 128: tok0 + (tt + 1) * 128, :], vst[tt], reads=[r_vst[tt]], writes=[r_v])
                wv, rw = wload("w_in", Wb["w_in"], 0, 8, G0, 16)
                for g in range(4):
                    bk, rb = nbank()
                    mm(bk[0:4, :], [(wv[:, kc, 4 * g:4 * g + 4], hTb[j][:, kc, :]) for kc in range(8)], [rw, r_hTb[j]], rb)
                    fw.op("act", lambda e, g=g, bk=bk: e.activation(gt[g], bk[0:4, :], AF.Identity, bias=bg[:, g:g + 1]), reads=[rb, r_const], writes=[r_g])
                IFt, IBt, FFt, FBt = gt[0], gt[1], gt[2], gt[3]
                SPf, SPb, csf, csb, NBf, NBb, Af, Ab, Gf, Gb, tmp = gt[4:15]
                G = lambda fn: fw.op(fn[0], fn[1], reads=[r_g, r_const], writes=[r_g])
                for FX, SP in ((FFt, SPf), (FBt, SPb)):
                    G(("act", lambda e, FX=FX: e.activation(tmp, FX, AF.Exp, scale=-1.0)))
                    G(("act", lambda e, SP=SP: e.activation(SP, tmp, AF.Ln, bias=1.0)))
                G(("dve", lambda e: e.tensor_tensor_scan(csf, cmask[:], SPf, 0.0, ALU.mult, ALU.add)))
                G(("dve", lambda e: e.tensor_tensor_scan(csb, cmask[:], SPb, 0.0, ALU.mult, ALU.add)))
                v3 = lambda t: t.rearrange("p (a n) -> p a n", a=4)
                tot = lambda t: t.rearrange("p (a n) -> p a n", a=4)[:, :, 127:128].to_broadcast([4, 4, 128])
                G(("dve", lambda e: e.tensor_scalar(NBf, csf, -1.0, None, ALU.mult)))
                G(("dve", lambda e: e.tensor_tensor(Af, IFt, csf, ALU.add)))
                G(("dve", lambda e: e.tensor_tensor(v3(tmp), v3(Af), tot(csf), ALU.subtract)))
                G(("act", lambda e: e.activation(Gf, tmp, AF.Exp)))
                G(("dve", lambda e: e.tensor_tensor(NBb, SPb, csb, ALU.subtract)))
                G(("dve", lambda e: e.tensor_tensor(v3(NBb), v3(NBb), tot(csb), ALU.add)))
                G(("dve", lambda e: e.tensor_tensor(Ab, IBt, NBb, ALU.add)))
                G(("dve", lambda e: e.tensor_tensor(v3(tmp), v3(Ab), tot(csb), ALU.subtract)))
                G(("act", lambda e: e.activation(Gb, tmp, AF.Exp)))
                G(("dve", lambda e: e.tensor_scalar(NBb, NBb, -1.0, None, ALU.mult)))
                fw.dma("pool", nb_scr[0, :, tok0:tok0 + 512], NBf, reads=[r_g], writes=[r_nb])
                fw.dma("pool", nb_scr[1, :, tok0:tok0 + 512], NBb, reads=[r_g], writes=[r_nb])
                bk, rb = nbank()
                for qi, Q in enumerate((Af, Ab, Gf, Gb)):
                    for tt in range(4):
                        c0 = (qi * 4 + tt) * 4
                        transp(bk[:, c0:c0 + 4], Q[:, tt * 128:(tt + 1) * 128], identf[0:4, 0:4], [r_g, r_const], rb, signal=(qi == 3 and tt == 3))
                fw.op("dve", lambda e, bk=bk, b=b: e.tensor_copy(colsT[:, :, b * 4:(b + 1) * 4, :], bk[:, 0:64].rearrange("p (q t h) -> p q t h", q=4, t=4)),
                      reads=[rb], writes=[r_const])
                wv, rw = wload("w_in", Wb["w_in"], 0, 8, CQ0, 512)
                for tt in range(4):
                    bk, rb = nbank()
                    mm(bk[:, :], [(hTb[j][:, kc, tt * 128:(tt + 1) * 128], wv[:, kc, :]) for kc in range(8)], [rw, r_hTb[j]], rb)
                    i = tt & 1
                    stt = st1[i]
                    fw.op("act", lambda e, i=i, bk=bk, stt=stt: e.activation(cnb[i][:, 0:256], bk[:, 0:256], AF.Square, accum_out=stt[:, 4:5]), reads=[rb], writes=[r_cnb[i], r_st1[i]])
                    fw.op("act", lambda e, i=i, bk=bk, stt=stt: e.activation(cnb[i][:, 256:512], bk[:, 256:512], AF.Square, accum_out=stt[:, 5:6]), reads=[rb], writes=[r_cnb[i], r_st1[i]])
                    fw.op("act", lambda e, stt=stt: e.activation(stt[:, 6:8], stt[:, 4:6], AF.Sqrt, bias=EPS, scale=1.0 / 256), reads=[r_st1[i]], writes=[r_st1[i]])
                    fw.op("dve", lambda e, stt=stt: e.reciprocal(stt[:, 4:6], stt[:, 6:8]), reads=[r_st1[i]], writes=[r_st1[i]])
                    fw.op("dve", lambda e, i=i, bk=bk, stt=stt: e.tensor_scalar(cnb[i][:, 0:256], bk[:, 0:256], stt[:, 4:5], None, ALU.mult), reads=[rb, r_st1[i]], writes=[r_cnb[i]])
                    fw.op("act", lambda e, i=i, bk=bk, stt=stt: e.activation(cnb[i][:, 256:512], bk[:, 256:512], AF.Copy, scale=stt[:, 5:6]), reads=[rb, r_st1[i]], writes=[r_cnb[i]])
                    transpose_to(lat[:, :, tt * 128:(tt + 1) * 128], cnb[i], 4, [r_cnb[i]], r_lat)
                fw.dma("pool", cqnT_scr[:, :, tok0:tok0 + 512].rearrange("k p n -> p k n"), lat[:, 0:2, :], reads=[r_lat], writes=[r_cq])
                fw.dma("pool", ckvnT_scr[:, :, tok0:tok0 + 512].rearrange("k p n -> p k n"), lat[:, 2:4, :], reads=[r_lat], writes=[r_ckv])
                wa, rwa = wload("w_in", Wb["w_in"], 0, 8, KR0, 64)
                wb_, rwb = wload("w_kr_sw", w_kr_sw, 0, 8, 0, 64)
                bka, rba = nbank()
                mm(bka[0:64, :], [(wa[:, kc, :], hTb[j][:, kc, :]) for kc in range(8)], [rwa, r_hTb[j]], rba)
                bkb, rbb = nbank()
                mm(bkb[0:64, :], [(wb_[:, kc, :], hTb[j][:, kc, :]) for kc in range(8)], [rwb, r_hTb[j]], rbb)
                fw.dma("sp", cst, cs_scr[:, :, tok0:tok0 + 512].rearrange("c p n -> p c n"), writes=[r_cst])
                fw.op("dve", lambda e, bka=bka: e.tensor_tensor(krt[0], bka[0:64, :], cst[:, 0, :], ALU.mult), reads=[rba, r_cst], writes=[r_krt])
                fw.op("dve", lambda e, bkb=bkb: e.tensor_tensor(krt[1], bkb[0:64, :], cst[:, 1, :], ALU.mult), reads=[rbb, r_cst], writes=[r_krt])
                fw.op("pool", lambda e: e.tensor_tensor(krb, krt[0], krt[1], ALU.add), reads=[r_krt], writes=[r_krb])
                fw.dma("pool", krT_scr[:, tok0:tok0 + 512], krb, reads=[r_krb], writes=[r_kr])
            fw.op("dve", lambda e: e.tensor_scalar(colsT[:, 0:2], colsT[:, 0:2], LN_SQ, None, ALU.add), reads=[r_const], writes=[r_const])
            if dbg == "P1":
                fw.dma("pool", dbg_out["d_colsT"], colsT[:].rearrange("p q t h -> p (q t h)"), reads=[r_const], writes=[RES("@dbgP1")], sem_res=r_st1[0])

        def phase3(sq):
            xa4 = VF(0, 4096).rearrange("p (t n) -> p t n", t=4)
            r_xa4 = [RES("xa4_%d" % i) for i in range(4)]
            xn4 = VB(16384, 4096).rearrange("p (t n) -> p t n", t=4)
            r_xn4 = [RES("xn4_%d" % i) for i in range(4)]
            k8 = lambda ap: ap.rearrange("p (k n) -> p k n", k=8)
            hTb = k8(VB(24576, 4096)); r_hTb = RES("p3hTb")
            sgo = k8(VB(32768, 4096)); r_sgo = RES("sgo")
            hmT = k8(VB(40960, 4096)); r_hmT = RES("hmTb")
            oTb = k8(VB(49152, 4096)); r_oTb = RES("oTb")
            mrg = k8(VB(57344, 4096)); r_mrg = RES("mrg")
            sg = [VF(65536 + i * 2048, 512) for i in range(2)]; r_sg = [RES("sg%d" % i) for i in range(2)]
            t12 = [VF(69632 + i * 2048, 512) for i in range(2)]; r_t12 = [RES("t12_%d" % i) for i in range(2)]
            actT = VB(73728, 22 * 512).rearrange("p (k n) -> p k n", k=22); r_actT = RES("actT")
            pa = VF(96256, 1024).rearrange("p (t n) -> p t n", t=4); r_pa = RES("pa")
            pn = VB(100352, 1024).rearrange("p (t n) -> p t n", t=4); r_pn = RES("pn")
            pT = VB(102400, 1024).rearrange("p (k n) -> p k n", k=2); r_pT = RES("pT")
            r_y = RES("@y")
            for b in range(NBLK):
                tok0 = b * 512
                tsl = lambda tt: slice(tt * 128, (tt + 1) * 128)
                for tt in range(4):
                    i = tt & 1
                    fw.dma("sp", xa4[:, tt, :], xs[sq, tok0 + tt * 128: tok0 + (tt + 1) * 128, :], writes=[r_xa4[tt]])
                    rms_tile(xa4[:, tt, :], xn4[:, tt, :], st1[i], r_xa4[tt], r_xn4[tt], r_st1[i], 1024)
                    transpose_to(hTb[:, :, tsl(tt)], xn4[:, tt, :], 8, [r_xn4[tt]], r_hTb)
                fw.dma("sp", pa, ps[sq, tok0:tok0 + 512, :].rearrange("(t p) n -> p t n", p=128), writes=[r_pa])
                if dbg:
                    fw.op("pool", lambda e: e.memset(hmT, 0.0), writes=[r_hmT])
                else:
                    fw.dma("sp", hmT, hmT_scr[:, :, tok0:tok0 + 512].rearrange("k p n -> p k n"), reads=[RES("@hm_dummy")], writes=[r_hmT])
                if dbg == 2:
                    fw.op("pool", lambda e: e.memset(oTb, 0.0), writes=[r_oTb])
                else:
                    fw.dma("sp", oTb, oT_scr[:, :, tok0:tok0 + 512].rearrange("k p n -> p k n"), reads=[RES("@o_dummy")], writes=[r_oTb])
                for grp in range(2):
                    wv, rw = wload("w_in", Wb["w_in"], 0, 8, O0 + grp * 512, 512)
                    for cl in range(4):
                        bk, rb = nbank()
                        mm(bk[:, :], [(wv[:, kc, cl * 128:(cl + 1) * 128], hTb[:, kc, :]) for kc in range(8)], [rw, r_hTb], rb)
                        fw.op("act", lambda e, bk=bk, c=grp * 4 + cl: e.activation(sgo[:, c, :], bk[:, :], AF.Sigmoid), reads=[rb], writes=[r_sgo])
                fw.op("dve", lambda e: e.tensor_tensor(sgo, sgo, hmT, ALU.mult), reads=[r_sgo, r_hmT], writes=[r_sgo])
                for grp in range(2):
                    wm, rwm = wload("w_mo", Wb["w_mo"], 0, 8, grp * 512, 512)
                    wa, rwa = wload("w_ao", Wb["w_ao"], 0, 8, grp * 512, 512)
                    wg1, rw1 = wload("w_in", Wb["w_in"], 0, 8, GM0 + grp * 512, 512)
                    wg2, rw2 = wload("w_in", Wb["w_in"], 0, 8, GM0 + 1024 + grp * 512, 512)
                    for cl in range(4):
                        c = grp * 4 + cl
                        cs_ = slice(cl * 128, (cl + 1) * 128)
                        bm, rbm = nbank(); mm(bm[:, :], [(wm[:, kc, cs_], sgo[:, kc, :]) for kc in range(8)], [rwm, r_sgo], rbm)
                        ba, rba = nbank(); mm(ba[:, :], [(wa[:, kc, cs_], oTb[:, kc, :]) for kc in range(8)], [rwa, r_oTb], rba)
                        b1, rb1 = nbank(); mm(b1[:, :], [(wg1[:, kc, cs_], hTb[:, kc, :]) for kc in range(8)], [rw1, r_hTb], rb1)
                        b2, rb2 = nbank(); mm(b2[:, :], [(wg2[:, kc, cs_], hTb[:, kc, :]) for kc in range(8)], [rw2, r_hTb], rb2)
                        fw.op("act", lambda e, b1=b1: e.activation(sg[0], b1[:, :], AF.Sigmoid), reads=[rb1], writes=[r_sg[0]])
                        fw.op("act", lambda e, b2=b2: e.activation(sg[1], b2[:, :], AF.Sigmoid), reads=[rb2], writes=[r_sg[1]])
                        fw.op("dve", lambda e, bm=bm: e.tensor_tensor(t12[0], bm[:, :], sg[0], ALU.mult), reads=[rbm, r_sg[0]], writes=[r_t12[0]])
                        fw.op("dve", lambda e, ba=ba: e.tensor_tensor(t12[1], ba[:, :], sg[1], ALU.mult), reads=[rba, r_sg[1]], writes=[r_t12[1]])
                        fw.op("pool", lambda e, c=c: e.tensor_tensor(mrg[:, c, :], t12[0], t12[1], ALU.add), reads=r_t12, writes=[r_mrg])
                for half in range(2):
                    wo, rwo = wload("w_out", Wb["w_out"], 0, 8, half * 512, 512)
                    for tt in range(4):
                        bk, rb = nbank()
                        mm(bk[:, :], [(mrg[:, kc, tsl(tt)], wo[:, kc, :]) for kc in range(8)], [rwo, r_mrg], rb)
                        xsl = xa4[:, tt, half * 512:(half + 1) * 512]
                        fw.op("dve", lambda e, bk=bk, xsl=xsl: e.tensor_tensor(xsl, xsl, bk[:, :], ALU.add), reads=[rb], writes=[r_xa4[tt]])
                for tt in range(4):
                    i = tt & 1
                    rms_tile(xa4[:, tt, :], xn4[:, tt, :], st1[i], r_xa4[tt], r_xn4[tt], r_st1[i], 1024)
                    transpose_to(hTb[:, :, tsl(tt)], xn4[:, tt, :], 8, [r_xn4[tt]], r_hTb)
                for g in range(6):
                    n_ = 512 if g < 5 else 256
                    wg, rwg = wload("w_gate", Wb["w_gate"], 0, 8, g * 512, n_)
                    wu, rwu = wload("w_up", Wb["w_up"], 0, 8, g * 512, n_)
                    for cl in range(n_ // 128):
                        f = g * 4 + cl
                        cs_ = slice(cl * 128, (cl + 1) * 128)
                        bg_, rbg = nbank(); mm(bg_[:, :], [(wg[:, kc, cs_], hTb[:, kc, :]) for kc in range(8)], [rwg, r_hTb], rbg)
                        bu, rbu = nbank(); mm(bu[:, :], [(wu[:, kc, cs_], hTb[:, kc, :]) for kc in range(8)], [rwu, r_hTb], rbu)
                        i = f & 1
                        fw.op("act", lambda e, bg_=bg_, i=i: e.activation(sg[i], bg_[:, :], AF.Silu), reads=[rbg], writes=[r_sg[i]])
                        fw.op("dve", lambda e, bu=bu, i=i, f=f: e.tensor_tensor(actT[:, f, :], bu[:, :], sg[i], ALU.mult), reads=[rbu, r_sg[i]], writes=[r_actT])
                for half in range(2):
                    parts = []
                    for k0, nk in ((0, 8), (8, 8), (16, 6)):
                        wd, rwd = wload("w_down", Wb["w_down"], k0, nk, half * 512, 512)
                        parts.append((k0, nk, wd, rwd))
                    for tt in range(4):
                        bk, rb = nbank()
                        pairs = []
                        for k0, nk, wd, rwd in parts:
                            pairs += [(actT[:, k0 + kk, tsl(tt)], wd[:, kk, :]) for kk in range(nk)]
                        mm(bk[:, :], pairs, [p_[3] for p_ in parts] + [r_actT], rb)
                        xsl = xa4[:, tt, half * 512:(half + 1) * 512]
                        fw.op("dve", lambda e, bk=bk, xsl=xsl: e.tensor_tensor(xsl, xsl, bk[:, :], ALU.add), reads=[rb], writes=[r_xa4[tt]])
                fw.op("act", lambda e: e.copy(pn, pa), reads=[r_pa], writes=[r_pn])
                for tt in range(4):
                    fw.op("act", lambda e, tt=tt: e.copy(xn4[:, tt, :], xa4[:, tt, :]), reads=[r_xa4[tt]], writes=[r_xn4[tt]])
                    transpose_to(hTb[:, :, tsl(tt)], xn4[:, tt, :], 8, [r_xn4[tt]], r_hTb)
                    transpose_to(pT[:, :, tsl(tt)], pn[:, tt, :], 2, [r_pn], r_pT)
                for half in range(2):
                    wpg, rwpg = wload("w_ple_gate", Wb["w_ple_gate"], 0, 8, half * 512, 512)
                    wp, rwp = wload("w_ple", Wb["w_ple"], 0, 2, half * 512, 512)
                    for tt in range(4):
                        bgt, rbgt = nbank(); mm(bgt[:, :], [(hTb[:, kc, tsl(tt)], wpg[:, kc, :]) for kc in range(8)], [rwpg, r_hTb], rbgt)
                        bp, rbp = nbank(); mm(bp[:, :], [(pT[:, kc, tsl(tt)], wp[:, kc, :]) for kc in range(2)], [rwp, r_pT], rbp)
                        i = tt & 1
                        fw.op("act", lambda e, bgt=bgt, i=i: e.activation(sg[i], bgt[:, :], AF.Sigmoid), reads=[rbgt], writes=[r_sg[i]])
                        fw.op("dve", lambda e, bp=bp, i=i: e.tensor_tensor(t12[i], bp[:, :], sg[i], ALU.mult), reads=[rbp, r_sg[i]], writes=[r_t12[i]])
                        xsl = xa4[:, tt, half * 512:(half + 1) * 512]
                        fw.op("dve", lambda e, xsl=xsl, i=i: e.tensor_tensor(xsl, xsl, t12[i], ALU.add), reads=[r_t12[i]], writes=[r_xa4[tt]])
                for tt in range(4):
                    i = tt & 1
                    stt = st1[i]
                    fw.op("act", lambda e, tt=tt, stt=stt: e.activation(xn4[:, tt, :], xa4[:, tt, :], AF.Square, accum_out=stt[:, 0:1]), reads=[r_xa4[tt]], writes=[r_xn4[tt], r_st1[i]])
                    fw.op("act", lambda e, stt=stt: e.activation(stt[:, 1:2], stt[:, 0:1], AF.Sqrt, bias=EPS, scale=1.0 / 1024), reads=[r_st1[i]], writes=[r_st1[i]])
                    fw.op("dve", lambda e, stt=stt: e.reciprocal(stt[:, 2:3], stt[:, 1:2]), reads=[r_st1[i]], writes=[r_st1[i]])
                    fw.op("dve", lambda e, tt=tt, stt=stt: e.scalar_tensor_tensor(xa4[:, tt, :], xa4[:, tt, :], stt[:, 2:3], gfin[:], ALU.mult, ALU.mult),
                          reads=[r_st1[i], r_const], writes=[r_xa4[tt]])
                    fw.dma("pool", y[sq, tok0 + tt * 128: tok0 + (tt + 1) * 128, :], xa4[:, tt, :], reads=[r_xa4[tt]], writes=[r_y])

        def phase_mlstm(sq, heads=range(4)):
            QT = VB(0, 4096); KT = VB(8192, 4096); Qp = VB(16384, 4096)
            Ktok = VB(24576, 4096).rearrange("p (t n) -> p t n", t=NT)
            Kp = VB(32768, 4096).rearrange("p (t n) -> p t n", t=NT)
            Vaug = VB(40960, NT * 258).rearrange("p (t n) -> p t n", t=NT)
            bbc = VF(57472, 4096)
            tmpE = [VF(73856 + i * 2048, 512) for i in range(2)]
            Cst = VF(77952, 258); Cbf = VB(78992, 258)
            Dt = [VF(79520 + i * 512, 128) for i in range(2)]
            Sp = [VB(80544 + i * 256, 128) for i in range(2)]
            aC = VF(81056, 32)
            hfo = [VF(81184 + i * 1024, 256) for i in range(3)]
            hs = [VF(84256 + i * 1024, 256) for i in range(2)]
            hn = [VB(86304 + i * 512, 256) for i in range(2)]
            hT2 = [VB(87328 + i * 512, 256).rearrange("p (k n) -> p k n", k=2) for i in range(2)]
            r_QT, r_KT, r_Qp, r_Ktok, r_Kp, r_V, r_bbc = (RES(n) for n in ("QT", "KT", "Qp", "Ktok", "Kp", "Vaug", "bbc"))
            r_tmpE = [RES("tmpE%d" % i) for i in range(2)]
            r_C, r_Cbf, r_aC = RES("Cst"), RES("Cbf"), RES("aC")
            r_Dt = [RES("Dt%d" % i) for i in range(2)]; r_Sp = [RES("Sp%d" % i) for i in range(2)]
            r_hfo = [RES("hfo%d" % i) for i in range(3)]; r_hs = [RES("hs%d" % i) for i in range(2)]
            r_hn = [RES("hn%d" % i) for i in range(2)]; r_hT2 = [RES("hT2_%d" % i) for i in range(2)]
            r_hf = [RES("@hf%d" % c) for c in range(NT)]
            r_hm = RES("@hmT")
            for h in heads:
                fw.dma("sp", QT, qk_scr[h], writes=[r_QT])
                fw.dma("sp", KT, qk_scr[4 + h], writes=[r_KT])
                fw.op("pool", lambda e: e.memset(Vaug[:, :, 256:258], 1.0), writes=[r_V])
                fw.dma("sp", Vaug[:, :, 0:256], v_scr[:, h * 256:(h + 1) * 256].rearrange("(t p) n -> p t n", p=128), reads=[r_V], writes=[r_V])
                for c4 in range(4):
                    transpose_to(Ktok[:, c4 * 8:(c4 + 1) * 8, :], KT[:, c4 * 1024:(c4 + 1) * 1024], 8, [r_KT], r_Ktok)
                for d in range(2):
                    msk = maskf if d == 0 else maskb
                    far = 127 if d == 0 else 0
                    fw.dma("sp", bbc, nb_scr[d, h:h + 1, :].partition_broadcast(128).rearrange("p o n -> p (o n)"), writes=[r_bbc])
                    fw.op("act", lambda e, far=far: e.activation(aC, bbc.rearrange("p (t n) -> p t n", t=NT)[:, :, far], AF.Exp), reads=[r_bbc], writes=[r_aC])
                    for blk in range(NBLK):
                        i = blk & 1
                        bs_ = slice(blk * 512, (blk + 1) * 512)
                        fw.op("act", lambda e, i=i, bs_=bs_: e.activation(tmpE[i], bbc[:, bs_], AF.Exp, bias=LN_SQ), reads=[r_bbc], writes=[r_tmpE[i]])
                        fw.op("dve", lambda e, i=i, bs_=bs_: e.tensor_tensor(Qp[:, bs_], QT[:, bs_], tmpE[i], ALU.mult), reads=[r_QT, r_tmpE[i]], writes=[r_Qp])
                    fw.op("pool", lambda e, msk=msk: e.tensor_tensor(bbc.rearrange("p (t n) -> p t n", t=NT), bbc.rearrange("p (t n) -> p t n", t=NT),
                                                                    msk[:].unsqueeze(1).to_broadcast([128, NT, 128]), ALU.add), reads=[r_bbc, r_const], writes=[r_bbc])
                    fw.op("dve", lambda e, d=d, h=h: e.tensor_tensor(Kp, Ktok, colsT[:, 2 + d, :, h].unsqueeze(2).to_broadcast([128, NT, 128]), ALU.mult),
                          reads=[r_Ktok, r_const], writes=[r_Kp])
                    fw.op("dve", lambda e: e.memset(Cst, 0.0), writes=[r_C])
                    fw.op("pool", lambda e: e.memset(Cbf, 0.0), writes=[r_Cbf])
                    order = list(range(NT)) if d == 0 else list(range(NT - 1, -1, -1))
                    for n_, c in enumerate(order):
                        cs_ = slice(c * 128, (c + 1) * 128)
                        j = n_ & 1
                        fw.op("act", lambda e, j=j, cs_=cs_, d=d, c=c, h=h: e.activation(Dt[j], bbc[:, cs_], AF.Exp, bias=colsT[:, d, c, h:h + 1]),
                              reads=[r_bbc, r_const], writes=[r_Dt[j]])
                        bS, rbS = nbank()
                        mm(bS[:, 0:128], [(KT[:, cs_], QT[:, cs_])], [r_KT, r_QT], rbS)
                        fw.op("dve", lambda e, j=j, bS=bS: e.tensor_tensor(Sp[j], bS[:, 0:128], Dt[j], ALU.mult), reads=[rbS, r_Dt[j]], writes=[r_Sp[j]])
                        bN, rbN = nbank()
                        mm(bN[:, 0:257], [(Sp[j], Vaug[:, c, 0:257]), (Qp[:, cs_], Cbf[:, 0:257])], [r_Sp[j], r_V, r_Qp, r_Cbf], rbN)
                        bU, rbU = nbank()
                        mm(bU[:, 0:257], [(Kp[:, c, :], Vaug[:, c, 0:257])], [r_Kp, r_V], rbU)
                        fw.op("dve", lambda e, c=c, bU=bU: e.scalar_tensor_tensor(Cst[:, 0:257], Cst[:, 0:257], aC[:, c:c + 1], bU[:, 0:257], ALU.mult, ALU.add),
                              reads=[r_C, r_aC, rbU], writes=[r_C])
                        fw.op("act", lambda e: e.copy(Cbf[:, 0:257], Cst[:, 0:257]), reads=[r_C], writes=[r_Cbf])
                        k = n_ & 1
                        stt, r_stt = st1[k], r_st1[k]
                        fw.op("act", lambda e, bN=bN, stt=stt: e.activation(stt[:, 0:1], bN[:, 256:257], AF.Abs), reads=[rbN], writes=[r_stt])
                        fw.op("dve", lambda e, stt=stt: e.tensor_scalar_max(stt[:, 0:1], stt[:, 0:1], 1.0), reads=[r_stt], writes=[r_stt])
                        fw.op("dve", lambda e, stt=stt: e.reciprocal(stt[:, 1:2], stt[:, 0:1]), reads=[r_stt], writes=[r_stt])
                        m3 = n_ % 3
                        if d == 0:
                            fw.op("dve", lambda e, m3=m3, bN=bN, stt=stt: e.tensor_scalar(hfo[m3], bN[:, 0:256], stt[:, 1:2], None, ALU.mult), reads=[rbN, r_stt], writes=[r_hfo[m3]])
                            fw.dma("pool", hf_scr[cs_, :], hfo[m3], reads=[r_hfo[m3]], writes=[r_hf[c]])
                        else:
                            fw.dma("sp", hfo[m3], hf_scr[cs_, :], reads=[r_hf[c]], writes=[r_hfo[m3]])
                            fw.op("dve", lambda e, m3=m3, k=k, bN=bN, stt=stt: e.scalar_tensor_tensor(hs[k], bN[:, 0:256], stt[:, 1:2], hfo[m3], ALU.mult, ALU.add),
                                  reads=[rbN, r_stt, r_hfo[m3]], writes=[r_hs[k]])
                            fw.op("act", lambda e, k=k, stt=stt: e.activation(hn[k], hs[k], AF.Square, accum_out=stt[:, 2:3]), reads=[r_hs[k]], writes=[r_hn[k], r_stt])
                            fw.op("act", lambda e, stt=stt: e.activation(stt[:, 3:4], stt[:, 2:3], AF.Sqrt, bias=EPS, scale=1.0 / 256), reads=[r_stt], writes=[r_stt])
                            fw.op("dve", lambda e, stt=stt: e.reciprocal(stt[:, 2:3], stt[:, 3:4]), reads=[r_stt], writes=[r_stt])
                            fw.op("dve", lambda e, k=k, stt=stt: e.tensor_scalar(hn[k], hs[k], stt[:, 2:3], None, ALU.mult), reads=[r_hs[k], r_stt], writes=[r_hn[k]])
                            transpose_to(hT2[k], hn[k], 2, [r_hn[k]], r_hT2[k])
                            fw.dma("pool", hmT_scr[2 * h:2 * h + 2, :, cs_].rearrange("k p n -> p k n"), hT2[k], reads=[r_hT2[k]], writes=[r_hm])

        def phase_attn(sq, heads=range(8), main_loop=True, qbs=None):
            cqnT = VB(0, 8192).rearrange("p (k n) -> p k n", k=2); r_cqn = RES("cqnT")
            ckvnT = VB(16384, 8192).rearrange("p (k n) -> p k n", k=2); r_ckvn = RES("ckvnT")
            krT = VB(32768, 4096, 128); r_krT = RES("krT")
            KhT = VB(40960, 4096); r_KhT = RES("KhT")
            Vh = VB(49152, 4096); r_Vh = RES("Vh")
            QhT = VB(57344, 4096); r_QhT = RES("QhT")
            QrT = VB(65536, 4096, 128); r_QrT = RES("QrT")
            cstq = VF(73728, 1024, 64).rearrange("p (c n) -> p c n", c=2); r_cstq = RES("cstq")
            rt = [VF(77824 + i * 2048, 512, 64) for i in range(2)]; r_rt = RES("rt")
            PT = [VB(81920 + i * 1024, 512) for i in range(3)]; r_PT = [RES("PT%d" % i) for i in range(3)]
            rsum = VF(84992, 512); r_rsum = RES("rsum")
            oTs = [VB(87040 + i * 1024, 512) for i in range(2)]; r_oTs = [RES("oTs%d" % i) for i in range(2)]
            r_oT = RES("@oT")
            fw.dma("sp", cqnT, cqnT_scr.rearrange("k p n -> p k n"), writes=[r_cqn])
            fw.dma("sp", ckvnT, ckvnT_scr.rearrange("k p n -> p k n"), writes=[r_ckvn])
            fw.op("pool", lambda e: e.memset(krT[64:128, :], 0.0), writes=[r_krT])
            fw.op("pool", lambda e: e.memset(QrT[64:128, :], 0.0), writes=[r_QrT])
            fw.dma("sp", krT[0:64, :], krT_scr, reads=[r_krT], writes=[r_krT])
            for h in heads:
                wk, rwk = wload("w_ukv", Wb["w_ukv"], 0, 2, h * 256, 256)
                wq, rwq = wload("w_uq", Wb["w_uq"], 0, 2, h * 192, 192)
                wqs, rwqs = wload("w_uq_sw", w_uq_sw, 0, 2, h * 64, 64)
                for b in range(NBLK):
                    bs_ = slice(b * 512, (b + 1) * 512)
                    bk, rb = nbank()
                    mm(bk[:, :], [(wk[:, kc, 0:128], ckvnT[:, kc, bs_]) for kc in range(2)], [rwk, r_ckvn], rb)
                    copy_op(evac_eng(), KhT[:, bs_], bk[:, :], [rb], [r_KhT])
                    bk, rb = nbank()
                    mm(bk[:, :], [(wq[:, kc, 0:128], cqnT[:, kc, bs_]) for kc in range(2)], [rwq, r_cqn], rb)
                    copy_op(evac_eng(), QhT[:, bs_], bk[:, :], [rb], [r_QhT])
                    bka, rba = nbank()
                    mm(bka[0:64, :], [(wq[:, kc, 128:192], cqnT[:, kc, bs_]) for kc in range(2)], [rwq, r_cqn], rba)
                    bkb, rbb = nbank()
                    mm(bkb[0:64, :], [(wqs[:, kc, :], cqnT[:, kc, bs_]) for kc in range(2)], [rwqs, r_cqn], rbb)
                    fw.dma("sp", cstq, cs_scr[:, :, bs_].rearrange("c p n -> p c n"), writes=[r_cstq])
                    fw.op("dve", lambda e, bka=bka: e.tensor_tensor(rt[0], bka[0:64, :], cstq[:, 0, :], ALU.mult), reads=[rba, r_cstq], writes=[r_rt])
                    fw.op("dve", lambda e, bkb=bkb: e.tensor_tensor(rt[1], bkb[0:64, :], cstq[:, 1, :], ALU.mult), reads=[rbb, r_cstq], writes=[r_rt])
                    fw.op("pool", lambda e, bs_=bs_: e.tensor_tensor(QrT[0:64, bs_], rt[0], rt[1], ALU.add), reads=[r_rt], writes=[r_QrT])
                    bk, rb = nbank()
                    for tt in range(4):
                        ts_ = slice(b * 512 + tt * 128, b * 512 + (tt + 1) * 128)
                        mm(bk[:, tt * 128:(tt + 1) * 128], [(ckvnT[:, kc, ts_], wk[:, kc, 128:256]) for kc in range(2)], [rwk, r_ckvn], rb)
                    copy_op(evac_eng(), Vh[:, bs_], bk[:, :], [rb], [r_Vh])
                for qb in ((qbs if qbs is not None else range(NBLK)) if main_loop else ()):
                    qs_ = slice(qb * 512, (qb + 1) * 512)
                    bo, rbo = banks[4 + (qb & 1)], r_bank[4 + (qb & 1)]
                    bsm, rbsm = banks[6 + (qb & 1)], r_bank[6 + (qb & 1)]

                    def S_step(kc):
                        ks_ = slice(kc * 128, (kc + 1) * 128)
                        bS, rbS = banks[kc % 4], r_bank[kc % 4]
                        mm(bS[:, :], [(KhT[:, ks_], QhT[:, qs_]), (krT[:, ks_], QrT[:, qs_])], [r_KhT, r_QhT, r_krT, r_QrT], rbS)
                        fw.op("act", lambda e, bS=bS, kc=kc: e.activation(PT[kc % 3], bS[:, :], AF.Exp, scale=A_SCALE), reads=[rbS], writes=[r_PT[kc % 3]])

                    def O_step(kc):
                        ks_ = slice(kc * 128, (kc + 1) * 128)
                        fw.op("pe", lambda e, kc=kc, ks_=ks_, bo=bo: e.matmul(bo[:, :], Vh[:, ks_], PT[kc % 3], start=(kc == 0), stop=(kc == NT - 1)),
                              reads=[r_Vh, r_PT[kc % 3]], writes=[rbo], signal=False)
                        fw.op("pe", lambda e, kc=kc, bsm=bsm: e.matmul(bsm[:, :], onesb[:], PT[kc % 3], start=(kc == 0), stop=(kc == NT - 1)),
                              reads=[r_const, r_PT[kc % 3]], writes=[rbsm], signal=True)
                    S_step(0)
                    S_step(1)
                    for kc in range(NT):
                        if kc + 2 < NT:
                            S_step(kc + 2)
                        O_step(kc)
                    fw.op("dve", lambda e, bsm=bsm: e.reciprocal(rsum, bsm[:, :]), reads=[rbsm], writes=[r_rsum])
                    o_ = oTs[qb & 1]
                    fw.op("dve", lambda e, bo=bo, o_=o_: e.tensor_tensor(o_, bo[:, :], rsum, ALU.mult), reads=[rbo, r_rsum], writes=[r_oTs[qb & 1]])
                    fw.dma("pool", oT_scr[h, :, qs_], o_, reads=[r_oTs[qb & 1]], writes=[r_oT])
            if dbg == "M1":
                fw.dma("pool", dbg_out["d_oT"], oTs[0], reads=[r_oTs[0]], writes=[RES("@dbgM1")])
            if isinstance(dbg, str) and dbg.startswith("M:"):
                fw.dma("pool", dbg_out["d_oTs0"], oTs[0], reads=[r_oTs[0]], writes=[RES("@dbgMa")])
                fw.dma("pool", dbg_out["d_oTs1"], oTs[1], reads=[r_oTs[1]], writes=[RES("@dbgMb")])
            if dbg == "E":
                rd = RES("@dbgE")
                for n_, ap_, r_ in (("d_cqnT", cqnT, r_cqn), ("d_ckvnT", ckvnT, r_ckvn), ("d_krT", krT[0:64, :], r_krT),
                                    ("d_KhT", KhT, r_KhT), ("d_QhT", QhT, r_QhT), ("d_QrT", QrT[0:64, :], r_QrT), ("d_Vh", Vh, r_Vh)):
                    fw.dma("pool", dbg_out[n_], ap_, reads=[r_], writes=[rd])

        for sq in range(NSEQ):
            if 1 in phases:
                phase1(sq)
                fw.barrier()
            if 2 in phases:
                phase_mlstm(sq, heads=([0] if dbg == "L1" else range(4)))
                fw.barrier()
            if 3 in phases:
                if dbg == "E":
                    phase_attn(sq, heads=[0], main_loop=False)
                elif dbg == "M1":
                    phase_attn(sq, heads=[0], main_loop=True, qbs=[0])
                elif isinstance(dbg, str) and dbg.startswith("M:"):
                    _, nh_, nq_ = dbg.split(":")
                    phase_attn(sq, heads=list(range(int(nh_))), main_loop=True, qbs=list(range(int(nq_))))
                else:
                    phase_attn(sq)
                fw.barrier()
            if 4 in phases:
                phase3(sq)
                fw.barrier()
        fw.barrier()
        fw.emit_all()
    return nc


NSEQ_PER_CORE = 3


def _consts():
    ident = np.eye(128, dtype=np.float32)
    s = np.arange(128)[:, None]
    t = np.arange(128)[None, :]
    maskf = np.where(t >= s, 0.0, -1000.0).astype(np.float32)
    maskb = np.where(t <= s, 0.0, -1000.0).astype(np.float32)
    return {"c_ident": ident, "c_maskf": maskf, "c_maskb": maskb}


def kernel(**inputs):
    X = np.concatenate([inputs["x_prompt"], inputs["x_sample"]], axis=0)
    P = np.concatenate([inputs["p_prompt"][0], inputs["p_sample"][0]], axis=0)
    nseq = X.shape[0]
    base = {}
    for k in WSHAPES:
        base[k] = np.ascontiguousarray(inputs[k][0])
    for k in GSHAPES:
        base[k] = np.ascontiguousarray(inputs[k].reshape(1, -1))
    base["g_final"] = np.ascontiguousarray(inputs["g_final"].reshape(1, -1))
    base["b_gate"] = np.ascontiguousarray(inputs["b_gate"].reshape(1, 16))
    base["conv_w"] = np.ascontiguousarray(inputs["conv_w"][0])
    base.update(_consts())
    slots = [[c, c + 8, c + 16 if c + 16 < nseq else c] for c in range(8)]
    maps = []
    for sl in slots:
        m = dict(base)
        m["xs"] = np.ascontiguousarray(X[sl])
        m["ps"] = np.ascontiguousarray(P[sl])
        maps.append(m)
    nc = build(NSEQ_PER_CORE, dbg=False, phases=(1, 2, 3, 4))
    res = run_bass_kernel_spmd(nc, maps, core_ids=list(range(8)))
    Y = np.zeros((nseq, S, D), dtype=np.float32)
    for c, sl in enumerate(slots):
        yc = res.results[c]["y"]
        for j, sidx in enumerate(sl):
            if j == 2 and c + 16 >= nseq:
                continue
            Y[sidx] = yc[j]
    nb = inputs["x_prompt"].shape[0]
    return (np.ascontiguousarray(Y[:nb], dtype=np.float32), np.ascontiguousarray(Y[nb:], dtype=np.float32))
```

```python
import numpy as np
import concourse.bass as bass
import concourse.mybir as mybir
from concourse.bass_utils import run_bass_kernel_spmd

F32 = mybir.dt.float32
BF16 = mybir.dt.bfloat16
I32 = mybir.dt.int32
AF = mybir.ActivationFunctionType
ALU = mybir.AluOpType
AX = mybir.AxisListType


class Sem:
    def __init__(self, handle, name, dma=False, owner=None):
        self.h = handle
        self.name = name
        self.dma = dma
        self.owner = owner
        self.issued = 0


class Res:
    __slots__ = ("name", "w", "r", "dsem")

    def __init__(self, name):
        self.name = name
        self.w = None
        self.r = {}
        self.dsem = None


class Eng:
    def __init__(self, name, sem, is_pe=False):
        self.name = name
        self.sem = sem
        self.ops = []
        self.seen = {}
        self.is_pe = is_pe
        self.pending = False


class FW:
    def __init__(self, nc, stack):
        self.nc = nc
        self.stack = stack
        self.engs = {}
        self.nsem = 0
        self.all_sems = []
        for n in ("pe", "act", "dve", "pool", "sp"):
            self.engs[n] = Eng(n, None, is_pe=(n == "pe"))
            self.engs[n].sem = self._newsem("s_" + n, owner=self.engs[n])
        self.dma_res = []
        self.n_ins = 0

    SEM_LIMIT = 6000

    def _newsem(self, name, dma=False, owner=None):
        self.nsem += 1
        h = self.stack.enter_context(self.nc.semaphore("%s_%d" % (name, self.nsem)))
        sm = Sem(h, name, dma=dma, owner=owner)
        self.all_sems.append(sm)
        return sm

    def dsem(self, res, qclass):
        if res.dsem is None:
            res.dsem = {}
            self.dma_res.append(res)
        sm = res.dsem.get(qclass)
        if sm is None or sm.issued >= self.SEM_LIMIT:
            sm = res.dsem[qclass] = self._newsem("d%s_%s" % (qclass[0], res.name.replace("@", "h_")), dma=True)
        return sm

    def _collect(self, E, reads, writes):
        waits = {}

        def need(ev):
            if ev is None:
                return
            sem, val = ev
            if sem.owner is E:
                if E.is_pe:
                    return
            else:
                if sem.dma and sem.issued > val:
                    val = sem.issued
            if E.seen.get(sem, 0) >= val:
                return
            if waits.get(sem, 0) < val:
                waits[sem] = val

        for r in reads:
            need(r.w)
        for w in writes:
            need(w.w)
            for s, v in w.r.items():
                need((s, v))
        for s, v in waits.items():
            E.seen[s] = v
        return list(waits.items())

    def _commit(self, ev, reads, writes):
        for w in writes:
            w.w = ev
            w.r = {}
        for r in reads:
            if r in writes:
                continue
            s, v = ev
            if r.r.get(s, 0) < v:
                r.r[s] = v

    def op(self, eng, fn, reads=(), writes=(), signal=True):
        E = self.engs[eng]
        waits = self._collect(E, reads, writes)
        if signal:
            E.sem.issued += 1
        val = E.sem.issued if signal else E.sem.issued + 1
        sem_h = E.sem.h

        def emit(e, waits=waits, fn=fn, signal=signal):
            for s, v in waits:
                e.wait_ge(s.h, v)
            ins = fn(e)
            if signal:
                ins.then_inc(sem_h, 1)
        E.ops.append(emit)
        E.pending = not signal
        self._commit((E.sem, val), reads, writes)
        self.n_ins += 1
        if signal and E.sem.issued >= self.SEM_LIMIT:
            E.sem = self._newsem("s_" + E.name, owner=E)

    def dma(self, eng, out, in_, reads=(), writes=(), sem_res=None, **kw):
        E = self.engs[eng]
        waits = self._collect(E, reads, writes)
        if sem_res is None:
            sem_res = writes[0] if (writes and writes[0].name[0] != "@") else reads[0]
        ds = self.dsem(sem_res, "sw" if eng == "pool" else "hw")
        ds.issued += 16
        val = ds.issued

        def emit(e, waits=waits):
            for s, v in waits:
                e.wait_ge(s.h, v)
            e.dma_start(out=out, in_=in_, **kw).then_inc(ds.h, 16)
        E.ops.append(emit)
        self._commit((ds, val), reads, writes)
        self.n_ins += 1

    def barrier(self):
        for E in self.engs.values():
            assert not E.pending
        evs = [(sm, sm.issued) for sm in self.all_sems if sm.issued > 0]
        for E in self.engs.values():
            waits = []
            for s, v in evs:
                if s.owner is E and E.is_pe:
                    continue
                if E.seen.get(s, 0) < v:
                    waits.append((s, v))
                    E.seen[s] = v

            def emit(e, waits=waits):
                for s, v in waits:
                    e.wait_ge(s.h, v)
            E.ops.append(emit)

    def wait_all(self, eng, resources):
        E = self.engs[eng]
        waits = self._collect(E, resources, [])

        def emit(e, waits=waits):
            for s, v in waits:
                e.wait_ge(s.h, v)
        E.ops.append(emit)

    def emit_all(self):
        nc = self.nc
        with nc.Block() as block:
            @block.tensor
            def _(e):
                for f in self.engs["pe"].ops:
                    f(e)

            @block.scalar
            def _(e):
                for f in self.engs["act"].ops:
                    f(e)

            @block.vector
            def _(e):
                for f in self.engs["dve"].ops:
                    f(e)

            @block.gpsimd
            def _(e):
                for f in self.engs["pool"].ops:
                    f(e)

            @block.sync
            def _(e):
                for f in self.engs["sp"].ops:
                    f(e)


import math
from contextlib import ExitStack

S = 4096
D = 1024
NBLK = 8
NT = 32
DFF = 2816
V0, O0, G0, CQ0, CKV0, KR0, GM0, INW = 1024, 2048, 3072, 3088, 3344, 3600, 3664, 5712
EPS = 1e-6
LN_SQ = math.log(128 ** -0.5)
A_SCALE = 192 ** -0.5

WSHAPES = {
    "w_in": (1024, 5712), "w_mo": (1024, 1024), "w_uq": (256, 1536), "w_ukv": (256, 2048),
    "w_ao": (1024, 1024), "w_out": (1024, 1024), "w_gate": (1024, 2816), "w_up": (1024, 2816),
    "w_down": (2816, 1024), "w_ple": (256, 1024), "w_ple_gate": (1024, 1024),
}
GAINS = {"w_in": "g_mix", "w_mo": "g_mhead", "w_uq": "g_q", "w_ukv": "g_kv", "w_gate": "g_ffn", "w_up": "g_ffn"}
GSHAPES = {"g_mix": 1024, "g_mhead": 1024, "g_q": 256, "g_kv": 256, "g_ffn": 1024}


def build(NSEQ, dbg=False, phases=(1, 2, 3, 4)):
    nc = bass.Bass("TRN2", target_bir_lowering=False)
    din = lambda n, sh, dt=F32: nc.dram_tensor(n, list(sh), dt, kind="ExternalInput").ap()
    dscr = lambda n, sh, dt: nc.dram_tensor(n, list(sh), dt, kind="Internal").ap()
    xs = din("xs", [NSEQ, S, D])
    ps = din("ps", [NSEQ, S, 256])
    Wf = {k: din(k, sh) for k, sh in WSHAPES.items()}
    Gd = {k: din(k, [1, n]) for k, n in GSHAPES.items()}
    g_final = din("g_final", [1, D])
    b_gate = din("b_gate", [1, 16])
    conv_w = din("conv_w", [5, 1024])
    c_ident = din("c_ident", [128, 128])
    c_maskf = din("c_maskf", [128, 128])
    c_maskb = din("c_maskb", [128, 128])
    y = nc.dram_tensor("y", [NSEQ, S, D], F32, kind="ExternalOutput").ap()
    dbg_out = {}
    if dbg == "P1":
        dbg_out["d_colsT"] = nc.dram_tensor("d_colsT", [128, 512], F32, kind="ExternalOutput").ap()
    if dbg == "M1":
        dbg_out["d_oT"] = nc.dram_tensor("d_oT", [128, 512], BF16, kind="ExternalOutput").ap()
    if isinstance(dbg, str) and dbg.startswith("M:"):
        dbg_out["d_oTs0"] = nc.dram_tensor("d_oTs0", [128, 512], BF16, kind="ExternalOutput").ap()
        dbg_out["d_oTs1"] = nc.dram_tensor("d_oTs1", [128, 512], BF16, kind="ExternalOutput").ap()
    if dbg == "E":
        for n_, sh_ in (("d_cqnT", [128, 2, S]), ("d_ckvnT", [128, 2, S]), ("d_krT", [64, S]),
                        ("d_KhT", [128, S]), ("d_QhT", [128, S]), ("d_QrT", [64, S]), ("d_Vh", [128, S])):
            dbg_out[n_] = nc.dram_tensor(n_, sh_, BF16, kind="ExternalOutput").ap()
    Wb = {k: dscr(k + "_b", sh, BF16) for k, sh in WSHAPES.items()}
    w_kr_sw = dscr("w_kr_sw", [1024, 64], BF16)
    w_uq_sw = dscr("w_uq_sw", [256, 512], BF16)
    cs_scr = dscr("cs_scr", [2, 64, S], F32)
    qk_scr = (nc.dram_tensor("qk_scr", [8, 128, S], BF16, kind="ExternalOutput").ap() if dbg == "P1" else dscr("qk_scr", [8, 128, S], BF16))
    v_scr = (nc.dram_tensor("v_scr", [S, 1024], BF16, kind="ExternalOutput").ap() if dbg == "P1" else dscr("v_scr", [S, 1024], BF16))
    nb_scr = (nc.dram_tensor("nb_scr", [2, 4, S], F32, kind="ExternalOutput").ap() if dbg == "P1" else dscr("nb_scr", [2, 4, S], F32))
    hf_scr = dscr("hf_scr", [S, 256], F32)
    hmT_scr = (nc.dram_tensor("hmT_scr", [8, 128, S], BF16, kind="ExternalOutput").ap() if dbg == "L1" else dscr("hmT_scr", [8, 128, S], BF16))
    if isinstance(dbg, str) and dbg.startswith("M:"):
        oT_scr = nc.dram_tensor("oT_scr", [8, 128, S], BF16, kind="ExternalOutput").ap()
    else:
        oT_scr = dscr("oT_scr", [8, 128, S], BF16)
    cqnT_scr = dscr("cqnT_scr", [2, 128, S], BF16)
    ckvnT_scr = dscr("ckvnT_scr", [2, 128, S], BF16)
    krT_scr = dscr("krT_scr", [64, S], BF16)

    _res_registry = {}
    _ResClass = Res

    def RES(name):
        r = _res_registry.get(name)
        if r is None:
            r = _res_registry[name] = _ResClass(name)
        return r

    with ExitStack() as st:
        fw = FW(nc, st)
        T = lambda name, shape, dt: st.enter_context(nc.sbuf_tensor(name, shape, dt))
        identf = T("identf", [128, 128], F32)
        identb = T("identb", [128, 128], BF16)
        maskf = T("maskf", [128, 128], F32)
        maskb = T("maskb", [128, 128], F32)
        onesb = T("onesb", [128, 128], BF16)
        gfin = T("gfin", [128, D], F32)
        convw = T("convw", [128, 5, 8], F32)
        gcol = {k: T("gc_" + k, [128, n // 128], F32) for k, n in GSHAPES.items()}
        gneg = {k: T("gn_" + k, [128, GSHAPES[k] // 128], F32) for k in ("g_mix", "g_q")}
        bg = T("bg", [4, 4], F32)
        colsT = T("colsT", [128, 4, NT, 4], F32)
        wsl = [T("wsl%d" % i, [128, 4096], BF16) for i in range(6)]
        r_wsl = [RES("wsl%d" % i) for i in range(6)]
        AR = 34816
        arena = T("arena", [128, AR], F32)
        arena_b = arena.bitcast(BF16)
        arena_i = arena.bitcast(I32)
        banks = [st.enter_context(nc.psum_tensor("pb%d" % i, [128, 512], F32)) for i in range(8)]
        r_bank = [RES("pb%d" % i) for i in range(8)]
        r_const = RES("const")
        state = {"bank": 0, "wsl": 0, "ev": 0}

        def VF(off, n, parts=128):
            assert off % 4 == 0 and off // 4 + n <= AR
            return arena[0:parts, off // 4: off // 4 + n]

        def VB(off, n, parts=128):
            assert off % 2 == 0 and off // 2 + n <= 2 * AR
            return arena_b[0:parts, off // 2: off // 2 + n]

        def nbank():
            i = state["bank"]
            state["bank"] = (i + 1) % 8
            return banks[i], r_bank[i]

        def evac_eng():
            state["ev"] ^= 1
            return "act" if state["ev"] else "dve"

        def copy_op(eng, out, in_, reads, writes):
            if eng == "act":
                fw.op("act", lambda e: e.copy(out, in_), reads=reads, writes=writes)
            else:
                fw.op(eng, lambda e: e.tensor_copy(out, in_), reads=reads, writes=writes)

        def mm(out_ap, pairs, reads, wres):
            n = len(pairs)
            for i, (l, r) in enumerate(pairs):
                fw.op("pe", lambda e, l=l, r=r, i=i: e.matmul(out_ap, l, r, start=(i == 0), stop=(i == n - 1)),
                      reads=reads, writes=[wres], signal=(i == n - 1))

        def transp(out_ap, in_ap, ident, reads, wres, signal=True):
            fw.op("pe", lambda e: e.transpose(out_ap, in_ap, ident), reads=reads, writes=[wres], signal=signal)

        r_W = {k: RES("@" + k) for k in WSHAPES}
        r_W["w_kr_sw"] = RES("@w_kr_sw")
        r_W["w_uq_sw"] = RES("@w_uq_sw")

        def wload(name, W_ap, k0, nk, c0, ncols):
            i = state["wsl"]
            state["wsl"] = (i + 1) % 6
            assert nk * ncols <= 4096
            view = wsl[i][:, 0:nk * ncols].rearrange("p (k n) -> p k n", k=nk)
            src = W_ap[k0 * 128:(k0 + nk) * 128, c0:c0 + ncols].rearrange("(k p) n -> p k n", p=128)
            fw.dma("sp", view, src, reads=[r_W[name]], writes=[r_wsl[i]])
            return view, r_wsl[i]

        fw.dma("sp", identf[:], c_ident, writes=[r_const])
        fw.dma("sp", maskf[:], c_maskf, writes=[r_const])
        fw.dma("sp", maskb[:], c_maskb, writes=[r_const])
        fw.dma("sp", gfin[:], g_final.partition_broadcast(128).rearrange("p o n -> p (o n)"), writes=[r_const])
        for j in range(5):
            fw.dma("sp", convw[:, j, :], conv_w[j:j + 1, :].rearrange("o (c p) -> p (o c)", p=128), writes=[r_const], allow_slow_non_contiguous=True)
        for k in GSHAPES:
            fw.dma("sp", gcol[k][:], Gd[k].rearrange("o (k p) -> p (o k)", p=128), writes=[r_const], allow_slow_non_contiguous=True)
        fw.dma("sp", bg[:], b_gate.rearrange("o (g h) -> h (o g)", h=4), writes=[r_const], allow_slow_non_contiguous=True)
        fw.op("dve", lambda e: e.tensor_copy(identb[:], identf[:]), reads=[r_const], writes=[r_const])
        fw.op("dve", lambda e: e.memset(onesb[:], 1.0), writes=[r_const])
        for k in gneg:
            fw.op("dve", lambda e, k=k: e.tensor_scalar(gneg[k][:], gcol[k][:], -1.0, None, ALU.mult), reads=[r_const], writes=[r_const])
        fw.barrier()

        stg_in = [VF(i * 8192, 2048) for i in range(2)]
        stg_out = [VB(16384 + i * 4096, 2048) for i in range(2)]
        r_si = [RES("stg_in%d" % i) for i in range(2)]
        r_so = [RES("stg_out%d" % i) for i in range(2)]
        pw = {"i": 0}

        def prep(name):
            K, N = WSHAPES[name]
            g = gcol[GAINS[name]] if name in GAINS else None
            for kc in range(K // 128):
                for c0 in range(0, N, 2048):
                    n_ = min(2048, N - c0)
                    i = pw["i"] & 1
                    pw["i"] += 1
                    fw.dma("sp", stg_in[i][:, 0:n_], Wf[name][kc * 128:(kc + 1) * 128, c0:c0 + n_], writes=[r_si[i]])
                    eng = evac_eng()
                    if g is None:
                        copy_op(eng, stg_out[i][:, 0:n_], stg_in[i][:, 0:n_], [r_si[i]], [r_so[i]])
                    elif eng == "act":
                        fw.op("act", lambda e, i=i, kc=kc, n_=n_, g=g: e.activation(stg_out[i][:, 0:n_], stg_in[i][:, 0:n_], AF.Copy, scale=g[:, kc:kc + 1]),
                              reads=[r_si[i], r_const], writes=[r_so[i]])
                    else:
                        fw.op("dve", lambda e, i=i, kc=kc, n_=n_, g=g: e.tensor_scalar(stg_out[i][:, 0:n_], stg_in[i][:, 0:n_], g[:, kc:kc + 1], None, ALU.mult),
                              reads=[r_si[i], r_const], writes=[r_so[i]])
                    fw.dma("pool", Wb[name][kc * 128:(kc + 1) * 128, c0:c0 + n_], stg_out[i][:, 0:n_], reads=[r_so[i]], writes=[r_W[name]])

        def prep_sw(dst, dname, src, K, bases, g, gn):
            for kc in range(K // 128):
                i = pw["i"] & 1
                pw["i"] += 1
                nb_ = len(bases)
                for j, b in enumerate(bases):
                    fw.dma("sp", stg_in[i][:, j * 64:(j + 1) * 64], src[kc * 128:(kc + 1) * 128, b:b + 64], writes=[r_si[i]])
                for j in range(nb_):
                    fw.op("dve", lambda e, i=i, j=j, kc=kc: e.tensor_scalar(stg_out[i][:, j * 64:j * 64 + 32], stg_in[i][:, j * 64 + 32:j * 64 + 64], gn[:, kc:kc + 1], None, ALU.mult),
                          reads=[r_si[i], r_const], writes=[r_so[i]])
                    fw.op("dve", lambda e, i=i, j=j, kc=kc: e.tensor_scalar(stg_out[i][:, j * 64 + 32:j * 64 + 64], stg_in[i][:, j * 64:j * 64 + 32], g[:, kc:kc + 1], None, ALU.mult),
                          reads=[r_si[i], r_const], writes=[r_so[i]])
                fw.dma("pool", dst[kc * 128:(kc + 1) * 128, 0:64 * nb_], stg_out[i][:, 0:64 * nb_], reads=[r_so[i]], writes=[r_W[dname]])

        for name in WSHAPES:
            prep(name)
        prep_sw(w_kr_sw, "w_kr_sw", Wf["w_in"], 1024, [KR0], gcol["g_mix"], gneg["g_mix"])
        prep_sw(w_uq_sw, "w_uq_sw", Wf["w_uq"], 256, [h * 192 + 128 for h in range(8)], gcol["g_q"], gneg["g_q"])

        r_tr = RES("trig")
        TB0 = 24576
        posf = VF(TB0, S, 64)
        tq = VF(TB0 + 16384, S, 64)
        ti = arena_i[0:64, (TB0 + 32768) // 4:(TB0 + 32768) // 4 + S]
        tk = VF(TB0 + 49152, S, 64)
        tu = VF(TB0 + 65536, S, 64)
        pidx = T("pidx", [64, 1], F32)
        invf = T("invf", [64, 1], F32)
        pidx_i = T("pidx_i", [64, 1], I32)
        fw.op("pool", lambda e: e.iota(ti, [[1, S]], base=0, channel_multiplier=0), writes=[r_tr])
        fw.op("pool", lambda e: e.iota(pidx_i[0:32, :], [[0, 1]], base=0, channel_multiplier=1), writes=[r_tr])
        fw.op("pool", lambda e: e.iota(pidx_i[32:64, :], [[0, 1]], base=0, channel_multiplier=1), writes=[r_tr])
        fw.op("dve", lambda e: e.tensor_copy(posf, ti), reads=[r_tr], writes=[r_tr])
        fw.op("dve", lambda e: e.tensor_copy(pidx[:], pidx_i[:]), reads=[r_tr], writes=[r_tr])
        fw.op("act", lambda e: e.activation(invf[:], pidx[:], AF.Exp, scale=-math.log(10000.0) / 32.0), reads=[r_tr], writes=[r_tr])
        fw.op("dve", lambda e: e.tensor_scalar(tq, posf, invf[:, 0:1], 1.0 / (2 * math.pi), ALU.mult, ALU.mult), reads=[r_tr], writes=[r_tr])
        for which, shift in ((1, 0.0), (0, 0.25)):
            fw.op("dve", lambda e, shift=shift: e.tensor_scalar(tu, tq, shift, None, ALU.add), reads=[r_tr], writes=[r_tr])
            fw.op("dve", lambda e: e.tensor_copy(ti, tu), reads=[r_tr], writes=[r_tr])
            fw.op("dve", lambda e: e.tensor_copy(tk, ti), reads=[r_tr], writes=[r_tr])
            fw.op("dve", lambda e: e.tensor_tensor(tu, tu, tk, ALU.subtract), reads=[r_tr], writes=[r_tr])
            fw.op("dve", lambda e: e.tensor_scalar(tk, tu, 0.5, None, ALU.is_gt), reads=[r_tr], writes=[r_tr])
            fw.op("dve", lambda e: e.tensor_tensor(tu, tu, tk, ALU.subtract), reads=[r_tr], writes=[r_tr])
            fw.op("dve", lambda e: e.tensor_scalar(tk, tu, -0.5, None, ALU.is_lt), reads=[r_tr], writes=[r_tr])
            fw.op("dve", lambda e: e.tensor_tensor(tu, tu, tk, ALU.add), reads=[r_tr], writes=[r_tr])
            fw.op("act", lambda e: e.activation(tk, tu, AF.Sin, scale=2 * math.pi), reads=[r_tr], writes=[r_tr])
            fw.dma("pool", cs_scr[which], tk, reads=[r_tr], writes=[RES("@cs")], sem_res=r_tr)
        fw.barrier()

        _once = {}

        def T_once(name, shape, dt):
            if name not in _once:
                _once[name] = T(name, shape, dt)
            return _once[name]
        mix_stats = []
        st1 = [T("st1_%d" % i, [128, 8], F32) for i in range(2)]
        r_st1 = [RES("st1_%d" % i) for i in range(2)]
        cmask = T("cmask", [4, 512], F32)
        fw.op("pool", lambda e: e.memset(cmask[:], 1.0), writes=[r_const])
        for j in range(4):
            fw.op("pool", lambda e, j=j: e.memset(cmask[:, j * 128:j * 128 + 1], 0.0), writes=[r_const])
        fw.barrier()

        def rms_tile(x_ap, xn_ap, stt, r_x, r_xn, r_stt, n):
            fw.op("act", lambda e: e.activation(xn_ap, x_ap, AF.Square, accum_out=stt[:, 0:1]), reads=[r_x], writes=[r_xn, r_stt])
            fw.op("act", lambda e: e.activation(stt[:, 1:2], stt[:, 0:1], AF.Sqrt, bias=EPS, scale=1.0 / n), reads=[r_stt], writes=[r_stt])
            fw.op("dve", lambda e: e.reciprocal(stt[:, 2:3], stt[:, 1:2]), reads=[r_stt], writes=[r_stt])
            fw.op("dve", lambda e: e.tensor_scalar(xn_ap, x_ap, stt[:, 2:3], None, ALU.mult), reads=[r_x, r_stt], writes=[r_xn])

        def transpose_to(dst3, src_ap, nk, reads, r_dst):
            bk, rb = nbank()
            pbt = bk.bitcast(BF16)
            for k in range(nk):
                transp(pbt[:, k * 128:(k + 1) * 128], src_ap[:, k * 128:(k + 1) * 128], identb[:], reads + [r_const], rb, signal=(k == nk - 1))
            eng = evac_eng()
            copy_op(eng, dst3, pbt[:, 0:nk * 128].rearrange("p (k n) -> p k n", k=nk), [rb], [r_dst])

        def phase1(sq):
            hTb = [VB(i * 8192, 4096).rearrange("p (k n) -> p k n", k=8) for i in range(2)]
            r_hTb = [RES("hTb%d" % i) for i in range(2)]
            xa = [VF(16384 + i * 4096, 1024) for i in range(2)]
            r_xa = [RES("xa%d" % i) for i in range(2)]
            xnb = [VB(24576 + i * 2048, 1024) for i in range(2)]
            r_xnb = [RES("xnb%d" % i) for i in range(2)]
            U = [VF(28672 + ct * 2064, 516) for ct in range(8)]
            r_U = [RES("U%d" % ct) for ct in range(8)]
            acc = [VF(45184 + i * 2048, 512) for i in range(2)]
            r_acc = [RES("acc%d" % i) for i in range(2)]
            qkb = [VB(49280 + i * 1024, 512) for i in range(2)]
            r_qkb = [RES("qkb%d" % i) for i in range(2)]
            vst = [VB(51328 + i * 2048, 1024) for i in range(4)]
            r_vst = [RES("vst%d" % i) for i in range(4)]
            GB = 59520
            gt = [VF(GB + i * 2048, 512, 4) for i in range(16)]
            r_g = RES("gates")
            LB = GB + 32768
            lat = VB(LB, 2048).rearrange("p (k n) -> p k n", k=4)
            r_lat = RES("lat")
            cnb = [VB(LB + 4096 + i * 1024, 512) for i in range(2)]
            r_cnb = [RES("cnb%d" % i) for i in range(2)]
            cst = VF(LB + 6144, 1024, 64).rearrange("p (c n) -> p c n", c=2)
            r_cst = RES("cst")
            krt = [VF(LB + 10240 + i * 2048, 512, 64) for i in range(2)]
            r_krt = RES("krt")
            krb = VB(LB + 14336, 512, 64)
            r_krb = RES("krb")
            r_qk = RES("@qk_scr"); r_v = RES("@v_scr"); r_nb = RES("@nb_scr")
            r_cq = RES("@cq"); r_ckv = RES("@ckv"); r_kr = RES("@kr")
            xi = 0
            for b in range(NBLK):
                tok0 = b * 512
                j = b & 1
                for tt in range(4):
                    i = xi & 1
                    xi += 1
                    fw.dma("sp", xa[i], xs[sq, tok0 + tt * 128: tok0 + (tt + 1) * 128, :], writes=[r_xa[i]])
                    rms_tile(xa[i], xnb[i], st1[i], r_xa[i], r_xnb[i], r_st1[i], 1024)
                    transpose_to(hTb[j][:, :, tt * 128:(tt + 1) * 128], xnb[i], 8, [r_xnb[i]], r_hTb[j])
                for grp in range(2):
                    wv, rw = wload("w_in", Wb["w_in"], 0, 8, grp * 512, 512)
                    for cl in range(4):
                        ct = grp * 4 + cl
                        bk, rb = nbank()
                        mm(bk[:, :], [(wv[:, kc, cl * 128:(cl + 1) * 128], hTb[j][:, kc, :]) for kc in range(8)], [rw, r_hTb[j]], rb)
                        if b == 0:
                            fw.op("pool", lambda e, ct=ct: e.memset(U[ct][:, 0:4], 0.0), writes=[r_U[ct]])
                        fw.op("act", lambda e, ct=ct, bk=bk: e.copy(U[ct][:, 4:516], bk[:, :]), reads=[rb], writes=[r_U[ct]])
                        a = ct & 1

                        def conv(n, a=a, ct=ct):
                            fw.op("dve", lambda e: e.tensor_scalar(acc[a][:, 0:n], U[ct][:, 0:n], convw[:, 0, ct:ct + 1], None, ALU.mult),
                                  reads=[r_U[ct], r_const], writes=[r_acc[a]])
                            for jj in range(1, 5):
                                fw.op("dve", lambda e, jj=jj: e.scalar_tensor_tensor(acc[a][:, 0:n], U[ct][:, jj:jj + n], convw[:, jj, ct:ct + 1], acc[a][:, 0:n], ALU.mult, ALU.add),
                                      reads=[r_U[ct], r_const, r_acc[a]], writes=[r_acc[a]])
                            fw.op("act", lambda e: e.activation(qkb[a][:, 0:n], acc[a][:, 0:n], AF.Silu), reads=[r_acc[a]], writes=[r_qkb[a]])
                        conv(512)
                        if b == 0:
                            fw.dma("pool", qk_scr[ct, :, 0:510], qkb[a][:, 2:512], reads=[r_qkb[a]], writes=[r_qk])
                        else:
                            fw.dma("pool", qk_scr[ct, :, tok0 - 2:tok0 + 510], qkb[a][:, 0:512], reads=[r_qkb[a]], writes=[r_qk])
                        fw.op("pool", lambda e, ct=ct: e.tensor_copy(U[ct][:, 0:4], U[ct][:, 512:516]), reads=[r_U[ct]], writes=[r_U[ct]])
                        if b == NBLK - 1:
                            fw.op("pool", lambda e, ct=ct: e.memset(U[ct][:, 4:8], 0.0), writes=[r_U[ct]])
                            conv(2)
                            fw.dma("pool", qk_scr[ct, :, S - 2:S], qkb[a][:, 0:2], reads=[r_qkb[a]], writes=[r_qk])
                for half in range(2):
                    wv, rw = wload("w_in", Wb["w_in"], 0, 8, V0 + half * 512, 512)
                    for tt in range(4):
                        bk, rb = nbank()
                        mm(bk[:, :], [(hTb[j][:, kc, tt * 128:(tt + 1) * 128], wv[:, kc, :]) for kc in range(8)], [rw, r_hTb[j]], rb)
                        copy_op(evac_eng(), vst[tt][:, half * 512:(half + 1) * 512], bk[:, :], [rb], [r_vst[tt]])
                for tt in range(4):
                    fw.dma("pool", v_scr[tok0 + tt * 128: tok0 + (tt + 1) * 128, :], vst[tt], reads=[r_vst[tt]], writes=[r_v])
                wv, rw = wload("w_in", Wb["w_in"], 0, 8, G0, 16)
                for g in range(4):
                    bk, rb = nbank()
                    mm(bk[0:4, :], [(wv[:, kc, 4 * g:4 * g + 4], hTb[j][:, kc, :]) for kc in range(8)], [rw, r_hTb[j]], rb)
                    fw.op("act", lambda e, g=g, bk=bk: e.activation(gt[g], bk[0:4, :], AF.Identity, bias=bg[:, g:g + 1]), reads=[rb, r_const], writes=[r_g])
                IFt, IBt, FFt, FBt = gt[0], gt[1], gt[2], gt[3]
                SPf, SPb, csf, csb, NBf, NBb, Af, Ab, Gf, Gb, tmp = gt[4:15]
                G = lambda fn: fw.op(fn[0], fn[1], reads=[r_g, r_const], writes=[r_g])
                for FX, SP in ((FFt, SPf), (FBt, SPb)):
                    G(("act", lambda e, FX=FX: e.activation(tmp, FX, AF.Exp, scale=-1.0)))
                    G(("act", lambda e, SP=SP: e.activation(SP, tmp, AF.Ln, bias=1.0)))
                G(("dve", lambda e: e.tensor_tensor_scan(csf, cmask[:], SPf, 0.0, ALU.mult, ALU.add)))
                G(("dve", lambda e: e.tensor_tensor_scan(csb, cmask[:], SPb, 0.0, ALU.mult, ALU.add)))
                v3 = lambda t: t.rearrange("p (a n) -> p a n", a=4)
                tot = lambda t: t.rearrange("p (a n) -> p a n", a=4)[:, :, 127:128].to_broadcast([4, 4, 128])
                G(("dve", lambda e: e.tensor_scalar(NBf, csf, -1.0, None, ALU.mult)))
                G(("dve", lambda e: e.tensor_tensor(Af, IFt, csf, ALU.add)))
                G(("dve", lambda e: e.tensor_tensor(v3(tmp), v3(Af), tot(csf), ALU.subtract)))
                G(("act", lambda e: e.activation(Gf, tmp, AF.Exp)))
                G(("dve", lambda e: e.tensor_tensor(NBb, SPb, csb, ALU.subtract)))
                G(("dve", lambda e: e.tensor_tensor(v3(NBb), v3(NBb), tot(csb), ALU.add)))
                G(("dve", lambda e: e.tensor_tensor(Ab, IBt, NBb, ALU.add)))
                G(("dve", lambda e: e.tensor_tensor(v3(tmp), v3(Ab), tot(csb), ALU.subtract)))
                G(("act", lambda e: e.activation(Gb, tmp, AF.Exp)))
                G(("dve", lambda e: e.tensor_scalar(NBb, NBb, -1.0, None, ALU.mult)))
                fw.dma("pool", nb_scr[0, :, tok0:tok0 + 512], NBf, reads=[r_g], writes=[r_nb])
                fw.dma("pool", nb_scr[1, :, tok0:tok0 + 512], NBb, reads=[r_g], writes=[r_nb])
                bk, rb = nbank()
                for qi, Q in enumerate((Af, Ab, Gf, Gb)):
                    for tt in range(4):
                        c0 = (qi * 4 + tt) * 4
                        transp(bk[:, c0:c0 + 4], Q[:, tt * 128:(tt + 1) * 128], identf[0:4, 0:4], [r_g, r_const], rb, signal=(qi == 3 and tt == 3))
                fw.op("dve", lambda e, bk=bk, b=b: e.tensor_copy(colsT[:, :, b * 4:(b + 1) * 4, :], bk[:, 0:64].rearrange("p (q t h) -> p q t h", q=4, t=4)),
                      reads=[rb], writes=[r_const])
                wv, rw = wload("w_in", Wb["w_in"], 0, 8, CQ0, 512)
                for tt in range(4):
                    bk, rb = nbank()
                    mm(bk[:, :], [(hTb[j][:, kc, tt * 128:(tt + 1) * 128], wv[:, kc, :]) for kc in range(8)], [rw, r_hTb[j]], rb)
                    i = tt & 1
                    stt = st1[i]
                    fw.op("act", lambda e, i=i, bk=bk, stt=stt: e.activation(cnb[i][:, 0:256], bk[:, 0:256], AF.Square, accum_out=stt[:, 4:5]), reads=[rb], writes=[r_cnb[i], r_st1[i]])
                    fw.op("act", lambda e, i=i, bk=bk, stt=stt: e.activation(cnb[i][:, 256:512], bk[:, 256:512], AF.Square, accum_out=stt[:, 5:6]), reads=[rb], writes=[r_cnb[i], r_st1[i]])
                    fw.op("act", lambda e, stt=stt: e.activation(stt[:, 6:8], stt[:, 4:6], AF.Sqrt, bias=EPS, scale=1.0 / 256), reads=[r_st1[i]], writes=[r_st1[i]])
                    fw.op("dve", lambda e, stt=stt: e.reciprocal(stt[:, 4:6], stt[:, 6:8]), reads=[r_st1[i]], writes=[r_st1[i]])
                    fw.op("dve", lambda e, i=i, bk=bk, stt=stt: e.tensor_scalar(cnb[i][:, 0:256], bk[:, 0:256], stt[:, 4:5], None, ALU.mult), reads=[rb, r_st1[i]], writes=[r_cnb[i]])
                    fw.op("act", lambda e, i=i, bk=bk, stt=stt: e.activation(cnb[i][:, 256:512], bk[:, 256:512], AF.Copy, scale=stt[:, 5:6]), reads=[rb, r_st1[i]], writes=[r_cnb[i]])
                    transpose_to(lat[:, :, tt * 128:(tt + 1) * 128], cnb[i], 4, [r_cnb[i]], r_lat)
                fw.dma("pool", cqnT_scr[:, :, tok0:tok0 + 512].rearrange("k p n -> p k n"), lat[:, 0:2, :], reads=[r_lat], writes=[r_cq])
                fw.dma("pool", ckvnT_scr[:, :, tok0:tok0 + 512].rearrange("k p n -> p k n"), lat[:, 2:4, :], reads=[r_lat], writes=[r_ckv])
                wa, rwa = wload("w_in", Wb["w_in"], 0, 8, KR0, 64)
                wb_, rwb = wload("w_kr_sw", w_kr_sw, 0, 8, 0, 64)
                bka, rba = nbank()
                mm(bka[0:64, :], [(wa[:, kc, :], hTb[j][:, kc, :]) for kc in range(8)], [rwa, r_hTb[j]], rba)
                bkb, rbb = nbank()
                mm(bkb[0:64, :], [(wb_[:, kc, :], hTb[j][:, kc, :]) for kc in range(8)], [rwb, r_hTb[j]], rbb)
                fw.dma("sp", cst, cs_scr[:, :, tok0:tok0 + 512].rearrange("c p n -> p c n"), writes=[r_cst])
                fw.op("dve", lambda e, bka=bka: e.tensor_tensor(krt[0], bka[0:64, :], cst[:, 0, :], ALU.mult), reads=[rba, r_cst], writes=[r_krt])
                fw.op("dve", lambda e, bkb=bkb: e.tensor_tensor(krt[1], bkb[0:64, :], cst[:, 1, :], ALU.mult), reads=[rbb, r_cst], writes=[r_krt])
                fw.op("pool", lambda e: e.tensor_tensor(krb, krt[0], krt[1], ALU.add), reads=[r_krt], writes=[r_krb])
                fw.dma("pool", krT_scr[:, tok0:tok0 + 512], krb, reads=[r_krb], writes=[r_kr])
            fw.op("dve", lambda e: e.tensor_scalar(colsT[:, 0:2], colsT[:, 0:2], LN_SQ, None, ALU.add), reads=[r_const], writes=[r_const])
            if dbg == "P1":
                fw.dma("pool", dbg_out["d_colsT"], colsT[:].rearrange("p q t h -> p (q t h)"), reads=[r_const], writes=[RES("@dbgP1")], sem_res=r_st1[0])

        def phase3(sq):
            xa4 = VF(0, 4096).rearrange("p (t n) -> p t n", t=4)
            r_xa4 = [RES("xa4_%d" % i) for i in range(4)]
            xn4 = VB(16384, 4096).rearrange("p (t n) -> p t n", t=4)
            r_xn4 = [RES("xn4_%d" % i) for i in range(4)]
            k8 = lambda ap: ap.rearrange("p (k n) -> p k n", k=8)
            hTb = k8(VB(24576, 4096)); r_hTb = RES("p3hTb")
            sgo = k8(VB(32768, 4096)); r_sgo = RES("sgo")
            hmT = k8(VB(40960, 4096)); r_hmT = RES("hmTb")
            oTb = k8(VB(49152, 4096)); r_oTb = RES("oTb")
            mrg = k8(VB(57344, 4096)); r_mrg = RES("mrg")
            sg = [VF(65536 + i * 2048, 512) for i in range(2)]; r_sg = [RES("sg%d" % i) for i in range(2)]
            t12 = [VF(69632 + i * 2048, 512) for i in range(2)]; r_t12 = [RES("t12_%d" % i) for i in range(2)]
            actT = VB(73728, 22 * 512).rearrange("p (k n) -> p k n", k=22); r_actT = RES("actT")
            pa = VF(96256, 1024).rearrange("p (t n) -> p t n", t=4); r_pa = RES("pa")
            pn = VB(100352, 1024).rearrange("p (t n) -> p t n", t=4); r_pn = RES("pn")
            pT = VB(102400, 1024).rearrange("p (k n) -> p k n", k=2); r_pT = RES("pT")
            r_y = RES("@y")
            for b in range(NBLK):
                tok0 = b * 512
                tsl = lambda tt: slice(tt * 128, (tt + 1) * 128)
                for tt in range(4):
                    i = tt & 1
                    fw.dma("sp", xa4[:, tt, :], xs[sq, tok0 + tt * 128: tok0 + (tt + 1) * 128, :], writes=[r_xa4[tt]])
                    rms_tile(xa4[:, tt, :], xn4[:, tt, :], st1[i], r_xa4[tt], r_xn4[tt], r_st1[i], 1024)
                    transpose_to(hTb[:, :, tsl(tt)], xn4[:, tt, :], 8, [r_xn4[tt]], r_hTb)
                fw.dma("sp", pa, ps[sq, tok0:tok0 + 512, :].rearrange("(t p) n -> p t n", p=128), writes=[r_pa])
                if dbg:
                    fw.op("pool", lambda e: e.memset(hmT, 0.0), writes=[r_hmT])
                else:
                    fw.dma("sp", hmT, hmT_scr[:, :, tok0:tok0 + 512].rearrange("k p n -> p k n"), reads=[RES("@hm_dummy")], writes=[r_hmT])
                if dbg == 2:
                    fw.op("pool", lambda e: e.memset(oTb, 0.0), writes=[r_oTb])
                else:
                    fw.dma("sp", oTb, oT_scr[:, :, tok0:tok0 + 512].rearrange("k p n -> p k n"), reads=[RES("@o_dummy")], writes=[r_oTb])
                for grp in range(2):
                    wv, rw = wload("w_in", Wb["w_in"], 0, 8, O0 + grp * 512, 512)
                    for cl in range(4):
                        bk, rb = nbank()
                        mm(bk[:, :], [(wv[:, kc, cl * 128:(cl + 1) * 128], hTb[:, kc, :]) for kc in range(8)], [rw, r_hTb], rb)
                        fw.op("act", lambda e, bk=bk, c=grp * 4 + cl: e.activation(sgo[:, c, :], bk[:, :], AF.Sigmoid), reads=[rb], writes=[r_sgo])
                fw.op("dve", lambda e: e.tensor_tensor(sgo, sgo, hmT, ALU.mult), reads=[r_sgo, r_hmT], writes=[r_sgo])
                for grp in range(2):
                    wm, rwm = wload("w_mo", Wb["w_mo"], 0, 8, grp * 512, 512)
                    wa, rwa = wload("w_ao", Wb["w_ao"], 0, 8, grp * 512, 512)
                    wg1, rw1 = wload("w_in", Wb["w_in"], 0, 8, GM0 + grp * 512, 512)
                    wg2, rw2 = wload("w_in", Wb["w_in"], 0, 8, GM0 + 1024 + grp * 512, 512)
                    for cl in range(4):
                        c = grp * 4 + cl
                        cs_ = slice(cl * 128, (cl + 1) * 128)
                        bm, rbm = nbank(); mm(bm[:, :], [(wm[:, kc, cs_], sgo[:, kc, :]) for kc in range(8)], [rwm, r_sgo], rbm)
                        ba, rba = nbank(); mm(ba[:, :], [(wa[:, kc, cs_], oTb[:, kc, :]) for kc in range(8)], [rwa, r_oTb], rba)
                        b1, rb1 = nbank(); mm(b1[:, :], [(wg1[:, kc, cs_], hTb[:, kc, :]) for kc in range(8)], [rw1, r_hTb], rb1)
                        b2, rb2 = nbank(); mm(b2[:, :], [(wg2[:, kc, cs_], hTb[:, kc, :]) for kc in range(8)], [rw2, r_hTb], rb2)
                        fw.op("act", lambda e, b1=b1: e.activation(sg[0], b1[:, :], AF.Sigmoid), reads=[rb1], writes=[r_sg[0]])
                        fw.op("act", lambda e, b2=b2: e.activation(sg[1], b2[:, :], AF.Sigmoid), reads=[rb2], writes=[r_sg[1]])
                        fw.op("dve", lambda e, bm=bm: e.tensor_tensor(t12[0], bm[:, :], sg[0], ALU.mult), reads=[rbm, r_sg[0]], writes=[r_t12[0]])
                        fw.op("dve", lambda e, ba=ba: e.tensor_tensor(t12[1], ba[:, :], sg[1], ALU.mult), reads=[rba, r_sg[1]], writes=[r_t12[1]])
                        fw.op("pool", lambda e, c=c: e.tensor_tensor(mrg[:, c, :], t12[0], t12[1], ALU.add), reads=r_t12, writes=[r_mrg])
                for half in range(2):
                    wo, rwo = wload("w_out", Wb["w_out"], 0, 8, half * 512, 512)
                    for tt in range(4):
                        bk, rb = nbank()
                        mm(bk[:, :], [(mrg[:, kc, tsl(tt)], wo[:, kc, :]) for kc in range(8)], [rwo, r_mrg], rb)
                        xsl = xa4[:, tt, half * 512:(half + 1) * 512]
                        fw.op("dve", lambda e, bk=bk, xsl=xsl: e.tensor_tensor(xsl, xsl, bk[:, :], ALU.add), reads=[rb], writes=[r_xa4[tt]])
                for tt in range(4):
                    i = tt & 1
                    rms_tile(xa4[:, tt, :], xn4[:, tt, :], st1[i], r_xa4[tt], r_xn4[tt], r_st1[i], 1024)
                    transpose_to(hTb[:, :, tsl(tt)], xn4[:, tt, :], 8, [r_xn4[tt]], r_hTb)
                for g in range(6):
                    n_ = 512 if g < 5 else 256
                    wg, rwg = wload("w_gate", Wb["w_gate"], 0, 8, g * 512, n_)
                    wu, rwu = wload("w_up", Wb["w_up"], 0, 8, g * 512, n_)
                    for cl in range(n_ // 128):
                        f = g * 4 + cl
                        cs_ = slice(cl * 128, (cl + 1) * 128)
                        bg_, rbg = nbank(); mm(bg_[:, :], [(wg[:, kc, cs_], hTb[:, kc, :]) for kc in range(8)], [rwg, r_hTb], rbg)
                        bu, rbu = nbank(); mm(bu[:, :], [(wu[:, kc, cs_], hTb[:, kc, :]) for kc in range(8)], [rwu, r_hTb], rbu)
                        i = f & 1
                        fw.op("act", lambda e, bg_=bg_, i=i: e.activation(sg[i], bg_[:, :], AF.Silu), reads=[rbg], writes=[r_sg[i]])
                        fw.op("dve", lambda e, bu=bu, i=i, f=f: e.tensor_tensor(actT[:, f, :], bu[:, :], sg[i], ALU.mult), reads=[rbu, r_sg[i]], writes=[r_actT])
                for half in range(2):
                    parts = []
                    for k0, nk in ((0, 8), (8, 8), (16, 6)):
                        wd, rwd = wload("w_down", Wb["w_down"], k0, nk, half * 512, 512)
                        parts.append((k0, nk, wd, rwd))
                    for tt in range(4):
                        bk, rb = nbank()
                        pairs = []
                        for k0, nk, wd, rwd in parts:
                            pairs += [(actT[:, k0 + kk, tsl(tt)], wd[:, kk, :]) for kk in range(nk)]
                        mm(bk[:, :], pairs, [p_[3] for p_ in parts] + [r_actT], rb)
                        xsl = xa4[:, tt, half * 512:(half + 1) * 512]
                        fw.op("dve", lambda e, bk=bk, xsl=xsl: e.tensor_tensor(xsl, xsl, bk[:, :], ALU.add), reads=[rb], writes=[r_xa4[tt]])
                fw.op("act", lambda e: e.copy(pn, pa), reads=[r_pa], writes=[r_pn])
                for tt in range(4):
                    fw.op("act", lambda e, tt=tt: e.copy(xn4[:, tt, :], xa4[:, tt, :]), reads=[r_xa4[tt]], writes=[r_xn4[tt]])
                    transpose_to(hTb[:, :, tsl(tt)], xn4[:, tt, :], 8, [r_xn4[tt]], r_hTb)
                    transpose_to(pT[:, :, tsl(tt)], pn[:, tt, :], 2, [r_pn], r_pT)
                for half in range(2):
                    wpg, rwpg = wload("w_ple_gate", Wb["w_ple_gate"], 0, 8, half * 512, 512)
                    wp, rwp = wload("w_ple", Wb["w_ple"], 0, 2, half * 512, 512)
                    for tt in range(4):
                        bgt, rbgt = nbank(); mm(bgt[:, :], [(hTb[:, kc, tsl(tt)], wpg[:, kc, :]) for kc in range(8)], [rwpg, r_hTb], rbgt)
                        bp, rbp = nbank(); mm(bp[:, :], [(pT[:, kc, tsl(tt)], wp[:, kc, :]) for kc in range(2)], [rwp, r_pT], rbp)
                        i = tt & 1
                        fw.op("act", lambda e, bgt=bgt, i=i: e.activation(sg[i], bgt[:, :], AF.Sigmoid), reads=[rbgt], writes=[r_sg[i]])
                        fw.op("dve", lambda e, bp=bp, i=i: e.tensor_tensor(t12[i], bp[:, :], sg[i], ALU.mult), reads=[rbp, r_sg[i]], writes=[r_t12[i]])
                        xsl = xa4[:, tt, half * 512:(half + 1) * 512]
                        fw.op("dve", lambda e, xsl=xsl, i=i: e.tensor_tensor(xsl, xsl, t12[i], ALU.add), reads=[r_t12[i]], writes=[r_xa4[tt]])
                for tt in range(4):
                    i = tt & 1
                    stt = st1[i]
                    fw.op("act", lambda e, tt=tt, stt=stt: e.activation(xn4[:, tt, :], xa4[:, tt, :], AF.Square, accum_out=stt[:, 0:1]), reads=[r_xa4[tt]], writes=[r_xn4[tt], r_st1[i]])
                    fw.op("act", lambda e, stt=stt: e.activation(stt[:, 1:2], stt[:, 0:1], AF.Sqrt, bias=EPS, scale=1.0 / 1024), reads=[r_st1[i]], writes=[r_st1[i]])
                    fw.op("dve", lambda e, stt=stt: e.reciprocal(stt[:, 2:3], stt[:, 1:2]), reads=[r_st1[i]], writes=[r_st1[i]])
                    fw.op("dve", lambda e, tt=tt, stt=stt: e.scalar_tensor_tensor(xa4[:, tt, :], xa4[:, tt, :], stt[:, 2:3], gfin[:], ALU.mult, ALU.mult),
                          reads=[r_st1[i], r_const], writes=[r_xa4[tt]])
                    fw.dma("pool", y[sq, tok0 + tt * 128: tok0 + (tt + 1) * 128, :], xa4[:, tt, :], reads=[r_xa4[tt]], writes=[r_y])

        def phase_mlstm(sq, heads=range(4)):
            QT = VB(0, 4096); KT = VB(8192, 4096); Qp = VB(16384, 4096)
            Ktok = VB(24576, 4096).rearrange("p (t n) -> p t n", t=NT)
            Kp = VB(32768, 4096).rearrange("p (t n) -> p t n", t=NT)
            Vaug = VB(40960, NT * 258).rearrange("p (t n) -> p t n", t=NT)
            bbc = VF(57472, 4096)
            tmpE = [VF(73856 + i * 2048, 512) for i in range(2)]
            Cst = VF(77952, 258); Cbf = VB(78992, 258)
            Dt = [VF(79520 + i * 512, 128) for i in range(2)]
            Sp = [VB(80544 + i * 256, 128) for i in range(2)]
            aC = VF(81056, 32)
            hfo = [VF(81184 + i * 1024, 256) for i in range(3)]
            hs = [VF(84256 + i * 1024, 256) for i in range(2)]
            hn = [VB(86304 + i * 512, 256) for i in range(2)]
            hT2 = [VB(87328 + i * 512, 256).rearrange("p (k n) -> p k n", k=2) for i in range(2)]
            r_QT, r_KT, r_Qp, r_Ktok, r_Kp, r_V, r_bbc = (RES(n) for n in ("QT", "KT", "Qp", "Ktok", "Kp", "Vaug", "bbc"))
            r_tmpE = [RES("tmpE%d" % i) for i in range(2)]
            r_C, r_Cbf, r_aC = RES("Cst"), RES("Cbf"), RES("aC")
            r_Dt = [RES("Dt%d" % i) for i in range(2)]; r_Sp = [RES("Sp%d" % i) for i in range(2)]
            r_hfo = [RES("hfo%d" % i) for i in range(3)]; r_hs = [RES("hs%d" % i) for i in range(2)]
            r_hn = [RES("hn%d" % i) for i in range(2)]; r_hT2 = [RES("hT2_%d" % i) for i in range(2)]
            r_hf = [RES("@hf%d" % c) for c in range(NT)]
            r_hm = RES("@hmT")
            for h in heads:
                fw.dma("sp", QT, qk_scr[h], writes=[r_QT])
                fw.dma("sp", KT, qk_scr[4 + h], writes=[r_KT])
                fw.op("pool", lambda e: e.memset(Vaug[:, :, 256:258], 1.0), writes=[r_V])
                fw.dma("sp", Vaug[:, :, 0:256], v_scr[:, h * 256:(h + 1) * 256].rearrange("(t p) n -> p t n", p=128), reads=[r_V], writes=[r_V])
                for c4 in range(4):
                    transpose_to(Ktok[:, c4 * 8:(c4 + 1) * 8, :], KT[:, c4 * 1024:(c4 + 1) * 1024], 8, [r_KT], r_Ktok)
                for d in range(2):
                    msk = maskf if d == 0 else maskb
                    far = 127 if d == 0 else 0
                    fw.dma("sp", bbc, nb_scr[d, h:h + 1, :].partition_broadcast(128).rearrange("p o n -> p (o n)"), writes=[r_bbc])
                    fw.op("act", lambda e, far=far: e.activation(aC, bbc.rearrange("p (t n) -> p t n", t=NT)[:, :, far], AF.Exp), reads=[r_bbc], writes=[r_aC])
                    for blk in range(NBLK):
                        i = blk & 1
                        bs_ = slice(blk * 512, (blk + 1) * 512)
                        fw.op("act", lambda e, i=i, bs_=bs_: e.activation(tmpE[i], bbc[:, bs_], AF.Exp, bias=LN_SQ), reads=[r_bbc], writes=[r_tmpE[i]])
                        fw.op("dve", lambda e, i=i, bs_=bs_: e.tensor_tensor(Qp[:, bs_], QT[:, bs_], tmpE[i], ALU.mult), reads=[r_QT, r_tmpE[i]], writes=[r_Qp])
                    fw.op("pool", lambda e, msk=msk: e.tensor_tensor(bbc.rearrange("p (t n) -> p t n", t=NT), bbc.rearrange("p (t n) -> p t n", t=NT),
                                                                    msk[:].unsqueeze(1).to_broadcast([128, NT, 128]), ALU.add), reads=[r_bbc, r_const], writes=[r_bbc])
                    fw.op("dve", lambda e, d=d, h=h: e.tensor_tensor(Kp, Ktok, colsT[:, 2 + d, :, h].unsqueeze(2).to_broadcast([128, NT, 128]), ALU.mult),
                          reads=[r_Ktok, r_const], writes=[r_Kp])
                    fw.op("dve", lambda e: e.memset(Cst, 0.0), writes=[r_C])
                    fw.op("pool", lambda e: e.memset(Cbf, 0.0), writes=[r_Cbf])
                    order = list(range(NT)) if d == 0 else list(range(NT - 1, -1, -1))
                    for n_, c in enumerate(order):
                        cs_ = slice(c * 128, (c + 1) * 128)
                        j = n_ & 1
                        fw.op("act", lambda e, j=j, cs_=cs_, d=d, c=c, h=h: e.activation(Dt[j], bbc[:, cs_], AF.Exp, bias=colsT[:, d, c, h:h + 1]),
                              reads=[r_bbc, r_const], writes=[r_Dt[j]])
                        bS, rbS = nbank()
                        mm(bS[:, 0:128], [(KT[:, cs_], QT[:, cs_])], [r_KT, r_QT], rbS)
                        fw.op("dve", lambda e, j=j, bS=bS: e.tensor_tensor(Sp[j], bS[:, 0:128], Dt[j], ALU.mult), reads=[rbS, r_Dt[j]], writes=[r_Sp[j]])
                        bN, rbN = nbank()
                        mm(bN[:, 0:257], [(Sp[j], Vaug[:, c, 0:257]), (Qp[:, cs_], Cbf[:, 0:257])], [r_Sp[j], r_V, r_Qp, r_Cbf], rbN)
                        bU, rbU = nbank()
                        mm(bU[:, 0:257], [(Kp[:, c, :], Vaug[:, c, 0:257])], [r_Kp, r_V], rbU)
                        fw.op("dve", lambda e, c=c, bU=bU: e.scalar_tensor_tensor(Cst[:, 0:257], Cst[:, 0:257], aC[:, c:c + 1], bU[:, 0:257], ALU.mult, ALU.add),
                              reads=[r_C, r_aC, rbU], writes=[r_C])
                        fw.op("act", lambda e: e.copy(Cbf[:, 0:257], Cst[:, 0:257]), reads=[r_C], writes=[r_Cbf])
                        k = n_ & 1
                        stt, r_stt = st1[k], r_st1[k]
                        fw.op("act", lambda e, bN=bN, stt=stt: e.activation(stt[:, 0:1], bN[:, 256:257], AF.Abs), reads=[rbN], writes=[r_stt])
                        fw.op("dve", lambda e, stt=stt: e.tensor_scalar_max(stt[:, 0:1], stt[:, 0:1], 1.0), reads=[r_stt], writes=[r_stt])
                        fw.op("dve", lambda e, stt=stt: e.reciprocal(stt[:, 1:2], stt[:, 0:1]), reads=[r_stt], writes=[r_stt])
                        m3 = n_ % 3
                        if d == 0:
                            fw.op("dve", lambda e, m3=m3, bN=bN, stt=stt: e.tensor_scalar(hfo[m3], bN[:, 0:256], stt[:, 1:2], None, ALU.mult), reads=[rbN, r_stt], writes=[r_hfo[m3]])
                            fw.dma("pool", hf_scr[cs_, :], hfo[m3], reads=[r_hfo[m3]], writes=[r_hf[c]])
                        else:
                            fw.dma("sp", hfo[m3], hf_scr[cs_, :], reads=[r_hf[c]], writes=[r_hfo[m3]])
                            fw.op("dve", lambda e, m3=m3, k=k, bN=bN, stt=stt: e.scalar_tensor_tensor(hs[k], bN[:, 0:256], stt[:, 1:2], hfo[m3], ALU.mult, ALU.add),
                                  reads=[rbN, r_stt, r_hfo[m3]], writes=[r_hs[k]])
                            fw.op("act", lambda e, k=k, stt=stt: e.activation(hn[k], hs[k], AF.Square, accum_out=stt[:, 2:3]), reads=[r_hs[k]], writes=[r_hn[k], r_stt])
                            fw.op("act", lambda e, stt=stt: e.activation(stt[:, 3:4], stt[:, 2:3], AF.Sqrt, bias=EPS, scale=1.0 / 256), reads=[r_stt], writes=[r_stt])
                            fw.op("dve", lambda e, stt=stt: e.reciprocal(stt[:, 2:3], stt[:, 3:4]), reads=[r_stt], writes=[r_stt])
                            fw.op("dve", lambda e, k=k, stt=stt: e.tensor_scalar(hn[k], hs[k], stt[:, 2:3], None, ALU.mult), reads=[r_hs[k], r_stt], writes=[r_hn[k]])
                            transpose_to(hT2[k], hn[k], 2, [r_hn[k]], r_hT2[k])
                            fw.dma("pool", hmT_scr[2 * h:2 * h + 2, :, cs_].rearrange("k p n -> p k n"), hT2[k], reads=[r_hT2[k]], writes=[r_hm])

        def phase_mix(sq):
            cqnT = VB(0, 8192).rearrange("p (k n) -> p k n", k=2); r_cqn = RES("cqnT")
            ckvnT = VB(16384, 8192).rearrange("p (k n) -> p k n", k=2); r_ckvn = RES("ckvnT")
            krT = VB(32768, 4096, 128); r_krT = RES("krT")
            KhT = VB(40960, 4096); r_KhT = RES("KhT")
            Vh = VB(49152, 4096); r_Vh = RES("Vh")
            QhT = VB(57344, 4096); r_QhT = RES("QhT")
            QrT = VB(65536, 4096, 128); r_QrT = RES("QrT")
            cstq = VF(73728, 1024, 64).rearrange("p (c n) -> p c n", c=2); r_cstq = RES("cstq")
            rt = [VF(77824 + i * 2048, 512, 64) for i in range(2)]; r_rt = RES("rt")
            PT = [VB(81920 + i * 1024, 512) for i in range(3)]; r_PT = [RES("PT%d" % i) for i in range(3)]
            rsum = VF(84992, 512); r_rsum = RES("rsum")
            oTs = [VB(87040 + i * 1024, 512) for i in range(2)]; r_oTs = [RES("oTs%d" % i) for i in range(2)]
            r_oT = RES("@oT")
            QT, KT, Qp = wsl[1][:, :], wsl[2][:, :], wsl[3][:, :]
            Ktok = wsl[4][:, :].rearrange("p (t n) -> p t n", t=NT)
            Kp = wsl[5][:, :].rearrange("p (t n) -> p t n", t=NT)
            M0 = 89088
            Vaug = VB(M0, NT * 258).rearrange("p (t n) -> p t n", t=NT)
            bbc = VF(M0 + 16512, 4096)
            tmpE = [VF(M0 + 32896 + i * 2048, 512) for i in range(2)]
            Cst = VF(M0 + 36992, 258); Cbf = VB(M0 + 38032, 258)
            Dt = [VF(M0 + 38560 + i * 512, 128) for i in range(2)]
            Sp = [VB(M0 + 39584 + i * 256, 128) for i in range(2)]
            aC = VF(M0 + 40096, 32)
            hfo = [VF(M0 + 40224 + i * 1024, 256) for i in range(3)]
            hs = [VF(M0 + 43296 + i * 1024, 256) for i in range(2)]
            hn = [VB(M0 + 45344 + i * 512, 256) for i in range(2)]
            hT2 = [VB(M0 + 46368 + i * 512, 256).rearrange("p (k n) -> p k n", k=2) for i in range(2)]
            stm = [T_once("stm%d" % i, [128, 8], F32) for i in range(2)]; r_stm = [RES("stm%d" % i) for i in range(2)]
            assert M0 + 47392 <= AR * 4
            r_QT, r_KT, r_Qp, r_Ktok, r_Kp, r_V, r_bbc = (RES(n) for n in ("QT", "KT", "Qp", "Ktok", "Kp", "Vaug", "bbc"))
            r_tmpE = [RES("tmpE%d" % i) for i in range(2)]
            r_C, r_Cbf, r_aC = RES("Cst"), RES("Cbf"), RES("aC")
            r_Dt = [RES("Dt%d" % i) for i in range(2)]; r_Sp = [RES("Sp%d" % i) for i in range(2)]
            r_hfo = [RES("hfo%d" % i) for i in range(3)]; r_hs = [RES("hs%d" % i) for i in range(2)]
            r_hn = [RES("hn%d" % i) for i in range(2)]; r_hT2 = [RES("hT2_%d" % i) for i in range(2)]
            r_hf = [RES("@hf%d" % c) for c in range(NT)]
            r_hm = RES("@hmT")
            mst = {"b": 0}

            def mbank():
                i = 5 + mst["b"]
                mst["b"] = (mst["b"] + 1) % 3
                return banks[i], r_bank[i]

            def m_transpose(dst3, src_ap, nk, reads, r_dst):
                bk, rb = mbank()
                pbt = bk.bitcast(BF16)
                for k in range(nk):
                    transp(pbt[:, k * 128:(k + 1) * 128], src_ap[:, k * 128:(k + 1) * 128], identb[:], reads + [r_const], rb, signal=(k == nk - 1))
                copy_op(evac_eng(), dst3, pbt[:, 0:nk * 128].rearrange("p (k n) -> p k n", k=nk), [rb], [r_dst])

            b3 = lambda: bbc.rearrange("p (t n) -> p t n", t=NT)

            def mlstm_gen():
                for h in range(4):
                    fw.dma("sp", QT, qk_scr[h], writes=[r_QT])
                    fw.dma("sp", KT, qk_scr[4 + h], writes=[r_KT])
                    fw.op("pool", lambda e: e.memset(Vaug[:, :, 256:258], 1.0), writes=[r_V])
                    fw.dma("sp", Vaug[:, :, 0:256], v_scr[:, h * 256:(h + 1) * 256].rearrange("(t p) n -> p t n", p=128), reads=[r_V], writes=[r_V])
                    yield
                    for c8 in range(8):
                        m_transpose(Ktok[:, c8 * 4:(c8 + 1) * 4, :], KT[:, c8 * 512:(c8 + 1) * 512], 4, [r_KT], r_Ktok)
                        yield
                    for d in range(2):
                        msk = maskf if d == 0 else maskb
                        far = 127 if d == 0 else 0
                        fw.dma("sp", bbc, nb_scr[d, h:h + 1, :].partition_broadcast(128).rearrange("p o n -> p (o n)"), writes=[r_bbc])
                        yield
                        fw.op("act", lambda e, far=far: e.activation(aC, bbc.rearrange("p (t n) -> p t n", t=NT)[:, :, far], AF.Exp), reads=[r_bbc], writes=[r_aC])
                        for blk in range(NBLK):
                            i = blk & 1
                            bs_ = slice(blk * 512, (blk + 1) * 512)
                            fw.op("act", lambda e, i=i, bs_=bs_: e.activation(tmpE[i], bbc[:, bs_], AF.Exp, bias=LN_SQ), reads=[r_bbc], writes=[r_tmpE[i]])
                            yield
                            fw.op("dve", lambda e, i=i, bs_=bs_: e.tensor_tensor(Qp[:, bs_], QT[:, bs_], tmpE[i], ALU.mult), reads=[r_QT, r_tmpE[i]], writes=[r_Qp])
                            yield
                        for q4 in range(4):
                            ts_ = slice(q4 * 8, (q4 + 1) * 8)
                            fw.op("pool", lambda e, msk=msk, ts_=ts_: e.tensor_tensor(b3()[:, ts_, :], b3()[:, ts_, :], msk[:].unsqueeze(1).to_broadcast([128, 8, 128]), ALU.add),
                                  reads=[r_bbc, r_const], writes=[r_bbc])
                            fw.op("dve", lambda e, d=d, h=h, ts_=ts_: e.tensor_tensor(Kp[:, ts_, :], Ktok[:, ts_, :], colsT[:, 2 + d, ts_, h].unsqueeze(2).to_broadcast([128, 8, 128]), ALU.mult),
                                  reads=[r_Ktok, r_const], writes=[r_Kp])
                            yield
                        fw.op("dve", lambda e: e.memset(Cst, 0.0), writes=[r_C])
                        fw.op("pool", lambda e: e.memset(Cbf, 0.0), writes=[r_Cbf])
                        yield
                        order = list(range(NT)) if d == 0 else list(range(NT - 1, -1, -1))
                        for n_, c in enumerate(order):
                            cs_ = slice(c * 128, (c + 1) * 128)
                            j = n_ & 1
                            k = n_ & 1
                            m3 = n_ % 3
                            stt, r_stt = stm[k], r_stm[k]
                            fw.op("act", lambda e, j=j, cs_=cs_, d=d, c=c, h=h: e.activation(Dt[j], bbc[:, cs_], AF.Exp, bias=colsT[:, d, c, h:h + 1]),
                                  reads=[r_bbc, r_const], writes=[r_Dt[j]])
                            bS, rbS = mbank()
                            mm(bS[:, 0:128], [(KT[:, cs_], QT[:, cs_])], [r_KT, r_QT], rbS)
                            if d == 1:
                                fw.dma("sp", hfo[m3], hf_scr[cs_, :], reads=[r_hf[c]], writes=[r_hfo[m3]])
                            yield
                            fw.op("dve", lambda e, j=j, bS=bS: e.tensor_tensor(Sp[j], bS[:, 0:128], Dt[j], ALU.mult), reads=[rbS, r_Dt[j]], writes=[r_Sp[j]])
                            yield
                            bN, rbN = mbank()
                            mm(bN[:, 0:257], [(Sp[j], Vaug[:, c, 0:257]), (Qp[:, cs_], Cbf[:, 0:257])], [r_Sp[j], r_V, r_Qp, r_Cbf], rbN)
                            bU, rbU = mbank()
                            mm(bU[:, 0:257], [(Kp[:, c, :], Vaug[:, c, 0:257])], [r_Kp, r_V], rbU)
                            yield
                            fw.op("dve", lambda e, c=c, bU=bU: e.scalar_tensor_tensor(Cst[:, 0:257], Cst[:, 0:257], aC[:, c:c + 1], bU[:, 0:257], ALU.mult, ALU.add),
                                  reads=[r_C, r_aC, rbU], writes=[r_C])
                            fw.op("act", lambda e, bN=bN, stt=stt: e.activation(stt[:, 0:1], bN[:, 256:257], AF.Abs), reads=[rbN], writes=[r_stt])
                            yield
                            fw.op("act", lambda e: e.copy(Cbf[:, 0:257], Cst[:, 0:257]), reads=[r_C], writes=[r_Cbf])
                            fw.op("dve", lambda e, stt=stt: e.tensor_scalar_max(stt[:, 0:1], stt[:, 0:1], 1.0), reads=[r_stt], writes=[r_stt])
                            fw.op("dve", lambda e, stt=stt: e.reciprocal(stt[:, 1:2], stt[:, 0:1]), reads=[r_stt], writes=[r_stt])
                            yield
                            if d == 0:
                                fw.op("dve", lambda e, m3=m3, bN=bN, stt=stt: e.tensor_scalar(hfo[m3], bN[:, 0:256], stt[:, 1:2], None, ALU.mult), reads=[rbN, r_stt], writes=[r_hfo[m3]])
                                fw.dma("pool", hf_scr[cs_, :], hfo[m3], reads=[r_hfo[m3]], writes=[r_hf[c]])
                                yield
                            else:
                                fw.op("dve", lambda e, m3=m3, k=k, bN=bN, stt=stt: e.scalar_tensor_tensor(hs[k], bN[:, 0:256], stt[:, 1:2], hfo[m3], ALU.mult, ALU.add),
                                      reads=[rbN, r_stt, r_hfo[m3]], writes=[r_hs[k]])
                                yield
                                fw.op("act", lambda e, k=k, stt=stt: e.activation(hn[k], hs[k], AF.Square, accum_out=stt[:, 2:3]), reads=[r_hs[k]], writes=[r_hn[k], r_stt])
                                yield
                                fw.op("act", lambda e, stt=stt: e.activation(stt[:, 3:4], stt[:, 2:3], AF.Sqrt, bias=EPS, scale=1.0 / 256), reads=[r_stt], writes=[r_stt])
                                yield
                                fw.op("dve", lambda e, stt=stt: e.reciprocal(stt[:, 2:3], stt[:, 3:4]), reads=[r_stt], writes=[r_stt])
                                yield
                                fw.op("dve", lambda e, k=k, stt=stt: e.tensor_scalar(hn[k], hs[k], stt[:, 2:3], None, ALU.mult), reads=[r_hs[k], r_stt], writes=[r_hn[k]])
                                yield
                                m_transpose(hT2[k], hn[k], 2, [r_hn[k]], r_hT2[k])
                                fw.dma("pool", hmT_scr[2 * h:2 * h + 2, :, cs_].rearrange("k p n -> p k n"), hT2[k], reads=[r_hT2[k]], writes=[r_hm])
                                yield

            gen = mlstm_gen()
            gst = {"done": False, "n": 0, "calls": 0}

            def advance():
                gst["calls"] += 1
                for _ in range(2 if gst["calls"] % 8 == 0 else 1):
                    if not gst["done"]:
                        try:
                            next(gen)
                            gst["n"] += 1
                        except StopIteration:
                            gst["done"] = True

            def wload0(name, W_ap, k0, nk, c0, ncols, off):
                view = wsl[0][:, off:off + nk * ncols].rearrange("p (k n) -> p k n", k=nk)
                src = W_ap[k0 * 128:(k0 + nk) * 128, c0:c0 + ncols].rearrange("(k p) n -> p k n", p=128)
                fw.dma("sp", view, src, reads=[r_W[name]], writes=[r_wsl[0]])
                return view, r_wsl[0]

            fw.dma("sp", cqnT, cqnT_scr.rearrange("k p n -> p k n"), writes=[r_cqn])
            fw.dma("sp", ckvnT, ckvnT_scr.rearrange("k p n -> p k n"), writes=[r_ckvn])
            fw.op("pool", lambda e: e.memset(krT[64:128, :], 0.0), writes=[r_krT])
            fw.op("pool", lambda e: e.memset(QrT[64:128, :], 0.0), writes=[r_QrT])
            fw.dma("sp", krT[0:64, :], krT_scr, reads=[r_krT], writes=[r_krT])
            abank = {"i": 0}

            def xbank():
                i = abank["i"]
                abank["i"] = (i + 1) % 5
                return banks[i], r_bank[i]

            for h in range(8):
                wk, rwk = wload0("w_ukv", Wb["w_ukv"], 0, 2, h * 256, 256, 0)
                wq, rwq = wload0("w_uq", Wb["w_uq"], 0, 2, h * 192, 192, 512)
                wqs, rwqs = wload0("w_uq_sw", w_uq_sw, 0, 2, h * 64, 64, 896)
                for b in range(NBLK):
                    bs_ = slice(b * 512, (b + 1) * 512)
                    bk, rb = xbank()
                    mm(bk[:, :], [(wk[:, kc, 0:128], ckvnT[:, kc, bs_]) for kc in range(2)], [rwk, r_ckvn], rb)
                    copy_op(evac_eng(), KhT[:, bs_], bk[:, :], [rb], [r_KhT])
                    bk, rb = xbank()
                    mm(bk[:, :], [(wq[:, kc, 0:128], cqnT[:, kc, bs_]) for kc in range(2)], [rwq, r_cqn], rb)
                    copy_op(evac_eng(), QhT[:, bs_], bk[:, :], [rb], [r_QhT])
                    bka, rba = xbank()
                    mm(bka[0:64, :], [(wq[:, kc, 128:192], cqnT[:, kc, bs_]) for kc in range(2)], [rwq, r_cqn], rba)
                    bkb, rbb = xbank()
                    mm(bkb[0:64, :], [(wqs[:, kc, :], cqnT[:, kc, bs_]) for kc in range(2)], [rwqs, r_cqn], rbb)
                    fw.dma("sp", cstq, cs_scr[:, :, bs_].rearrange("c p n -> p c n"), writes=[r_cstq])
                    fw.op("dve", lambda e, bka=bka: e.tensor_tensor(rt[0], bka[0:64, :], cstq[:, 0, :], ALU.mult), reads=[rba, r_cstq], writes=[r_rt])
                    fw.op("dve", lambda e, bkb=bkb: e.tensor_tensor(rt[1], bkb[0:64, :], cstq[:, 1, :], ALU.mult), reads=[rbb, r_cstq], writes=[r_rt])
                    fw.op("pool", lambda e, bs_=bs_: e.tensor_tensor(QrT[0:64, bs_], rt[0], rt[1], ALU.add), reads=[r_rt], writes=[r_QrT])
                    bk, rb = xbank()
                    for tt in range(4):
                        ts_ = slice(b * 512 + tt * 128, b * 512 + (tt + 1) * 128)
                        mm(bk[:, tt * 128:(tt + 1) * 128], [(ckvnT[:, kc, ts_], wk[:, kc, 128:256]) for kc in range(2)], [rwk, r_ckvn], rb)
                    copy_op(evac_eng(), Vh[:, bs_], bk[:, :], [rb], [r_Vh])
                    advance()
                for qb in range(NBLK):
                    qs_ = slice(qb * 512, (qb + 1) * 512)
                    bo, rbo = banks[2 + (qb & 1)], r_bank[2 + (qb & 1)]
                    bsm, rbsm = banks[4], r_bank[4]

                    def S_step(kc, qs_=qs_):
                        ks_ = slice(kc * 128, (kc + 1) * 128)
                        bS, rbS = banks[kc & 1], r_bank[kc & 1]
                        mm(bS[:, :], [(KhT[:, ks_], QhT[:, qs_]), (krT[:, ks_], QrT[:, qs_])], [r_KhT, r_QhT, r_krT, r_QrT], rbS)
                        fw.op("act", lambda e, bS=bS, kc=kc: e.activation(PT[kc % 3], bS[:, :], AF.Exp, scale=A_SCALE), reads=[rbS], writes=[r_PT[kc % 3]])

                    def O_step(kc, bo=bo, rbo=rbo, bsm=bsm, rbsm=rbsm):
                        ks_ = slice(kc * 128, (kc + 1) * 128)
                        fw.op("pe", lambda e, kc=kc, ks_=ks_, bo=bo: e.matmul(bo[:, :], Vh[:, ks_], PT[kc % 3], start=(kc == 0), stop=(kc == NT - 1)),
                              reads=[r_Vh, r_PT[kc % 3]], writes=[rbo], signal=False)
                        fw.op("pe", lambda e, kc=kc, bsm=bsm: e.matmul(bsm[:, :], onesb[:], PT[kc % 3], start=(kc == 0), stop=(kc == NT - 1)),
                              reads=[r_const, r_PT[kc % 3]], writes=[rbsm], signal=True)
                    S_step(0)
                    for kc in range(NT):
                        if kc + 1 < NT:
                            S_step(kc + 1)
                        O_step(kc)
                        advance()
                    fw.op("dve", lambda e, bsm=bsm: e.reciprocal(rsum, bsm[:, :]), reads=[rbsm], writes=[r_rsum])
                    o_ = oTs[qb & 1]
                    fw.op("dve", lambda e, bo=bo, o_=o_: e.tensor_tensor(o_, bo[:, :], rsum, ALU.mult), reads=[rbo, r_rsum], writes=[r_oTs[qb & 1]])
                    fw.dma("pool", oT_scr[h, :, qs_], o_, reads=[r_oTs[qb & 1]], writes=[r_oT])
            n_inside = gst["n"]
            while not gst["done"]:
                advance()
            mix_stats.append((n_inside, gst["n"]))

        def phase_attn(sq, heads=range(8), main_loop=True, qbs=None):
            cqnT = VB(0, 8192).rearrange("p (k n) -> p k n", k=2); r_cqn = RES("cqnT")
            ckvnT = VB(16384, 8192).rearrange("p (k n) -> p k n", k=2); r_ckvn = RES("ckvnT")
            krT = VB(32768, 4096, 128); r_krT = RES("krT")
            KhT = VB(40960, 4096); r_KhT = RES("KhT")
            Vh = VB(49152, 4096); r_Vh = RES("Vh")
            QhT = VB(57344, 4096); r_QhT = RES("QhT")
            QrT = VB(65536, 4096, 128); r_QrT = RES("QrT")
            cstq = VF(73728, 1024, 64).rearrange("p (c n) -> p c n", c=2); r_cstq = RES("cstq")
            rt = [VF(77824 + i * 2048, 512, 64) for i in range(2)]; r_rt = RES("rt")
            PT = [VB(81920 + i * 1024, 512) for i in range(3)]; r_PT = [RES("PT%d" % i) for i in range(3)]
            rsum = VF(84992, 512); r_rsum = RES("rsum")
            oTs = [VB(87040 + i * 1024, 512) for i in range(2)]; r_oTs = [RES("oTs%d" % i) for i in range(2)]
            r_oT = RES("@oT")
            fw.dma("sp", cqnT, cqnT_scr.rearrange("k p n -> p k n"), writes=[r_cqn])
            fw.dma("sp", ckvnT, ckvnT_scr.rearrange("k p n -> p k n"), writes=[r_ckvn])
            fw.op("pool", lambda e: e.memset(krT[64:128, :], 0.0), writes=[r_krT])
            fw.op("pool", lambda e: e.memset(QrT[64:128, :], 0.0), writes=[r_QrT])
            fw.dma("sp", krT[0:64, :], krT_scr, reads=[r_krT], writes=[r_krT])
            for h in heads:
                wk, rwk = wload("w_ukv", Wb["w_ukv"], 0, 2, h * 256, 256)
                wq, rwq = wload("w_uq", Wb["w_uq"], 0, 2, h * 192, 192)
                wqs, rwqs = wload("w_uq_sw", w_uq_sw, 0, 2, h * 64, 64)
                for b in range(NBLK):
                    bs_ = slice(b * 512, (b + 1) * 512)
                    bk, rb = nbank()
                    mm(bk[:, :], [(wk[:, kc, 0:128], ckvnT[:, kc, bs_]) for kc in range(2)], [rwk, r_ckvn], rb)
                    copy_op(evac_eng(), KhT[:, bs_], bk[:, :], [rb], [r_KhT])
                    bk, rb = nbank()
                    mm(bk[:, :], [(wq[:, kc, 0:128], cqnT[:, kc, bs_]) for kc in range(2)], [rwq, r_cqn], rb)
                    copy_op(evac_eng(), QhT[:, bs_], bk[:, :], [rb], [r_QhT])
                    bka, rba = nbank()
                    mm(bka[0:64, :], [(wq[:, kc, 128:192], cqnT[:, kc, bs_]) for kc in range(2)], [rwq, r_cqn], rba)
                    bkb, rbb = nbank()
                    mm(bkb[0:64, :], [(wqs[:, kc, :], cqnT[:, kc, bs_]) for kc in range(2)], [rwqs, r_cqn], rbb)
                    fw.dma("sp", cstq, cs_scr[:, :, bs_].rearrange("c p n -> p c n"), writes=[r_cstq])
                    fw.op("dve", lambda e, bka=bka: e.tensor_tensor(rt[0], bka[0:64, :], cstq[:, 0, :], ALU.mult), reads=[rba, r_cstq], writes=[r_rt])
                    fw.op("dve", lambda e, bkb=bkb: e.tensor_tensor(rt[1], bkb[0:64, :], cstq[:, 1, :], ALU.mult), reads=[rbb, r_cstq], writes=[r_rt])
                    fw.op("pool", lambda e, bs_=bs_: e.tensor_tensor(QrT[0:64, bs_], rt[0], rt[1], ALU.add), reads=[r_rt], writes=[r_QrT])
                    bk, rb = nbank()
                    for tt in range(4):
                        ts_ = slice(b * 512 + tt * 128, b * 512 + (tt + 1) * 128)
                        mm(bk[:, tt * 128:(tt + 1) * 128], [(ckvnT[:, kc, ts_], wk[:, kc, 128:256]) for kc in range(2)], [rwk, r_ckvn], rb)
                    copy_op(evac_eng(), Vh[:, bs_], bk[:, :], [rb], [r_Vh])
                for qb in ((qbs if qbs is not None else range(NBLK)) if main_loop else ()):
                    qs_ = slice(qb * 512, (qb + 1) * 512)
                    bo, rbo = banks[4 + (qb & 1)], r_bank[4 + (qb & 1)]
                    bsm, rbsm = banks[6 + (qb & 1)], r_bank[6 + (qb & 1)]

                    def S_step(kc):
                        ks_ = slice(kc * 128, (kc + 1) * 128)
                        bS, rbS = banks[kc % 4], r_bank[kc % 4]
                        mm(bS[:, :], [(KhT[:, ks_], QhT[:, qs_]), (krT[:, ks_], QrT[:, qs_])], [r_KhT, r_QhT, r_krT, r_QrT], rbS)
                        fw.op("act", lambda e, bS=bS, kc=kc: e.activation(PT[kc % 3], bS[:, :], AF.Exp, scale=A_SCALE), reads=[rbS], writes=[r_PT[kc % 3]])

                    def O_step(kc):
                        ks_ = slice(kc * 128, (kc + 1) * 128)
                        fw.op("pe", lambda e, kc=kc, ks_=ks_, bo=bo: e.matmul(bo[:, :], Vh[:, ks_], PT[kc % 3], start=(kc == 0), stop=(kc == NT - 1)),
                              reads=[r_Vh, r_PT[kc % 3]], writes=[rbo], signal=False)
                        fw.op("pe", lambda e, kc=kc, bsm=bsm: e.matmul(bsm[:, :], onesb[:], PT[kc % 3], start=(kc == 0), stop=(kc == NT - 1)),
                              reads=[r_const, r_PT[kc % 3]], writes=[rbsm], signal=True)
                    S_step(0)
                    S_step(1)
                    for kc in range(NT):
                        if kc + 2 < NT:
                            S_step(kc + 2)
                        O_step(kc)
                    fw.op("dve", lambda e, bsm=bsm: e.reciprocal(rsum, bsm[:, :]), reads=[rbsm], writes=[r_rsum])
                    o_ = oTs[qb & 1]
                    fw.op("dve", lambda e, bo=bo, o_=o_: e.tensor_tensor(o_, bo[:, :], rsum, ALU.mult), reads=[rbo, r_rsum], writes=[r_oTs[qb & 1]])
                    fw.dma("pool", oT_scr[h, :, qs_], o_, reads=[r_oTs[qb & 1]], writes=[r_oT])
            if dbg == "M1":
                fw.dma("pool", dbg_out["d_oT"], oTs[0], reads=[r_oTs[0]], writes=[RES("@dbgM1")])
            if isinstance(dbg, str) and dbg.startswith("M:"):
                fw.dma("pool", dbg_out["d_oTs0"], oTs[0], reads=[r_oTs[0]], writes=[RES("@dbgMa")])
                fw.dma("pool", dbg_out["d_oTs1"], oTs[1], reads=[r_oTs[1]], writes=[RES("@dbgMb")])
            if dbg == "E":
                rd = RES("@dbgE")
                for n_, ap_, r_ in (("d_cqnT", cqnT, r_cqn), ("d_ckvnT", ckvnT, r_ckvn), ("d_krT", krT[0:64, :], r_krT),
                                    ("d_KhT", KhT, r_KhT), ("d_QhT", QhT, r_QhT), ("d_QrT", QrT[0:64, :], r_QrT), ("d_Vh", Vh, r_Vh)):
                    fw.dma("pool", dbg_out[n_], ap_, reads=[r_], writes=[rd])

        for sq in range(NSEQ):
            if 1 in phases:
                phase1(sq)
                fw.barrier()
            if 5 in phases:
                phase_mix(sq)
                fw.barrier()
            if 2 in phases:
                phase_mlstm(sq, heads=([0] if dbg == "L1" else range(4)))
                fw.barrier()
            if 3 in phases:
                if dbg == "E":
                    phase_attn(sq, heads=[0], main_loop=False)
                elif dbg == "M1":
                    phase_attn(sq, heads=[0], main_loop=True, qbs=[0])
                elif isinstance(dbg, str) and dbg.startswith("M:"):
                    _, nh_, nq_ = dbg.split(":")
                    phase_attn(sq, heads=list(range(int(nh_))), main_loop=True, qbs=list(range(int(nq_))))
                else:
                    phase_attn(sq)
                fw.barrier()
            if 4 in phases:
                phase3(sq)
                fw.barrier()
        fw.barrier()
        fw.emit_all()
    if mix_stats:
        print("[phase_mix] scan generator steps issued inside attention / total, per sequence:", mix_stats)
    return nc


NSEQ_PER_CORE = 3


def _consts():
    ident = np.eye(128, dtype=np.float32)
    s = np.arange(128)[:, None]
    t = np.arange(128)[None, :]
    maskf = np.where(t >= s, 0.0, -1000.0).astype(np.float32)
    maskb = np.where(t <= s, 0.0, -1000.0).astype(np.float32)
    return {"c_ident": ident, "c_maskf": maskf, "c_maskb": maskb}


def kernel(**inputs):
    X = np.concatenate([inputs["x_prompt"], inputs["x_sample"]], axis=0)
    P = np.concatenate([inputs["p_prompt"][0], inputs["p_sample"][0]], axis=0)
    nseq = X.shape[0]
    base = {}
    for k in WSHAPES:
        base[k] = np.ascontiguousarray(inputs[k][0])
    for k in GSHAPES:
        base[k] = np.ascontiguousarray(inputs[k].reshape(1, -1))
    base["g_final"] = np.ascontiguousarray(inputs["g_final"].reshape(1, -1))
    base["b_gate"] = np.ascontiguousarray(inputs["b_gate"].reshape(1, 16))
    base["conv_w"] = np.ascontiguousarray(inputs["conv_w"][0])
    base.update(_consts())
    slots = [[c, c + 8, c + 16 if c + 16 < nseq else c] for c in range(8)]
    maps = []
    for sl in slots:
        m = dict(base)
        m["xs"] = np.ascontiguousarray(X[sl])
        m["ps"] = np.ascontiguousarray(P[sl])
        maps.append(m)
    nc = build(NSEQ_PER_CORE, dbg=False, phases=(1, 5, 4))
    res = run_bass_kernel_spmd(nc, maps, core_ids=list(range(8)))
    Y = np.zeros((nseq, S, D), dtype=np.float32)
    for c, sl in enumerate(slots):
        yc = res.results[c]["y"]
        for j, sidx in enumerate(sl):
            if j == 2 and c + 16 >= nseq:
                continue
            Y[sidx] = yc[j]
    nb = inputs["x_prompt"].shape[0]
    return (np.ascontiguousarray(Y[:nb], dtype=np.float32), np.ascontiguousarray(Y[nb:], dtype=np.float32))
```

```python
import numpy as np
import concourse.bass as bass
import concourse.mybir as mybir
from concourse.bass_utils import run_bass_kernel_spmd

F32 = mybir.dt.float32
BF16 = mybir.dt.bfloat16
I32 = mybir.dt.int32
AF = mybir.ActivationFunctionType
ALU = mybir.AluOpType
AX = mybir.AxisListType


class Sem:
    def __init__(self, handle, name, dma=False, owner=None):
        self.h = handle
        self.name = name
        self.dma = dma
        self.owner = owner
        self.issued = 0


class Res:
    __slots__ = ("name", "w", "r", "dsem")

    def __init__(self, name):
        self.name = name
        self.w = None
        self.r = {}
        self.dsem = None


class Eng:
    def __init__(self, name, sem, is_pe=False):
        self.name = name
        self.sem = sem
        self.ops = []
        self.seen = {}
        self.is_pe = is_pe
        self.pending = False


class FW:
    def __init__(self, nc, stack):
        self.nc = nc
        self.stack = stack
        self.engs = {}
        self.nsem = 0
        self.all_sems = []
        for n in ("pe", "act", "dve", "pool", "sp"):
            self.engs[n] = Eng(n, None, is_pe=(n == "pe"))
            self.engs[n].sem = self._newsem("s_" + n, owner=self.engs[n])
        self.dma_res = []
        self.n_ins = 0

    SEM_LIMIT = 6000

    def _newsem(self, name, dma=False, owner=None):
        self.nsem += 1
        h = self.stack.enter_context(self.nc.semaphore("%s_%d" % (name, self.nsem)))
        sm = Sem(h, name, dma=dma, owner=owner)
        self.all_sems.append(sm)
        return sm

    def dsem(self, res, qclass):
        if res.dsem is None:
            res.dsem = {}
            self.dma_res.append(res)
        sm = res.dsem.get(qclass)
        if sm is None or sm.issued >= self.SEM_LIMIT:
            sm = res.dsem[qclass] = self._newsem("d%s_%s" % (qclass[0], res.name.replace("@", "h_")), dma=True)
        return sm

    def _collect(self, E, reads, writes):
        waits = {}

        def need(ev):
            if ev is None:
                return
            sem, val = ev
            if sem.owner is E:
                if E.is_pe:
                    return
            else:
                if sem.dma and sem.issued > val:
                    val = sem.issued
            if E.seen.get(sem, 0) >= val:
                return
            if waits.get(sem, 0) < val:
                waits[sem] = val

        for r in reads:
            need(r.w)
        for w in writes:
            need(w.w)
            for s, v in w.r.items():
                need((s, v))
        for s, v in waits.items():
            E.seen[s] = v
        return list(waits.items())

    def _commit(self, ev, reads, writes):
        for w in writes:
            w.w = ev
            w.r = {}
        for r in reads:
            if r in writes:
                continue
            s, v = ev
            if r.r.get(s, 0) < v:
                r.r[s] = v

    def op(self, eng, fn, reads=(), writes=(), signal=True):
        E = self.engs[eng]
        waits = self._collect(E, reads, writes)
        if signal:
            E.sem.issued += 1
        val = E.sem.issued if signal else E.sem.issued + 1
        sem_h = E.sem.h

        def emit(e, waits=waits, fn=fn, signal=signal):
            for s, v in waits:
                e.wait_ge(s.h, v)
            ins = fn(e)
            if signal:
                ins.then_inc(sem_h, 1)
        E.ops.append(emit)
        E.pending = not signal
        self._commit((E.sem, val), reads, writes)
        self.n_ins += 1
        if signal and E.sem.issued >= self.SEM_LIMIT:
            E.sem = self._newsem("s_" + E.name, owner=E)

    def dma(self, eng, out, in_, reads=(), writes=(), sem_res=None, **kw):
        E = self.engs[eng]
        waits = self._collect(E, reads, writes)
        if sem_res is None:
            sem_res = writes[0] if (writes and writes[0].name[0] != "@") else reads[0]
        ds = self.dsem(sem_res, "sw" if eng == "pool" else "hw")
        ds.issued += 16
        val = ds.issued

        def emit(e, waits=waits):
            for s, v in waits:
                e.wait_ge(s.h, v)
            e.dma_start(out=out, in_=in_, **kw).then_inc(ds.h, 16)
        E.ops.append(emit)
        self._commit((ds, val), reads, writes)
        self.n_ins += 1

    def barrier(self):
        for E in self.engs.values():
            assert not E.pending
        evs = [(sm, sm.issued) for sm in self.all_sems if sm.issued > 0]
        for E in self.engs.values():
            waits = []
            for s, v in evs:
                if s.owner is E and E.is_pe:
                    continue
                if E.seen.get(s, 0) < v:
                    waits.append((s, v))
                    E.seen[s] = v

            def emit(e, waits=waits):
                for s, v in waits:
                    e.wait_ge(s.h, v)
            E.ops.append(emit)

    def wait_all(self, eng, resources):
        E = self.engs[eng]
        waits = self._collect(E, resources, [])

        def emit(e, waits=waits):
            for s, v in waits:
                e.wait_ge(s.h, v)
        E.ops.append(emit)

    def emit_all(self):
        nc = self.nc
        with nc.Block() as block:
            @block.tensor
            def _(e):
                for f in self.engs["pe"].ops:
                    f(e)

            @block.scalar
            def _(e):
                for f in self.engs["act"].ops:
                    f(e)

            @block.vector
            def _(e):
                for f in self.engs["dve"].ops:
                    f(e)

            @block.gpsimd
            def _(e):
                for f in self.engs["pool"].ops:
                    f(e)

            @block.sync
            def _(e):
                for f in self.engs["sp"].ops:
                    f(e)


import math
from contextlib import ExitStack

S = 4096
D = 1024
NBLK = 8
NT = 32
DFF = 2816
V0, O0, G0, CQ0, CKV0, KR0, GM0, INW = 1024, 2048, 3072, 3088, 3344, 3600, 3664, 5712
EPS = 1e-6
LN_SQ = math.log(128 ** -0.5)
A_SCALE = 192 ** -0.5

WSHAPES = {
    "w_in": (1024, 5712), "w_mo": (1024, 1024), "w_uq": (256, 1536), "w_ukv": (256, 2048),
    "w_ao": (1024, 1024), "w_out": (1024, 1024), "w_gate": (1024, 2816), "w_up": (1024, 2816),
    "w_down": (2816, 1024), "w_ple": (256, 1024), "w_ple_gate": (1024, 1024),
}
GAINS = {"w_in": "g_mix", "w_mo": "g_mhead", "w_uq": "g_q", "w_ukv": "g_kv", "w_gate": "g_ffn", "w_up": "g_ffn"}
GSHAPES = {"g_mix": 1024, "g_mhead": 1024, "g_q": 256, "g_kv": 256, "g_ffn": 1024}


def build(NSEQ, dbg=False, phases=(1, 2, 3, 4)):
    nc = bass.Bass("TRN2", target_bir_lowering=False)
    din = lambda n, sh, dt=F32: nc.dram_tensor(n, list(sh), dt, kind="ExternalInput").ap()
    dscr = lambda n, sh, dt: nc.dram_tensor(n, list(sh), dt, kind="Internal").ap()
    xs = din("xs", [NSEQ, S, D])
    ps = din("ps", [NSEQ, S, 256])
    Wf = {k: din(k, sh) for k, sh in WSHAPES.items()}
    Gd = {k: din(k, [1, n]) for k, n in GSHAPES.items()}
    g_final = din("g_final", [1, D])
    b_gate = din("b_gate", [1, 16])
    conv_w = din("conv_w", [5, 1024])
    c_ident = din("c_ident", [128, 128])
    c_maskf = din("c_maskf", [128, 128])
    c_maskb = din("c_maskb", [128, 128])
    y = nc.dram_tensor("y", [NSEQ, S, D], F32, kind="ExternalOutput").ap()
    dbg_out = {}
    if dbg == "P1":
        dbg_out["d_colsT"] = nc.dram_tensor("d_colsT", [128, 512], F32, kind="ExternalOutput").ap()
    if dbg == "M1":
        dbg_out["d_oT"] = nc.dram_tensor("d_oT", [128, 512], BF16, kind="ExternalOutput").ap()
    if isinstance(dbg, str) and dbg.startswith("M:"):
        dbg_out["d_oTs0"] = nc.dram_tensor("d_oTs0", [128, 512], BF16, kind="ExternalOutput").ap()
        dbg_out["d_oTs1"] = nc.dram_tensor("d_oTs1", [128, 512], BF16, kind="ExternalOutput").ap()
    if dbg == "E":
        for n_, sh_ in (("d_cqnT", [128, 2, S]), ("d_ckvnT", [128, 2, S]), ("d_krT", [64, S]),
                        ("d_KhT", [128, S]), ("d_QhT", [128, S]), ("d_QrT", [64, S]), ("d_Vh", [128, S])):
            dbg_out[n_] = nc.dram_tensor(n_, sh_, BF16, kind="ExternalOutput").ap()
    Wb = {k: dscr(k + "_b", sh, BF16) for k, sh in WSHAPES.items()}
    w_kr_sw = dscr("w_kr_sw", [1024, 64], BF16)
    w_uq_sw = dscr("w_uq_sw", [256, 512], BF16)
    cs_scr = dscr("cs_scr", [2, 64, S], F32)
    qk_scr = (nc.dram_tensor("qk_scr", [8, 128, S], BF16, kind="ExternalOutput").ap() if dbg == "P1" else dscr("qk_scr", [8, 128, S], BF16))
    v_scr = (nc.dram_tensor("v_scr", [S, 1024], BF16, kind="ExternalOutput").ap() if dbg == "P1" else dscr("v_scr", [S, 1024], BF16))
    nb_scr = (nc.dram_tensor("nb_scr", [2, 4, S], F32, kind="ExternalOutput").ap() if dbg == "P1" else dscr("nb_scr", [2, 4, S], F32))
    hf_scr = dscr("hf_scr", [S, 256], F32)
    hmT_scr = (nc.dram_tensor("hmT_scr", [8, 128, S], BF16, kind="ExternalOutput").ap() if dbg == "L1" else dscr("hmT_scr", [8, 128, S], BF16))
    if isinstance(dbg, str) and dbg.startswith("M:"):
        oT_scr = nc.dram_tensor("oT_scr", [8, 128, S], BF16, kind="ExternalOutput").ap()
    else:
        oT_scr = dscr("oT_scr", [8, 128, S], BF16)
    cqnT_scr = dscr("cqnT_scr", [2, 128, S], BF16)
    ckvnT_scr = dscr("ckvnT_scr", [2, 128, S], BF16)
    krT_scr = dscr("krT_scr", [64, S], BF16)

    _res_registry = {}
    _ResClass = Res

    def RES(name):
        r = _res_registry.get(name)
        if r is None:
            r = _res_registry[name] = _ResClass(name)
        return r

    with ExitStack() as st:
        fw = FW(nc, st)
        T = lambda name, shape, dt: st.enter_context(nc.sbuf_tensor(name, shape, dt))
        identf = T("identf", [128, 128], F32)
        identb = T("identb", [128, 128], BF16)
        maskf = T("maskf", [128, 128], F32)
        maskb = T("maskb", [128, 128], F32)
        onesb = T("onesb", [128, 128], BF16)
        gfin = T("gfin", [128, D], F32)
        convw = T("convw", [128, 5, 8], F32)
        gcol = {k: T("gc_" + k, [128, n // 128], F32) for k, n in GSHAPES.items()}
        gneg = {k: T("gn_" + k, [128, GSHAPES[k] // 128], F32) for k in ("g_mix", "g_q")}
        bg = T("bg", [4, 4], F32)
        colsT = T("colsT", [128, 4, NT, 4], F32)
        wsl = [T("wsl%d" % i, [128, 4096], BF16) for i in range(6)]
        r_wsl = [RES("wsl%d" % i) for i in range(6)]
        AR = 34816
        arena = T("arena", [128, AR], F32)
        arena_b = arena.bitcast(BF16)
        arena_i = arena.bitcast(I32)
        banks = [st.enter_context(nc.psum_tensor("pb%d" % i, [128, 512], F32)) for i in range(8)]
        r_bank = [RES("pb%d" % i) for i in range(8)]
        r_const = RES("const")
        state = {"bank": 0, "wsl": 0, "ev": 0}

        def VF(off, n, parts=128):
            assert off % 4 == 0 and off // 4 + n <= AR
            return arena[0:parts, off // 4: off // 4 + n]

        def VB(off, n, parts=128):
            assert off % 2 == 0 and off // 2 + n <= 2 * AR
            return arena_b[0:parts, off // 2: off // 2 + n]

        def nbank():
            i = state["bank"]
            state["bank"] = (i + 1) % 8
            return banks[i], r_bank[i]

        def evac_eng():
            state["ev"] ^= 1
            return "act" if state["ev"] else "dve"

        def copy_op(eng, out, in_, reads, writes):
            if eng == "act":
                fw.op("act", lambda e: e.copy(out, in_), reads=reads, writes=writes)
            else:
                fw.op(eng, lambda e: e.tensor_copy(out, in_), reads=reads, writes=writes)

        def mm(out_ap, pairs, reads, wres):
            n = len(pairs)
            for i, (l, r) in enumerate(pairs):
                fw.op("pe", lambda e, l=l, r=r, i=i: e.matmul(out_ap, l, r, start=(i == 0), stop=(i == n - 1)),
                      reads=reads, writes=[wres], signal=(i == n - 1))

        def transp(out_ap, in_ap, ident, reads, wres, signal=True):
            fw.op("pe", lambda e: e.transpose(out_ap, in_ap, ident), reads=reads, writes=[wres], signal=signal)

        r_W = {k: RES("@" + k) for k in WSHAPES}
        r_W["w_kr_sw"] = RES("@w_kr_sw")
        r_W["w_uq_sw"] = RES("@w_uq_sw")

        def wload(name, W_ap, k0, nk, c0, ncols):
            i = state["wsl"]
            state["wsl"] = (i + 1) % 6
            assert nk * ncols <= 4096
            view = wsl[i][:, 0:nk * ncols].rearrange("p (k n) -> p k n", k=nk)
            src = W_ap[k0 * 128:(k0 + nk) * 128, c0:c0 + ncols].rearrange("(k p) n -> p k n", p=128)
            fw.dma("sp", view, src, reads=[r_W[name]], writes=[r_wsl[i]])
            return view, r_wsl[i]

        fw.dma("sp", identf[:], c_ident, writes=[r_const])
        fw.dma("sp", maskf[:], c_maskf, writes=[r_const])
        fw.dma("sp", maskb[:], c_maskb, writes=[r_const])
        fw.dma("sp", gfin[:], g_final.partition_broadcast(128).rearrange("p o n -> p (o n)"), writes=[r_const])
        for j in range(5):
            fw.dma("sp", convw[:, j, :], conv_w[j:j + 1, :].rearrange("o (c p) -> p (o c)", p=128), writes=[r_const], allow_slow_non_contiguous=True)
        for k in GSHAPES:
            fw.dma("sp", gcol[k][:], Gd[k].rearrange("o (k p) -> p (o k)", p=128), writes=[r_const], allow_slow_non_contiguous=True)
        fw.dma("sp", bg[:], b_gate.rearrange("o (g h) -> h (o g)", h=4), writes=[r_const], allow_slow_non_contiguous=True)
        fw.op("dve", lambda e: e.tensor_copy(identb[:], identf[:]), reads=[r_const], writes=[r_const])
        fw.op("dve", lambda e: e.memset(onesb[:], 1.0), writes=[r_const])
        for k in gneg:
            fw.op("dve", lambda e, k=k: e.tensor_scalar(gneg[k][:], gcol[k][:], -1.0, None, ALU.mult), reads=[r_const], writes=[r_const])
        fw.barrier()

        stg_in = [VF(i * 8192, 2048) for i in range(2)]
        stg_out = [VB(16384 + i * 4096, 2048) for i in range(2)]
        r_si = [RES("stg_in%d" % i) for i in range(2)]
        r_so = [RES("stg_out%d" % i) for i in range(2)]
        pw = {"i": 0}

        def prep(name):
            K, N = WSHAPES[name]
            g = gcol[GAINS[name]] if name in GAINS else None
            for kc in range(K // 128):
                for c0 in range(0, N, 2048):
                    n_ = min(2048, N - c0)
                    i = pw["i"] & 1
                    pw["i"] += 1
                    fw.dma("sp", stg_in[i][:, 0:n_], Wf[name][kc * 128:(kc + 1) * 128, c0:c0 + n_], writes=[r_si[i]])
                    eng = evac_eng()
                    if g is None:
                        copy_op(eng, stg_out[i][:, 0:n_], stg_in[i][:, 0:n_], [r_si[i]], [r_so[i]])
                    elif eng == "act":
                        fw.op("act", lambda e, i=i, kc=kc, n_=n_, g=g: e.activation(stg_out[i][:, 0:n_], stg_in[i][:, 0:n_], AF.Copy, scale=g[:, kc:kc + 1]),
                              reads=[r_si[i], r_const], writes=[r_so[i]])
                    else:
                        fw.op("dve", lambda e, i=i, kc=kc, n_=n_, g=g: e.tensor_scalar(stg_out[i][:, 0:n_], stg_in[i][:, 0:n_], g[:, kc:kc + 1], None, ALU.mult),
                              reads=[r_si[i], r_const], writes=[r_so[i]])
                    fw.dma("pool", Wb[name][kc * 128:(kc + 1) * 128, c0:c0 + n_], stg_out[i][:, 0:n_], reads=[r_so[i]], writes=[r_W[name]])

        def prep_sw(dst, dname, src, K, bases, g, gn):
            for kc in range(K // 128):
                i = pw["i"] & 1
                pw["i"] += 1
                nb_ = len(bases)
                for j, b in enumerate(bases):
                    fw.dma("sp", stg_in[i][:, j * 64:(j + 1) * 64], src[kc * 128:(kc + 1) * 128, b:b + 64], writes=[r_si[i]])
                for j in range(nb_):
                    fw.op("dve", lambda e, i=i, j=j, kc=kc: e.tensor_scalar(stg_out[i][:, j * 64:j * 64 + 32], stg_in[i][:, j * 64 + 32:j * 64 + 64], gn[:, kc:kc + 1], None, ALU.mult),
                          reads=[r_si[i], r_const], writes=[r_so[i]])
                    fw.op("dve", lambda e, i=i, j=j, kc=kc: e.tensor_scalar(stg_out[i][:, j * 64 + 32:j * 64 + 64], stg_in[i][:, j * 64:j * 64 + 32], g[:, kc:kc + 1], None, ALU.mult),
                          reads=[r_si[i], r_const], writes=[r_so[i]])
                fw.dma("pool", dst[kc * 128:(kc + 1) * 128, 0:64 * nb_], stg_out[i][:, 0:64 * nb_], reads=[r_so[i]], writes=[r_W[dname]])

        for name in WSHAPES:
            prep(name)
        prep_sw(w_kr_sw, "w_kr_sw", Wf["w_in"], 1024, [KR0], gcol["g_mix"], gneg["g_mix"])
        prep_sw(w_uq_sw, "w_uq_sw", Wf["w_uq"], 256, [h * 192 + 128 for h in range(8)], gcol["g_q"], gneg["g_q"])

        r_tr = RES("trig")
        TB0 = 24576
        posf = VF(TB0, S, 64)
        tq = VF(TB0 + 16384, S, 64)
        ti = arena_i[0:64, (TB0 + 32768) // 4:(TB0 + 32768) // 4 + S]
        tk = VF(TB0 + 49152, S, 64)
        tu = VF(TB0 + 65536, S, 64)
        pidx = T("pidx", [64, 1], F32)
        invf = T("invf", [64, 1], F32)
        pidx_i = T("pidx_i", [64, 1], I32)
        fw.op("pool", lambda e: e.iota(ti, [[1, S]], base=0, channel_multiplier=0), writes=[r_tr])
        fw.op("pool", lambda e: e.iota(pidx_i[0:32, :], [[0, 1]], base=0, channel_multiplier=1), writes=[r_tr])
        fw.op("pool", lambda e: e.iota(pidx_i[32:64, :], [[0, 1]], base=0, channel_multiplier=1), writes=[r_tr])
        fw.op("dve", lambda e: e.tensor_copy(posf, ti), reads=[r_tr], writes=[r_tr])
        fw.op("dve", lambda e: e.tensor_copy(pidx[:], pidx_i[:]), reads=[r_tr], writes=[r_tr])
        fw.op("act", lambda e: e.activation(invf[:], pidx[:], AF.Exp, scale=-math.log(10000.0) / 32.0), reads=[r_tr], writes=[r_tr])
        fw.op("dve", lambda e: e.tensor_scalar(tq, posf, invf[:, 0:1], 1.0 / (2 * math.pi), ALU.mult, ALU.mult), reads=[r_tr], writes=[r_tr])
        for which, shift in ((1, 0.0), (0, 0.25)):
            fw.op("dve", lambda e, shift=shift: e.tensor_scalar(tu, tq, shift, None, ALU.add), reads=[r_tr], writes=[r_tr])
            fw.op("dve", lambda e: e.tensor_copy(ti, tu), reads=[r_tr], writes=[r_tr])
            fw.op("dve", lambda e: e.tensor_copy(tk, ti), reads=[r_tr], writes=[r_tr])
            fw.op("dve", lambda e: e.tensor_tensor(tu, tu, tk, ALU.subtract), reads=[r_tr], writes=[r_tr])
            fw.op("dve", lambda e: e.tensor_scalar(tk, tu, 0.5, None, ALU.is_gt), reads=[r_tr], writes=[r_tr])
            fw.op("dve", lambda e: e.tensor_tensor(tu, tu, tk, ALU.subtract), reads=[r_tr], writes=[r_tr])
            fw.op("dve", lambda e: e.tensor_scalar(tk, tu, -0.5, None, ALU.is_lt), reads=[r_tr], writes=[r_tr])
            fw.op("dve", lambda e: e.tensor_tensor(tu, tu, tk, ALU.add), reads=[r_tr], writes=[r_tr])
            fw.op("act", lambda e: e.activation(tk, tu, AF.Sin, scale=2 * math.pi), reads=[r_tr], writes=[r_tr])
            fw.dma("pool", cs_scr[which], tk, reads=[r_tr], writes=[RES("@cs")], sem_res=r_tr)
        fw.barrier()

        _once = {}

        def T_once(name, shape, dt):
            if name not in _once:
                _once[name] = T(name, shape, dt)
            return _once[name]
        mix_stats = []
        st1 = [T("st1_%d" % i, [128, 8], F32) for i in range(2)]
        r_st1 = [RES("st1_%d" % i) for i in range(2)]
        cmask = T("cmask", [4, 512], F32)
        fw.op("pool", lambda e: e.memset(cmask[:], 1.0), writes=[r_const])
        for j in range(4):
            fw.op("pool", lambda e, j=j: e.memset(cmask[:, j * 128:j * 128 + 1], 0.0), writes=[r_const])
        fw.barrier()

        def rms_tile(x_ap, xn_ap, stt, r_x, r_xn, r_stt, n):
            fw.op("act", lambda e: e.activation(xn_ap, x_ap, AF.Square, accum_out=stt[:, 0:1]), reads=[r_x], writes=[r_xn, r_stt])
            fw.op("act", lambda e: e.activation(stt[:, 1:2], stt[:, 0:1], AF.Sqrt, bias=EPS, scale=1.0 / n), reads=[r_stt], writes=[r_stt])
            fw.op("dve", lambda e: e.reciprocal(stt[:, 2:3], stt[:, 1:2]), reads=[r_stt], writes=[r_stt])
            fw.op("dve", lambda e: e.tensor_scalar(xn_ap, x_ap, stt[:, 2:3], None, ALU.mult), reads=[r_x, r_stt], writes=[r_xn])

        def transpose_to(dst3, src_ap, nk, reads, r_dst):
            bk, rb = nbank()
            pbt = bk.bitcast(BF16)
            for k in range(nk):
                transp(pbt[:, k * 128:(k + 1) * 128], src_ap[:, k * 128:(k + 1) * 128], identb[:], reads + [r_const], rb, signal=(k == nk - 1))
            eng = evac_eng()
            copy_op(eng, dst3, pbt[:, 0:nk * 128].rearrange("p (k n) -> p k n", k=nk), [rb], [r_dst])

        def phase1(sq):
            hTb = [VB(i * 8192, 4096).rearrange("p (k n) -> p k n", k=8) for i in range(2)]
            r_hTb = [RES("hTb%d" % i) for i in range(2)]
            xa = [VF(16384 + i * 4096, 1024) for i in range(2)]
            r_xa = [RES("xa%d" % i) for i in range(2)]
            xnb = [VB(24576 + i * 2048, 1024) for i in range(2)]
            r_xnb = [RES("xnb%d" % i) for i in range(2)]
            U = [VF(28672 + ct * 2064, 516) for ct in range(8)]
            r_U = [RES("U%d" % ct) for ct in range(8)]
            acc = [VF(45184 + i * 2048, 512) for i in range(2)]
            r_acc = [RES("acc%d" % i) for i in range(2)]
            qkb = [VB(49280 + i * 1024, 512) for i in range(2)]
            r_qkb = [RES("qkb%d" % i) for i in range(2)]
            vst = [VB(51328 + i * 2048, 1024) for i in range(4)]
            r_vst = [RES("vst%d" % i) for i in range(4)]
            GB = 59520
            gt = [VF(GB + i * 2048, 512, 4) for i in range(16)]
            r_g = RES("gates")
            LB = GB + 32768
            lat = VB(LB, 2048).rearrange("p (k n) -> p k n", k=4)
            r_lat = RES("lat")
            cnb = [VB(LB + 4096 + i * 1024, 512) for i in range(2)]
            r_cnb = [RES("cnb%d" % i) for i in range(2)]
            cst = VF(LB + 6144, 1024, 64).rearrange("p (c n) -> p c n", c=2)
            r_cst = RES("cst")
            krt = [VF(LB + 10240 + i * 2048, 512, 64) for i in range(2)]
            r_krt = RES("krt")
            krb = VB(LB + 14336, 512, 64)
            r_krb = RES("krb")
            r_qk = RES("@qk_scr"); r_v = RES("@v_scr"); r_nb = RES("@nb_scr")
            r_cq = RES("@cq"); r_ckv = RES("@ckv"); r_kr = RES("@kr")
            xi = 0
            for b in range(NBLK):
                tok0 = b * 512
                j = b & 1
                for tt in range(4):
                    i = xi & 1
                    xi += 1
                    fw.dma("sp", xa[i], xs[sq, tok0 + tt * 128: tok0 + (tt + 1) * 128, :], writes=[r_xa[i]])
                    rms_tile(xa[i], xnb[i], st1[i], r_xa[i], r_xnb[i], r_st1[i], 1024)
                    transpose_to(hTb[j][:, :, tt * 128:(tt + 1) * 128], xnb[i], 8, [r_xnb[i]], r_hTb[j])
                for grp in range(2):
                    wv, rw = wload("w_in", Wb["w_in"], 0, 8, grp * 512, 512)
                    for cl in range(4):
                        ct = grp * 4 + cl
                        bk, rb = nbank()
                        mm(bk[:, :], [(wv[:, kc, cl * 128:(cl + 1) * 128], hTb[j][:, kc, :]) for kc in range(8)], [rw, r_hTb[j]], rb)
                        if b == 0:
                            fw.op("pool", lambda e, ct=ct: e.memset(U[ct][:, 0:4], 0.0), writes=[r_U[ct]])
                        fw.op("act", lambda e, ct=ct, bk=bk: e.copy(U[ct][:, 4:516], bk[:, :]), reads=[rb], writes=[r_U[ct]])
                        a = ct & 1

                        def conv(n, a=a, ct=ct):
                            fw.op("dve", lambda e: e.tensor_scalar(acc[a][:, 0:n], U[ct][:, 0:n], convw[:, 0, ct:ct + 1], None, ALU.mult),
                                  reads=[r_U[ct], r_const], writes=[r_acc[a]])
                            for jj in range(1, 5):
                                fw.op("dve", lambda e, jj=jj: e.scalar_tensor_tensor(acc[a][:, 0:n], U[ct][:, jj:jj + n], convw[:, jj, ct:ct + 1], acc[a][:, 0:n], ALU.mult, ALU.add),
                                      reads=[r_U[ct], r_const, r_acc[a]], writes=[r_acc[a]])
                            fw.op("act", lambda e: e.activation(qkb[a][:, 0:n], acc[a][:, 0:n], AF.Silu), reads=[r_acc[a]], writes=[r_qkb[a]])
                        conv(512)
                        if b == 0:
                            fw.dma("pool", qk_scr[ct, :, 0:510], qkb[a][:, 2:512], reads=[r_qkb[a]], writes=[r_qk])
                        else:
                            fw.dma("pool", qk_scr[ct, :, tok0 - 2:tok0 + 510], qkb[a][:, 0:512], reads=[r_qkb[a]], writes=[r_qk])
                        fw.op("pool", lambda e, ct=ct: e.tensor_copy(U[ct][:, 0:4], U[ct][:, 512:516]), reads=[r_U[ct]], writes=[r_U[ct]])
                        if b == NBLK - 1:
                            fw.op("pool", lambda e, ct=ct: e.memset(U[ct][:, 4:8], 0.0), writes=[r_U[ct]])
                            conv(2)
                            fw.dma("pool", qk_scr[ct, :, S - 2:S], qkb[a][:, 0:2], reads=[r_qkb[a]], writes=[r_qk])
                for half in range(2):
                    wv, rw = wload("w_in", Wb["w_in"], 0, 8, V0 + half * 512, 512)
                    for tt in range(4):
                        bk, rb = nbank()
                        mm(bk[:, :], [(hTb[j][:, kc, tt * 128:(tt + 1) * 128], wv[:, kc, :]) for kc in range(8)], [rw, r_hTb[j]], rb)
                        copy_op(evac_eng(), vst[tt][:, half * 512:(half + 1) * 512], bk[:, :], [rb], [r_vst[tt]])
                for tt in range(4):
                    fw.dma("pool", v_scr[tok0 + tt * 128: tok0 + (tt + 1) * 128, :], vst[tt], reads=[r_vst[tt]], writes=[r_v])
                wv, rw = wload("w_in", Wb["w_in"], 0, 8, G0, 16)
                for g in range(4):
                    bk, rb = nbank()
                    mm(bk[0:4, :], [(wv[:, kc, 4 * g:4 * g + 4], hTb[j][:, kc, :]) for kc in range(8)], [rw, r_hTb[j]], rb)
                    fw.op("act", lambda e, g=g, bk=bk: e.activation(gt[g], bk[0:4, :], AF.Identity, bias=bg[:, g:g + 1]), reads=[rb, r_const], writes=[r_g])
                IFt, IBt, FFt, FBt = gt[0], gt[1], gt[2], gt[3]
                SPf, SPb, csf, csb, NBf, NBb, Af, Ab, Gf, Gb, tmp = gt[4:15]
                G = lambda fn: fw.op(fn[0], fn[1], reads=[r_g, r_const], writes=[r_g])
                for FX, SP in ((FFt, SPf), (FBt, SPb)):
                    G(("act", lambda e, FX=FX: e.activation(tmp, FX, AF.Exp, scale=-1.0)))
                    G(("act", lambda e, SP=SP: e.activation(SP, tmp, AF.Ln, bias=1.0)))
                G(("dve", lambda e: e.tensor_tensor_scan(csf, cmask[:], SPf, 0.0, ALU.mult, ALU.add)))
                G(("dve", lambda e: e.tensor_tensor_scan(csb, cmask[:], SPb, 0.0, ALU.mult, ALU.add)))
                v3 = lambda t: t.rearrange("p (a n) -> p a n", a=4)
                tot = lambda t: t.rearrange("p (a n) -> p a n", a=4)[:, :, 127:128].to_broadcast([4, 4, 128])
                G(("dve", lambda e: e.tensor_scalar(NBf, csf, -1.0, None, ALU.mult)))
                G(("dve", lambda e: e.tensor_tensor(Af, IFt, csf, ALU.add)))
                G(("dve", lambda e: e.tensor_tensor(v3(tmp), v3(Af), tot(csf), ALU.subtract)))
                G(("act", lambda e: e.activation(Gf, tmp, AF.Exp)))
                G(("dve", lambda e: e.tensor_tensor(NBb, SPb, csb, ALU.subtract)))
                G(("dve", lambda e: e.tensor_tensor(v3(NBb), v3(NBb), tot(csb), ALU.add)))
                G(("dve", lambda e: e.tensor_tensor(Ab, IBt, NBb, ALU.add)))
                G(("dve", lambda e: e.tensor_tensor(v3(tmp), v3(Ab), tot(csb), ALU.subtract)))
                G(("act", lambda e: e.activation(Gb, tmp, AF.Exp)))
                G(("dve", lambda e: e.tensor_scalar(NBb, NBb, -1.0, None, ALU.mult)))
                fw.dma("pool", nb_scr[0, :, tok0:tok0 + 512], NBf, reads=[r_g], writes=[r_nb])
                fw.dma("pool", nb_scr[1, :, tok0:tok0 + 512], NBb, reads=[r_g], writes=[r_nb])
                bk, rb = nbank()
                for qi, Q in enumerate((Af, Ab, Gf, Gb)):
                    for tt in range(4):
                        c0 = (qi * 4 + tt) * 4
                        transp(bk[:, c0:c0 + 4], Q[:, tt * 128:(tt + 1) * 128], identf[0:4, 0:4], [r_g, r_const], rb, signal=(qi == 3 and tt == 3))
                fw.op("dve", lambda e, bk=bk, b=b: e.tensor_copy(colsT[:, :, b * 4:(b + 1) * 4, :], bk[:, 0:64].rearrange("p (q t h) -> p q t h", q=4, t=4)),
                      reads=[rb], writes=[r_const])
                wv, rw = wload("w_in", Wb["w_in"], 0, 8, CQ0, 512)
                for tt in range(4):
                    bk, rb = nbank()
                    mm(bk[:, :], [(hTb[j][:, kc, tt * 128:(tt + 1) * 128], wv[:, kc, :]) for kc in range(8)], [rw, r_hTb[j]], rb)
                    i = tt & 1
                    stt = st1[i]
                    fw.op("act", lambda e, i=i, bk=bk, stt=stt: e.activation(cnb[i][:, 0:256], bk[:, 0:256], AF.Square, accum_out=stt[:, 4:5]), reads=[rb], writes=[r_cnb[i], r_st1[i]])
                    fw.op("act", lambda e, i=i, bk=bk, stt=stt: e.activation(cnb[i][:, 256:512], bk[:, 256:512], AF.Square, accum_out=stt[:, 5:6]), reads=[rb], writes=[r_cnb[i], r_st1[i]])
                    fw.op("act", lambda e, stt=stt: e.activation(stt[:, 6:8], stt[:, 4:6], AF.Sqrt, bias=EPS, scale=1.0 / 256), reads=[r_st1[i]], writes=[r_st1[i]])
                    fw.op("dve", lambda e, stt=stt: e.reciprocal(stt[:, 4:6], stt[:, 6:8]), reads=[r_st1[i]], writes=[r_st1[i]])
                    fw.op("dve", lambda e, i=i, bk=bk, stt=stt: e.tensor_scalar(cnb[i][:, 0:256], bk[:, 0:256], stt[:, 4:5], None, ALU.mult), reads=[rb, r_st1[i]], writes=[r_cnb[i]])
                    fw.op("act", lambda e, i=i, bk=bk, stt=stt: e.activation(cnb[i][:, 256:512], bk[:, 256:512], AF.Copy, scale=stt[:, 5:6]), reads=[rb, r_st1[i]], writes=[r_cnb[i]])
                    transpose_to(lat[:, :, tt * 128:(tt + 1) * 128], cnb[i], 4, [r_cnb[i]], r_lat)
                fw.dma("pool", cqnT_scr[:, :, tok0:tok0 + 512].rearrange("k p n -> p k n"), lat[:, 0:2, :], reads=[r_lat], writes=[r_cq])
                fw.dma("pool", ckvnT_scr[:, :, tok0:tok0 + 512].rearrange("k p n -> p k n"), lat[:, 2:4, :], reads=[r_lat], writes=[r_ckv])
                wa, rwa = wload("w_in", Wb["w_in"], 0, 8, KR0, 64)
                wb_, rwb = wload("w_kr_sw", w_kr_sw, 0, 8, 0, 64)
                bka, rba = nbank()
                mm(bka[0:64, :], [(wa[:, kc, :], hTb[j][:, kc, :]) for kc in range(8)], [rwa, r_hTb[j]], rba)
                bkb, rbb = nbank()
                mm(bkb[0:64, :], [(wb_[:, kc, :], hTb[j][:, kc, :]) for kc in range(8)], [rwb, r_hTb[j]], rbb)
                fw.dma("sp", cst, cs_scr[:, :, tok0:tok0 + 512].rearrange("c p n -> p c n"), writes=[r_cst])
                fw.op("dve", lambda e, bka=bka: e.tensor_tensor(krt[0], bka[0:64, :], cst[:, 0, :], ALU.mult), reads=[rba, r_cst], writes=[r_krt])
                fw.op("dve", lambda e, bkb=bkb: e.tensor_tensor(krt[1], bkb[0:64, :], cst[:, 1, :], ALU.mult), reads=[rbb, r_cst], writes=[r_krt])
                fw.op("pool", lambda e: e.tensor_tensor(krb, krt[0], krt[1], ALU.add), reads=[r_krt], writes=[r_krb])
                fw.dma("pool", krT_scr[:, tok0:tok0 + 512], krb, reads=[r_krb], writes=[r_kr])
            fw.op("dve", lambda e: e.tensor_scalar(colsT[:, 0:2], colsT[:, 0:2], LN_SQ, None, ALU.add), reads=[r_const], writes=[r_const])
            if dbg == "P1":
                fw.dma("pool", dbg_out["d_colsT"], colsT[:].rearrange("p q t h -> p (q t h)"), reads=[r_const], writes=[RES("@dbgP1")], sem_res=r_st1[0])

        def phase3(sq):
            xa4 = VF(0, 4096).rearrange("p (t n) -> p t n", t=4)
            r_xa4 = [RES("xa4_%d" % i) for i in range(4)]
            xn4 = VB(16384, 4096).rearrange("p (t n) -> p t n", t=4)
            r_xn4 = [RES("xn4_%d" % i) for i in range(4)]
            k8 = lambda ap: ap.rearrange("p (k n) -> p k n", k=8)
            hTb = k8(VB(24576, 4096)); r_hTb = RES("p3hTb")
            sgo = k8(VB(32768, 4096)); r_sgo = RES("sgo")
            hmT = k8(VB(40960, 4096)); r_hmT = RES("hmTb")
            oTb = k8(VB(49152, 4096)); r_oTb = RES("oTb")
            mrg = k8(VB(57344, 4096)); r_mrg = RES("mrg")
            sg = [VF(65536 + i * 2048, 512) for i in range(2)]; r_sg = [RES("sg%d" % i) for i in range(2)]
            t12 = [VF(69632 + i * 2048, 512) for i in range(2)]; r_t12 = [RES("t12_%d" % i) for i in range(2)]
            actT = VB(73728, 22 * 512).rearrange("p (k n) -> p k n", k=22); r_actT = RES("actT")
            pa = VF(96256, 1024).rearrange("p (t n) -> p t n", t=4); r_pa = RES("pa")
            pn = VB(100352, 1024).rearrange("p (t n) -> p t n", t=4); r_pn = RES("pn")
            pT = VB(102400, 1024).rearrange("p (k n) -> p k n", k=2); r_pT = RES("pT")
            r_y = RES("@y")
            for b in range(NBLK):
                tok0 = b * 512
                tsl = lambda tt: slice(tt * 128, (tt + 1) * 128)
                for tt in range(4):
                    i = tt & 1
                    fw.dma("sp", xa4[:, tt, :], xs[sq, tok0 + tt * 128: tok0 + (tt + 1) * 128, :], writes=[r_xa4[tt]])
                    rms_tile(xa4[:, tt, :], xn4[:, tt, :], st1[i], r_xa4[tt], r_xn4[tt], r_st1[i], 1024)
                    transpose_to(hTb[:, :, tsl(tt)], xn4[:, tt, :], 8, [r_xn4[tt]], r_hTb)
                fw.dma("sp", pa, ps[sq, tok0:tok0 + 512, :].rearrange("(t p) n -> p t n", p=128), writes=[r_pa])
                if dbg:
                    fw.op("pool", lambda e: e.memset(hmT, 0.0), writes=[r_hmT])
                else:
                    fw.dma("sp", hmT, hmT_scr[:, :, tok0:tok0 + 512].rearrange("k p n -> p k n"), reads=[RES("@hm_dummy")], writes=[r_hmT])
                if dbg == 2:
                    fw.op("pool", lambda e: e.memset(oTb, 0.0), writes=[r_oTb])
                else:
                    fw.dma("sp", oTb, oT_scr[:, :, tok0:tok0 + 512].rearrange("k p n -> p k n"), reads=[RES("@o_dummy")], writes=[r_oTb])
                for grp in range(2):
                    wv, rw = wload("w_in", Wb["w_in"], 0, 8, O0 + grp * 512, 512)
                    for cl in range(4):
                        bk, rb = nbank()
                        mm(bk[:, :], [(wv[:, kc, cl * 128:(cl + 1) * 128], hTb[:, kc, :]) for kc in range(8)], [rw, r_hTb], rb)
                        fw.op("act", lambda e, bk=bk, c=grp * 4 + cl: e.activation(sgo[:, c, :], bk[:, :], AF.Sigmoid), reads=[rb], writes=[r_sgo])
                fw.op("dve", lambda e: e.tensor_tensor(sgo, sgo, hmT, ALU.mult), reads=[r_sgo, r_hmT], writes=[r_sgo])
                for grp in range(2):
                    wm, rwm = wload("w_mo", Wb["w_mo"], 0, 8, grp * 512, 512)
                    wa, rwa = wload("w_ao", Wb["w_ao"], 0, 8, grp * 512, 512)
                    wg1, rw1 = wload("w_in", Wb["w_in"], 0, 8, GM0 + grp * 512, 512)
                    wg2, rw2 = wload("w_in", Wb["w_in"], 0, 8, GM0 + 1024 + grp * 512, 512)
                    for cl in range(4):
                        c = grp * 4 + cl
                        cs_ = slice(cl * 128, (cl + 1) * 128)
                        bm, rbm = nbank(); mm(bm[:, :], [(wm[:, kc, cs_], sgo[:, kc, :]) for kc in range(8)], [rwm, r_sgo], rbm)
                        ba, rba = nbank(); mm(ba[:, :], [(wa[:, kc, cs_], oTb[:, kc, :]) for kc in range(8)], [rwa, r_oTb], rba)
                        b1, rb1 = nbank(); mm(b1[:, :], [(wg1[:, kc, cs_], hTb[:, kc, :]) for kc in range(8)], [rw1, r_hTb], rb1)
                        b2, rb2 = nbank(); mm(b2[:, :], [(wg2[:, kc, cs_], hTb[:, kc, :]) for kc in range(8)], [rw2, r_hTb], rb2)
                        fw.op("act", lambda e, b1=b1: e.activation(sg[0], b1[:, :], AF.Sigmoid), reads=[rb1], writes=[r_sg[0]])
                        fw.op("act", lambda e, b2=b2: e.activation(sg[1], b2[:, :], AF.Sigmoid), reads=[rb2], writes=[r_sg[1]])
                        fw.op("dve", lambda e, bm=bm: e.tensor_tensor(t12[0], bm[:, :], sg[0], ALU.mult), reads=[rbm, r_sg[0]], writes=[r_t12[0]])
                        fw.op("dve", lambda e, ba=ba: e.tensor_tensor(t12[1], ba[:, :], sg[1], ALU.mult), reads=[rba, r_sg[1]], writes=[r_t12[1]])
                        fw.op("pool", lambda e, c=c: e.tensor_tensor(mrg[:, c, :], t12[0], t12[1], ALU.add), reads=r_t12, writes=[r_mrg])
                for half in range(2):
                    wo, rwo = wload("w_out", Wb["w_out"], 0, 8, half * 512, 512)
                    for tt in range(4):
                        bk, rb = nbank()
                        mm(bk[:, :], [(mrg[:, kc, tsl(tt)], wo[:, kc, :]) for kc in range(8)], [rwo, r_mrg], rb)
                        xsl = xa4[:, tt, half * 512:(half + 1) * 512]
                        fw.op("dve", lambda e, bk=bk, xsl=xsl: e.tensor_tensor(xsl, xsl, bk[:, :], ALU.add), reads=[rb], writes=[r_xa4[tt]])
                for tt in range(4):
                    i = tt & 1
                    rms_tile(xa4[:, tt, :], xn4[:, tt, :], st1[i], r_xa4[tt], r_xn4[tt], r_st1[i], 1024)
                    transpose_to(hTb[:, :, tsl(tt)], xn4[:, tt, :], 8, [r_xn4[tt]], r_hTb)
                for g in range(6):
                    n_ = 512 if g < 5 else 256
                    wg, rwg = wload("w_gate", Wb["w_gate"], 0, 8, g * 512, n_)
                    wu, rwu = wload("w_up", Wb["w_up"], 0, 8, g * 512, n_)
                    for cl in range(n_ // 128):
                        f = g * 4 + cl
                        cs_ = slice(cl * 128, (cl + 1) * 128)
                        bg_, rbg = nbank(); mm(bg_[:, :], [(wg[:, kc, cs_], hTb[:, kc, :]) for kc in range(8)], [rwg, r_hTb], rbg)
                        bu, rbu = nbank(); mm(bu[:, :], [(wu[:, kc, cs_], hTb[:, kc, :]) for kc in range(8)], [rwu, r_hTb], rbu)
                        i = f & 1
                        fw.op("act", lambda e, bg_=bg_, i=i: e.activation(sg[i], bg_[:, :], AF.Silu), reads=[rbg], writes=[r_sg[i]])
                        fw.op("dve", lambda e, bu=bu, i=i, f=f: e.tensor_tensor(actT[:, f, :], bu[:, :], sg[i], ALU.mult), reads=[rbu, r_sg[i]], writes=[r_actT])
                for half in range(2):
                    parts = []
                    for k0, nk in ((0, 8), (8, 8), (16, 6)):
                        wd, rwd = wload("w_down", Wb["w_down"], k0, nk, half * 512, 512)
                        parts.append((k0, nk, wd, rwd))
                    for tt in range(4):
                        bk, rb = nbank()
                        pairs = []
                        for k0, nk, wd, rwd in parts:
                            pairs += [(actT[:, k0 + kk, tsl(tt)], wd[:, kk, :]) for kk in range(nk)]
                        mm(bk[:, :], pairs, [p_[3] for p_ in parts] + [r_actT], rb)
                        xsl = xa4[:, tt, half * 512:(half + 1) * 512]
                        fw.op("dve", lambda e, bk=bk, xsl=xsl: e.tensor_tensor(xsl, xsl, bk[:, :], ALU.add), reads=[rb], writes=[r_xa4[tt]])
                fw.op("act", lambda e: e.copy(pn, pa), reads=[r_pa], writes=[r_pn])
                for tt in range(4):
                    fw.op("act", lambda e, tt=tt: e.copy(xn4[:, tt, :], xa4[:, tt, :]), reads=[r_xa4[tt]], writes=[r_xn4[tt]])
                    transpose_to(hTb[:, :, tsl(tt)], xn4[:, tt, :], 8, [r_xn4[tt]], r_hTb)
                    transpose_to(pT[:, :, tsl(tt)], pn[:, tt, :], 2, [r_pn], r_pT)
                for half in range(2):
                    wpg, rwpg = wload("w_ple_gate", Wb["w_ple_gate"], 0, 8, half * 512, 512)
                    wp, rwp = wload("w_ple", Wb["w_ple"], 0, 2, half * 512, 512)
                    for tt in range(4):
                        bgt, rbgt = nbank(); mm(bgt[:, :], [(hTb[:, kc, tsl(tt)], wpg[:, kc, :]) for kc in range(8)], [rwpg, r_hTb], rbgt)
                        bp, rbp = nbank(); mm(bp[:, :], [(pT[:, kc, tsl(tt)], wp[:, kc, :]) for kc in range(2)], [rwp, r_pT], rbp)
                        i = tt & 1
                        fw.op("act", lambda e, bgt=bgt, i=i: e.activation(sg[i], bgt[:, :], AF.Sigmoid), reads=[rbgt], writes=[r_sg[i]])
                        fw.op("dve", lambda e, bp=bp, i=i: e.tensor_tensor(t12[i], bp[:, :], sg[i], ALU.mult), reads=[rbp, r_sg[i]], writes=[r_t12[i]])
                        xsl = xa4[:, tt, half * 512:(half + 1) * 512]
                        fw.op("dve", lambda e, xsl=xsl, i=i: e.tensor_tensor(xsl, xsl, t12[i], ALU.add), reads=[r_t12[i]], writes=[r_xa4[tt]])
                for tt in range(4):
                    i = tt & 1
                    stt = st1[i]
                    fw.op("act", lambda e, tt=tt, stt=stt: e.activation(xn4[:, tt, :], xa4[:, tt, :], AF.Square, accum_out=stt[:, 0:1]), reads=[r_xa4[tt]], writes=[r_xn4[tt], r_st1[i]])
                    fw.op("act", lambda e, stt=stt: e.activation(stt[:, 1:2], stt[:, 0:1], AF.Sqrt, bias=EPS, scale=1.0 / 1024), reads=[r_st1[i]], writes=[r_st1[i]])
                    fw.op("dve", lambda e, stt=stt: e.reciprocal(stt[:, 2:3], stt[:, 1:2]), reads=[r_st1[i]], writes=[r_st1[i]])
                    fw.op("dve", lambda e, tt=tt, stt=stt: e.scalar_tensor_tensor(xa4[:, tt, :], xa4[:, tt, :], stt[:, 2:3], gfin[:], ALU.mult, ALU.mult),
                          reads=[r_st1[i], r_const], writes=[r_xa4[tt]])
                    fw.dma("pool", y[sq, tok0 + tt * 128: tok0 + (tt + 1) * 128, :], xa4[:, tt, :], reads=[r_xa4[tt]], writes=[r_y])

        def phase_mlstm(sq, heads=range(4)):
            QT = VB(0, 4096); KT = VB(8192, 4096); Qp = VB(16384, 4096)
            Ktok = VB(24576, 4096).rearrange("p (t n) -> p t n", t=NT)
            Kp = VB(32768, 4096).rearrange("p (t n) -> p t n", t=NT)
            Vaug = VB(40960, NT * 258).rearrange("p (t n) -> p t n", t=NT)
            bbc = VF(57472, 4096)
            tmpE = [VF(73856 + i * 2048, 512) for i in range(2)]
            Cst = VF(77952, 258); Cbf = VB(78992, 258)
            Dt = [VF(79520 + i * 512, 128) for i in range(2)]
            Sp = [VB(80544 + i * 256, 128) for i in range(2)]
            aC = VF(81056, 32)
            hfo = [VF(81184 + i * 1024, 256) for i in range(3)]
            hs = [VF(84256 + i * 1024, 256) for i in range(2)]
            hn = [VB(86304 + i * 512, 256) for i in range(2)]
            hT2 = [VB(87328 + i * 512, 256).rearrange("p (k n) -> p k n", k=2) for i in range(2)]
            r_QT, r_KT, r_Qp, r_Ktok, r_Kp, r_V, r_bbc = (RES(n) for n in ("QT", "KT", "Qp", "Ktok", "Kp", "Vaug", "bbc"))
            r_tmpE = [RES("tmpE%d" % i) for i in range(2)]
            r_C, r_Cbf, r_aC = RES("Cst"), RES("Cbf"), RES("aC")
            r_Dt = [RES("Dt%d" % i) for i in range(2)]; r_Sp = [RES("Sp%d" % i) for i in range(2)]
            r_hfo = [RES("hfo%d" % i) for i in range(3)]; r_hs = [RES("hs%d" % i) for i in range(2)]
            r_hn = [RES("hn%d" % i) for i in range(2)]; r_hT2 = [RES("hT2_%d" % i) for i in range(2)]
            r_hf = [RES("@hf%d" % c) for c in range(NT)]
            r_hm = RES("@hmT")
            for h in heads:
                fw.dma("sp", QT, qk_scr[h], writes=[r_QT])
                fw.dma("sp", KT, qk_scr[4 + h], writes=[r_KT])
                fw.op("pool", lambda e: e.memset(Vaug[:, :, 256:258], 1.0), writes=[r_V])
                fw.dma("sp", Vaug[:, :, 0:256], v_scr[:, h * 256:(h + 1) * 256].rearrange("(t p) n -> p t n", p=128), reads=[r_V], writes=[r_V])
                for c4 in range(4):
                    transpose_to(Ktok[:, c4 * 8:(c4 + 1) * 8, :], KT[:, c4 * 1024:(c4 + 1) * 1024], 8, [r_KT], r_Ktok)
                for d in range(2):
                    msk = maskf if d == 0 else maskb
                    far = 127 if d == 0 else 0
                    fw.dma("sp", bbc, nb_scr[d, h:h + 1, :].partition_broadcast(128).rearrange("p o n -> p (o n)"), writes=[r_bbc])
                    fw.op("act", lambda e, far=far: e.activation(aC, bbc.rearrange("p (t n) -> p t n", t=NT)[:, :, far], AF.Exp), reads=[r_bbc], writes=[r_aC])
                    for blk in range(NBLK):
                        i = blk & 1
                        bs_ = slice(blk * 512, (blk + 1) * 512)
                        fw.op("act", lambda e, i=i, bs_=bs_: e.activation(tmpE[i], bbc[:, bs_], AF.Exp, bias=LN_SQ), reads=[r_bbc], writes=[r_tmpE[i]])
                        fw.op("dve", lambda e, i=i, bs_=bs_: e.tensor_tensor(Qp[:, bs_], QT[:, bs_], tmpE[i], ALU.mult), reads=[r_QT, r_tmpE[i]], writes=[r_Qp])
                    fw.op("pool", lambda e, msk=msk: e.tensor_tensor(bbc.rearrange("p (t n) -> p t n", t=NT), bbc.rearrange("p (t n) -> p t n", t=NT),
                                                                    msk[:].unsqueeze(1).to_broadcast([128, NT, 128]), ALU.add), reads=[r_bbc, r_const], writes=[r_bbc])
                    fw.op("dve", lambda e, d=d, h=h: e.tensor_tensor(Kp, Ktok, colsT[:, 2 + d, :, h].unsqueeze(2).to_broadcast([128, NT, 128]), ALU.mult),
                          reads=[r_Ktok, r_const], writes=[r_Kp])
                    fw.op("dve", lambda e: e.memset(Cst, 0.0), writes=[r_C])
                    fw.op("pool", lambda e: e.memset(Cbf, 0.0), writes=[r_Cbf])
                    order = list(range(NT)) if d == 0 else list(range(NT - 1, -1, -1))
                    for n_, c in enumerate(order):
                        cs_ = slice(c * 128, (c + 1) * 128)
                        j = n_ & 1
                        fw.op("act", lambda e, j=j, cs_=cs_, d=d, c=c, h=h: e.activation(Dt[j], bbc[:, cs_], AF.Exp, bias=colsT[:, d, c, h:h + 1]),
                              reads=[r_bbc, r_const], writes=[r_Dt[j]])
                        bS, rbS = nbank()
                        mm(bS[:, 0:128], [(KT[:, cs_], QT[:, cs_])], [r_KT, r_QT], rbS)
                        fw.op("dve", lambda e, j=j, bS=bS: e.tensor_tensor(Sp[j], bS[:, 0:128], Dt[j], ALU.mult), reads=[rbS, r_Dt[j]], writes=[r_Sp[j]])
                        bN, rbN = nbank()
                        mm(bN[:, 0:257], [(Sp[j], Vaug[:, c, 0:257]), (Qp[:, cs_], Cbf[:, 0:257])], [r_Sp[j], r_V, r_Qp, r_Cbf], rbN)
                        bU, rbU = nbank()
                        mm(bU[:, 0:257], [(Kp[:, c, :], Vaug[:, c, 0:257])], [r_Kp, r_V], rbU)
                        fw.op("dve", lambda e, c=c, bU=bU: e.scalar_tensor_tensor(Cst[:, 0:257], Cst[:, 0:257], aC[:, c:c + 1], bU[:, 0:257], ALU.mult, ALU.add),
                              reads=[r_C, r_aC, rbU], writes=[r_C])
                        fw.op("act", lambda e: e.copy(Cbf[:, 0:257], Cst[:, 0:257]), reads=[r_C], writes=[r_Cbf])
                        k = n_ & 1
                        stt, r_stt = st1[k], r_st1[k]
                        fw.op("act", lambda e, bN=bN, stt=stt: e.activation(stt[:, 0:1], bN[:, 256:257], AF.Abs), reads=[rbN], writes=[r_stt])
                        fw.op("dve", lambda e, stt=stt: e.tensor_scalar_max(stt[:, 0:1], stt[:, 0:1], 1.0), reads=[r_stt], writes=[r_stt])
                        fw.op("dve", lambda e, stt=stt: e.reciprocal(stt[:, 1:2], stt[:, 0:1]), reads=[r_stt], writes=[r_stt])
                        m3 = n_ % 3
                        if d == 0:
                            fw.op("dve", lambda e, m3=m3, bN=bN, stt=stt: e.tensor_scalar(hfo[m3], bN[:, 0:256], stt[:, 1:2], None, ALU.mult), reads=[rbN, r_stt], writes=[r_hfo[m3]])
                            fw.dma("pool", hf_scr[cs_, :], hfo[m3], reads=[r_hfo[m3]], writes=[r_hf[c]])
                        else:
                            fw.dma("sp", hfo[m3], hf_scr[cs_, :], reads=[r_hf[c]], writes=[r_hfo[m3]])
                            fw.op("dve", lambda e, m3=m3, k=k, bN=bN, stt=stt: e.scalar_tensor_tensor(hs[k], bN[:, 0:256], stt[:, 1:2], hfo[m3], ALU.mult, ALU.add),
                                  reads=[rbN, r_stt, r_hfo[m3]], writes=[r_hs[k]])
                            fw.op("act", lambda e, k=k, stt=stt: e.activation(hn[k], hs[k], AF.Square, accum_out=stt[:, 2:3]), reads=[r_hs[k]], writes=[r_hn[k], r_stt])
                            fw.op("act", lambda e, stt=stt: e.activation(stt[:, 3:4], stt[:, 2:3], AF.Sqrt, bias=EPS, scale=1.0 / 256), reads=[r_stt], writes=[r_stt])
                            fw.op("dve", lambda e, stt=stt: e.reciprocal(stt[:, 2:3], stt[:, 3:4]), reads=[r_stt], writes=[r_stt])
                            fw.op("dve", lambda e, k=k, stt=stt: e.tensor_scalar(hn[k], hs[k], stt[:, 2:3], None, ALU.mult), reads=[r_hs[k], r_stt], writes=[r_hn[k]])
                            transpose_to(hT2[k], hn[k], 2, [r_hn[k]], r_hT2[k])
                            fw.dma("pool", hmT_scr[2 * h:2 * h + 2, :, cs_].rearrange("k p n -> p k n"), hT2[k], reads=[r_hT2[k]], writes=[r_hm])

        def phase_mix(sq):
            cqnT = VB(0, 8192).rearrange("p (k n) -> p k n", k=2); r_cqn = RES("cqnT")
            ckvnT = VB(16384, 8192).rearrange("p (k n) -> p k n", k=2); r_ckvn = RES("ckvnT")
            krT = VB(32768, 4096, 128); r_krT = RES("krT")
            KhT = VB(40960, 4096); r_KhT = RES("KhT")
            Vh = VB(49152, 4096); r_Vh = RES("Vh")
            QhT = VB(57344, 4096); r_QhT = RES("QhT")
            QrT = VB(65536, 4096, 128); r_QrT = RES("QrT")
            cstq = VF(73728, 1024, 64).rearrange("p (c n) -> p c n", c=2); r_cstq = RES("cstq")
            rt = [VF(77824 + i * 2048, 512, 64) for i in range(2)]; r_rt = RES("rt")
            PT = [VB(81920 + i * 1024, 512) for i in range(3)]; r_PT = [RES("PT%d" % i) for i in range(3)]
            rsum = VF(84992, 512); r_rsum = RES("rsum")
            oTs = [VB(87040 + i * 1024, 512) for i in range(2)]; r_oTs = [RES("oTs%d" % i) for i in range(2)]
            r_oT = RES("@oT")
            QT, KT, Qp = wsl[1][:, :], wsl[2][:, :], wsl[3][:, :]
            Ktok = wsl[4][:, :].rearrange("p (t n) -> p t n", t=NT)
            Kp = wsl[5][:, :].rearrange("p (t n) -> p t n", t=NT)
            M0 = 89088
            Vaug = VB(M0, NT * 258).rearrange("p (t n) -> p t n", t=NT)
            bbc = VF(M0 + 16512, 4096)
            tmpE = [VF(M0 + 32896 + i * 2048, 512) for i in range(2)]
            Cst = VF(M0 + 36992, 258); Cbf = VB(M0 + 38032, 258)
            Dt = [VF(M0 + 38560 + i * 512, 128) for i in range(2)]
            Sp = [VB(M0 + 39584 + i * 256, 128) for i in range(2)]
            aC = VF(M0 + 40096, 32)
            hfo = [VF(M0 + 40224 + i * 1024, 256) for i in range(3)]
            hs = [VF(M0 + 43296 + i * 1024, 256) for i in range(2)]
            hn = [VB(M0 + 45344 + i * 512, 256) for i in range(2)]
            hT2 = [VB(M0 + 46368 + i * 512, 256).rearrange("p (k n) -> p k n", k=2) for i in range(2)]
            stm = [T_once("stm%d" % i, [128, 8], F32) for i in range(2)]; r_stm = [RES("stm%d" % i) for i in range(2)]
            assert M0 + 47392 <= AR * 4
            r_QT, r_KT, r_Qp, r_Ktok, r_Kp, r_V, r_bbc = (RES(n) for n in ("QT", "KT", "Qp", "Ktok", "Kp", "Vaug", "bbc"))
            r_tmpE = [RES("tmpE%d" % i) for i in range(2)]
            r_C, r_Cbf, r_aC = RES("Cst"), RES("Cbf"), RES("aC")
            r_Dt = [RES("Dt%d" % i) for i in range(2)]; r_Sp = [RES("Sp%d" % i) for i in range(2)]
            r_hfo = [RES("hfo%d" % i) for i in range(3)]; r_hs = [RES("hs%d" % i) for i in range(2)]
            r_hn = [RES("hn%d" % i) for i in range(2)]; r_hT2 = [RES("hT2_%d" % i) for i in range(2)]
            r_hf = [RES("@hf%d" % c) for c in range(NT)]
            r_hm = RES("@hmT")
            mst = {"b": 0}

            def mbank():
                i = 5 + mst["b"]
                mst["b"] = (mst["b"] + 1) % 3
                return banks[i], r_bank[i]

            def m_transpose(dst3, src_ap, nk, reads, r_dst):
                bk, rb = mbank()
                pbt = bk.bitcast(BF16)
                for k in range(nk):
                    transp(pbt[:, k * 128:(k + 1) * 128], src_ap[:, k * 128:(k + 1) * 128], identb[:], reads + [r_const], rb, signal=(k == nk - 1))
                copy_op(evac_eng(), dst3, pbt[:, 0:nk * 128].rearrange("p (k n) -> p k n", k=nk), [rb], [r_dst])

            b3 = lambda: bbc.rearrange("p (t n) -> p t n", t=NT)

            def mlstm_gen():
                for h in range(4):
                    fw.dma("sp", QT, qk_scr[h], writes=[r_QT])
                    fw.dma("sp", KT, qk_scr[4 + h], writes=[r_KT])
                    fw.op("pool", lambda e: e.memset(Vaug[:, :, 256:258], 1.0), writes=[r_V])
                    fw.dma("sp", Vaug[:, :, 0:256], v_scr[:, h * 256:(h + 1) * 256].rearrange("(t p) n -> p t n", p=128), reads=[r_V], writes=[r_V])
                    yield
                    for c8 in range(8):
                        m_transpose(Ktok[:, c8 * 4:(c8 + 1) * 4, :], KT[:, c8 * 512:(c8 + 1) * 512], 4, [r_KT], r_Ktok)
                        yield
                    for d in range(2):
                        msk = maskf if d == 0 else maskb
                        far = 127 if d == 0 else 0
                        fw.dma("sp", bbc, nb_scr[d, h:h + 1, :].partition_broadcast(128).rearrange("p o n -> p (o n)"), writes=[r_bbc])
                        yield
                        fw.op("act", lambda e, far=far: e.activation(aC, bbc.rearrange("p (t n) -> p t n", t=NT)[:, :, far], AF.Exp), reads=[r_bbc], writes=[r_aC])
                        for blk in range(NBLK):
                            i = blk & 1
                            bs_ = slice(blk * 512, (blk + 1) * 512)
                            fw.op("act", lambda e, i=i, bs_=bs_: e.activation(tmpE[i], bbc[:, bs_], AF.Exp, bias=LN_SQ), reads=[r_bbc], writes=[r_tmpE[i]])
                            yield
                            fw.op("dve", lambda e, i=i, bs_=bs_: e.tensor_tensor(Qp[:, bs_], QT[:, bs_], tmpE[i], ALU.mult), reads=[r_QT, r_tmpE[i]], writes=[r_Qp])
                            yield
                        for q4 in range(4):
                            ts_ = slice(q4 * 8, (q4 + 1) * 8)
                            fw.op("pool", lambda e, msk=msk, ts_=ts_: e.tensor_tensor(b3()[:, ts_, :], b3()[:, ts_, :], msk[:].unsqueeze(1).to_broadcast([128, 8, 128]), ALU.add),
                                  reads=[r_bbc, r_const], writes=[r_bbc])
                            fw.op("dve", lambda e, d=d, h=h, ts_=ts_: e.tensor_tensor(Kp[:, ts_, :], Ktok[:, ts_, :], colsT[:, 2 + d, ts_, h].unsqueeze(2).to_broadcast([128, 8, 128]), ALU.mult),
                                  reads=[r_Ktok, r_const], writes=[r_Kp])
                            yield
                        fw.op("dve", lambda e: e.memset(Cst, 0.0), writes=[r_C])
                        fw.op("pool", lambda e: e.memset(Cbf, 0.0), writes=[r_Cbf])
                        yield
                        order = list(range(NT)) if d == 0 else list(range(NT - 1, -1, -1))
                        for n_, c in enumerate(order):
                            cs_ = slice(c * 128, (c + 1) * 128)
                            j = n_ & 1
                            k = n_ & 1
                            m3 = n_ % 3
                            stt, r_stt = stm[k], r_stm[k]
                            fw.op("act", lambda e, j=j, cs_=cs_, d=d, c=c, h=h: e.activation(Dt[j], bbc[:, cs_], AF.Exp, bias=colsT[:, d, c, h:h + 1]),
                                  reads=[r_bbc, r_const], writes=[r_Dt[j]])
                            bS, rbS = mbank()
                            mm(bS[:, 0:128], [(KT[:, cs_], QT[:, cs_])], [r_KT, r_QT], rbS)
                            if d == 1:
                                fw.dma("sp", hfo[m3], hf_scr[cs_, :], reads=[r_hf[c]], writes=[r_hfo[m3]])
                            yield
                            fw.op("dve", lambda e, j=j, bS=bS: e.tensor_tensor(Sp[j], bS[:, 0:128], Dt[j], ALU.mult), reads=[rbS, r_Dt[j]], writes=[r_Sp[j]])
                            yield
                            bN, rbN = mbank()
                            mm(bN[:, 0:257], [(Sp[j], Vaug[:, c, 0:257]), (Qp[:, cs_], Cbf[:, 0:257])], [r_Sp[j], r_V, r_Qp, r_Cbf], rbN)
                            bU, rbU = mbank()
                            mm(bU[:, 0:257], [(Kp[:, c, :], Vaug[:, c, 0:257])], [r_Kp, r_V], rbU)
                            yield
                            fw.op("dve", lambda e, c=c, bU=bU: e.scalar_tensor_tensor(Cst[:, 0:257], Cst[:, 0:257], aC[:, c:c + 1], bU[:, 0:257], ALU.mult, ALU.add),
                                  reads=[r_C, r_aC, rbU], writes=[r_C])
                            fw.op("act", lambda e, bN=bN, stt=stt: e.activation(stt[:, 0:1], bN[:, 256:257], AF.Abs), reads=[rbN], writes=[r_stt])
                            yield
                            fw.op("pool", lambda e: e.tensor_copy(Cbf[:, 0:257], Cst[:, 0:257]), reads=[r_C], writes=[r_Cbf])
                            fw.op("dve", lambda e, stt=stt: e.tensor_scalar_max(stt[:, 0:1], stt[:, 0:1], 1.0), reads=[r_stt], writes=[r_stt])
                            fw.op("dve", lambda e, stt=stt: e.reciprocal(stt[:, 1:2], stt[:, 0:1]), reads=[r_stt], writes=[r_stt])
                            yield
                            if d == 0:
                                fw.op("dve", lambda e, m3=m3, bN=bN, stt=stt: e.tensor_scalar(hfo[m3], bN[:, 0:256], stt[:, 1:2], None, ALU.mult), reads=[rbN, r_stt], writes=[r_hfo[m3]])
                                fw.dma("pool", hf_scr[cs_, :], hfo[m3], reads=[r_hfo[m3]], writes=[r_hf[c]])
                                yield
                            else:
                                fw.op("dve", lambda e, m3=m3, k=k, bN=bN, stt=stt: e.scalar_tensor_tensor(hs[k], bN[:, 0:256], stt[:, 1:2], hfo[m3], ALU.mult, ALU.add),
                                      reads=[rbN, r_stt, r_hfo[m3]], writes=[r_hs[k]])
                                yield
                                fw.op("act", lambda e, k=k, stt=stt: e.activation(hn[k], hs[k], AF.Square, accum_out=stt[:, 2:3]), reads=[r_hs[k]], writes=[r_hn[k], r_stt])
                                yield
                                fw.op("act", lambda e, stt=stt: e.activation(stt[:, 3:4], stt[:, 2:3], AF.Sqrt, bias=EPS, scale=1.0 / 256), reads=[r_stt], writes=[r_stt])
                                yield
                                fw.op("dve", lambda e, stt=stt: e.reciprocal(stt[:, 2:3], stt[:, 3:4]), reads=[r_stt], writes=[r_stt])
                                yield
                                fw.op("dve", lambda e, k=k, stt=stt: e.tensor_scalar(hn[k], hs[k], stt[:, 2:3], None, ALU.mult), reads=[r_hs[k], r_stt], writes=[r_hn[k]])
                                yield
                                m_transpose(hT2[k], hn[k], 2, [r_hn[k]], r_hT2[k])
                                fw.dma("pool", hmT_scr[2 * h:2 * h + 2, :, cs_].rearrange("k p n -> p k n"), hT2[k], reads=[r_hT2[k]], writes=[r_hm])
                                yield

            gen = mlstm_gen()
            gst = {"done": False, "n": 0, "calls": 0}

            def advance():
                gst["calls"] += 1
                for _ in range(2 if gst["calls"] % 8 == 0 else 1):
                    if not gst["done"]:
                        try:
                            next(gen)
                            gst["n"] += 1
                        except StopIteration:
                            gst["done"] = True

            def wload0(name, W_ap, k0, nk, c0, ncols, off):
                view = wsl[0][:, off:off + nk * ncols].rearrange("p (k n) -> p k n", k=nk)
                src = W_ap[k0 * 128:(k0 + nk) * 128, c0:c0 + ncols].rearrange("(k p) n -> p k n", p=128)
                fw.dma("sp", view, src, reads=[r_W[name]], writes=[r_wsl[0]])
                return view, r_wsl[0]

            fw.dma("sp", cqnT, cqnT_scr.rearrange("k p n -> p k n"), writes=[r_cqn])
            fw.dma("sp", ckvnT, ckvnT_scr.rearrange("k p n -> p k n"), writes=[r_ckvn])
            fw.op("pool", lambda e: e.memset(krT[64:128, :], 0.0), writes=[r_krT])
            fw.op("pool", lambda e: e.memset(QrT[64:128, :], 0.0), writes=[r_QrT])
            fw.dma("sp", krT[0:64, :], krT_scr, reads=[r_krT], writes=[r_krT])
            abank = {"i": 0}

            def xbank():
                i = abank["i"]
                abank["i"] = (i + 1) % 5
                return banks[i], r_bank[i]

            for h in range(8):
                wk, rwk = wload0("w_ukv", Wb["w_ukv"], 0, 2, h * 256, 256, 0)
                wq, rwq = wload0("w_uq", Wb["w_uq"], 0, 2, h * 192, 192, 512)
                wqs, rwqs = wload0("w_uq_sw", w_uq_sw, 0, 2, h * 64, 64, 896)
                for b in range(NBLK):
                    bs_ = slice(b * 512, (b + 1) * 512)
                    bk, rb = xbank()
                    mm(bk[:, :], [(wk[:, kc, 0:128], ckvnT[:, kc, bs_]) for kc in range(2)], [rwk, r_ckvn], rb)
                    copy_op(evac_eng(), KhT[:, bs_], bk[:, :], [rb], [r_KhT])
                    bk, rb = xbank()
                    mm(bk[:, :], [(wq[:, kc, 0:128], cqnT[:, kc, bs_]) for kc in range(2)], [rwq, r_cqn], rb)
                    copy_op(evac_eng(), QhT[:, bs_], bk[:, :], [rb], [r_QhT])
                    bka, rba = xbank()
                    mm(bka[0:64, :], [(wq[:, kc, 128:192], cqnT[:, kc, bs_]) for kc in range(2)], [rwq, r_cqn], rba)
                    bkb, rbb = xbank()
                    mm(bkb[0:64, :], [(wqs[:, kc, :], cqnT[:, kc, bs_]) for kc in range(2)], [rwqs, r_cqn], rbb)
                    fw.dma("sp", cstq, cs_scr[:, :, bs_].rearrange("c p n -> p c n"), writes=[r_cstq])
                    fw.op("dve", lambda e, bka=bka: e.tensor_tensor(rt[0], bka[0:64, :], cstq[:, 0, :], ALU.mult), reads=[rba, r_cstq], writes=[r_rt])
                    fw.op("dve", lambda e, bkb=bkb: e.tensor_tensor(rt[1], bkb[0:64, :], cstq[:, 1, :], ALU.mult), reads=[rbb, r_cstq], writes=[r_rt])
                    fw.op("pool", lambda e, bs_=bs_: e.tensor_tensor(QrT[0:64, bs_], rt[0], rt[1], ALU.add), reads=[r_rt], writes=[r_QrT])
                    bk, rb = xbank()
                    for tt in range(4):
                        ts_ = slice(b * 512 + tt * 128, b * 512 + (tt + 1) * 128)
                        mm(bk[:, tt * 128:(tt + 1) * 128], [(ckvnT[:, kc, ts_], wk[:, kc, 128:256]) for kc in range(2)], [rwk, r_ckvn], rb)
                    copy_op(evac_eng(), Vh[:, bs_], bk[:, :], [rb], [r_Vh])
                    advance()
                for qb in range(NBLK):
                    qs_ = slice(qb * 512, (qb + 1) * 512)
                    bo, rbo = banks[3], r_bank[3]
                    bsm, rbsm = banks[4], r_bank[4]

                    def S_step(kc, qs_=qs_):
                        ks_ = slice(kc * 128, (kc + 1) * 128)
                        bS, rbS = banks[kc % 3], r_bank[kc % 3]
                        mm(bS[:, :], [(KhT[:, ks_], QhT[:, qs_]), (krT[:, ks_], QrT[:, qs_])], [r_KhT, r_QhT, r_krT, r_QrT], rbS)
                        fw.op("act", lambda e, bS=bS, kc=kc: e.activation(PT[kc % 3], bS[:, :], AF.Exp, scale=A_SCALE), reads=[rbS], writes=[r_PT[kc % 3]])

                    def O_step(kc, bo=bo, rbo=rbo, bsm=bsm, rbsm=rbsm):
                        ks_ = slice(kc * 128, (kc + 1) * 128)
                        fw.op("pe", lambda e, kc=kc, ks_=ks_, bo=bo: e.matmul(bo[:, :], Vh[:, ks_], PT[kc % 3], start=(kc == 0), stop=(kc == NT - 1)),
                              reads=[r_Vh, r_PT[kc % 3]], writes=[rbo], signal=False)
                        fw.op("pe", lambda e, kc=kc, bsm=bsm: e.matmul(bsm[:, :], onesb[:], PT[kc % 3], start=(kc == 0), stop=(kc == NT - 1)),
                              reads=[r_const, r_PT[kc % 3]], writes=[rbsm], signal=True)
                    S_step(0)
                    S_step(1)
                    for kc in range(NT):
                        if kc + 2 < NT:
                            S_step(kc + 2)
                        O_step(kc)
                        advance()
                    fw.op("dve", lambda e, bsm=bsm: e.reciprocal(rsum, bsm[:, :]), reads=[rbsm], writes=[r_rsum])
                    o_ = oTs[qb & 1]
                    fw.op("dve", lambda e, bo=bo, o_=o_: e.tensor_tensor(o_, bo[:, :], rsum, ALU.mult), reads=[rbo, r_rsum], writes=[r_oTs[qb & 1]])
                    fw.dma("pool", oT_scr[h, :, qs_], o_, reads=[r_oTs[qb & 1]], writes=[r_oT])
            n_inside = gst["n"]
            while not gst["done"]:
                advance()
            mix_stats.append((n_inside, gst["n"]))

        def phase_attn(sq, heads=range(8), main_loop=True, qbs=None):
            cqnT = VB(0, 8192).rearrange("p (k n) -> p k n", k=2); r_cqn = RES("cqnT")
            ckvnT = VB(16384, 8192).rearrange("p (k n) -> p k n", k=2); r_ckvn = RES("ckvnT")
            krT = VB(32768, 4096, 128); r_krT = RES("krT")
            KhT = VB(40960, 4096); r_KhT = RES("KhT")
            Vh = VB(49152, 4096); r_Vh = RES("Vh")
            QhT = VB(57344, 4096); r_QhT = RES("QhT")
            QrT = VB(65536, 4096, 128); r_QrT = RES("QrT")
            cstq = VF(73728, 1024, 64).rearrange("p (c n) -> p c n", c=2); r_cstq = RES("cstq")
            rt = [VF(77824 + i * 2048, 512, 64) for i in range(2)]; r_rt = RES("rt")
            PT = [VB(81920 + i * 1024, 512) for i in range(3)]; r_PT = [RES("PT%d" % i) for i in range(3)]
            rsum = VF(84992, 512); r_rsum = RES("rsum")
            oTs = [VB(87040 + i * 1024, 512) for i in range(2)]; r_oTs = [RES("oTs%d" % i) for i in range(2)]
            r_oT = RES("@oT")
            fw.dma("sp", cqnT, cqnT_scr.rearrange("k p n -> p k n"), writes=[r_cqn])
            fw.dma("sp", ckvnT, ckvnT_scr.rearrange("k p n -> p k n"), writes=[r_ckvn])
            fw.op("pool", lambda e: e.memset(krT[64:128, :], 0.0), writes=[r_krT])
            fw.op("pool", lambda e: e.memset(QrT[64:128, :], 0.0), writes=[r_QrT])
            fw.dma("sp", krT[0:64, :], krT_scr, reads=[r_krT], writes=[r_krT])
            for h in heads:
                wk, rwk = wload("w_ukv", Wb["w_ukv"], 0, 2, h * 256, 256)
                wq, rwq = wload("w_uq", Wb["w_uq"], 0, 2, h * 192, 192)
                wqs, rwqs = wload("w_uq_sw", w_uq_sw, 0, 2, h * 64, 64)
                for b in range(NBLK):
                    bs_ = slice(b * 512, (b + 1) * 512)
                    bk, rb = nbank()
                    mm(bk[:, :], [(wk[:, kc, 0:128], ckvnT[:, kc, bs_]) for kc in range(2)], [rwk, r_ckvn], rb)
                    copy_op(evac_eng(), KhT[:, bs_], bk[:, :], [rb], [r_KhT])
                    bk, rb = nbank()
                    mm(bk[:, :], [(wq[:, kc, 0:128], cqnT[:, kc, bs_]) for kc in range(2)], [rwq, r_cqn], rb)
                    copy_op(evac_eng(), QhT[:, bs_], bk[:, :], [rb], [r_QhT])
                    bka, rba = nbank()
                    mm(bka[0:64, :], [(wq[:, kc, 128:192], cqnT[:, kc, bs_]) for kc in range(2)], [rwq, r_cqn], rba)
                    bkb, rbb = nbank()
                    mm(bkb[0:64, :], [(wqs[:, kc, :], cqnT[:, kc, bs_]) for kc in range(2)], [rwqs, r_cqn], rbb)
                    fw.dma("sp", cstq, cs_scr[:, :, bs_].rearrange("c p n -> p c n"), writes=[r_cstq])
                    fw.op("dve", lambda e, bka=bka: e.tensor_tensor(rt[0], bka[0:64, :], cstq[:, 0, :], ALU.mult), reads=[rba, r_cstq], writes=[r_rt])
                    fw.op("dve", lambda e, bkb=bkb: e.tensor_tensor(rt[1], bkb[0:64, :], cstq[:, 1, :], ALU.mult), reads=[rbb, r_cstq], writes=[r_rt])
                    fw.op("pool", lambda e, bs_=bs_: e.tensor_tensor(QrT[0:64, bs_], rt[0], rt[1], ALU.add), reads=[r_rt], writes=[r_QrT])
                    bk, rb = nbank()
                    for tt in range(4):
                        ts_ = slice(b * 512 + tt * 128, b * 512 + (tt + 1) * 128)
                        mm(bk[:, tt * 128:(tt + 1) * 128], [(ckvnT[:, kc, ts_], wk[:, kc, 128:256]) for kc in range(2)], [rwk, r_ckvn], rb)
                    copy_op(evac_eng(), Vh[:, bs_], bk[:, :], [rb], [r_Vh])
                for qb in ((qbs if qbs is not None else range(NBLK)) if main_loop else ()):
                    qs_ = slice(qb * 512, (qb + 1) * 512)
                    bo, rbo = banks[4 + (qb & 1)], r_bank[4 + (qb & 1)]
                    bsm, rbsm = banks[6 + (qb & 1)], r_bank[6 + (qb & 1)]

                    def S_step(kc):
                        ks_ = slice(kc * 128, (kc + 1) * 128)
                        bS, rbS = banks[kc % 4], r_bank[kc % 4]
                        mm(bS[:, :], [(KhT[:, ks_], QhT[:, qs_]), (krT[:, ks_], QrT[:, qs_])], [r_KhT, r_QhT, r_krT, r_QrT], rbS)
                        fw.op("act", lambda e, bS=bS, kc=kc: e.activation(PT[kc % 3], bS[:, :], AF.Exp, scale=A_SCALE), reads=[rbS], writes=[r_PT[kc % 3]])

                    def O_step(kc):
                        ks_ = slice(kc * 128, (kc + 1) * 128)
                        fw.op("pe", lambda e, kc=kc, ks_=ks_, bo=bo: e.matmul(bo[:, :], Vh[:, ks_], PT[kc % 3], start=(kc == 0), stop=(kc == NT - 1)),
                              reads=[r_Vh, r_PT[kc % 3]], writes=[rbo], signal=False)
                        fw.op("pe", lambda e, kc=kc, bsm=bsm: e.matmul(bsm[:, :], onesb[:], PT[kc % 3], start=(kc == 0), stop=(kc == NT - 1)),
                              reads=[r_const, r_PT[kc % 3]], writes=[rbsm], signal=True)
                    S_step(0)
                    S_step(1)
                    for kc in range(NT):
                        if kc + 2 < NT:
                            S_step(kc + 2)
                        O_step(kc)
                    fw.op("dve", lambda e, bsm=bsm: e.reciprocal(rsum, bsm[:, :]), reads=[rbsm], writes=[r_rsum])
                    o_ = oTs[qb & 1]
                    fw.op("dve", lambda e, bo=bo, o_=o_: e.tensor_tensor(o_, bo[:, :], rsum, ALU.mult), reads=[rbo, r_rsum], writes=[r_oTs[qb & 1]])
                    fw.dma("pool", oT_scr[h, :, qs_], o_, reads=[r_oTs[qb & 1]], writes=[r_oT])
            if dbg == "M1":
                fw.dma("pool", dbg_out["d_oT"], oTs[0], reads=[r_oTs[0]], writes=[RES("@dbgM1")])
            if isinstance(dbg, str) and dbg.startswith("M:"):
                fw.dma("pool", dbg_out["d_oTs0"], oTs[0], reads=[r_oTs[0]], writes=[RES("@dbgMa")])
                fw.dma("pool", dbg_out["d_oTs1"], oTs[1], reads=[r_oTs[1]], writes=[RES("@dbgMb")])
            if dbg == "E":
                rd = RES("@dbgE")
                for n_, ap_, r_ in (("d_cqnT", cqnT, r_cqn), ("d_ckvnT", ckvnT, r_ckvn), ("d_krT", krT[0:64, :], r_krT),
                                    ("d_KhT", KhT, r_KhT), ("d_QhT", QhT, r_QhT), ("d_QrT", QrT[0:64, :], r_QrT), ("d_Vh", Vh, r_Vh)):
                    fw.dma("pool", dbg_out[n_], ap_, reads=[r_], writes=[rd])

        for sq in range(NSEQ):
            if 1 in phases:
                phase1(sq)
                fw.barrier()
            if 5 in phases:
                phase_mix(sq)
                fw.barrier()
            if 2 in phases:
                phase_mlstm(sq, heads=([0] if dbg == "L1" else range(4)))
                fw.barrier()
            if 3 in phases:
                if dbg == "E":
                    phase_attn(sq, heads=[0], main_loop=False)
                elif dbg == "M1":
                    phase_attn(sq, heads=[0], main_loop=True, qbs=[0])
                elif isinstance(dbg, str) and dbg.startswith("M:"):
                    _, nh_, nq_ = dbg.split(":")
                    phase_attn(sq, heads=list(range(int(nh_))), main_loop=True, qbs=list(range(int(nq_))))
                else:
                    phase_attn(sq)
                fw.barrier()
            if 4 in phases:
                phase3(sq)
                fw.barrier()
        fw.barrier()
        fw.emit_all()
    if mix_stats:
        print("[phase_mix] scan generator steps issued inside attention / total, per sequence:", mix_stats)
    return nc


NSEQ_PER_CORE = 3


def _consts():
    ident = np.eye(128, dtype=np.float32)
    s = np.arange(128)[:, None]
    t = np.arange(128)[None, :]
    maskf = np.where(t >= s, 0.0, -1000.0).astype(np.float32)
    maskb = np.where(t <= s, 0.0, -1000.0).astype(np.float32)
    return {"c_ident": ident, "c_maskf": maskf, "c_maskb": maskb}


def kernel(**inputs):
    X = np.concatenate([inputs["x_prompt"], inputs["x_sample"]], axis=0)
    P = np.concatenate([inputs["p_prompt"][0], inputs["p_sample"][0]], axis=0)
    nseq = X.shape[0]
    base = {}
    for k in WSHAPES:
        base[k] = np.ascontiguousarray(inputs[k][0])
    for k in GSHAPES:
        base[k] = np.ascontiguousarray(inputs[k].reshape(1, -1))
    base["g_final"] = np.ascontiguousarray(inputs["g_final"].reshape(1, -1))
    base["b_gate"] = np.ascontiguousarray(inputs["b_gate"].reshape(1, 16))
    base["conv_w"] = np.ascontiguousarray(inputs["conv_w"][0])
    base.update(_consts())
    slots = [[c, c + 8, c + 16 if c + 16 < nseq else c] for c in range(8)]
    maps = []
    for sl in slots:
        m = dict(base)
        m["xs"] = np.ascontiguousarray(X[sl])
        m["ps"] = np.ascontiguousarray(P[sl])
        maps.append(m)
    nc = build(NSEQ_PER_CORE, dbg=False, phases=(1, 5, 4))
    res = run_bass_kernel_spmd(nc, maps, core_ids=list(range(8)))
    Y = np.zeros((nseq, S, D), dtype=np.float32)
    for c, sl in enumerate(slots):
        yc = res.results[c]["y"]
        for j, sidx in enumerate(sl):
            if j == 2 and c + 16 >= nseq:
                continue
            Y[sidx] = yc[j]
    nb = inputs["x_prompt"].shape[0]
    return (np.ascontiguousarray(Y[:nb], dtype=np.float32), np.ascontiguousarray(Y[nb:], dtype=np.float32))
```

```python
import numpy as np
import concourse.bass as bass
import concourse.mybir as mybir
from concourse.bass_utils import run_bass_kernel_spmd

F32 = mybir.dt.float32
BF16 = mybir.dt.bfloat16
I32 = mybir.dt.int32
AF = mybir.ActivationFunctionType
ALU = mybir.AluOpType
AX = mybir.AxisListType


class Sem:
    def __init__(self, handle, name, dma=False, owner=None):
        self.h = handle
        self.name = name
        self.dma = dma
        self.owner = owner
        self.issued = 0


class Res:
    __slots__ = ("name", "w", "r", "dsem")

    def __init__(self, name):
        self.name = name
        self.w = None
        self.r = {}
        self.dsem = None


class Eng:
    def __init__(self, name, sem, is_pe=False):
        self.name = name
        self.sem = sem
        self.ops = []
        self.seen = {}
        self.is_pe = is_pe
        self.pending = False


class FW:
    def __init__(self, nc, stack):
        self.nc = nc
        self.stack = stack
        self.engs = {}
        self.nsem = 0
        self.all_sems = []
        for n in ("pe", "act", "dve", "pool", "sp"):
            self.engs[n] = Eng(n, None, is_pe=(n == "pe"))
            self.engs[n].sem = self._newsem("s_" + n, owner=self.engs[n])
        self.dma_res = []
        self.n_ins = 0

    SEM_LIMIT = 6000

    def _newsem(self, name, dma=False, owner=None):
        self.nsem += 1
        h = self.stack.enter_context(self.nc.semaphore("%s_%d" % (name, self.nsem)))
        sm = Sem(h, name, dma=dma, owner=owner)
        self.all_sems.append(sm)
        return sm

    def dsem(self, res, qclass):
        if res.dsem is None:
            res.dsem = {}
            self.dma_res.append(res)
        sm = res.dsem.get(qclass)
        if sm is None or sm.issued >= self.SEM_LIMIT:
            sm = res.dsem[qclass] = self._newsem("d%s_%s" % (qclass[0], res.name.replace("@", "h_")), dma=True)
        return sm

    def _collect(self, E, reads, writes):
        waits = {}

        def need(ev):
            if ev is None:
                return
            sem, val = ev
            if sem.owner is E:
                if E.is_pe:
                    return
            else:
                if sem.dma and sem.issued > val:
                    val = sem.issued
            if E.seen.get(sem, 0) >= val:
                return
            if waits.get(sem, 0) < val:
                waits[sem] = val

        for r in reads:
            need(r.w)
        for w in writes:
            need(w.w)
            for s, v in w.r.items():
                need((s, v))
        for s, v in waits.items():
            E.seen[s] = v
        return list(waits.items())

    def _commit(self, ev, reads, writes):
        for w in writes:
            w.w = ev
            w.r = {}
        for r in reads:
            if r in writes:
                continue
            s, v = ev
            if r.r.get(s, 0) < v:
                r.r[s] = v

    def op(self, eng, fn, reads=(), writes=(), signal=True):
        E = self.engs[eng]
        waits = self._collect(E, reads, writes)
        if signal:
            E.sem.issued += 1
        val = E.sem.issued if signal else E.sem.issued + 1
        sem_h = E.sem.h

        def emit(e, waits=waits, fn=fn, signal=signal):
            for s, v in waits:
                e.wait_ge(s.h, v)
            ins = fn(e)
            if signal:
                ins.then_inc(sem_h, 1)
        E.ops.append(emit)
        E.pending = not signal
        self._commit((E.sem, val), reads, writes)
        self.n_ins += 1
        if signal and E.sem.issued >= self.SEM_LIMIT:
            E.sem = self._newsem("s_" + E.name, owner=E)

    def dma(self, eng, out, in_, reads=(), writes=(), sem_res=None, **kw):
        E = self.engs[eng]
        waits = self._collect(E, reads, writes)
        if sem_res is None:
            sem_res = writes[0] if (writes and writes[0].name[0] != "@") else reads[0]
        ds = self.dsem(sem_res, "sw" if eng == "pool" else "hw")
        ds.issued += 16
        val = ds.issued

        def emit(e, waits=waits):
            for s, v in waits:
                e.wait_ge(s.h, v)
            e.dma_start(out=out, in_=in_, **kw).then_inc(ds.h, 16)
        E.ops.append(emit)
        self._commit((ds, val), reads, writes)
        self.n_ins += 1

    def barrier(self):
        for E in self.engs.values():
            assert not E.pending
        evs = [(sm, sm.issued) for sm in self.all_sems if sm.issued > 0]
        for E in self.engs.values():
            waits = []
            for s, v in evs:
                if s.owner is E and E.is_pe:
                    continue
                if E.seen.get(s, 0) < v:
                    waits.append((s, v))
                    E.seen[s] = v

            def emit(e, waits=waits):
                for s, v in waits:
                    e.wait_ge(s.h, v)
            E.ops.append(emit)

    def wait_all(self, eng, resources):
        E = self.engs[eng]
        waits = self._collect(E, resources, [])

        def emit(e, waits=waits):
            for s, v in waits:
                e.wait_ge(s.h, v)
        E.ops.append(emit)

    def emit_all(self):
        nc = self.nc
        with nc.Block() as block:
            @block.tensor
            def _(e):
                for f in self.engs["pe"].ops:
                    f(e)

            @block.scalar
            def _(e):
                for f in self.engs["act"].ops:
                    f(e)

            @block.vector
            def _(e):
                for f in self.engs["dve"].ops:
                    f(e)

            @block.gpsimd
            def _(e):
                for f in self.engs["pool"].ops:
                    f(e)

            @block.sync
            def _(e):
                for f in self.engs["sp"].ops:
                    f(e)


import math
from contextlib import ExitStack

S = 4096
D = 1024
NBLK = 8
NT = 32
DFF = 2816
V0, O0, G0, CQ0, CKV0, KR0, GM0, INW = 1024, 2048, 3072, 3088, 3344, 3600, 3664, 5712
EPS = 1e-6
LN_SQ = math.log(128 ** -0.5)
A_SCALE = 192 ** -0.5

WSHAPES = {
    "w_in": (1024, 5712), "w_mo": (1024, 1024), "w_uq": (256, 1536), "w_ukv": (256, 2048),
    "w_ao": (1024, 1024), "w_out": (1024, 1024), "w_gate": (1024, 2816), "w_up": (1024, 2816),
    "w_down": (2816, 1024), "w_ple": (256, 1024), "w_ple_gate": (1024, 1024),
}
GAINS = {"w_in": "g_mix", "w_mo": "g_mhead", "w_uq": "g_q", "w_ukv": "g_kv", "w_gate": "g_ffn", "w_up": "g_ffn"}
GSHAPES = {"g_mix": 1024, "g_mhead": 1024, "g_q": 256, "g_kv": 256, "g_ffn": 1024}


def build(NSEQ, dbg=False, phases=(1, 2, 3, 4)):
    nc = bass.Bass("TRN2", target_bir_lowering=False)
    din = lambda n, sh, dt=F32: nc.dram_tensor(n, list(sh), dt, kind="ExternalInput").ap()
    dscr = lambda n, sh, dt: nc.dram_tensor(n, list(sh), dt, kind="Internal").ap()
    xs = din("xs", [NSEQ, S, D])
    ps = din("ps", [NSEQ, S, 256])
    Wf = {k: din(k, sh) for k, sh in WSHAPES.items()}
    Gd = {k: din(k, [1, n]) for k, n in GSHAPES.items()}
    g_final = din("g_final", [1, D])
    b_gate = din("b_gate", [1, 16])
    conv_w = din("conv_w", [5, 1024])
    c_ident = din("c_ident", [128, 128])
    c_maskf = din("c_maskf", [128, 128])
    c_maskb = din("c_maskb", [128, 128])
    y = nc.dram_tensor("y", [NSEQ, S, D], F32, kind="ExternalOutput").ap()
    dbg_out = {}
    if dbg == "P1":
        dbg_out["d_colsT"] = nc.dram_tensor("d_colsT", [128, 512], F32, kind="ExternalOutput").ap()
    if dbg == "M1":
        dbg_out["d_oT"] = nc.dram_tensor("d_oT", [128, 512], BF16, kind="ExternalOutput").ap()
    if isinstance(dbg, str) and dbg.startswith("M:"):
        dbg_out["d_oTs0"] = nc.dram_tensor("d_oTs0", [128, 512], BF16, kind="ExternalOutput").ap()
        dbg_out["d_oTs1"] = nc.dram_tensor("d_oTs1", [128, 512], BF16, kind="ExternalOutput").ap()
    if dbg == "E":
        for n_, sh_ in (("d_cqnT", [128, 2, S]), ("d_ckvnT", [128, 2, S]), ("d_krT", [64, S]),
                        ("d_KhT", [128, S]), ("d_QhT", [128, S]), ("d_QrT", [64, S]), ("d_Vh", [128, S])):
            dbg_out[n_] = nc.dram_tensor(n_, sh_, BF16, kind="ExternalOutput").ap()
    Wb = {k: dscr(k + "_b", sh, BF16) for k, sh in WSHAPES.items()}
    w_kr_sw = dscr("w_kr_sw", [1024, 64], BF16)
    w_uq_sw = dscr("w_uq_sw", [256, 512], BF16)
    cs_scr = dscr("cs_scr", [2, 64, S], F32)
    qk_scr = (nc.dram_tensor("qk_scr", [8, 128, S], BF16, kind="ExternalOutput").ap() if dbg == "P1" else dscr("qk_scr", [8, 128, S], BF16))
    v_scr = (nc.dram_tensor("v_scr", [S, 1024], BF16, kind="ExternalOutput").ap() if dbg == "P1" else dscr("v_scr", [S, 1024], BF16))
    nb_scr = (nc.dram_tensor("nb_scr", [2, 4, S], F32, kind="ExternalOutput").ap() if dbg == "P1" else dscr("nb_scr", [2, 4, S], F32))
    hf_scr = dscr("hf_scr", [S, 256], F32)
    hmT_scr = (nc.dram_tensor("hmT_scr", [8, 128, S], BF16, kind="ExternalOutput").ap() if dbg == "L1" else dscr("hmT_scr", [8, 128, S], BF16))
    if isinstance(dbg, str) and dbg.startswith("M:"):
        oT_scr = nc.dram_tensor("oT_scr", [8, 128, S], BF16, kind="ExternalOutput").ap()
    else:
        oT_scr = dscr("oT_scr", [8, 128, S], BF16)
    cqnT_scr = dscr("cqnT_scr", [2, 128, S], BF16)
    ckvnT_scr = dscr("ckvnT_scr", [2, 128, S], BF16)
    krT_scr = dscr("krT_scr", [64, S], BF16)

    _res_registry = {}
    _ResClass = Res

    def RES(name):
        r = _res_registry.get(name)
        if r is None:
            r = _res_registry[name] = _ResClass(name)
        return r

    with ExitStack() as st:
        fw = FW(nc, st)
        T = lambda name, shape, dt: st.enter_context(nc.sbuf_tensor(name, shape, dt))
        identf = T("identf", [128, 128], F32)
        identb = T("identb", [128, 128], BF16)
        maskf = T("maskf", [128, 128], F32)
        maskb = T("maskb", [128, 128], F32)
        onesb = T("onesb", [128, 128], BF16)
        gfin = T("gfin", [128, D], F32)
        convw = T("convw", [128, 5, 8], F32)
        gcol = {k: T("gc_" + k, [128, n // 128], F32) for k, n in GSHAPES.items()}
        gneg = {k: T("gn_" + k, [128, GSHAPES[k] // 128], F32) for k in ("g_mix", "g_q")}
        bg = T("bg", [4, 4], F32)
        colsT = T("colsT", [128, 4, NT, 4], F32)
        wsl = [T("wsl%d" % i, [128, 4096], BF16) for i in range(6)]
        r_wsl = [RES("wsl%d" % i) for i in range(6)]
        AR = 34816
        arena = T("arena", [128, AR], F32)
        arena_b = arena.bitcast(BF16)
        arena_i = arena.bitcast(I32)
        banks = [st.enter_context(nc.psum_tensor("pb%d" % i, [128, 512], F32)) for i in range(8)]
        r_bank = [RES("pb%d" % i) for i in range(8)]
        r_const = RES("const")
        state = {"bank": 0, "wsl": 0, "ev": 0}

        def VF(off, n, parts=128):
            assert off % 4 == 0 and off // 4 + n <= AR
            return arena[0:parts, off // 4: off // 4 + n]

        def VB(off, n, parts=128):
            assert off % 2 == 0 and off // 2 + n <= 2 * AR
            return arena_b[0:parts, off // 2: off // 2 + n]

        def nbank():
            i = state["bank"]
            state["bank"] = (i + 1) % 8
            return banks[i], r_bank[i]

        def evac_eng():
            state["ev"] ^= 1
            return "act" if state["ev"] else "dve"

        def copy_op(eng, out, in_, reads, writes):
            if eng == "act":
                fw.op("act", lambda e: e.copy(out, in_), reads=reads, writes=writes)
            else:
                fw.op(eng, lambda e: e.tensor_copy(out, in_), reads=reads, writes=writes)

        def mm(out_ap, pairs, reads, wres):
            n = len(pairs)
            for i, (l, r) in enumerate(pairs):
                fw.op("pe", lambda e, l=l, r=r, i=i: e.matmul(out_ap, l, r, start=(i == 0), stop=(i == n - 1)),
                      reads=reads, writes=[wres], signal=(i == n - 1))

        def transp(out_ap, in_ap, ident, reads, wres, signal=True):
            fw.op("pe", lambda e: e.transpose(out_ap, in_ap, ident), reads=reads, writes=[wres], signal=signal)

        r_W = {k: RES("@" + k) for k in WSHAPES}
        r_W["w_kr_sw"] = RES("@w_kr_sw")
        r_W["w_uq_sw"] = RES("@w_uq_sw")

        def wload(name, W_ap, k0, nk, c0, ncols):
            i = state["wsl"]
            state["wsl"] = (i + 1) % 6
            assert nk * ncols <= 4096
            view = wsl[i][:, 0:nk * ncols].rearrange("p (k n) -> p k n", k=nk)
            src = W_ap[k0 * 128:(k0 + nk) * 128, c0:c0 + ncols].rearrange("(k p) n -> p k n", p=128)
            fw.dma("sp", view, src, reads=[r_W[name]], writes=[r_wsl[i]])
            return view, r_wsl[i]

        fw.dma("sp", identf[:], c_ident, writes=[r_const])
        fw.dma("sp", maskf[:], c_maskf, writes=[r_const])
        fw.dma("sp", maskb[:], c_maskb, writes=[r_const])
        fw.dma("sp", gfin[:], g_final.partition_broadcast(128).rearrange("p o n -> p (o n)"), writes=[r_const])
        for j in range(5):
            fw.dma("sp", convw[:, j, :], conv_w[j:j + 1, :].rearrange("o (c p) -> p (o c)", p=128), writes=[r_const], allow_slow_non_contiguous=True)
        for k in GSHAPES:
            fw.dma("sp", gcol[k][:], Gd[k].rearrange("o (k p) -> p (o k)", p=128), writes=[r_const], allow_slow_non_contiguous=True)
        fw.dma("sp", bg[:], b_gate.rearrange("o (g h) -> h (o g)", h=4), writes=[r_const], allow_slow_non_contiguous=True)
        fw.op("dve", lambda e: e.tensor_copy(identb[:], identf[:]), reads=[r_const], writes=[r_const])
        fw.op("dve", lambda e: e.memset(onesb[:], 1.0), writes=[r_const])
        for k in gneg:
            fw.op("dve", lambda e, k=k: e.tensor_scalar(gneg[k][:], gcol[k][:], -1.0, None, ALU.mult), reads=[r_const], writes=[r_const])
        fw.barrier()

        stg_in = [VF(i * 8192, 2048) for i in range(2)]
        stg_out = [VB(16384 + i * 4096, 2048) for i in range(2)]
        r_si = [RES("stg_in%d" % i) for i in range(2)]
        r_so = [RES("stg_out%d" % i) for i in range(2)]
        pw = {"i": 0}

        def prep(name):
            K, N = WSHAPES[name]
            g = gcol[GAINS[name]] if name in GAINS else None
            for kc in range(K // 128):
                for c0 in range(0, N, 2048):
                    n_ = min(2048, N - c0)
                    i = pw["i"] & 1
                    pw["i"] += 1
                    fw.dma("sp", stg_in[i][:, 0:n_], Wf[name][kc * 128:(kc + 1) * 128, c0:c0 + n_], writes=[r_si[i]])
                    eng = evac_eng()
                    if g is None:
                        copy_op(eng, stg_out[i][:, 0:n_], stg_in[i][:, 0:n_], [r_si[i]], [r_so[i]])
                    elif eng == "act":
                        fw.op("act", lambda e, i=i, kc=kc, n_=n_, g=g: e.activation(stg_out[i][:, 0:n_], stg_in[i][:, 0:n_], AF.Copy, scale=g[:, kc:kc + 1]),
                              reads=[r_si[i], r_const], writes=[r_so[i]])
                    else:
                        fw.op("dve", lambda e, i=i, kc=kc, n_=n_, g=g: e.tensor_scalar(stg_out[i][:, 0:n_], stg_in[i][:, 0:n_], g[:, kc:kc + 1], None, ALU.mult),
                              reads=[r_si[i], r_const], writes=[r_so[i]])
                    fw.dma("pool", Wb[name][kc * 128:(kc + 1) * 128, c0:c0 + n_], stg_out[i][:, 0:n_], reads=[r_so[i]], writes=[r_W[name]])

        def prep_sw(dst, dname, src, K, bases, g, gn):
            for kc in range(K // 128):
                i = pw["i"] & 1
                pw["i"] += 1
                nb_ = len(bases)
                for j, b in enumerate(bases):
                    fw.dma("sp", stg_in[i][:, j * 64:(j + 1) * 64], src[kc * 128:(kc + 1) * 128, b:b + 64], writes=[r_si[i]])
                for j in range(nb_):
                    fw.op("dve", lambda e, i=i, j=j, kc=kc: e.tensor_scalar(stg_out[i][:, j * 64:j * 64 + 32], stg_in[i][:, j * 64 + 32:j * 64 + 64], gn[:, kc:kc + 1], None, ALU.mult),
                          reads=[r_si[i], r_const], writes=[r_so[i]])
                    fw.op("dve", lambda e, i=i, j=j, kc=kc: e.tensor_scalar(stg_out[i][:, j * 64 + 32:j * 64 + 64], stg_in[i][:, j * 64:j * 64 + 32], g[:, kc:kc + 1], None, ALU.mult),
                          reads=[r_si[i], r_const], writes=[r_so[i]])
                fw.dma("pool", dst[kc * 128:(kc + 1) * 128, 0:64 * nb_], stg_out[i][:, 0:64 * nb_], reads=[r_so[i]], writes=[r_W[dname]])

        for name in WSHAPES:
            prep(name)
        prep_sw(w_kr_sw, "w_kr_sw", Wf["w_in"], 1024, [KR0], gcol["g_mix"], gneg["g_mix"])
        prep_sw(w_uq_sw, "w_uq_sw", Wf["w_uq"], 256, [h * 192 + 128 for h in range(8)], gcol["g_q"], gneg["g_q"])

        r_tr = RES("trig")
        TB0 = 24576
        posf = VF(TB0, S, 64)
        tq = VF(TB0 + 16384, S, 64)
        ti = arena_i[0:64, (TB0 + 32768) // 4:(TB0 + 32768) // 4 + S]
        tk = VF(TB0 + 49152, S, 64)
        tu = VF(TB0 + 65536, S, 64)
        pidx = T("pidx", [64, 1], F32)
        invf = T("invf", [64, 1], F32)
        pidx_i = T("pidx_i", [64, 1], I32)
        fw.op("pool", lambda e: e.iota(ti, [[1, S]], base=0, channel_multiplier=0), writes=[r_tr])
        fw.op("pool", lambda e: e.iota(pidx_i[0:32, :], [[0, 1]], base=0, channel_multiplier=1), writes=[r_tr])
        fw.op("pool", lambda e: e.iota(pidx_i[32:64, :], [[0, 1]], base=0, channel_multiplier=1), writes=[r_tr])
        fw.op("dve", lambda e: e.tensor_copy(posf, ti), reads=[r_tr], writes=[r_tr])
        fw.op("dve", lambda e: e.tensor_copy(pidx[:], pidx_i[:]), reads=[r_tr], writes=[r_tr])
        fw.op("act", lambda e: e.activation(invf[:], pidx[:], AF.Exp, scale=-math.log(10000.0) / 32.0), reads=[r_tr], writes=[r_tr])
        fw.op("dve", lambda e: e.tensor_scalar(tq, posf, invf[:, 0:1], 1.0 / (2 * math.pi), ALU.mult, ALU.mult), reads=[r_tr], writes=[r_tr])
        for which, shift in ((1, 0.0), (0, 0.25)):
            fw.op("dve", lambda e, shift=shift: e.tensor_scalar(tu, tq, shift, None, ALU.add), reads=[r_tr], writes=[r_tr])
            fw.op("dve", lambda e: e.tensor_copy(ti, tu), reads=[r_tr], writes=[r_tr])
            fw.op("dve", lambda e: e.tensor_copy(tk, ti), reads=[r_tr], writes=[r_tr])
            fw.op("dve", lambda e: e.tensor_tensor(tu, tu, tk, ALU.subtract), reads=[r_tr], writes=[r_tr])
            fw.op("dve", lambda e: e.tensor_scalar(tk, tu, 0.5, None, ALU.is_gt), reads=[r_tr], writes=[r_tr])
            fw.op("dve", lambda e: e.tensor_tensor(tu, tu, tk, ALU.subtract), reads=[r_tr], writes=[r_tr])
            fw.op("dve", lambda e: e.tensor_scalar(tk, tu, -0.5, None, ALU.is_lt), reads=[r_tr], writes=[r_tr])
            fw.op("dve", lambda e: e.tensor_tensor(tu, tu, tk, ALU.add), reads=[r_tr], writes=[r_tr])
            fw.op("act", lambda e: e.activation(tk, tu, AF.Sin, scale=2 * math.pi), reads=[r_tr], writes=[r_tr])
            fw.dma("pool", cs_scr[which], tk, reads=[r_tr], writes=[RES("@cs")], sem_res=r_tr)
        fw.barrier()

        _once = {}

        def T_once(name, shape, dt):
            if name not in _once:
                _once[name] = T(name, shape, dt)
            return _once[name]
        mix_stats = []
        st1 = [T("st1_%d" % i, [128, 8], F32) for i in range(4)]
        r_st1 = [RES("st1_%d" % i) for i in range(4)]
        cmask = T("cmask", [4, 512], F32)
        fw.op("pool", lambda e: e.memset(cmask[:], 1.0), writes=[r_const])
        for j in range(4):
            fw.op("pool", lambda e, j=j: e.memset(cmask[:, j * 128:j * 128 + 1], 0.0), writes=[r_const])
        fw.barrier()

        def rms_tile(x_ap, xn_ap, stt, r_x, r_xn, r_stt, n):
            fw.op("act", lambda e: e.activation(xn_ap, x_ap, AF.Square, accum_out=stt[:, 0:1]), reads=[r_x], writes=[r_xn, r_stt])
            fw.op("act", lambda e: e.activation(stt[:, 1:2], stt[:, 0:1], AF.Sqrt, bias=EPS, scale=1.0 / n), reads=[r_stt], writes=[r_stt])
            fw.op("dve", lambda e: e.reciprocal(stt[:, 2:3], stt[:, 1:2]), reads=[r_stt], writes=[r_stt])
            fw.op("dve", lambda e: e.tensor_scalar(xn_ap, x_ap, stt[:, 2:3], None, ALU.mult), reads=[r_x, r_stt], writes=[r_xn])

        def transpose_to(dst3, src_ap, nk, reads, r_dst):
            bk, rb = nbank()
            pbt = bk.bitcast(BF16)
            for k in range(nk):
                transp(pbt[:, k * 128:(k + 1) * 128], src_ap[:, k * 128:(k + 1) * 128], identb[:], reads + [r_const], rb, signal=(k == nk - 1))
            eng = evac_eng()
            copy_op(eng, dst3, pbt[:, 0:nk * 128].rearrange("p (k n) -> p k n", k=nk), [rb], [r_dst])

        def phase1(sq):
            hTb = [VB(i * 8192, 4096).rearrange("p (k n) -> p k n", k=8) for i in range(2)]
            r_hTb = [RES("hTb%d" % i) for i in range(2)]
            xa = [VF(16384 + i * 4096, 1024) for i in range(2)] + [VF(108544 + i * 4096, 1024) for i in range(2)]
            r_xa = [RES("xa%d" % i) for i in range(4)]
            xnb = [VB(24576 + i * 2048, 1024) for i in range(2)] + [VB(116736 + i * 2048, 1024) for i in range(2)]
            r_xnb = [RES("xnb%d" % i) for i in range(4)]
            U = [VF(28672 + ct * 2064, 516) for ct in range(8)]
            r_U = [RES("U%d" % ct) for ct in range(8)]
            acc = [VF(45184 + i * 2048, 512) for i in range(2)]
            r_acc = [RES("acc%d" % i) for i in range(2)]
            qkb = [VB(49280 + i * 1024, 512) for i in range(2)]
            r_qkb = [RES("qkb%d" % i) for i in range(2)]
            vst = [VB(51328 + i * 2048, 1024) for i in range(4)]
            r_vst = [RES("vst%d" % i) for i in range(4)]
            GB = 59520
            gt = [VF(GB + i * 2048, 512, 4) for i in range(16)]
            r_g = RES("gates")
            LB = GB + 32768
            lat = VB(LB, 2048).rearrange("p (k n) -> p k n", k=4)
            r_lat = RES("lat")
            cnb = [VB(LB + 4096 + i * 1024, 512) for i in range(2)]
            r_cnb = [RES("cnb%d" % i) for i in range(2)]
            cst = VF(LB + 6144, 1024, 64).rearrange("p (c n) -> p c n", c=2)
            r_cst = RES("cst")
            krt = [VF(LB + 10240 + i * 2048, 512, 64) for i in range(2)]
            r_krt = RES("krt")
            krb = VB(LB + 14336, 512, 64)
            r_krb = RES("krb")
            r_qk = RES("@qk_scr"); r_v = RES("@v_scr"); r_nb = RES("@nb_scr")
            r_cq = RES("@cq"); r_ckv = RES("@ckv"); r_kr = RES("@kr")
            xi = 0
            for b in range(NBLK):
                tok0 = b * 512
                j = b & 1
                for tt in range(4):
                    i = xi & 3
                    xi += 1
                    fw.dma("sp", xa[i], xs[sq, tok0 + tt * 128: tok0 + (tt + 1) * 128, :], writes=[r_xa[i]])
                    rms_tile(xa[i], xnb[i], st1[i], r_xa[i], r_xnb[i], r_st1[i], 1024)
                    transpose_to(hTb[j][:, :, tt * 128:(tt + 1) * 128], xnb[i], 8, [r_xnb[i]], r_hTb[j])
                for grp in range(2):
                    wv, rw = wload("w_in", Wb["w_in"], 0, 8, grp * 512, 512)
                    for cl in range(4):
                        ct = grp * 4 + cl
                        bk, rb = nbank()
                        mm(bk[:, :], [(wv[:, kc, cl * 128:(cl + 1) * 128], hTb[j][:, kc, :]) for kc in range(8)], [rw, r_hTb[j]], rb)
                        if b == 0:
                            fw.op("pool", lambda e, ct=ct: e.memset(U[ct][:, 0:4], 0.0), writes=[r_U[ct]])
                        fw.op("act", lambda e, ct=ct, bk=bk: e.copy(U[ct][:, 4:516], bk[:, :]), reads=[rb], writes=[r_U[ct]])
                        a = ct & 1

                        def conv(n, a=a, ct=ct):
                            fw.op("dve", lambda e: e.tensor_scalar(acc[a][:, 0:n], U[ct][:, 0:n], convw[:, 0, ct:ct + 1], None, ALU.mult),
                                  reads=[r_U[ct], r_const], writes=[r_acc[a]])
                            for jj in range(1, 5):
                                fw.op("dve", lambda e, jj=jj: e.scalar_tensor_tensor(acc[a][:, 0:n], U[ct][:, jj:jj + n], convw[:, jj, ct:ct + 1], acc[a][:, 0:n], ALU.mult, ALU.add),
                                      reads=[r_U[ct], r_const, r_acc[a]], writes=[r_acc[a]])
                            fw.op("act", lambda e: e.activation(qkb[a][:, 0:n], acc[a][:, 0:n], AF.Silu), reads=[r_acc[a]], writes=[r_qkb[a]])
                        conv(512)
                        if b == 0:
                            fw.dma("pool", qk_scr[ct, :, 0:510], qkb[a][:, 2:512], reads=[r_qkb[a]], writes=[r_qk])
                        else:
                            fw.dma("pool", qk_scr[ct, :, tok0 - 2:tok0 + 510], qkb[a][:, 0:512], reads=[r_qkb[a]], writes=[r_qk])
                        fw.op("pool", lambda e, ct=ct: e.tensor_copy(U[ct][:, 0:4], U[ct][:, 512:516]), reads=[r_U[ct]], writes=[r_U[ct]])
                        if b == NBLK - 1:
                            fw.op("pool", lambda e, ct=ct: e.memset(U[ct][:, 4:8], 0.0), writes=[r_U[ct]])
                            conv(2)
                            fw.dma("pool", qk_scr[ct, :, S - 2:S], qkb[a][:, 0:2], reads=[r_qkb[a]], writes=[r_qk])
                for half in range(2):
                    wv, rw = wload("w_in", Wb["w_in"], 0, 8, V0 + half * 512, 512)
                    for tt in range(4):
                        bk, rb = nbank()
                        mm(bk[:, :], [(hTb[j][:, kc, tt * 128:(tt + 1) * 128], wv[:, kc, :]) for kc in range(8)], [rw, r_hTb[j]], rb)
                        copy_op(evac_eng(), vst[tt][:, half * 512:(half + 1) * 512], bk[:, :], [rb], [r_vst[tt]])
                for tt in range(4):
                    fw.dma("pool", v_scr[tok0 + tt * 128: tok0 + (tt + 1) * 128, :], vst[tt], reads=[r_vst[tt]], writes=[r_v])
                wv, rw = wload("w_in", Wb["w_in"], 0, 8, G0, 16)
                for g in range(4):
                    bk, rb = nbank()
                    mm(bk[0:4, :], [(wv[:, kc, 4 * g:4 * g + 4], hTb[j][:, kc, :]) for kc in range(8)], [rw, r_hTb[j]], rb)
                    fw.op("act", lambda e, g=g, bk=bk: e.activation(gt[g], bk[0:4, :], AF.Identity, bias=bg[:, g:g + 1]), reads=[rb, r_const], writes=[r_g])
                IFt, IBt, FFt, FBt = gt[0], gt[1], gt[2], gt[3]
                SPf, SPb, csf, csb, NBf, NBb, Af, Ab, Gf, Gb, tmp = gt[4:15]
                G = lambda fn: fw.op(fn[0], fn[1], reads=[r_g, r_const], writes=[r_g])
                for FX, SP in ((FFt, SPf), (FBt, SPb)):
                    G(("act", lambda e, FX=FX: e.activation(tmp, FX, AF.Exp, scale=-1.0)))
                    G(("act", lambda e, SP=SP: e.activation(SP, tmp, AF.Ln, bias=1.0)))
                G(("dve", lambda e: e.tensor_tensor_scan(csf, cmask[:], SPf, 0.0, ALU.mult, ALU.add)))
                G(("dve", lambda e: e.tensor_tensor_scan(csb, cmask[:], SPb, 0.0, ALU.mult, ALU.add)))
                v3 = lambda t: t.rearrange("p (a n) -> p a n", a=4)
                tot = lambda t: t.rearrange("p (a n) -> p a n", a=4)[:, :, 127:128].to_broadcast([4, 4, 128])
                G(("dve", lambda e: e.tensor_scalar(NBf, csf, -1.0, None, ALU.mult)))
                G(("dve", lambda e: e.tensor_tensor(Af, IFt, csf, ALU.add)))
                G(("dve", lambda e: e.tensor_tensor(v3(tmp), v3(Af), tot(csf), ALU.subtract)))
                G(("act", lambda e: e.activation(Gf, tmp, AF.Exp)))
                G(("dve", lambda e: e.tensor_tensor(NBb, SPb, csb, ALU.subtract)))
                G(("dve", lambda e: e.tensor_tensor(v3(NBb), v3(NBb), tot(csb), ALU.add)))
                G(("dve", lambda e: e.tensor_tensor(Ab, IBt, NBb, ALU.add)))
                G(("dve", lambda e: e.tensor_tensor(v3(tmp), v3(Ab), tot(csb), ALU.subtract)))
                G(("act", lambda e: e.activation(Gb, tmp, AF.Exp)))
                G(("dve", lambda e: e.tensor_scalar(NBb, NBb, -1.0, None, ALU.mult)))
                fw.dma("pool", nb_scr[0, :, tok0:tok0 + 512], NBf, reads=[r_g], writes=[r_nb])
                fw.dma("pool", nb_scr[1, :, tok0:tok0 + 512], NBb, reads=[r_g], writes=[r_nb])
                wv, rw = wload("w_in", Wb["w_in"], 0, 8, CQ0, 512)
                for tt in range(4):
                    bk, rb = nbank()
                    mm(bk[:, :], [(hTb[j][:, kc, tt * 128:(tt + 1) * 128], wv[:, kc, :]) for kc in range(8)], [rw, r_hTb[j]], rb)
                    i = tt & 1
                    stt = st1[i]
                    fw.op("act", lambda e, i=i, bk=bk, stt=stt: e.activation(cnb[i][:, 0:256], bk[:, 0:256], AF.Square, accum_out=stt[:, 4:5]), reads=[rb], writes=[r_cnb[i], r_st1[i]])
                    fw.op("act", lambda e, i=i, bk=bk, stt=stt: e.activation(cnb[i][:, 256:512], bk[:, 256:512], AF.Square, accum_out=stt[:, 5:6]), reads=[rb], writes=[r_cnb[i], r_st1[i]])
                    fw.op("act", lambda e, stt=stt: e.activation(stt[:, 6:8], stt[:, 4:6], AF.Sqrt, bias=EPS, scale=1.0 / 256), reads=[r_st1[i]], writes=[r_st1[i]])
                    fw.op("dve", lambda e, stt=stt: e.reciprocal(stt[:, 4:6], stt[:, 6:8]), reads=[r_st1[i]], writes=[r_st1[i]])
                    fw.op("dve", lambda e, i=i, bk=bk, stt=stt: e.tensor_scalar(cnb[i][:, 0:256], bk[:, 0:256], stt[:, 4:5], None, ALU.mult), reads=[rb, r_st1[i]], writes=[r_cnb[i]])
                    fw.op("act", lambda e, i=i, bk=bk, stt=stt: e.activation(cnb[i][:, 256:512], bk[:, 256:512], AF.Copy, scale=stt[:, 5:6]), reads=[rb, r_st1[i]], writes=[r_cnb[i]])
                    transpose_to(lat[:, :, tt * 128:(tt + 1) * 128], cnb[i], 4, [r_cnb[i]], r_lat)
                fw.dma("pool", cqnT_scr[:, :, tok0:tok0 + 512].rearrange("k p n -> p k n"), lat[:, 0:2, :], reads=[r_lat], writes=[r_cq])
                fw.dma("pool", ckvnT_scr[:, :, tok0:tok0 + 512].rearrange("k p n -> p k n"), lat[:, 2:4, :], reads=[r_lat], writes=[r_ckv])
                wa, rwa = wload("w_in", Wb["w_in"], 0, 8, KR0, 64)
                wb_, rwb = wload("w_kr_sw", w_kr_sw, 0, 8, 0, 64)
                bka, rba = nbank()
                mm(bka[0:64, :], [(wa[:, kc, :], hTb[j][:, kc, :]) for kc in range(8)], [rwa, r_hTb[j]], rba)
                bkb, rbb = nbank()
                mm(bkb[0:64, :], [(wb_[:, kc, :], hTb[j][:, kc, :]) for kc in range(8)], [rwb, r_hTb[j]], rbb)
                fw.dma("sp", cst, cs_scr[:, :, tok0:tok0 + 512].rearrange("c p n -> p c n"), writes=[r_cst])
                fw.op("dve", lambda e, bka=bka: e.tensor_tensor(krt[0], bka[0:64, :], cst[:, 0, :], ALU.mult), reads=[rba, r_cst], writes=[r_krt])
                fw.op("dve", lambda e, bkb=bkb: e.tensor_tensor(krt[1], bkb[0:64, :], cst[:, 1, :], ALU.mult), reads=[rbb, r_cst], writes=[r_krt])
                fw.op("pool", lambda e: e.tensor_tensor(krb, krt[0], krt[1], ALU.add), reads=[r_krt], writes=[r_krb])
                fw.dma("pool", krT_scr[:, tok0:tok0 + 512], krb, reads=[r_krb], writes=[r_kr])
                bk, rb = nbank()
                for qi, Q in enumerate((Af, Ab, Gf, Gb)):
                    for tt in range(4):
                        c0 = (qi * 4 + tt) * 4
                        transp(bk[:, c0:c0 + 4], Q[:, tt * 128:(tt + 1) * 128], identf[0:4, 0:4], [r_g, r_const], rb, signal=(qi == 3 and tt == 3))
                fw.op("dve", lambda e, bk=bk, b=b: e.tensor_copy(colsT[:, :, b * 4:(b + 1) * 4, :], bk[:, 0:64].rearrange("p (q t h) -> p q t h", q=4, t=4)),
                      reads=[rb], writes=[r_const])
            fw.op("dve", lambda e: e.tensor_scalar(colsT[:, 0:2], colsT[:, 0:2], LN_SQ, None, ALU.add), reads=[r_const], writes=[r_const])
            if dbg == "P1":
                fw.dma("pool", dbg_out["d_colsT"], colsT[:].rearrange("p q t h -> p (q t h)"), reads=[r_const], writes=[RES("@dbgP1")], sem_res=r_st1[0])

        def phase3(sq):
            xa4 = VF(0, 4096).rearrange("p (t n) -> p t n", t=4)
            r_xa4 = [RES("xa4_%d" % i) for i in range(4)]
            xn4 = VB(16384, 4096).rearrange("p (t n) -> p t n", t=4)
            r_xn4 = [RES("xn4_%d" % i) for i in range(4)]
            k8 = lambda ap: ap.rearrange("p (k n) -> p k n", k=8)
            hTb = k8(VB(24576, 4096)); r_hTb = RES("p3hTb")
            sgo = k8(VB(32768, 4096)); r_sgo = RES("sgo")
            hmT = k8(VB(40960, 4096)); r_hmT = RES("hmTb")
            oTb = k8(VB(49152, 4096)); r_oTb = RES("oTb")
            mrg = k8(VB(57344, 4096)); r_mrg = RES("mrg")
            sg = [VF(65536 + i * 2048, 512) for i in range(2)]; r_sg = [RES("sg%d" % i) for i in range(2)]
            t12 = [VF(69632 + i * 2048, 512) for i in range(2)]; r_t12 = [RES("t12_%d" % i) for i in range(2)]
            actT = VB(73728, 22 * 512).rearrange("p (k n) -> p k n", k=22); r_actT = RES("actT")
            pa = VF(96256, 1024).rearrange("p (t n) -> p t n", t=4); r_pa = RES("pa")
            pn = VB(100352, 1024).rearrange("p (t n) -> p t n", t=4); r_pn = RES("pn")
            pT = VB(102400, 1024).rearrange("p (k n) -> p k n", k=2); r_pT = RES("pT")
            r_y = RES("@y")
            for b in range(NBLK):
                tok0 = b * 512
                tsl = lambda tt: slice(tt * 128, (tt + 1) * 128)
                for tt in range(4):
                    i = tt & 1
                    fw.dma("sp", xa4[:, tt, :], xs[sq, tok0 + tt * 128: tok0 + (tt + 1) * 128, :], writes=[r_xa4[tt]])
                    rms_tile(xa4[:, tt, :], xn4[:, tt, :], st1[i], r_xa4[tt], r_xn4[tt], r_st1[i], 1024)
                    transpose_to(hTb[:, :, tsl(tt)], xn4[:, tt, :], 8, [r_xn4[tt]], r_hTb)
                fw.dma("sp", pa, ps[sq, tok0:tok0 + 512, :].rearrange("(t p) n -> p t n", p=128), writes=[r_pa])
                if dbg:
                    fw.op("pool", lambda e: e.memset(hmT, 0.0), writes=[r_hmT])
                else:
                    fw.dma("sp", hmT, hmT_scr[:, :, tok0:tok0 + 512].rearrange("k p n -> p k n"), reads=[RES("@hm_dummy")], writes=[r_hmT])
                if dbg == 2:
                    fw.op("pool", lambda e: e.memset(oTb, 0.0), writes=[r_oTb])
                else:
                    fw.dma("sp", oTb, oT_scr[:, :, tok0:tok0 + 512].rearrange("k p n -> p k n"), reads=[RES("@o_dummy")], writes=[r_oTb])
                for grp in range(2):
                    wv, rw = wload("w_in", Wb["w_in"], 0, 8, O0 + grp * 512, 512)
                    for cl in range(4):
                        bk, rb = nbank()
                        mm(bk[:, :], [(wv[:, kc, cl * 128:(cl + 1) * 128], hTb[:, kc, :]) for kc in range(8)], [rw, r_hTb], rb)
                        fw.op("act", lambda e, bk=bk, c=grp * 4 + cl: e.activation(sgo[:, c, :], bk[:, :], AF.Sigmoid), reads=[rb], writes=[r_sgo])
                fw.op("dve", lambda e: e.tensor_tensor(sgo, sgo, hmT, ALU.mult), reads=[r_sgo, r_hmT], writes=[r_sgo])
                for grp in range(2):
                    wm, rwm = wload("w_mo", Wb["w_mo"], 0, 8, grp * 512, 512)
                    wa, rwa = wload("w_ao", Wb["w_ao"], 0, 8, grp * 512, 512)
                    wg1, rw1 = wload("w_in", Wb["w_in"], 0, 8, GM0 + grp * 512, 512)
                    wg2, rw2 = wload("w_in", Wb["w_in"], 0, 8, GM0 + 1024 + grp * 512, 512)
                    for cl in range(4):
                        c = grp * 4 + cl
                        cs_ = slice(cl * 128, (cl + 1) * 128)
                        bm, rbm = nbank(); mm(bm[:, :], [(wm[:, kc, cs_], sgo[:, kc, :]) for kc in range(8)], [rwm, r_sgo], rbm)
                        ba, rba = nbank(); mm(ba[:, :], [(wa[:, kc, cs_], oTb[:, kc, :]) for kc in range(8)], [rwa, r_oTb], rba)
                        b1, rb1 = nbank(); mm(b1[:, :], [(wg1[:, kc, cs_], hTb[:, kc, :]) for kc in range(8)], [rw1, r_hTb], rb1)
                        b2, rb2 = nbank(); mm(b2[:, :], [(wg2[:, kc, cs_], hTb[:, kc, :]) for kc in range(8)], [rw2, r_hTb], rb2)
                        fw.op("act", lambda e, b1=b1: e.activation(sg[0], b1[:, :], AF.Sigmoid), reads=[rb1], writes=[r_sg[0]])
                        fw.op("act", lambda e, b2=b2: e.activation(sg[1], b2[:, :], AF.Sigmoid), reads=[rb2], writes=[r_sg[1]])
                        fw.op("dve", lambda e, bm=bm: e.tensor_tensor(t12[0], bm[:, :], sg[0], ALU.mult), reads=[rbm, r_sg[0]], writes=[r_t12[0]])
                        fw.op("dve", lambda e, ba=ba: e.tensor_tensor(t12[1], ba[:, :], sg[1], ALU.mult), reads=[rba, r_sg[1]], writes=[r_t12[1]])
                        fw.op("pool", lambda e, c=c: e.tensor_tensor(mrg[:, c, :], t12[0], t12[1], ALU.add), reads=r_t12, writes=[r_mrg])
                for half in range(2):
                    wo, rwo = wload("w_out", Wb["w_out"], 0, 8, half * 512, 512)
                    for tt in range(4):
                        bk, rb = nbank()
                        mm(bk[:, :], [(mrg[:, kc, tsl(tt)], wo[:, kc, :]) for kc in range(8)], [rwo, r_mrg], rb)
                        xsl = xa4[:, tt, half * 512:(half + 1) * 512]
                        fw.op("dve", lambda e, bk=bk, xsl=xsl: e.tensor_tensor(xsl, xsl, bk[:, :], ALU.add), reads=[rb], writes=[r_xa4[tt]])
                for tt in range(4):
                    i = tt & 1
                    rms_tile(xa4[:, tt, :], xn4[:, tt, :], st1[i], r_xa4[tt], r_xn4[tt], r_st1[i], 1024)
                    transpose_to(hTb[:, :, tsl(tt)], xn4[:, tt, :], 8, [r_xn4[tt]], r_hTb)
                for g in range(6):
                    n_ = 512 if g < 5 else 256
                    wg, rwg = wload("w_gate", Wb["w_gate"], 0, 8, g * 512, n_)
                    wu, rwu = wload("w_up", Wb["w_up"], 0, 8, g * 512, n_)
                    for cl in range(n_ // 128):
                        f = g * 4 + cl
                        cs_ = slice(cl * 128, (cl + 1) * 128)
                        bg_, rbg = nbank(); mm(bg_[:, :], [(wg[:, kc, cs_], hTb[:, kc, :]) for kc in range(8)], [rwg, r_hTb], rbg)
                        bu, rbu = nbank(); mm(bu[:, :], [(wu[:, kc, cs_], hTb[:, kc, :]) for kc in range(8)], [rwu, r_hTb], rbu)
                        i = f & 1
                        fw.op("act", lambda e, bg_=bg_, i=i: e.activation(sg[i], bg_[:, :], AF.Silu), reads=[rbg], writes=[r_sg[i]])
                        fw.op("dve", lambda e, bu=bu, i=i, f=f: e.tensor_tensor(actT[:, f, :], bu[:, :], sg[i], ALU.mult), reads=[rbu, r_sg[i]], writes=[r_actT])
                for half in range(2):
                    parts = []
                    for k0, nk in ((0, 8), (8, 8), (16, 6)):
                        wd, rwd = wload("w_down", Wb["w_down"], k0, nk, half * 512, 512)
                        parts.append((k0, nk, wd, rwd))
                    for tt in range(4):
                        bk, rb = nbank()
                        pairs = []
                        for k0, nk, wd, rwd in parts:
                            pairs += [(actT[:, k0 + kk, tsl(tt)], wd[:, kk, :]) for kk in range(nk)]
                        mm(bk[:, :], pairs, [p_[3] for p_ in parts] + [r_actT], rb)
                        xsl = xa4[:, tt, half * 512:(half + 1) * 512]
                        fw.op("dve", lambda e, bk=bk, xsl=xsl: e.tensor_tensor(xsl, xsl, bk[:, :], ALU.add), reads=[rb], writes=[r_xa4[tt]])
                fw.op("act", lambda e: e.copy(pn, pa), reads=[r_pa], writes=[r_pn])
                for tt in range(4):
                    fw.op("act", lambda e, tt=tt: e.copy(xn4[:, tt, :], xa4[:, tt, :]), reads=[r_xa4[tt]], writes=[r_xn4[tt]])
                    transpose_to(hTb[:, :, tsl(tt)], xn4[:, tt, :], 8, [r_xn4[tt]], r_hTb)
                    transpose_to(pT[:, :, tsl(tt)], pn[:, tt, :], 2, [r_pn], r_pT)
                for half in range(2):
                    wpg, rwpg = wload("w_ple_gate", Wb["w_ple_gate"], 0, 8, half * 512, 512)
                    wp, rwp = wload("w_ple", Wb["w_ple"], 0, 2, half * 512, 512)
                    for tt in range(4):
                        bgt, rbgt = nbank(); mm(bgt[:, :], [(hTb[:, kc, tsl(tt)], wpg[:, kc, :]) for kc in range(8)], [rwpg, r_hTb], rbgt)
                        bp, rbp = nbank(); mm(bp[:, :], [(pT[:, kc, tsl(tt)], wp[:, kc, :]) for kc in range(2)], [rwp, r_pT], rbp)
                        i = tt & 1
                        fw.op("act", lambda e, bgt=bgt, i=i: e.activation(sg[i], bgt[:, :], AF.Sigmoid), reads=[rbgt], writes=[r_sg[i]])
                        fw.op("dve", lambda e, bp=bp, i=i: e.tensor_tensor(t12[i], bp[:, :], sg[i], ALU.mult), reads=[rbp, r_sg[i]], writes=[r_t12[i]])
                        xsl = xa4[:, tt, half * 512:(half + 1) * 512]
                        fw.op("dve", lambda e, xsl=xsl, i=i: e.tensor_tensor(xsl, xsl, t12[i], ALU.add), reads=[r_t12[i]], writes=[r_xa4[tt]])
                for tt in range(4):
                    i = tt & 1
                    stt = st1[i]
                    fw.op("act", lambda e, tt=tt, stt=stt: e.activation(xn4[:, tt, :], xa4[:, tt, :], AF.Square, accum_out=stt[:, 0:1]), reads=[r_xa4[tt]], writes=[r_xn4[tt], r_st1[i]])
                    fw.op("act", lambda e, stt=stt: e.activation(stt[:, 1:2], stt[:, 0:1], AF.Sqrt, bias=EPS, scale=1.0 / 1024), reads=[r_st1[i]], writes=[r_st1[i]])
                    fw.op("dve", lambda e, stt=stt: e.reciprocal(stt[:, 2:3], stt[:, 1:2]), reads=[r_st1[i]], writes=[r_st1[i]])
                    fw.op("dve", lambda e, tt=tt, stt=stt: e.scalar_tensor_tensor(xa4[:, tt, :], xa4[:, tt, :], stt[:, 2:3], gfin[:], ALU.mult, ALU.mult),
                          reads=[r_st1[i], r_const], writes=[r_xa4[tt]])
                    fw.dma("pool", y[sq, tok0 + tt * 128: tok0 + (tt + 1) * 128, :], xa4[:, tt, :], reads=[r_xa4[tt]], writes=[r_y])

        def phase_mlstm(sq, heads=range(4)):
            QT = VB(0, 4096); KT = VB(8192, 4096); Qp = VB(16384, 4096)
            Ktok = VB(24576, 4096).rearrange("p (t n) -> p t n", t=NT)
            Kp = VB(32768, 4096).rearrange("p (t n) -> p t n", t=NT)
            Vaug = VB(40960, NT * 258).rearrange("p (t n) -> p t n", t=NT)
            bbc = VF(57472, 4096)
            tmpE = [VF(73856 + i * 2048, 512) for i in range(2)]
            Cst = VF(77952, 258); Cbf = VB(78992, 258)
            Dt = [VF(79520 + i * 512, 128) for i in range(2)]
            Sp = [VB(80544 + i * 256, 128) for i in range(2)]
            aC = VF(81056, 32)
            hfo = [VF(81184 + i * 1024, 256) for i in range(3)]
            hs = [VF(84256 + i * 1024, 256) for i in range(2)]
            hn = [VB(86304 + i * 512, 256) for i in range(2)]
            hT2 = [VB(87328 + i * 512, 256).rearrange("p (k n) -> p k n", k=2) for i in range(2)]
            r_QT, r_KT, r_Qp, r_Ktok, r_Kp, r_V, r_bbc = (RES(n) for n in ("QT", "KT", "Qp", "Ktok", "Kp", "Vaug", "bbc"))
            r_tmpE = [RES("tmpE%d" % i) for i in range(2)]
            r_C, r_Cbf, r_aC = RES("Cst"), RES("Cbf"), RES("aC")
            r_Dt = [RES("Dt%d" % i) for i in range(2)]; r_Sp = [RES("Sp%d" % i) for i in range(2)]
            r_hfo = [RES("hfo%d" % i) for i in range(3)]; r_hs = [RES("hs%d" % i) for i in range(2)]
            r_hn = [RES("hn%d" % i) for i in range(2)]; r_hT2 = [RES("hT2_%d" % i) for i in range(2)]
            r_hf = [RES("@hf%d" % c) for c in range(NT)]
            r_hm = RES("@hmT")
            for h in heads:
                fw.dma("sp", QT, qk_scr[h], writes=[r_QT])
                fw.dma("sp", KT, qk_scr[4 + h], writes=[r_KT])
                fw.op("pool", lambda e: e.memset(Vaug[:, :, 256:258], 1.0), writes=[r_V])
                fw.dma("sp", Vaug[:, :, 0:256], v_scr[:, h * 256:(h + 1) * 256].rearrange("(t p) n -> p t n", p=128), reads=[r_V], writes=[r_V])
                for c4 in range(4):
                    transpose_to(Ktok[:, c4 * 8:(c4 + 1) * 8, :], KT[:, c4 * 1024:(c4 + 1) * 1024], 8, [r_KT], r_Ktok)
                for d in range(2):
                    msk = maskf if d == 0 else maskb
                    far = 127 if d == 0 else 0
                    fw.dma("sp", bbc, nb_scr[d, h:h + 1, :].partition_broadcast(128).rearrange("p o n -> p (o n)"), writes=[r_bbc])
                    fw.op("act", lambda e, far=far: e.activation(aC, bbc.rearrange("p (t n) -> p t n", t=NT)[:, :, far], AF.Exp), reads=[r_bbc], writes=[r_aC])
                    for blk in range(NBLK):
                        i = blk & 1
                        bs_ = slice(blk * 512, (blk + 1) * 512)
                        fw.op("act", lambda e, i=i, bs_=bs_: e.activation(tmpE[i], bbc[:, bs_], AF.Exp, bias=LN_SQ), reads=[r_bbc], writes=[r_tmpE[i]])
                        fw.op("dve", lambda e, i=i, bs_=bs_: e.tensor_tensor(Qp[:, bs_], QT[:, bs_], tmpE[i], ALU.mult), reads=[r_QT, r_tmpE[i]], writes=[r_Qp])
                    fw.op("pool", lambda e, msk=msk: e.tensor_tensor(bbc.rearrange("p (t n) -> p t n", t=NT), bbc.rearrange("p (t n) -> p t n", t=NT),
                                                                    msk[:].unsqueeze(1).to_broadcast([128, NT, 128]), ALU.add), reads=[r_bbc, r_const], writes=[r_bbc])
                    fw.op("dve", lambda e, d=d, h=h: e.tensor_tensor(Kp, Ktok, colsT[:, 2 + d, :, h].unsqueeze(2).to_broadcast([128, NT, 128]), ALU.mult),
                          reads=[r_Ktok, r_const], writes=[r_Kp])
                    fw.op("dve", lambda e: e.memset(Cst, 0.0), writes=[r_C])
                    fw.op("pool", lambda e: e.memset(Cbf, 0.0), writes=[r_Cbf])
                    order = list(range(NT)) if d == 0 else list(range(NT - 1, -1, -1))
                    for n_, c in enumerate(order):
                        cs_ = slice(c * 128, (c + 1) * 128)
                        j = n_ & 1
                        fw.op("act", lambda e, j=j, cs_=cs_, d=d, c=c, h=h: e.activation(Dt[j], bbc[:, cs_], AF.Exp, bias=colsT[:, d, c, h:h + 1]),
                              reads=[r_bbc, r_const], writes=[r_Dt[j]])
                        bS, rbS = nbank()
                        mm(bS[:, 0:128], [(KT[:, cs_], QT[:, cs_])], [r_KT, r_QT], rbS)
                        fw.op("dve", lambda e, j=j, bS=bS: e.tensor_tensor(Sp[j], bS[:, 0:128], Dt[j], ALU.mult), reads=[rbS, r_Dt[j]], writes=[r_Sp[j]])
                        bN, rbN = nbank()
                        mm(bN[:, 0:257], [(Sp[j], Vaug[:, c, 0:257]), (Qp[:, cs_], Cbf[:, 0:257])], [r_Sp[j], r_V, r_Qp, r_Cbf], rbN)
                        bU, rbU = nbank()
                        mm(bU[:, 0:257], [(Kp[:, c, :], Vaug[:, c, 0:257])], [r_Kp, r_V], rbU)
                        fw.op("dve", lambda e, c=c, bU=bU: e.scalar_tensor_tensor(Cst[:, 0:257], Cst[:, 0:257], aC[:, c:c + 1], bU[:, 0:257], ALU.mult, ALU.add),
                              reads=[r_C, r_aC, rbU], writes=[r_C])
                        fw.op("act", lambda e: e.copy(Cbf[:, 0:257], Cst[:, 0:257]), reads=[r_C], writes=[r_Cbf])
                        k = n_ & 1
                        stt, r_stt = st1[k], r_st1[k]
                        fw.op("act", lambda e, bN=bN, stt=stt: e.activation(stt[:, 0:1], bN[:, 256:257], AF.Abs), reads=[rbN], writes=[r_stt])
                        fw.op("dve", lambda e, stt=stt: e.tensor_scalar_max(stt[:, 0:1], stt[:, 0:1], 1.0), reads=[r_stt], writes=[r_stt])
                        fw.op("dve", lambda e, stt=stt: e.reciprocal(stt[:, 1:2], stt[:, 0:1]), reads=[r_stt], writes=[r_stt])
                        m3 = n_ % 3
                        if d == 0:
                            fw.op("dve", lambda e, m3=m3, bN=bN, stt=stt: e.tensor_scalar(hfo[m3], bN[:, 0:256], stt[:, 1:2], None, ALU.mult), reads=[rbN, r_stt], writes=[r_hfo[m3]])
                            fw.dma("pool", hf_scr[cs_, :], hfo[m3], reads=[r_hfo[m3]], writes=[r_hf[c]])
                        else:
                            fw.dma("sp", hfo[m3], hf_scr[cs_, :], reads=[r_hf[c]], writes=[r_hfo[m3]])
                            fw.op("dve", lambda e, m3=m3, k=k, bN=bN, stt=stt: e.scalar_tensor_tensor(hs[k], bN[:, 0:256], stt[:, 1:2], hfo[m3], ALU.mult, ALU.add),
                                  reads=[rbN, r_stt, r_hfo[m3]], writes=[r_hs[k]])
                            fw.op("act", lambda e, k=k, stt=stt: e.activation(hn[k], hs[k], AF.Square, accum_out=stt[:, 2:3]), reads=[r_hs[k]], writes=[r_hn[k], r_stt])
                            fw.op("act", lambda e, stt=stt: e.activation(stt[:, 3:4], stt[:, 2:3], AF.Sqrt, bias=EPS, scale=1.0 / 256), reads=[r_stt], writes=[r_stt])
                            fw.op("dve", lambda e, stt=stt: e.reciprocal(stt[:, 2:3], stt[:, 3:4]), reads=[r_stt], writes=[r_stt])
                            fw.op("dve", lambda e, k=k, stt=stt: e.tensor_scalar(hn[k], hs[k], stt[:, 2:3], None, ALU.mult), reads=[r_hs[k], r_stt], writes=[r_hn[k]])
                            transpose_to(hT2[k], hn[k], 2, [r_hn[k]], r_hT2[k])
                            fw.dma("pool", hmT_scr[2 * h:2 * h + 2, :, cs_].rearrange("k p n -> p k n"), hT2[k], reads=[r_hT2[k]], writes=[r_hm])

        def phase_mix(sq):
            cqnT = VB(0, 8192).rearrange("p (k n) -> p k n", k=2); r_cqn = RES("cqnT")
            ckvnT = VB(16384, 8192).rearrange("p (k n) -> p k n", k=2); r_ckvn = RES("ckvnT")
            krT = VB(32768, 4096, 128); r_krT = RES("krT")
            KhT = VB(40960, 4096); r_KhT = RES("KhT")
            Vh = VB(49152, 4096); r_Vh = RES("Vh")
            QhT = VB(57344, 4096); r_QhT = RES("QhT")
            QrT = VB(65536, 4096, 128); r_QrT = RES("QrT")
            cstq2 = [VF(73728, 1024, 64).rearrange("p (c n) -> p c n", c=2),
                     wsl[0].bitcast(F32)[0:64, 512:1536].rearrange("p (c n) -> p c n", c=2)]
            r_cstq2 = [RES("cstq"), RES("cstq_b")]
            rt = [VF(77824 + i * 2048, 512, 64) for i in range(2)]; r_rt = RES("rt")
            PT = [VB(81920 + i * 1024, 512) for i in range(3)]; r_PT = [RES("PT%d" % i) for i in range(3)]
            rsum = VF(84992, 512); r_rsum = RES("rsum")
            oTs = [VB(87040 + i * 1024, 512) for i in range(2)]; r_oTs = [RES("oTs%d" % i) for i in range(2)]
            r_oT = RES("@oT")
            QT, KT, Qp = wsl[1][:, :], wsl[2][:, :], wsl[3][:, :]
            Ktok = wsl[4][:, :].rearrange("p (t n) -> p t n", t=NT)
            Kp = wsl[5][:, :].rearrange("p (t n) -> p t n", t=NT)
            M0 = 89088
            Vaug = VB(M0, NT * 258).rearrange("p (t n) -> p t n", t=NT)
            bbc = VF(M0 + 16512, 4096)
            tmpE = [VF(M0 + 32896 + i * 2048, 512) for i in range(2)]
            Cst = VF(M0 + 36992, 258); Cbf = VB(M0 + 38032, 258)
            Dt = [VF(M0 + 38560 + i * 512, 128) for i in range(2)]
            Sp = [VB(M0 + 39584 + i * 256, 128) for i in range(2)]
            aC = VF(M0 + 40096, 32)
            hfo = [VF(M0 + 40224 + i * 1024, 256) for i in range(3)]
            hs = [VF(M0 + 43296 + i * 1024, 256) for i in range(2)]
            hn = [VB(M0 + 45344 + i * 512, 256) for i in range(2)]
            hT2 = [VB(M0 + 46368 + i * 512, 256).rearrange("p (k n) -> p k n", k=2) for i in range(2)]
            stm = [T_once("stm%d" % i, [128, 8], F32) for i in range(2)]; r_stm = [RES("stm%d" % i) for i in range(2)]
            assert M0 + 47392 <= AR * 4
            r_QT, r_KT, r_Qp, r_Ktok, r_Kp, r_V, r_bbc = (RES(n) for n in ("QT", "KT", "Qp", "Ktok", "Kp", "Vaug", "bbc"))
            r_tmpE = [RES("tmpE%d" % i) for i in range(2)]
            r_C, r_Cbf, r_aC = RES("Cst"), RES("Cbf"), RES("aC")
            r_Dt = [RES("Dt%d" % i) for i in range(2)]; r_Sp = [RES("Sp%d" % i) for i in range(2)]
            r_hfo = [RES("hfo%d" % i) for i in range(3)]; r_hs = [RES("hs%d" % i) for i in range(2)]
            r_hn = [RES("hn%d" % i) for i in range(2)]; r_hT2 = [RES("hT2_%d" % i) for i in range(2)]
            r_hf = [RES("@hf%d" % c) for c in range(NT)]
            r_hm = RES("@hmT")
            mst = {"b": 0}

            def mbank():
                i = 5 + mst["b"]
                mst["b"] = (mst["b"] + 1) % 3
                return banks[i], r_bank[i]

            def m_transpose(dst3, src_ap, nk, reads, r_dst):
                bk, rb = mbank()
                pbt = bk.bitcast(BF16)
                for k in range(nk):
                    transp(pbt[:, k * 128:(k + 1) * 128], src_ap[:, k * 128:(k + 1) * 128], identb[:], reads + [r_const], rb, signal=(k == nk - 1))
                copy_op(evac_eng(), dst3, pbt[:, 0:nk * 128].rearrange("p (k n) -> p k n", k=nk), [rb], [r_dst])

            b3 = lambda: bbc.rearrange("p (t n) -> p t n", t=NT)

            def mlstm_gen():
                for h in range(4):
                    fw.dma("sp", QT, qk_scr[h], writes=[r_QT])
                    fw.dma("sp", KT, qk_scr[4 + h], writes=[r_KT])
                    fw.op("pool", lambda e: e.memset(Vaug[:, :, 256:258], 1.0), writes=[r_V])
                    fw.dma("sp", Vaug[:, :, 0:256], v_scr[:, h * 256:(h + 1) * 256].rearrange("(t p) n -> p t n", p=128), reads=[r_V], writes=[r_V])
                    yield
                    for c8 in range(8):
                        m_transpose(Ktok[:, c8 * 4:(c8 + 1) * 4, :], KT[:, c8 * 512:(c8 + 1) * 512], 4, [r_KT], r_Ktok)
                        yield
                    for d in range(2):
                        msk = maskf if d == 0 else maskb
                        far = 127 if d == 0 else 0
                        fw.dma("sp", bbc, nb_scr[d, h:h + 1, :].partition_broadcast(128).rearrange("p o n -> p (o n)"), writes=[r_bbc])
                        yield
                        fw.op("act", lambda e, far=far: e.activation(aC, bbc.rearrange("p (t n) -> p t n", t=NT)[:, :, far], AF.Exp), reads=[r_bbc], writes=[r_aC])
                        for blk in range(NBLK):
                            i = blk & 1
                            bs_ = slice(blk * 512, (blk + 1) * 512)
                            fw.op("act", lambda e, i=i, bs_=bs_: e.activation(tmpE[i], bbc[:, bs_], AF.Exp, bias=LN_SQ), reads=[r_bbc], writes=[r_tmpE[i]])
                            yield
                            fw.op("dve", lambda e, i=i, bs_=bs_: e.tensor_tensor(Qp[:, bs_], QT[:, bs_], tmpE[i], ALU.mult), reads=[r_QT, r_tmpE[i]], writes=[r_Qp])
                            yield
                        for q4 in range(4):
                            ts_ = slice(q4 * 8, (q4 + 1) * 8)
                            fw.op("pool", lambda e, msk=msk, ts_=ts_: e.tensor_tensor(b3()[:, ts_, :], b3()[:, ts_, :], msk[:].unsqueeze(1).to_broadcast([128, 8, 128]), ALU.add),
                                  reads=[r_bbc, r_const], writes=[r_bbc])
                            fw.op("dve", lambda e, d=d, h=h, ts_=ts_: e.tensor_tensor(Kp[:, ts_, :], Ktok[:, ts_, :], colsT[:, 2 + d, ts_, h].unsqueeze(2).to_broadcast([128, 8, 128]), ALU.mult),
                                  reads=[r_Ktok, r_const], writes=[r_Kp])
                            yield
                        fw.op("dve", lambda e: e.memset(Cst, 0.0), writes=[r_C])
                        fw.op("pool", lambda e: e.memset(Cbf, 0.0), writes=[r_Cbf])
                        yield
                        order = list(range(NT)) if d == 0 else list(range(NT - 1, -1, -1))
                        for n_, c in enumerate(order):
                            cs_ = slice(c * 128, (c + 1) * 128)
                            j = n_ & 1
                            k = n_ & 1
                            m3 = n_ % 3
                            stt, r_stt = stm[k], r_stm[k]
                            fw.op("act", lambda e, j=j, cs_=cs_, d=d, c=c, h=h: e.activation(Dt[j], bbc[:, cs_], AF.Exp, bias=colsT[:, d, c, h:h + 1]),
                                  reads=[r_bbc, r_const], writes=[r_Dt[j]])
                            bS, rbS = mbank()
                            mm(bS[:, 0:128], [(KT[:, cs_], QT[:, cs_])], [r_KT, r_QT], rbS)
                            if d == 1:
                                fw.dma("sp", hfo[m3], hf_scr[cs_, :], reads=[r_hf[c]], writes=[r_hfo[m3]])
                            yield
                            fw.op("dve", lambda e, j=j, bS=bS: e.tensor_tensor(Sp[j], bS[:, 0:128], Dt[j], ALU.mult), reads=[rbS, r_Dt[j]], writes=[r_Sp[j]])
                            yield
                            bN, rbN = mbank()
                            mm(bN[:, 0:257], [(Sp[j], Vaug[:, c, 0:257]), (Qp[:, cs_], Cbf[:, 0:257])], [r_Sp[j], r_V, r_Qp, r_Cbf], rbN)
                            bU, rbU = mbank()
                            mm(bU[:, 0:257], [(Kp[:, c, :], Vaug[:, c, 0:257])], [r_Kp, r_V], rbU)
                            yield
                            fw.op("dve", lambda e, c=c, bU=bU: e.scalar_tensor_tensor(Cst[:, 0:257], Cst[:, 0:257], aC[:, c:c + 1], bU[:, 0:257], ALU.mult, ALU.add),
                                  reads=[r_C, r_aC, rbU], writes=[r_C])
                            fw.op("act", lambda e, bN=bN, stt=stt: e.activation(stt[:, 0:1], bN[:, 256:257], AF.Abs), reads=[rbN], writes=[r_stt])
                            yield
                            fw.op("pool", lambda e: e.tensor_copy(Cbf[:, 0:257], Cst[:, 0:257]), reads=[r_C], writes=[r_Cbf])
                            fw.op("dve", lambda e, stt=stt: e.tensor_scalar_max(stt[:, 0:1], stt[:, 0:1], 1.0), reads=[r_stt], writes=[r_stt])
                            fw.op("dve", lambda e, stt=stt: e.reciprocal(stt[:, 1:2], stt[:, 0:1]), reads=[r_stt], writes=[r_stt])
                            yield
                            if d == 0:
                                fw.op("dve", lambda e, m3=m3, bN=bN, stt=stt: e.tensor_scalar(hfo[m3], bN[:, 0:256], stt[:, 1:2], None, ALU.mult), reads=[rbN, r_stt], writes=[r_hfo[m3]])
                                fw.dma("pool", hf_scr[cs_, :], hfo[m3], reads=[r_hfo[m3]], writes=[r_hf[c]])
                                yield
                            else:
                                fw.op("dve", lambda e, m3=m3, k=k, bN=bN, stt=stt: e.scalar_tensor_tensor(hs[k], bN[:, 0:256], stt[:, 1:2], hfo[m3], ALU.mult, ALU.add),
                                      reads=[rbN, r_stt, r_hfo[m3]], writes=[r_hs[k]])
                                yield
                                fw.op("act", lambda e, k=k, stt=stt: e.activation(hn[k], hs[k], AF.Square, accum_out=stt[:, 2:3]), reads=[r_hs[k]], writes=[r_hn[k], r_stt])
                                yield
                                fw.op("act", lambda e, stt=stt: e.activation(stt[:, 3:4], stt[:, 2:3], AF.Sqrt, bias=EPS, scale=1.0 / 256), reads=[r_stt], writes=[r_stt])
                                yield
                                fw.op("dve", lambda e, stt=stt: e.reciprocal(stt[:, 2:3], stt[:, 3:4]), reads=[r_stt], writes=[r_stt])
                                yield
                                fw.op("dve", lambda e, k=k, stt=stt: e.tensor_scalar(hn[k], hs[k], stt[:, 2:3], None, ALU.mult), reads=[r_hs[k], r_stt], writes=[r_hn[k]])
                                yield
                                m_transpose(hT2[k], hn[k], 2, [r_hn[k]], r_hT2[k])
                                fw.dma("pool", hmT_scr[2 * h:2 * h + 2, :, cs_].rearrange("k p n -> p k n"), hT2[k], reads=[r_hT2[k]], writes=[r_hm])
                                yield

            gen = mlstm_gen()
            gst = {"done": False, "n": 0, "calls": 0}

            def advance():
                gst["calls"] += 1
                for _ in range(2 if gst["calls"] % 8 == 0 else 1):
                    if not gst["done"]:
                        try:
                            next(gen)
                            gst["n"] += 1
                        except StopIteration:
                            gst["done"] = True

            def wload0(name, W_ap, k0, nk, c0, ncols, off):
                view = wsl[0][:, off:off + nk * ncols].rearrange("p (k n) -> p k n", k=nk)
                src = W_ap[k0 * 128:(k0 + nk) * 128, c0:c0 + ncols].rearrange("(k p) n -> p k n", p=128)
                fw.dma("sp", view, src, reads=[r_W[name]], writes=[r_wsl[0]])
                return view, r_wsl[0]

            fw.dma("sp", cqnT, cqnT_scr.rearrange("k p n -> p k n"), writes=[r_cqn])
            fw.dma("sp", ckvnT, ckvnT_scr.rearrange("k p n -> p k n"), writes=[r_ckvn])
            fw.op("pool", lambda e: e.memset(krT[64:128, :], 0.0), writes=[r_krT])
            fw.op("pool", lambda e: e.memset(QrT[64:128, :], 0.0), writes=[r_QrT])
            fw.dma("sp", krT[0:64, :], krT_scr, reads=[r_krT], writes=[r_krT])
            abank = {"i": 0}

            def xbank():
                i = abank["i"]
                abank["i"] = (i + 1) % 5
                return banks[i], r_bank[i]

            for h in range(8):
                wk, rwk = wload0("w_ukv", Wb["w_ukv"], 0, 2, h * 256, 256, 0)
                wq, rwq = wload0("w_uq", Wb["w_uq"], 0, 2, h * 192, 192, 512)
                wqs, rwqs = wload0("w_uq_sw", w_uq_sw, 0, 2, h * 64, 64, 896)
                for b in range(NBLK):
                    bs_ = slice(b * 512, (b + 1) * 512)
                    bk, rb = xbank()
                    mm(bk[:, :], [(wk[:, kc, 0:128], ckvnT[:, kc, bs_]) for kc in range(2)], [rwk, r_ckvn], rb)
                    copy_op(evac_eng(), KhT[:, bs_], bk[:, :], [rb], [r_KhT])
                    bk, rb = xbank()
                    mm(bk[:, :], [(wq[:, kc, 0:128], cqnT[:, kc, bs_]) for kc in range(2)], [rwq, r_cqn], rb)
                    copy_op(evac_eng(), QhT[:, bs_], bk[:, :], [rb], [r_QhT])
                    bka, rba = xbank()
                    mm(bka[0:64, :], [(wq[:, kc, 128:192], cqnT[:, kc, bs_]) for kc in range(2)], [rwq, r_cqn], rba)
                    bkb, rbb = xbank()
                    mm(bkb[0:64, :], [(wqs[:, kc, :], cqnT[:, kc, bs_]) for kc in range(2)], [rwqs, r_cqn], rbb)
                    cq_, r_cq_ = cstq2[b & 1], r_cstq2[b & 1]
                    if b == 0:
                        fw.dma("sp", cq_, cs_scr[:, :, bs_].rearrange("c p n -> p c n"), writes=[r_cq_])
                    if b + 1 < NBLK:
                        nb_ = slice((b + 1) * 512, (b + 2) * 512)
                        fw.dma("sp", cstq2[(b + 1) & 1], cs_scr[:, :, nb_].rearrange("c p n -> p c n"), writes=[r_cstq2[(b + 1) & 1]])
                    fw.op("dve", lambda e, bka=bka, cq_=cq_: e.tensor_tensor(rt[0], bka[0:64, :], cq_[:, 0, :], ALU.mult), reads=[rba, r_cq_], writes=[r_rt])
                    fw.op("dve", lambda e, bkb=bkb, cq_=cq_: e.tensor_tensor(rt[1], bkb[0:64, :], cq_[:, 1, :], ALU.mult), reads=[rbb, r_cq_], writes=[r_rt])
                    fw.op("pool", lambda e, bs_=bs_: e.tensor_tensor(QrT[0:64, bs_], rt[0], rt[1], ALU.add), reads=[r_rt], writes=[r_QrT])
                    bk, rb = xbank()
                    for tt in range(4):
                        ts_ = slice(b * 512 + tt * 128, b * 512 + (tt + 1) * 128)
                        mm(bk[:, tt * 128:(tt + 1) * 128], [(ckvnT[:, kc, ts_], wk[:, kc, 128:256]) for kc in range(2)], [rwk, r_ckvn], rb)
                    copy_op(evac_eng(), Vh[:, bs_], bk[:, :], [rb], [r_Vh])
                    advance()
                for qb in range(NBLK):
                    qs_ = slice(qb * 512, (qb + 1) * 512)
                    bo, rbo = banks[3], r_bank[3]
                    bsm, rbsm = banks[4], r_bank[4]

                    def S_step(kc, qs_=qs_):
                        ks_ = slice(kc * 128, (kc + 1) * 128)
                        bS, rbS = banks[kc % 3], r_bank[kc % 3]
                        mm(bS[:, :], [(KhT[:, ks_], QhT[:, qs_]), (krT[:, ks_], QrT[:, qs_])], [r_KhT, r_QhT, r_krT, r_QrT], rbS)
                        fw.op("act", lambda e, bS=bS, kc=kc: e.activation(PT[kc % 3], bS[:, :], AF.Exp, scale=A_SCALE), reads=[rbS], writes=[r_PT[kc % 3]])

                    def O_step(kc, bo=bo, rbo=rbo, bsm=bsm, rbsm=rbsm):
                        ks_ = slice(kc * 128, (kc + 1) * 128)
                        fw.op("pe", lambda e, kc=kc, ks_=ks_, bo=bo: e.matmul(bo[:, :], Vh[:, ks_], PT[kc % 3], start=(kc == 0), stop=(kc == NT - 1)),
                              reads=[r_Vh, r_PT[kc % 3]], writes=[rbo], signal=False)
                        fw.op("pe", lambda e, kc=kc, bsm=bsm: e.matmul(bsm[:, :], onesb[:], PT[kc % 3], start=(kc == 0), stop=(kc == NT - 1)),
                              reads=[r_const, r_PT[kc % 3]], writes=[rbsm], signal=True)
                    S_step(0)
                    S_step(1)
                    for kc in range(NT):
                        if kc + 2 < NT:
                            S_step(kc + 2)
                        O_step(kc)
                        advance()
                    fw.op("dve", lambda e, bsm=bsm: e.reciprocal(rsum, bsm[:, :]), reads=[rbsm], writes=[r_rsum])
                    o_ = oTs[qb & 1]
                    fw.op("dve", lambda e, bo=bo, o_=o_: e.tensor_tensor(o_, bo[:, :], rsum, ALU.mult), reads=[rbo, r_rsum], writes=[r_oTs[qb & 1]])
                    fw.dma("pool", oT_scr[h, :, qs_], o_, reads=[r_oTs[qb & 1]], writes=[r_oT])
            n_inside = gst["n"]
            while not gst["done"]:
                advance()
            mix_stats.append((n_inside, gst["n"]))

        def phase_attn(sq, heads=range(8), main_loop=True, qbs=None):
            cqnT = VB(0, 8192).rearrange("p (k n) -> p k n", k=2); r_cqn = RES("cqnT")
            ckvnT = VB(16384, 8192).rearrange("p (k n) -> p k n", k=2); r_ckvn = RES("ckvnT")
            krT = VB(32768, 4096, 128); r_krT = RES("krT")
            KhT = VB(40960, 4096); r_KhT = RES("KhT")
            Vh = VB(49152, 4096); r_Vh = RES("Vh")
            QhT = VB(57344, 4096); r_QhT = RES("QhT")
            QrT = VB(65536, 4096, 128); r_QrT = RES("QrT")
            cstq = VF(73728, 1024, 64).rearrange("p (c n) -> p c n", c=2); r_cstq = RES("cstq")
            rt = [VF(77824 + i * 2048, 512, 64) for i in range(2)]; r_rt = RES("rt")
            PT = [VB(81920 + i * 1024, 512) for i in range(3)]; r_PT = [RES("PT%d" % i) for i in range(3)]
            rsum = VF(84992, 512); r_rsum = RES("rsum")
            oTs = [VB(87040 + i * 1024, 512) for i in range(2)]; r_oTs = [RES("oTs%d" % i) for i in range(2)]
            r_oT = RES("@oT")
            fw.dma("sp", cqnT, cqnT_scr.rearrange("k p n -> p k n"), writes=[r_cqn])
            fw.dma("sp", ckvnT, ckvnT_scr.rearrange("k p n -> p k n"), writes=[r_ckvn])
            fw.op("pool", lambda e: e.memset(krT[64:128, :], 0.0), writes=[r_krT])
            fw.op("pool", lambda e: e.memset(QrT[64:128, :], 0.0), writes=[r_QrT])
            fw.dma("sp", krT[0:64, :], krT_scr, reads=[r_krT], writes=[r_krT])
            for h in heads:
                wk, rwk = wload("w_ukv", Wb["w_ukv"], 0, 2, h * 256, 256)
                wq, rwq = wload("w_uq", Wb["w_uq"], 0, 2, h * 192, 192)
                wqs, rwqs = wload("w_uq_sw", w_uq_sw, 0, 2, h * 64, 64)
                for b in range(NBLK):
                    bs_ = slice(b * 512, (b + 1) * 512)
                    bk, rb = nbank()
                    mm(bk[:, :], [(wk[:, kc, 0:128], ckvnT[:, kc, bs_]) for kc in range(2)], [rwk, r_ckvn], rb)
                    copy_op(evac_eng(), KhT[:, bs_], bk[:, :], [rb], [r_KhT])
                    bk, rb = nbank()
                    mm(bk[:, :], [(wq[:, kc, 0:128], cqnT[:, kc, bs_]) for kc in range(2)], [rwq, r_cqn], rb)
                    copy_op(evac_eng(), QhT[:, bs_], bk[:, :], [rb], [r_QhT])
                    bka, rba = nbank()
                    mm(bka[0:64, :], [(wq[:, kc, 128:192], cqnT[:, kc, bs_]) for kc in range(2)], [rwq, r_cqn], rba)
                    bkb, rbb = nbank()
                    mm(bkb[0:64, :], [(wqs[:, kc, :], cqnT[:, kc, bs_]) for kc in range(2)], [rwqs, r_cqn], rbb)
                    fw.dma("sp", cstq, cs_scr[:, :, bs_].rearrange("c p n -> p c n"), writes=[r_cstq])
                    fw.op("dve", lambda e, bka=bka: e.tensor_tensor(rt[0], bka[0:64, :], cstq[:, 0, :], ALU.mult), reads=[rba, r_cstq], writes=[r_rt])
                    fw.op("dve", lambda e, bkb=bkb: e.tensor_tensor(rt[1], bkb[0:64, :], cstq[:, 1, :], ALU.mult), reads=[rbb, r_cstq], writes=[r_rt])
                    fw.op("pool", lambda e, bs_=bs_: e.tensor_tensor(QrT[0:64, bs_], rt[0], rt[1], ALU.add), reads=[r_rt], writes=[r_QrT])
                    bk, rb = nbank()
                    for tt in range(4):
                        ts_ = slice(b * 512 + tt * 128, b * 512 + (tt + 1) * 128)
                        mm(bk[:, tt * 128:(tt + 1) * 128], [(ckvnT[:, kc, ts_], wk[:, kc, 128:256]) for kc in range(2)], [rwk, r_ckvn], rb)
                    copy_op(evac_eng(), Vh[:, bs_], bk[:, :], [rb], [r_Vh])
                for qb in ((qbs if qbs is not None else range(NBLK)) if main_loop else ()):
                    qs_ = slice(qb * 512, (qb + 1) * 512)
                    bo, rbo = banks[4 + (qb & 1)], r_bank[4 + (qb & 1)]
                    bsm, rbsm = banks[6 + (qb & 1)], r_bank[6 + (qb & 1)]

                    def S_step(kc):
                        ks_ = slice(kc * 128, (kc + 1) * 128)
                        bS, rbS = banks[kc % 4], r_bank[kc % 4]
                        mm(bS[:, :], [(KhT[:, ks_], QhT[:, qs_]), (krT[:, ks_], QrT[:, qs_])], [r_KhT, r_QhT, r_krT, r_QrT], rbS)
                        fw.op("act", lambda e, bS=bS, kc=kc: e.activation(PT[kc % 3], bS[:, :], AF.Exp, scale=A_SCALE), reads=[rbS], writes=[r_PT[kc % 3]])

                    def O_step(kc):
                        ks_ = slice(kc * 128, (kc + 1) * 128)
                        fw.op("pe", lambda e, kc=kc, ks_=ks_, bo=bo: e.matmul(bo[:, :], Vh[:, ks_], PT[kc % 3], start=(kc == 0), stop=(kc == NT - 1)),
                              reads=[r_Vh, r_PT[kc % 3]], writes=[rbo], signal=False)
                        fw.op("pe", lambda e, kc=kc, bsm=bsm: e.matmul(bsm[:, :], onesb[:], PT[kc % 3], start=(kc == 0), stop=(kc == NT - 1)),
                              reads=[r_const, r_PT[kc % 3]], writes=[rbsm], signal=True)
                    S_step(0)
                    S_step(1)
                    for kc in range(NT):
                        if kc + 2 < NT:
                            S_step(kc + 2)
                        O_step(kc)
                    fw.op("dve", lambda e, bsm=bsm: e.reciprocal(rsum, bsm[:, :]), reads=[rbsm], writes=[r_rsum])
                    o_ = oTs[qb & 1]
                    fw.op("dve", lambda e, bo=bo, o_=o_: e.tensor_tensor(o_, bo[:, :], rsum, ALU.mult), reads=[rbo, r_rsum], writes=[r_oTs[qb & 1]])
                    fw.dma("pool", oT_scr[h, :, qs_], o_, reads=[r_oTs[qb & 1]], writes=[r_oT])
            if dbg == "M1":
                fw.dma("pool", dbg_out["d_oT"], oTs[0], reads=[r_oTs[0]], writes=[RES("@dbgM1")])
            if isinstance(dbg, str) and dbg.startswith("M:"):
                fw.dma("pool", dbg_out["d_oTs0"], oTs[0], reads=[r_oTs[0]], writes=[RES("@dbgMa")])
                fw.dma("pool", dbg_out["d_oTs1"], oTs[1], reads=[r_oTs[1]], writes=[RES("@dbgMb")])
            if dbg == "E":
                rd = RES("@dbgE")
                for n_, ap_, r_ in (("d_cqnT", cqnT, r_cqn), ("d_ckvnT", ckvnT, r_ckvn), ("d_krT", krT[0:64, :], r_krT),
                                    ("d_KhT", KhT, r_KhT), ("d_QhT", QhT, r_QhT), ("d_QrT", QrT[0:64, :], r_QrT), ("d_Vh", Vh, r_Vh)):
                    fw.dma("pool", dbg_out[n_], ap_, reads=[r_], writes=[rd])

        for sq in range(NSEQ):
            if 1 in phases:
                phase1(sq)
                fw.barrier()
            if 5 in phases:
                phase_mix(sq)
                fw.barrier()
            if 2 in phases:
                phase_mlstm(sq, heads=([0] if dbg == "L1" else range(4)))
                fw.barrier()
            if 3 in phases:
                if dbg == "E":
                    phase_attn(sq, heads=[0], main_loop=False)
                elif dbg == "M1":
                    phase_attn(sq, heads=[0], main_loop=True, qbs=[0])
                elif isinstance(dbg, str) and dbg.startswith("M:"):
                    _, nh_, nq_ = dbg.split(":")
                    phase_attn(sq, heads=list(range(int(nh_))), main_loop=True, qbs=list(range(int(nq_))))
                else:
                    phase_attn(sq)
                fw.barrier()
            if 4 in phases:
                phase3(sq)
                fw.barrier()
        fw.barrier()
        fw.emit_all()
    if mix_stats:
        print("[phase_mix] scan generator steps issued inside attention / total, per sequence:", mix_stats)
    return nc


NSEQ_PER_CORE = 3


def _consts():
    ident = np.eye(128, dtype=np.float32)
    s = np.arange(128)[:, None]
    t = np.arange(128)[None, :]
    maskf = np.where(t >= s, 0.0, -1000.0).astype(np.float32)
    maskb = np.where(t <= s, 0.0, -1000.0).astype(np.float32)
    return {"c_ident": ident, "c_maskf": maskf, "c_maskb": maskb}


def kernel(**inputs):
    X = np.concatenate([inputs["x_prompt"], inputs["x_sample"]], axis=0)
    P = np.concatenate([inputs["p_prompt"][0], inputs["p_sample"][0]], axis=0)
    nseq = X.shape[0]
    base = {}
    for k in WSHAPES:
        base[k] = np.ascontiguousarray(inputs[k][0])
    for k in GSHAPES:
        base[k] = np.ascontiguousarray(inputs[k].reshape(1, -1))
    base["g_final"] = np.ascontiguousarray(inputs["g_final"].reshape(1, -1))
    base["b_gate"] = np.ascontiguousarray(inputs["b_gate"].reshape(1, 16))
    base["conv_w"] = np.ascontiguousarray(inputs["conv_w"][0])
    base.update(_consts())
    slots = [[c, c + 8, c + 16 if c + 16 < nseq else c] for c in range(8)]
    maps = []
    for sl in slots:
        m = dict(base)
        m["xs"] = np.ascontiguousarray(X[sl])
        m["ps"] = np.ascontiguousarray(P[sl])
        maps.append(m)
    nc = build(NSEQ_PER_CORE, dbg=False, phases=(1, 5, 4))
    res = run_bass_kernel_spmd(nc, maps, core_ids=list(range(8)))
    Y = np.zeros((nseq, S, D), dtype=np.float32)
    for c, sl in enumerate(slots):
        yc = res.results[c]["y"]
        for j, sidx in enumerate(sl):
            if j == 2 and c + 16 >= nseq:
                continue
            Y[sidx] = yc[j]
    nb = inputs["x_prompt"].shape[0]
    return (np.ascontiguousarray(Y[:nb], dtype=np.float32), np.ascontiguousarray(Y[nb:], dtype=np.float32))
```

```python
import numpy as np
import concourse.bass as bass
import concourse.mybir as mybir
from concourse.bass_utils import run_bass_kernel_spmd

F32 = mybir.dt.float32
BF16 = mybir.dt.bfloat16
I32 = mybir.dt.int32
AF = mybir.ActivationFunctionType
ALU = mybir.AluOpType
AX = mybir.AxisListType


class Sem:
    def __init__(self, handle, name, dma=False, owner=None):
        self.h = handle
        self.name = name
        self.dma = dma
        self.owner = owner
        self.issued = 0


class Res:
    __slots__ = ("name", "w", "r", "dsem")

    def __init__(self, name):
        self.name = name
        self.w = None
        self.r = {}
        self.dsem = None


class Eng:
    def __init__(self, name, sem, is_pe=False):
        self.name = name
        self.sem = sem
        self.ops = []
        self.seen = {}
        self.is_pe = is_pe
        self.pending = False


class FW:
    def __init__(self, nc, stack):
        self.nc = nc
        self.stack = stack
        self.engs = {}
        self.nsem = 0
        self.all_sems = []
        for n in ("pe", "act", "dve", "pool", "sp"):
            self.engs[n] = Eng(n, None, is_pe=(n == "pe"))
            self.engs[n].sem = self._newsem("s_" + n, owner=self.engs[n])
        self.dma_res = []
        self.n_ins = 0

    SEM_LIMIT = 6000

    def _newsem(self, name, dma=False, owner=None):
        self.nsem += 1
        h = self.stack.enter_context(self.nc.semaphore("%s_%d" % (name, self.nsem)))
        sm = Sem(h, name, dma=dma, owner=owner)
        self.all_sems.append(sm)
        return sm

    def dsem(self, res, qclass):
        if res.dsem is None:
            res.dsem = {}
            self.dma_res.append(res)
        sm = res.dsem.get(qclass)
        if sm is None or sm.issued >= self.SEM_LIMIT:
            sm = res.dsem[qclass] = self._newsem("d%s_%s" % (qclass[0], res.name.replace("@", "h_")), dma=True)
        return sm

    def _collect(self, E, reads, writes):
        waits = {}

        def need(ev):
            if ev is None:
                return
            sem, val = ev
            if sem.owner is E:
                if E.is_pe:
                    return
            else:
                if sem.dma and sem.issued > val:
                    val = sem.issued
            if E.seen.get(sem, 0) >= val:
                return
            if waits.get(sem, 0) < val:
                waits[sem] = val

        for r in reads:
            need(r.w)
        for w in writes:
            need(w.w)
            for s, v in w.r.items():
                need((s, v))
        for s, v in waits.items():
            E.seen[s] = v
        return list(waits.items())

    def _commit(self, ev, reads, writes):
        for w in writes:
            w.w = ev
            w.r = {}
        for r in reads:
            if r in writes:
                continue
            s, v = ev
            if r.r.get(s, 0) < v:
                r.r[s] = v

    def op(self, eng, fn, reads=(), writes=(), signal=True):
        E = self.engs[eng]
        waits = self._collect(E, reads, writes)
        if signal:
            E.sem.issued += 1
        val = E.sem.issued if signal else E.sem.issued + 1
        sem_h = E.sem.h

        def emit(e, waits=waits, fn=fn, signal=signal):
            for s, v in waits:
                e.wait_ge(s.h, v)
            ins = fn(e)
            if signal:
                ins.then_inc(sem_h, 1)
        E.ops.append(emit)
        E.pending = not signal
        self._commit((E.sem, val), reads, writes)
        self.n_ins += 1
        if signal and E.sem.issued >= self.SEM_LIMIT:
            E.sem = self._newsem("s_" + E.name, owner=E)

    def dma(self, eng, out, in_, reads=(), writes=(), sem_res=None, **kw):
        E = self.engs[eng]
        waits = self._collect(E, reads, writes)
        if sem_res is None:
            sem_res = writes[0] if (writes and writes[0].name[0] != "@") else reads[0]
        ds = self.dsem(sem_res, "sw" if eng == "pool" else "hw")
        ds.issued += 16
        val = ds.issued

        def emit(e, waits=waits):
            for s, v in waits:
                e.wait_ge(s.h, v)
            e.dma_start(out=out, in_=in_, **kw).then_inc(ds.h, 16)
        E.ops.append(emit)
        self._commit((ds, val), reads, writes)
        self.n_ins += 1

    def barrier(self):
        for E in self.engs.values():
            assert not E.pending
        evs = [(sm, sm.issued) for sm in self.all_sems if sm.issued > 0]
        for E in self.engs.values():
            waits = []
            for s, v in evs:
                if s.owner is E and E.is_pe:
                    continue
                if E.seen.get(s, 0) < v:
                    waits.append((s, v))
                    E.seen[s] = v

            def emit(e, waits=waits):
                for s, v in waits:
                    e.wait_ge(s.h, v)
            E.ops.append(emit)

    def wait_all(self, eng, resources):
        E = self.engs[eng]
        waits = self._collect(E, resources, [])

        def emit(e, waits=waits):
            for s, v in waits:
                e.wait_ge(s.h, v)
        E.ops.append(emit)

    def emit_all(self):
        nc = self.nc
        with nc.Block() as block:
            @block.tensor
            def _(e):
                for f in self.engs["pe"].ops:
                    f(e)

            @block.scalar
            def _(e):
                for f in self.engs["act"].ops:
                    f(e)

            @block.vector
            def _(e):
                for f in self.engs["dve"].ops:
                    f(e)

            @block.gpsimd
            def _(e):
                for f in self.engs["pool"].ops:
                    f(e)

            @block.sync
            def _(e):
                for f in self.engs["sp"].ops:
                    f(e)


import math
from contextlib import ExitStack

S = 4096
D = 1024
NBLK = 8
NT = 32
DFF = 2816
V0, O0, G0, CQ0, CKV0, KR0, GM0, INW = 1024, 2048, 3072, 3088, 3344, 3600, 3664, 5712
EPS = 1e-6
LN_SQ = math.log(128 ** -0.5)
A_SCALE = 192 ** -0.5

WSHAPES = {
    "w_in": (1024, 5712), "w_mo": (1024, 1024), "w_uq": (256, 1536), "w_ukv": (256, 2048),
    "w_ao": (1024, 1024), "w_out": (1024, 1024), "w_gate": (1024, 2816), "w_up": (1024, 2816),
    "w_down": (2816, 1024), "w_ple": (256, 1024), "w_ple_gate": (1024, 1024),
}
GAINS = {"w_in": "g_mix", "w_mo": "g_mhead", "w_uq": "g_q", "w_ukv": "g_kv", "w_gate": "g_ffn", "w_up": "g_ffn"}
GSHAPES = {"g_mix": 1024, "g_mhead": 1024, "g_q": 256, "g_kv": 256, "g_ffn": 1024}


def build(NSEQ, dbg=False, phases=(1, 2, 3, 4)):
    nc = bass.Bass("TRN2", target_bir_lowering=False)
    din = lambda n, sh, dt=F32: nc.dram_tensor(n, list(sh), dt, kind="ExternalInput").ap()
    dscr = lambda n, sh, dt: nc.dram_tensor(n, list(sh), dt, kind="Internal").ap()
    xs = din("xs", [NSEQ, S, D])
    ps = din("ps", [NSEQ, S, 256])
    Wf = {k: din(k, sh) for k, sh in WSHAPES.items()}
    Gd = {k: din(k, [1, n]) for k, n in GSHAPES.items()}
    g_final = din("g_final", [1, D])
    b_gate = din("b_gate", [1, 16])
    conv_w = din("conv_w", [5, 1024])
    c_ident = din("c_ident", [128, 128])
    c_maskf = din("c_maskf", [128, 128])
    c_maskb = din("c_maskb", [128, 128])
    y = nc.dram_tensor("y", [NSEQ, S, D], F32, kind="ExternalOutput").ap()
    dbg_out = {}
    if dbg == "P1":
        dbg_out["d_colsT"] = nc.dram_tensor("d_colsT", [128, 512], F32, kind="ExternalOutput").ap()
    if dbg == "M1":
        dbg_out["d_oT"] = nc.dram_tensor("d_oT", [128, 512], BF16, kind="ExternalOutput").ap()
    if isinstance(dbg, str) and dbg.startswith("M:"):
        dbg_out["d_oTs0"] = nc.dram_tensor("d_oTs0", [128, 512], BF16, kind="ExternalOutput").ap()
        dbg_out["d_oTs1"] = nc.dram_tensor("d_oTs1", [128, 512], BF16, kind="ExternalOutput").ap()
    if dbg == "E":
        for n_, sh_ in (("d_cqnT", [128, 2, S]), ("d_ckvnT", [128, 2, S]), ("d_krT", [64, S]),
                        ("d_KhT", [128, S]), ("d_QhT", [128, S]), ("d_QrT", [64, S]), ("d_Vh", [128, S])):
            dbg_out[n_] = nc.dram_tensor(n_, sh_, BF16, kind="ExternalOutput").ap()
    Wb = {k: dscr(k + "_b", sh, BF16) for k, sh in WSHAPES.items()}
    w_kr_sw = dscr("w_kr_sw", [1024, 64], BF16)
    w_uq_sw = dscr("w_uq_sw", [256, 512], BF16)
    cs_scr = dscr("cs_scr", [2, 64, S], F32)
    qk_scr = (nc.dram_tensor("qk_scr", [8, 128, S], BF16, kind="ExternalOutput").ap() if dbg == "P1" else dscr("qk_scr", [8, 128, S], BF16))
    v_scr = (nc.dram_tensor("v_scr", [S, 1024], BF16, kind="ExternalOutput").ap() if dbg == "P1" else dscr("v_scr", [S, 1024], BF16))
    nb_scr = (nc.dram_tensor("nb_scr", [2, 4, S], F32, kind="ExternalOutput").ap() if dbg == "P1" else dscr("nb_scr", [2, 4, S], F32))
    hf_scr = dscr("hf_scr", [S, 256], F32)
    hmT_scr = (nc.dram_tensor("hmT_scr", [8, 128, S], BF16, kind="ExternalOutput").ap() if dbg == "L1" else dscr("hmT_scr", [8, 128, S], BF16))
    if isinstance(dbg, str) and dbg.startswith("M:"):
        oT_scr = nc.dram_tensor("oT_scr", [8, 128, S], BF16, kind="ExternalOutput").ap()
    else:
        oT_scr = dscr("oT_scr", [8, 128, S], BF16)
    cqnT_scr = dscr("cqnT_scr", [2, 128, S], BF16)
    ckvnT_scr = dscr("ckvnT_scr", [2, 128, S], BF16)
    krT_scr = dscr("krT_scr", [64, S], BF16)

    _res_registry = {}
    _ResClass = Res

    def RES(name):
        r = _res_registry.get(name)
        if r is None:
            r = _res_registry[name] = _ResClass(name)
        return r

    with ExitStack() as st:
        fw = FW(nc, st)
        T = lambda name, shape, dt: st.enter_context(nc.sbuf_tensor(name, shape, dt))
        identf = T("identf", [128, 128], F32)
        identb = T("identb", [128, 128], BF16)
        maskf = T("maskf", [128, 128], F32)
        maskb = T("maskb", [128, 128], F32)
        onesb = T("onesb", [128, 128], BF16)
        gfin = T("gfin", [128, D], F32)
        convw = T("convw", [128, 5, 8], F32)
        gcol = {k: T("gc_" + k, [128, n // 128], F32) for k, n in GSHAPES.items()}
        gneg = {k: T("gn_" + k, [128, GSHAPES[k] // 128], F32) for k in ("g_mix", "g_q")}
        bg = T("bg", [4, 4], F32)
        colsT = T("colsT", [128, 4, NT, 4], F32)
        wsl = [T("wsl%d" % i, [128, 4096], BF16) for i in range(6)]
        r_wsl = [RES("wsl%d" % i) for i in range(6)]
        AR = 34816
        arena = T("arena", [128, AR], F32)
        arena_b = arena.bitcast(BF16)
        arena_i = arena.bitcast(I32)
        banks = [st.enter_context(nc.psum_tensor("pb%d" % i, [128, 512], F32)) for i in range(8)]
        r_bank = [RES("pb%d" % i) for i in range(8)]
        r_const = RES("const")
        state = {"bank": 0, "wsl": 0, "ev": 0}

        def VF(off, n, parts=128):
            assert off % 4 == 0 and off // 4 + n <= AR
            return arena[0:parts, off // 4: off // 4 + n]

        def VB(off, n, parts=128):
            assert off % 2 == 0 and off // 2 + n <= 2 * AR
            return arena_b[0:parts, off // 2: off // 2 + n]

        def nbank():
            i = state["bank"]
            state["bank"] = (i + 1) % 8
            return banks[i], r_bank[i]

        def evac_eng():
            state["ev"] ^= 1
            return "act" if state["ev"] else "dve"

        def copy_op(eng, out, in_, reads, writes):
            if eng == "act":
                fw.op("act", lambda e: e.copy(out, in_), reads=reads, writes=writes)
            else:
                fw.op(eng, lambda e: e.tensor_copy(out, in_), reads=reads, writes=writes)

        def mm(out_ap, pairs, reads, wres):
            n = len(pairs)
            for i, (l, r) in enumerate(pairs):
                fw.op("pe", lambda e, l=l, r=r, i=i: e.matmul(out_ap, l, r, start=(i == 0), stop=(i == n - 1)),
                      reads=reads, writes=[wres], signal=(i == n - 1))

        def transp(out_ap, in_ap, ident, reads, wres, signal=True):
            fw.op("pe", lambda e: e.transpose(out_ap, in_ap, ident), reads=reads, writes=[wres], signal=signal)

        r_W = {k: RES("@" + k) for k in WSHAPES}
        r_W["w_kr_sw"] = RES("@w_kr_sw")
        r_W["w_uq_sw"] = RES("@w_uq_sw")

        def wload(name, W_ap, k0, nk, c0, ncols):
            i = state["wsl"]
            state["wsl"] = (i + 1) % 6
            assert nk * ncols <= 4096
            view = wsl[i][:, 0:nk * ncols].rearrange("p (k n) -> p k n", k=nk)
            src = W_ap[k0 * 128:(k0 + nk) * 128, c0:c0 + ncols].rearrange("(k p) n -> p k n", p=128)
            fw.dma("sp", view, src, reads=[r_W[name]], writes=[r_wsl[i]])
            return view, r_wsl[i]

        fw.dma("sp", identf[:], c_ident, writes=[r_const])
        fw.dma("sp", maskf[:], c_maskf, writes=[r_const])
        fw.dma("sp", maskb[:], c_maskb, writes=[r_const])
        fw.dma("sp", gfin[:], g_final.partition_broadcast(128).rearrange("p o n -> p (o n)"), writes=[r_const])
        for j in range(5):
            fw.dma("sp", convw[:, j, :], conv_w[j:j + 1, :].rearrange("o (c p) -> p (o c)", p=128), writes=[r_const], allow_slow_non_contiguous=True)
        for k in GSHAPES:
            fw.dma("sp", gcol[k][:], Gd[k].rearrange("o (k p) -> p (o k)", p=128), writes=[r_const], allow_slow_non_contiguous=True)
        fw.dma("sp", bg[:], b_gate.rearrange("o (g h) -> h (o g)", h=4), writes=[r_const], allow_slow_non_contiguous=True)
        fw.op("dve", lambda e: e.tensor_copy(identb[:], identf[:]), reads=[r_const], writes=[r_const])
        fw.op("dve", lambda e: e.memset(onesb[:], 1.0), writes=[r_const])
        for k in gneg:
            fw.op("dve", lambda e, k=k: e.tensor_scalar(gneg[k][:], gcol[k][:], -1.0, None, ALU.mult), reads=[r_const], writes=[r_const])
        fw.barrier()

        stg_in = [VF(i * 8192, 2048) for i in range(2)]
        stg_out = [VB(16384 + i * 4096, 2048) for i in range(2)]
        r_si = [RES("stg_in%d" % i) for i in range(2)]
        r_so = [RES("stg_out%d" % i) for i in range(2)]
        pw = {"i": 0}

        def prep(name):
            K, N = WSHAPES[name]
            g = gcol[GAINS[name]] if name in GAINS else None
            for kc in range(K // 128):
                for c0 in range(0, N, 2048):
                    n_ = min(2048, N - c0)
                    i = pw["i"] & 1
                    pw["i"] += 1
                    fw.dma("sp", stg_in[i][:, 0:n_], Wf[name][kc * 128:(kc + 1) * 128, c0:c0 + n_], writes=[r_si[i]])
                    eng = evac_eng()
                    if g is None:
                        copy_op(eng, stg_out[i][:, 0:n_], stg_in[i][:, 0:n_], [r_si[i]], [r_so[i]])
                    elif eng == "act":
                        fw.op("act", lambda e, i=i, kc=kc, n_=n_, g=g: e.activation(stg_out[i][:, 0:n_], stg_in[i][:, 0:n_], AF.Copy, scale=g[:, kc:kc + 1]),
                              reads=[r_si[i], r_const], writes=[r_so[i]])
                    else:
                        fw.op("dve", lambda e, i=i, kc=kc, n_=n_, g=g: e.tensor_scalar(stg_out[i][:, 0:n_], stg_in[i][:, 0:n_], g[:, kc:kc + 1], None, ALU.mult),
                              reads=[r_si[i], r_const], writes=[r_so[i]])
                    fw.dma("pool", Wb[name][kc * 128:(kc + 1) * 128, c0:c0 + n_], stg_out[i][:, 0:n_], reads=[r_so[i]], writes=[r_W[name]])

        def prep_sw(dst, dname, src, K, bases, g, gn):
            for kc in range(K // 128):
                i = pw["i"] & 1
                pw["i"] += 1
                nb_ = len(bases)
                for j, b in enumerate(bases):
                    fw.dma("sp", stg_in[i][:, j * 64:(j + 1) * 64], src[kc * 128:(kc + 1) * 128, b:b + 64], writes=[r_si[i]])
                for j in range(nb_):
                    fw.op("dve", lambda e, i=i, j=j, kc=kc: e.tensor_scalar(stg_out[i][:, j * 64:j * 64 + 32], stg_in[i][:, j * 64 + 32:j * 64 + 64], gn[:, kc:kc + 1], None, ALU.mult),
                          reads=[r_si[i], r_const], writes=[r_so[i]])
                    fw.op("dve", lambda e, i=i, j=j, kc=kc: e.tensor_scalar(stg_out[i][:, j * 64 + 32:j * 64 + 64], stg_in[i][:, j * 64:j * 64 + 32], g[:, kc:kc + 1], None, ALU.mult),
                          reads=[r_si[i], r_const], writes=[r_so[i]])
                fw.dma("pool", dst[kc * 128:(kc + 1) * 128, 0:64 * nb_], stg_out[i][:, 0:64 * nb_], reads=[r_so[i]], writes=[r_W[dname]])

        for name in WSHAPES:
            prep(name)
        prep_sw(w_kr_sw, "w_kr_sw", Wf["w_in"], 1024, [KR0], gcol["g_mix"], gneg["g_mix"])
        prep_sw(w_uq_sw, "w_uq_sw", Wf["w_uq"], 256, [h * 192 + 128 for h in range(8)], gcol["g_q"], gneg["g_q"])

        r_tr = RES("trig")
        TB0 = 24576
        posf = VF(TB0, S, 64)
        tq = VF(TB0 + 16384, S, 64)
        ti = arena_i[0:64, (TB0 + 32768) // 4:(TB0 + 32768) // 4 + S]
        tk = VF(TB0 + 49152, S, 64)
        tu = VF(TB0 + 65536, S, 64)
        pidx = T("pidx", [64, 1], F32)
        invf = T("invf", [64, 1], F32)
        pidx_i = T("pidx_i", [64, 1], I32)
        fw.op("pool", lambda e: e.iota(ti, [[1, S]], base=0, channel_multiplier=0), writes=[r_tr])
        fw.op("pool", lambda e: e.iota(pidx_i[0:32, :], [[0, 1]], base=0, channel_multiplier=1), writes=[r_tr])
        fw.op("pool", lambda e: e.iota(pidx_i[32:64, :], [[0, 1]], base=0, channel_multiplier=1), writes=[r_tr])
        fw.op("dve", lambda e: e.tensor_copy(posf, ti), reads=[r_tr], writes=[r_tr])
        fw.op("dve", lambda e: e.tensor_copy(pidx[:], pidx_i[:]), reads=[r_tr], writes=[r_tr])
        fw.op("act", lambda e: e.activation(invf[:], pidx[:], AF.Exp, scale=-math.log(10000.0) / 32.0), reads=[r_tr], writes=[r_tr])
        fw.op("dve", lambda e: e.tensor_scalar(tq, posf, invf[:, 0:1], 1.0 / (2 * math.pi), ALU.mult, ALU.mult), reads=[r_tr], writes=[r_tr])
        for which, shift in ((1, 0.0), (0, 0.25)):
            fw.op("dve", lambda e, shift=shift: e.tensor_scalar(tu, tq, shift, None, ALU.add), reads=[r_tr], writes=[r_tr])
            fw.op("dve", lambda e: e.tensor_copy(ti, tu), reads=[r_tr], writes=[r_tr])
            fw.op("dve", lambda e: e.tensor_copy(tk, ti), reads=[r_tr], writes=[r_tr])
            fw.op("dve", lambda e: e.tensor_tensor(tu, tu, tk, ALU.subtract), reads=[r_tr], writes=[r_tr])
            fw.op("dve", lambda e: e.tensor_scalar(tk, tu, 0.5, None, ALU.is_gt), reads=[r_tr], writes=[r_tr])
            fw.op("dve", lambda e: e.tensor_tensor(tu, tu, tk, ALU.subtract), reads=[r_tr], writes=[r_tr])
            fw.op("dve", lambda e: e.tensor_scalar(tk, tu, -0.5, None, ALU.is_lt), reads=[r_tr], writes=[r_tr])
            fw.op("dve", lambda e: e.tensor_tensor(tu, tu, tk, ALU.add), reads=[r_tr], writes=[r_tr])
            fw.op("act", lambda e: e.activation(tk, tu, AF.Sin, scale=2 * math.pi), reads=[r_tr], writes=[r_tr])
            fw.dma("pool", cs_scr[which], tk, reads=[r_tr], writes=[RES("@cs")], sem_res=r_tr)
        fw.barrier()

        _once = {}

        def T_once(name, shape, dt):
            if name not in _once:
                _once[name] = T(name, shape, dt)
            return _once[name]
        mix_stats = []
        st1 = [T("st1_%d" % i, [128, 8], F32) for i in range(4)]
        r_st1 = [RES("st1_%d" % i) for i in range(4)]
        cmask = T("cmask", [4, 512], F32)
        fw.op("pool", lambda e: e.memset(cmask[:], 1.0), writes=[r_const])
        for j in range(4):
            fw.op("pool", lambda e, j=j: e.memset(cmask[:, j * 128:j * 128 + 1], 0.0), writes=[r_const])
        fw.barrier()

        def rms_tile(x_ap, xn_ap, stt, r_x, r_xn, r_stt, n):
            fw.op("act", lambda e: e.activation(xn_ap, x_ap, AF.Square, accum_out=stt[:, 0:1]), reads=[r_x], writes=[r_xn, r_stt])
            fw.op("act", lambda e: e.activation(stt[:, 1:2], stt[:, 0:1], AF.Sqrt, bias=EPS, scale=1.0 / n), reads=[r_stt], writes=[r_stt])
            fw.op("dve", lambda e: e.reciprocal(stt[:, 2:3], stt[:, 1:2]), reads=[r_stt], writes=[r_stt])
            fw.op("dve", lambda e: e.tensor_scalar(xn_ap, x_ap, stt[:, 2:3], None, ALU.mult), reads=[r_x, r_stt], writes=[r_xn])

        def transpose_to(dst3, src_ap, nk, reads, r_dst):
            bk, rb = nbank()
            pbt = bk.bitcast(BF16)
            for k in range(nk):
                transp(pbt[:, k * 128:(k + 1) * 128], src_ap[:, k * 128:(k + 1) * 128], identb[:], reads + [r_const], rb, signal=(k == nk - 1))
            eng = evac_eng()
            copy_op(eng, dst3, pbt[:, 0:nk * 128].rearrange("p (k n) -> p k n", k=nk), [rb], [r_dst])

        def phase1(sq):
            hTb = [VB(i * 8192, 4096).rearrange("p (k n) -> p k n", k=8) for i in range(2)]
            r_hTb = [RES("hTb%d" % i) for i in range(2)]
            xa = [VF(16384 + i * 4096, 1024) for i in range(2)] + [VF(108544 + i * 4096, 1024) for i in range(2)]
            r_xa = [RES("xa%d" % i) for i in range(4)]
            xnb = [VB(24576 + i * 2048, 1024) for i in range(2)] + [VB(116736 + i * 2048, 1024) for i in range(2)]
            r_xnb = [RES("xnb%d" % i) for i in range(4)]
            U = [VF(28672 + ct * 2064, 516) for ct in range(8)]
            r_U = [RES("U%d" % ct) for ct in range(8)]
            acc = [VF(45184 + i * 2048, 512) for i in range(2)]
            r_acc = [RES("acc%d" % i) for i in range(2)]
            qkb = [VB(49280 + i * 1024, 512) for i in range(2)]
            r_qkb = [RES("qkb%d" % i) for i in range(2)]
            vst = [VB(51328 + i * 2048, 1024) for i in range(4)]
            r_vst = [RES("vst%d" % i) for i in range(4)]
            GB = 59520
            gt = [VF(GB + i * 2048, 512, 4) for i in range(16)]
            r_g = RES("gates")
            LB = GB + 32768
            lat = VB(LB, 2048).rearrange("p (k n) -> p k n", k=4)
            r_lat = RES("lat")
            cnb4 = [VB(LB + 4096 + i * 1024, 512) for i in range(2)] + [VB(120832 + i * 1024, 512) for i in range(2)]
            r_cnb4 = [RES("cnb%d" % i) for i in range(4)]
            cst = VF(LB + 6144, 1024, 64).rearrange("p (c n) -> p c n", c=2)
            r_cst = RES("cst")
            krt = [VF(LB + 10240 + i * 2048, 512, 64) for i in range(2)]
            r_krt = RES("krt")
            krb = VB(LB + 14336, 512, 64)
            r_krb = RES("krb")
            r_qk = RES("@qk_scr"); r_v = RES("@v_scr"); r_nb = RES("@nb_scr")
            r_cq = RES("@cq"); r_ckv = RES("@ckv"); r_kr = RES("@kr")
            for b in range(NBLK):
                tok0 = b * 512
                j = b & 1
                sx, r_sx = st1[2 + (b & 1)], r_st1[2 + (b & 1)]
                for tt in range(4):
                    fw.dma("sp", xa[tt], xs[sq, tok0 + tt * 128: tok0 + (tt + 1) * 128, :], writes=[r_xa[tt]])
                for tt in range(4):
                    fw.op("act", lambda e, tt=tt, sx=sx: e.activation(xnb[tt], xa[tt], AF.Square, accum_out=sx[:, tt:tt + 1]), reads=[r_xa[tt]], writes=[r_xnb[tt], r_sx])
                fw.op("act", lambda e, sx=sx: e.activation(sx[:, 4:8], sx[:, 0:4], AF.Sqrt, bias=EPS, scale=1.0 / 1024), reads=[r_sx], writes=[r_sx])
                fw.op("dve", lambda e, sx=sx: e.reciprocal(sx[:, 0:4], sx[:, 4:8]), reads=[r_sx], writes=[r_sx])
                for tt in range(4):
                    fw.op("dve", lambda e, tt=tt, sx=sx: e.tensor_scalar(xnb[tt], xa[tt], sx[:, tt:tt + 1], None, ALU.mult), reads=[r_xa[tt], r_sx], writes=[r_xnb[tt]])
                for tt in range(4):
                    transpose_to(hTb[j][:, :, tt * 128:(tt + 1) * 128], xnb[tt], 8, [r_xnb[tt]], r_hTb[j])
                for grp in range(2):
                    wv, rw = wload("w_in", Wb["w_in"], 0, 8, grp * 512, 512)
                    for cl in range(4):
                        ct = grp * 4 + cl
                        bk, rb = nbank()
                        mm(bk[:, :], [(wv[:, kc, cl * 128:(cl + 1) * 128], hTb[j][:, kc, :]) for kc in range(8)], [rw, r_hTb[j]], rb)
                        if b == 0:
                            fw.op("pool", lambda e, ct=ct: e.memset(U[ct][:, 0:4], 0.0), writes=[r_U[ct]])
                        fw.op("act", lambda e, ct=ct, bk=bk: e.copy(U[ct][:, 4:516], bk[:, :]), reads=[rb], writes=[r_U[ct]])
                        a = ct & 1

                        def conv(n, a=a, ct=ct):
                            fw.op("dve", lambda e: e.tensor_scalar(acc[a][:, 0:n], U[ct][:, 0:n], convw[:, 0, ct:ct + 1], None, ALU.mult),
                                  reads=[r_U[ct], r_const], writes=[r_acc[a]])
                            for jj in range(1, 5):
                                fw.op("dve", lambda e, jj=jj: e.scalar_tensor_tensor(acc[a][:, 0:n], U[ct][:, jj:jj + n], convw[:, jj, ct:ct + 1], acc[a][:, 0:n], ALU.mult, ALU.add),
                                      reads=[r_U[ct], r_const, r_acc[a]], writes=[r_acc[a]])
                            fw.op("act", lambda e: e.activation(qkb[a][:, 0:n], acc[a][:, 0:n], AF.Silu), reads=[r_acc[a]], writes=[r_qkb[a]])
                        conv(512)
                        if b == 0:
                            fw.dma("pool", qk_scr[ct, :, 0:510], qkb[a][:, 2:512], reads=[r_qkb[a]], writes=[r_qk])
                        else:
                            fw.dma("pool", qk_scr[ct, :, tok0 - 2:tok0 + 510], qkb[a][:, 0:512], reads=[r_qkb[a]], writes=[r_qk])
                        fw.op("pool", lambda e, ct=ct: e.tensor_copy(U[ct][:, 0:4], U[ct][:, 512:516]), reads=[r_U[ct]], writes=[r_U[ct]])
                        if b == NBLK - 1:
                            fw.op("pool", lambda e, ct=ct: e.memset(U[ct][:, 4:8], 0.0), writes=[r_U[ct]])
                            conv(2)
                            fw.dma("pool", qk_scr[ct, :, S - 2:S], qkb[a][:, 0:2], reads=[r_qkb[a]], writes=[r_qk])
                for half in range(2):
                    wv, rw = wload("w_in", Wb["w_in"], 0, 8, V0 + half * 512, 512)
                    for tt in range(4):
                        bk, rb = nbank()
                        mm(bk[:, :], [(hTb[j][:, kc, tt * 128:(tt + 1) * 128], wv[:, kc, :]) for kc in range(8)], [rw, r_hTb[j]], rb)
                        copy_op(evac_eng(), vst[tt][:, half * 512:(half + 1) * 512], bk[:, :], [rb], [r_vst[tt]])
                for tt in range(4):
                    fw.dma("pool", v_scr[tok0 + tt * 128: tok0 + (tt + 1) * 128, :], vst[tt], reads=[r_vst[tt]], writes=[r_v])
                wv, rw = wload("w_in", Wb["w_in"], 0, 8, G0, 16)
                for g in range(4):
                    bk, rb = nbank()
                    mm(bk[0:4, :], [(wv[:, kc, 4 * g:4 * g + 4], hTb[j][:, kc, :]) for kc in range(8)], [rw, r_hTb[j]], rb)
                    fw.op("act", lambda e, g=g, bk=bk: e.activation(gt[g], bk[0:4, :], AF.Identity, bias=bg[:, g:g + 1]), reads=[rb, r_const], writes=[r_g])
                IFt, IBt, FFt, FBt = gt[0], gt[1], gt[2], gt[3]
                SPf, SPb, csf, csb, NBf, NBb, Af, Ab, Gf, Gb, tmp = gt[4:15]
                G = lambda fn: fw.op(fn[0], fn[1], reads=[r_g, r_const], writes=[r_g])
                for FX, SP in ((FFt, SPf), (FBt, SPb)):
                    G(("act", lambda e, FX=FX: e.activation(tmp, FX, AF.Exp, scale=-1.0)))
                    G(("act", lambda e, SP=SP: e.activation(SP, tmp, AF.Ln, bias=1.0)))
                G(("dve", lambda e: e.tensor_tensor_scan(csf, cmask[:], SPf, 0.0, ALU.mult, ALU.add)))
                G(("dve", lambda e: e.tensor_tensor_scan(csb, cmask[:], SPb, 0.0, ALU.mult, ALU.add)))
                v3 = lambda t: t.rearrange("p (a n) -> p a n", a=4)
                tot = lambda t: t.rearrange("p (a n) -> p a n", a=4)[:, :, 127:128].to_broadcast([4, 4, 128])
                G(("dve", lambda e: e.tensor_scalar(NBf, csf, -1.0, None, ALU.mult)))
                G(("dve", lambda e: e.tensor_tensor(Af, IFt, csf, ALU.add)))
                G(("dve", lambda e: e.tensor_tensor(v3(tmp), v3(Af), tot(csf), ALU.subtract)))
                G(("act", lambda e: e.activation(Gf, tmp, AF.Exp)))
                G(("dve", lambda e: e.tensor_tensor(NBb, SPb, csb, ALU.subtract)))
                G(("dve", lambda e: e.tensor_tensor(v3(NBb), v3(NBb), tot(csb), ALU.add)))
                G(("dve", lambda e: e.tensor_tensor(Ab, IBt, NBb, ALU.add)))
                G(("dve", lambda e: e.tensor_tensor(v3(tmp), v3(Ab), tot(csb), ALU.subtract)))
                G(("act", lambda e: e.activation(Gb, tmp, AF.Exp)))
                G(("dve", lambda e: e.tensor_scalar(NBb, NBb, -1.0, None, ALU.mult)))
                fw.dma("pool", nb_scr[0, :, tok0:tok0 + 512], NBf, reads=[r_g], writes=[r_nb])
                fw.dma("pool", nb_scr[1, :, tok0:tok0 + 512], NBb, reads=[r_g], writes=[r_nb])
                wv, rw = wload("w_in", Wb["w_in"], 0, 8, CQ0, 512)
                sc, r_sc = st1[b & 1], r_st1[b & 1]
                cbk = []
                for tt in range(4):
                    bk, rb = nbank()
                    mm(bk[:, :], [(hTb[j][:, kc, tt * 128:(tt + 1) * 128], wv[:, kc, :]) for kc in range(8)], [rw, r_hTb[j]], rb)
                    cbk.append((bk, rb))
                for tt in range(4):
                    bk, rb = cbk[tt]
                    fw.op("act", lambda e, tt=tt, bk=bk, sc=sc: e.activation(cnb4[tt][:, 0:256], bk[:, 0:256], AF.Square, accum_out=sc[:, 2 * tt:2 * tt + 1]), reads=[rb], writes=[r_cnb4[tt], r_sc])
                    fw.op("act", lambda e, tt=tt, bk=bk, sc=sc: e.activation(cnb4[tt][:, 256:512], bk[:, 256:512], AF.Square, accum_out=sc[:, 2 * tt + 1:2 * tt + 2]), reads=[rb], writes=[r_cnb4[tt], r_sc])
                fw.op("act", lambda e, sc=sc, scb=st1[2 + (b & 1)]: e.activation(scb[:, 0:8], sc[:, 0:8], AF.Sqrt, bias=EPS, scale=1.0 / 256), reads=[r_sc, r_st1[2 + (b & 1)]], writes=[r_st1[2 + (b & 1)]])
                fw.op("dve", lambda e, sc=sc, scb=st1[2 + (b & 1)]: e.reciprocal(sc[:, 0:8], scb[:, 0:8]), reads=[r_st1[2 + (b & 1)]], writes=[r_sc])
                for tt in range(4):
                    bk, rb = cbk[tt]
                    fw.op("dve", lambda e, tt=tt, bk=bk, sc=sc: e.tensor_scalar(cnb4[tt][:, 0:256], bk[:, 0:256], sc[:, 2 * tt:2 * tt + 1], None, ALU.mult), reads=[rb, r_sc], writes=[r_cnb4[tt]])
                    fw.op("act", lambda e, tt=tt, bk=bk, sc=sc: e.activation(cnb4[tt][:, 256:512], bk[:, 256:512], AF.Copy, scale=sc[:, 2 * tt + 1:2 * tt + 2]), reads=[rb, r_sc], writes=[r_cnb4[tt]])
                for tt in range(4):
                    transpose_to(lat[:, :, tt * 128:(tt + 1) * 128], cnb4[tt], 4, [r_cnb4[tt]], r_lat)
                fw.dma("pool", cqnT_scr[:, :, tok0:tok0 + 512].rearrange("k p n -> p k n"), lat[:, 0:2, :], reads=[r_lat], writes=[r_cq])
                fw.dma("pool", ckvnT_scr[:, :, tok0:tok0 + 512].rearrange("k p n -> p k n"), lat[:, 2:4, :], reads=[r_lat], writes=[r_ckv])
                wa, rwa = wload("w_in", Wb["w_in"], 0, 8, KR0, 64)
                wb_, rwb = wload("w_kr_sw", w_kr_sw, 0, 8, 0, 64)
                bka, rba = nbank()
                mm(bka[0:64, :], [(wa[:, kc, :], hTb[j][:, kc, :]) for kc in range(8)], [rwa, r_hTb[j]], rba)
                bkb, rbb = nbank()
                mm(bkb[0:64, :], [(wb_[:, kc, :], hTb[j][:, kc, :]) for kc in range(8)], [rwb, r_hTb[j]], rbb)
                fw.dma("sp", cst, cs_scr[:, :, tok0:tok0 + 512].rearrange("c p n -> p c n"), writes=[r_cst])
                fw.op("dve", lambda e, bka=bka: e.tensor_tensor(krt[0], bka[0:64, :], cst[:, 0, :], ALU.mult), reads=[rba, r_cst], writes=[r_krt])
                fw.op("dve", lambda e, bkb=bkb: e.tensor_tensor(krt[1], bkb[0:64, :], cst[:, 1, :], ALU.mult), reads=[rbb, r_cst], writes=[r_krt])
                fw.op("pool", lambda e: e.tensor_tensor(krb, krt[0], krt[1], ALU.add), reads=[r_krt], writes=[r_krb])
                fw.dma("pool", krT_scr[:, tok0:tok0 + 512], krb, reads=[r_krb], writes=[r_kr])
                bk, rb = nbank()
                for qi, Q in enumerate((Af, Ab, Gf, Gb)):
                    for tt in range(4):
                        c0 = (qi * 4 + tt) * 4
                        transp(bk[:, c0:c0 + 4], Q[:, tt * 128:(tt + 1) * 128], identf[0:4, 0:4], [r_g, r_const], rb, signal=(qi == 3 and tt == 3))
                fw.op("dve", lambda e, bk=bk, b=b: e.tensor_copy(colsT[:, :, b * 4:(b + 1) * 4, :], bk[:, 0:64].rearrange("p (q t h) -> p q t h", q=4, t=4)),
                      reads=[rb], writes=[r_const])
            fw.op("dve", lambda e: e.tensor_scalar(colsT[:, 0:2], colsT[:, 0:2], LN_SQ, None, ALU.add), reads=[r_const], writes=[r_const])
            if dbg == "P1":
                fw.dma("pool", dbg_out["d_colsT"], colsT[:].rearrange("p q t h -> p (q t h)"), reads=[r_const], writes=[RES("@dbgP1")], sem_res=r_st1[0])

        def phase3(sq):
            xa4 = VF(0, 4096).rearrange("p (t n) -> p t n", t=4)
            r_xa4 = [RES("xa4_%d" % i) for i in range(4)]
            xn4 = VB(16384, 4096).rearrange("p (t n) -> p t n", t=4)
            r_xn4 = [RES("xn4_%d" % i) for i in range(4)]
            k8 = lambda ap: ap.rearrange("p (k n) -> p k n", k=8)
            hTb = k8(VB(24576, 4096)); r_hTb = RES("p3hTb")
            sgo = k8(VB(32768, 4096)); r_sgo = RES("sgo")
            hmT = k8(VB(40960, 4096)); r_hmT = RES("hmTb")
            oTb = k8(VB(49152, 4096)); r_oTb = RES("oTb")
            mrg = k8(VB(57344, 4096)); r_mrg = RES("mrg")
            sg = [VF(65536 + i * 2048, 512) for i in range(2)]; r_sg = [RES("sg%d" % i) for i in range(2)]
            t12 = [VF(69632 + i * 2048, 512) for i in range(2)]; r_t12 = [RES("t12_%d" % i) for i in range(2)]
            actT = VB(73728, 22 * 512).rearrange("p (k n) -> p k n", k=22); r_actT = RES("actT")
            pa = VF(96256, 1024).rearrange("p (t n) -> p t n", t=4); r_pa = RES("pa")
            pn = VB(100352, 1024).rearrange("p (t n) -> p t n", t=4); r_pn = RES("pn")
            pT = VB(102400, 1024).rearrange("p (k n) -> p k n", k=2); r_pT = RES("pT")
            r_y = RES("@y")
            for b in range(NBLK):
                tok0 = b * 512
                tsl = lambda tt: slice(tt * 128, (tt + 1) * 128)
                for tt in range(4):
                    i = tt & 1
                    fw.dma("sp", xa4[:, tt, :], xs[sq, tok0 + tt * 128: tok0 + (tt + 1) * 128, :], writes=[r_xa4[tt]])
                    rms_tile(xa4[:, tt, :], xn4[:, tt, :], st1[i], r_xa4[tt], r_xn4[tt], r_st1[i], 1024)
                    transpose_to(hTb[:, :, tsl(tt)], xn4[:, tt, :], 8, [r_xn4[tt]], r_hTb)
                fw.dma("sp", pa, ps[sq, tok0:tok0 + 512, :].rearrange("(t p) n -> p t n", p=128), writes=[r_pa])
                if dbg:
                    fw.op("pool", lambda e: e.memset(hmT, 0.0), writes=[r_hmT])
                else:
                    fw.dma("sp", hmT, hmT_scr[:, :, tok0:tok0 + 512].rearrange("k p n -> p k n"), reads=[RES("@hm_dummy")], writes=[r_hmT])
                if dbg == 2:
                    fw.op("pool", lambda e: e.memset(oTb, 0.0), writes=[r_oTb])
                else:
                    fw.dma("sp", oTb, oT_scr[:, :, tok0:tok0 + 512].rearrange("k p n -> p k n"), reads=[RES("@o_dummy")], writes=[r_oTb])
                for grp in range(2):
                    wv, rw = wload("w_in", Wb["w_in"], 0, 8, O0 + grp * 512, 512)
                    for cl in range(4):
                        bk, rb = nbank()
                        mm(bk[:, :], [(wv[:, kc, cl * 128:(cl + 1) * 128], hTb[:, kc, :]) for kc in range(8)], [rw, r_hTb], rb)
                        fw.op("act", lambda e, bk=bk, c=grp * 4 + cl: e.activation(sgo[:, c, :], bk[:, :], AF.Sigmoid), reads=[rb], writes=[r_sgo])
                fw.op("dve", lambda e: e.tensor_tensor(sgo, sgo, hmT, ALU.mult), reads=[r_sgo, r_hmT], writes=[r_sgo])
                for grp in range(2):
                    wm, rwm = wload("w_mo", Wb["w_mo"], 0, 8, grp * 512, 512)
                    wa, rwa = wload("w_ao", Wb["w_ao"], 0, 8, grp * 512, 512)
                    wg1, rw1 = wload("w_in", Wb["w_in"], 0, 8, GM0 + grp * 512, 512)
                    wg2, rw2 = wload("w_in", Wb["w_in"], 0, 8, GM0 + 1024 + grp * 512, 512)
                    for cl in range(4):
                        c = grp * 4 + cl
                        cs_ = slice(cl * 128, (cl + 1) * 128)
                        bm, rbm = nbank(); mm(bm[:, :], [(wm[:, kc, cs_], sgo[:, kc, :]) for kc in range(8)], [rwm, r_sgo], rbm)
                        ba, rba = nbank(); mm(ba[:, :], [(wa[:, kc, cs_], oTb[:, kc, :]) for kc in range(8)], [rwa, r_oTb], rba)
                        b1, rb1 = nbank(); mm(b1[:, :], [(wg1[:, kc, cs_], hTb[:, kc, :]) for kc in range(8)], [rw1, r_hTb], rb1)
                        b2, rb2 = nbank(); mm(b2[:, :], [(wg2[:, kc, cs_], hTb[:, kc, :]) for kc in range(8)], [rw2, r_hTb], rb2)
                        fw.op("act", lambda e, b1=b1: e.activation(sg[0], b1[:, :], AF.Sigmoid), reads=[rb1], writes=[r_sg[0]])
                        fw.op("act", lambda e, b2=b2: e.activation(sg[1], b2[:, :], AF.Sigmoid), reads=[rb2], writes=[r_sg[1]])
                        fw.op("dve", lambda e, bm=bm: e.tensor_tensor(t12[0], bm[:, :], sg[0], ALU.mult), reads=[rbm, r_sg[0]], writes=[r_t12[0]])
                        fw.op("dve", lambda e, ba=ba: e.tensor_tensor(t12[1], ba[:, :], sg[1], ALU.mult), reads=[rba, r_sg[1]], writes=[r_t12[1]])
                        fw.op("pool", lambda e, c=c: e.tensor_tensor(mrg[:, c, :], t12[0], t12[1], ALU.add), reads=r_t12, writes=[r_mrg])
                for half in range(2):
                    wo, rwo = wload("w_out", Wb["w_out"], 0, 8, half * 512, 512)
                    for tt in range(4):
                        bk, rb = nbank()
                        mm(bk[:, :], [(mrg[:, kc, tsl(tt)], wo[:, kc, :]) for kc in range(8)], [rwo, r_mrg], rb)
                        xsl = xa4[:, tt, half * 512:(half + 1) * 512]
                        fw.op("dve", lambda e, bk=bk, xsl=xsl: e.tensor_tensor(xsl, xsl, bk[:, :], ALU.add), reads=[rb], writes=[r_xa4[tt]])
                for tt in range(4):
                    i = tt & 1
                    rms_tile(xa4[:, tt, :], xn4[:, tt, :], st1[i], r_xa4[tt], r_xn4[tt], r_st1[i], 1024)
                    transpose_to(hTb[:, :, tsl(tt)], xn4[:, tt, :], 8, [r_xn4[tt]], r_hTb)
                for g in range(6):
                    n_ = 512 if g < 5 else 256
                    wg, rwg = wload("w_gate", Wb["w_gate"], 0, 8, g * 512, n_)
                    wu, rwu = wload("w_up", Wb["w_up"], 0, 8, g * 512, n_)
                    for cl in range(n_ // 128):
                        f = g * 4 + cl
                        cs_ = slice(cl * 128, (cl + 1) * 128)
                        bg_, rbg = nbank(); mm(bg_[:, :], [(wg[:, kc, cs_], hTb[:, kc, :]) for kc in range(8)], [rwg, r_hTb], rbg)
                        bu, rbu = nbank(); mm(bu[:, :], [(wu[:, kc, cs_], hTb[:, kc, :]) for kc in range(8)], [rwu, r_hTb], rbu)
                        i = f & 1
                        fw.op("act", lambda e, bg_=bg_, i=i: e.activation(sg[i], bg_[:, :], AF.Silu), reads=[rbg], writes=[r_sg[i]])
                        fw.op("dve", lambda e, bu=bu, i=i, f=f: e.tensor_tensor(actT[:, f, :], bu[:, :], sg[i], ALU.mult), reads=[rbu, r_sg[i]], writes=[r_actT])
                for half in range(2):
                    parts = []
                    for k0, nk in ((0, 8), (8, 8), (16, 6)):
                        wd, rwd = wload("w_down", Wb["w_down"], k0, nk, half * 512, 512)
                        parts.append((k0, nk, wd, rwd))
                    for tt in range(4):
                        bk, rb = nbank()
                        pairs = []
                        for k0, nk, wd, rwd in parts:
                            pairs += [(actT[:, k0 + kk, tsl(tt)], wd[:, kk, :]) for kk in range(nk)]
                        mm(bk[:, :], pairs, [p_[3] for p_ in parts] + [r_actT], rb)
                        xsl = xa4[:, tt, half * 512:(half + 1) * 512]
                        fw.op("dve", lambda e, bk=bk, xsl=xsl: e.tensor_tensor(xsl, xsl, bk[:, :], ALU.add), reads=[rb], writes=[r_xa4[tt]])
                fw.op("act", lambda e: e.copy(pn, pa), reads=[r_pa], writes=[r_pn])
                for tt in range(4):
                    fw.op("act", lambda e, tt=tt: e.copy(xn4[:, tt, :], xa4[:, tt, :]), reads=[r_xa4[tt]], writes=[r_xn4[tt]])
                    transpose_to(hTb[:, :, tsl(tt)], xn4[:, tt, :], 8, [r_xn4[tt]], r_hTb)
                    transpose_to(pT[:, :, tsl(tt)], pn[:, tt, :], 2, [r_pn], r_pT)
                for half in range(2):
                    wpg, rwpg = wload("w_ple_gate", Wb["w_ple_gate"], 0, 8, half * 512, 512)
                    wp, rwp = wload("w_ple", Wb["w_ple"], 0, 2, half * 512, 512)
                    for tt in range(4):
                        bgt, rbgt = nbank(); mm(bgt[:, :], [(hTb[:, kc, tsl(tt)], wpg[:, kc, :]) for kc in range(8)], [rwpg, r_hTb], rbgt)
                        bp, rbp = nbank(); mm(bp[:, :], [(pT[:, kc, tsl(tt)], wp[:, kc, :]) for kc in range(2)], [rwp, r_pT], rbp)
                        i = tt & 1
                        fw.op("act", lambda e, bgt=bgt, i=i: e.activation(sg[i], bgt[:, :], AF.Sigmoid), reads=[rbgt], writes=[r_sg[i]])
                        fw.op("dve", lambda e, bp=bp, i=i: e.tensor_tensor(t12[i], bp[:, :], sg[i], ALU.mult), reads=[rbp, r_sg[i]], writes=[r_t12[i]])
                        xsl = xa4[:, tt, half * 512:(half + 1) * 512]
                        fw.op("dve", lambda e, xsl=xsl, i=i: e.tensor_tensor(xsl, xsl, t12[i], ALU.add), reads=[r_t12[i]], writes=[r_xa4[tt]])
                for tt in range(4):
                    i = tt & 1
                    stt = st1[i]
                    fw.op("act", lambda e, tt=tt, stt=stt: e.activation(xn4[:, tt, :], xa4[:, tt, :], AF.Square, accum_out=stt[:, 0:1]), reads=[r_xa4[tt]], writes=[r_xn4[tt], r_st1[i]])
                    fw.op("act", lambda e, stt=stt: e.activation(stt[:, 1:2], stt[:, 0:1], AF.Sqrt, bias=EPS, scale=1.0 / 1024), reads=[r_st1[i]], writes=[r_st1[i]])
                    fw.op("dve", lambda e, stt=stt: e.reciprocal(stt[:, 2:3], stt[:, 1:2]), reads=[r_st1[i]], writes=[r_st1[i]])
                    fw.op("dve", lambda e, tt=tt, stt=stt: e.scalar_tensor_tensor(xa4[:, tt, :], xa4[:, tt, :], stt[:, 2:3], gfin[:], ALU.mult, ALU.mult),
                          reads=[r_st1[i], r_const], writes=[r_xa4[tt]])
                    fw.dma("pool", y[sq, tok0 + tt * 128: tok0 + (tt + 1) * 128, :], xa4[:, tt, :], reads=[r_xa4[tt]], writes=[r_y])

        def phase_mlstm(sq, heads=range(4)):
            QT = VB(0, 4096); KT = VB(8192, 4096); Qp = VB(16384, 4096)
            Ktok = VB(24576, 4096).rearrange("p (t n) -> p t n", t=NT)
            Kp = VB(32768, 4096).rearrange("p (t n) -> p t n", t=NT)
            Vaug = VB(40960, NT * 258).rearrange("p (t n) -> p t n", t=NT)
            bbc = VF(57472, 4096)
            tmpE = [VF(73856 + i * 2048, 512) for i in range(2)]
            Cst = VF(77952, 258); Cbf = VB(78992, 258)
            Dt = [VF(79520 + i * 512, 128) for i in range(2)]
            Sp = [VB(80544 + i * 256, 128) for i in range(2)]
            aC = VF(81056, 32)
            hfo = [VF(81184 + i * 1024, 256) for i in range(3)]
            hs = [VF(84256 + i * 1024, 256) for i in range(2)]
            hn = [VB(86304 + i * 512, 256) for i in range(2)]
            hT2 = [VB(87328 + i * 512, 256).rearrange("p (k n) -> p k n", k=2) for i in range(2)]
            r_QT, r_KT, r_Qp, r_Ktok, r_Kp, r_V, r_bbc = (RES(n) for n in ("QT", "KT", "Qp", "Ktok", "Kp", "Vaug", "bbc"))
            r_tmpE = [RES("tmpE%d" % i) for i in range(2)]
            r_C, r_Cbf, r_aC = RES("Cst"), RES("Cbf"), RES("aC")
            r_Dt = [RES("Dt%d" % i) for i in range(2)]; r_Sp = [RES("Sp%d" % i) for i in range(2)]
            r_hfo = [RES("hfo%d" % i) for i in range(3)]; r_hs = [RES("hs%d" % i) for i in range(2)]
            r_hn = [RES("hn%d" % i) for i in range(2)]; r_hT2 = [RES("hT2_%d" % i) for i in range(2)]
            r_hf = [RES("@hf%d" % c) for c in range(NT)]
            r_hm = RES("@hmT")
            for h in heads:
                fw.dma("sp", QT, qk_scr[h], writes=[r_QT])
                fw.dma("sp", KT, qk_scr[4 + h], writes=[r_KT])
                fw.op("pool", lambda e: e.memset(Vaug[:, :, 256:258], 1.0), writes=[r_V])
                fw.dma("sp", Vaug[:, :, 0:256], v_scr[:, h * 256:(h + 1) * 256].rearrange("(t p) n -> p t n", p=128), reads=[r_V], writes=[r_V])
                for c4 in range(4):
                    transpose_to(Ktok[:, c4 * 8:(c4 + 1) * 8, :], KT[:, c4 * 1024:(c4 + 1) * 1024], 8, [r_KT], r_Ktok)
                for d in range(2):
                    msk = maskf if d == 0 else maskb
                    far = 127 if d == 0 else 0
                    fw.dma("sp", bbc, nb_scr[d, h:h + 1, :].partition_broadcast(128).rearrange("p o n -> p (o n)"), writes=[r_bbc])
                    fw.op("act", lambda e, far=far: e.activation(aC, bbc.rearrange("p (t n) -> p t n", t=NT)[:, :, far], AF.Exp), reads=[r_bbc], writes=[r_aC])
                    for blk in range(NBLK):
                        i = blk & 1
                        bs_ = slice(blk * 512, (blk + 1) * 512)
                        fw.op("act", lambda e, i=i, bs_=bs_: e.activation(tmpE[i], bbc[:, bs_], AF.Exp, bias=LN_SQ), reads=[r_bbc], writes=[r_tmpE[i]])
                        fw.op("dve", lambda e, i=i, bs_=bs_: e.tensor_tensor(Qp[:, bs_], QT[:, bs_], tmpE[i], ALU.mult), reads=[r_QT, r_tmpE[i]], writes=[r_Qp])
                    fw.op("pool", lambda e, msk=msk: e.tensor_tensor(bbc.rearrange("p (t n) -> p t n", t=NT), bbc.rearrange("p (t n) -> p t n", t=NT),
                                                                    msk[:].unsqueeze(1).to_broadcast([128, NT, 128]), ALU.add), reads=[r_bbc, r_const], writes=[r_bbc])
                    fw.op("dve", lambda e, d=d, h=h: e.tensor_tensor(Kp, Ktok, colsT[:, 2 + d, :, h].unsqueeze(2).to_broadcast([128, NT, 128]), ALU.mult),
                          reads=[r_Ktok, r_const], writes=[r_Kp])
                    fw.op("dve", lambda e: e.memset(Cst, 0.0), writes=[r_C])
                    fw.op("pool", lambda e: e.memset(Cbf, 0.0), writes=[r_Cbf])
                    order = list(range(NT)) if d == 0 else list(range(NT - 1, -1, -1))
                    for n_, c in enumerate(order):
                        cs_ = slice(c * 128, (c + 1) * 128)
                        j = n_ & 1
                        fw.op("act", lambda e, j=j, cs_=cs_, d=d, c=c, h=h: e.activation(Dt[j], bbc[:, cs_], AF.Exp, bias=colsT[:, d, c, h:h + 1]),
                              reads=[r_bbc, r_const], writes=[r_Dt[j]])
                        bS, rbS = nbank()
                        mm(bS[:, 0:128], [(KT[:, cs_], QT[:, cs_])], [r_KT, r_QT], rbS)
                        fw.op("dve", lambda e, j=j, bS=bS: e.tensor_tensor(Sp[j], bS[:, 0:128], Dt[j], ALU.mult), reads=[rbS, r_Dt[j]], writes=[r_Sp[j]])
                        bN, rbN = nbank()
                        mm(bN[:, 0:257], [(Sp[j], Vaug[:, c, 0:257]), (Qp[:, cs_], Cbf[:, 0:257])], [r_Sp[j], r_V, r_Qp, r_Cbf], rbN)
                        bU, rbU = nbank()
                        mm(bU[:, 0:257], [(Kp[:, c, :], Vaug[:, c, 0:257])], [r_Kp, r_V], rbU)
                        fw.op("dve", lambda e, c=c, bU=bU: e.scalar_tensor_tensor(Cst[:, 0:257], Cst[:, 0:257], aC[:, c:c + 1], bU[:, 0:257], ALU.mult, ALU.add),
                              reads=[r_C, r_aC, rbU], writes=[r_C])
                        fw.op("act", lambda e: e.copy(Cbf[:, 0:257], Cst[:, 0:257]), reads=[r_C], writes=[r_Cbf])
                        k = n_ & 1
                        stt, r_stt = st1[k], r_st1[k]
                        fw.op("act", lambda e, bN=bN, stt=stt: e.activation(stt[:, 0:1], bN[:, 256:257], AF.Abs), reads=[rbN], writes=[r_stt])
                        fw.op("dve", lambda e, stt=stt: e.tensor_scalar_max(stt[:, 0:1], stt[:, 0:1], 1.0), reads=[r_stt], writes=[r_stt])
                        fw.op("dve", lambda e, stt=stt: e.reciprocal(stt[:, 1:2], stt[:, 0:1]), reads=[r_stt], writes=[r_stt])
                        m3 = n_ % 3
                        if d == 0:
                            fw.op("dve", lambda e, m3=m3, bN=bN, stt=stt: e.tensor_scalar(hfo[m3], bN[:, 0:256], stt[:, 1:2], None, ALU.mult), reads=[rbN, r_stt], writes=[r_hfo[m3]])
                            fw.dma("pool", hf_scr[cs_, :], hfo[m3], reads=[r_hfo[m3]], writes=[r_hf[c]])
                        else:
                            fw.dma("sp", hfo[m3], hf_scr[cs_, :], reads=[r_hf[c]], writes=[r_hfo[m3]])
                            fw.op("dve", lambda e, m3=m3, k=k, bN=bN, stt=stt: e.scalar_tensor_tensor(hs[k], bN[:, 0:256], stt[:, 1:2], hfo[m3], ALU.mult, ALU.add),
                                  reads=[rbN, r_stt, r_hfo[m3]], writes=[r_hs[k]])
                            fw.op("act", lambda e, k=k, stt=stt: e.activation(hn[k], hs[k], AF.Square, accum_out=stt[:, 2:3]), reads=[r_hs[k]], writes=[r_hn[k], r_stt])
                            fw.op("act", lambda e, stt=stt: e.activation(stt[:, 3:4], stt[:, 2:3], AF.Sqrt, bias=EPS, scale=1.0 / 256), reads=[r_stt], writes=[r_stt])
                            fw.op("dve", lambda e, stt=stt: e.reciprocal(stt[:, 2:3], stt[:, 3:4]), reads=[r_stt], writes=[r_stt])
                            fw.op("dve", lambda e, k=k, stt=stt: e.tensor_scalar(hn[k], hs[k], stt[:, 2:3], None, ALU.mult), reads=[r_hs[k], r_stt], writes=[r_hn[k]])
                            transpose_to(hT2[k], hn[k], 2, [r_hn[k]], r_hT2[k])
                            fw.dma("pool", hmT_scr[2 * h:2 * h + 2, :, cs_].rearrange("k p n -> p k n"), hT2[k], reads=[r_hT2[k]], writes=[r_hm])

        def phase_mix(sq):
            cqnT = VB(0, 8192).rearrange("p (k n) -> p k n", k=2); r_cqn = RES("cqnT")
            ckvnT = VB(16384, 8192).rearrange("p (k n) -> p k n", k=2); r_ckvn = RES("ckvnT")
            krT = VB(32768, 4096, 128); r_krT = RES("krT")
            KhT = VB(40960, 4096); r_KhT = RES("KhT")
            Vh = VB(49152, 4096); r_Vh = RES("Vh")
            QhT = VB(57344, 4096); r_QhT = RES("QhT")
            QrT = VB(65536, 4096, 128); r_QrT = RES("QrT")
            cstq2 = [VF(73728, 1024, 64).rearrange("p (c n) -> p c n", c=2),
                     wsl[0].bitcast(F32)[0:64, 512:1536].rearrange("p (c n) -> p c n", c=2)]
            r_cstq2 = [RES("cstq"), RES("cstq_b")]
            rt = [VF(77824 + i * 2048, 512, 64) for i in range(2)]; r_rt = RES("rt")
            PT = [VB(81920 + i * 1024, 512) for i in range(3)]; r_PT = [RES("PT%d" % i) for i in range(3)]
            rsum = VF(84992, 512); r_rsum = RES("rsum")
            oTs = [VB(87040 + i * 1024, 512) for i in range(2)]; r_oTs = [RES("oTs%d" % i) for i in range(2)]
            r_oT = RES("@oT")
            QT, KT, Qp = wsl[1][:, :], wsl[2][:, :], wsl[3][:, :]
            Ktok = wsl[4][:, :].rearrange("p (t n) -> p t n", t=NT)
            Kp = wsl[5][:, :].rearrange("p (t n) -> p t n", t=NT)
            M0 = 89088
            Vaug = VB(M0, NT * 258).rearrange("p (t n) -> p t n", t=NT)
            bbc = VF(M0 + 16512, 4096)
            tmpE = [VF(M0 + 32896 + i * 2048, 512) for i in range(2)]
            Cst = VF(M0 + 36992, 258); Cbf = VB(M0 + 38032, 258)
            Dt = [VF(M0 + 38560 + i * 512, 128) for i in range(2)]
            Sp = [VB(M0 + 39584 + i * 256, 128) for i in range(2)]
            aC = VF(M0 + 40096, 32)
            hfo = [VF(M0 + 40224 + i * 1024, 256) for i in range(3)]
            hs = [VF(M0 + 43296 + i * 1024, 256) for i in range(2)]
            hn = [VB(M0 + 45344 + i * 512, 256) for i in range(2)]
            hT2 = [VB(M0 + 46368 + i * 512, 256).rearrange("p (k n) -> p k n", k=2) for i in range(2)]
            stm = [T_once("stm%d" % i, [128, 8], F32) for i in range(2)]; r_stm = [RES("stm%d" % i) for i in range(2)]
            assert M0 + 47392 <= AR * 4
            r_QT, r_KT, r_Qp, r_Ktok, r_Kp, r_V, r_bbc = (RES(n) for n in ("QT", "KT", "Qp", "Ktok", "Kp", "Vaug", "bbc"))
            r_tmpE = [RES("tmpE%d" % i) for i in range(2)]
            r_C, r_Cbf, r_aC = RES("Cst"), RES("Cbf"), RES("aC")
            r_Dt = [RES("Dt%d" % i) for i in range(2)]; r_Sp = [RES("Sp%d" % i) for i in range(2)]
            r_hfo = [RES("hfo%d" % i) for i in range(3)]; r_hs = [RES("hs%d" % i) for i in range(2)]
            r_hn = [RES("hn%d" % i) for i in range(2)]; r_hT2 = [RES("hT2_%d" % i) for i in range(2)]
            r_hf = [RES("@hf%d" % c) for c in range(NT)]
            r_hm = RES("@hmT")
            mst = {"b": 0}

            def mbank():
                i = 5 + mst["b"]
                mst["b"] = (mst["b"] + 1) % 3
                return banks[i], r_bank[i]

            def m_transpose(dst3, src_ap, nk, reads, r_dst):
                bk, rb = mbank()
                pbt = bk.bitcast(BF16)
                for k in range(nk):
                    transp(pbt[:, k * 128:(k + 1) * 128], src_ap[:, k * 128:(k + 1) * 128], identb[:], reads + [r_const], rb, signal=(k == nk - 1))
                copy_op(evac_eng(), dst3, pbt[:, 0:nk * 128].rearrange("p (k n) -> p k n", k=nk), [rb], [r_dst])

            b3 = lambda: bbc.rearrange("p (t n) -> p t n", t=NT)

            def mlstm_gen():
                for h in range(4):
                    fw.dma("sp", QT, qk_scr[h], writes=[r_QT])
                    fw.dma("sp", KT, qk_scr[4 + h], writes=[r_KT])
                    fw.op("pool", lambda e: e.memset(Vaug[:, :, 256:258], 1.0), writes=[r_V])
                    fw.dma("sp", Vaug[:, :, 0:256], v_scr[:, h * 256:(h + 1) * 256].rearrange("(t p) n -> p t n", p=128), reads=[r_V], writes=[r_V])
                    yield
                    for c8 in range(8):
                        m_transpose(Ktok[:, c8 * 4:(c8 + 1) * 4, :], KT[:, c8 * 512:(c8 + 1) * 512], 4, [r_KT], r_Ktok)
                        yield
                    for d in range(2):
                        msk = maskf if d == 0 else maskb
                        far = 127 if d == 0 else 0
                        fw.dma("sp", bbc, nb_scr[d, h:h + 1, :].partition_broadcast(128).rearrange("p o n -> p (o n)"), writes=[r_bbc])
                        yield
                        fw.op("act", lambda e, far=far: e.activation(aC, bbc.rearrange("p (t n) -> p t n", t=NT)[:, :, far], AF.Exp), reads=[r_bbc], writes=[r_aC])
                        for blk in range(NBLK):
                            i = blk & 1
                            bs_ = slice(blk * 512, (blk + 1) * 512)
                            fw.op("act", lambda e, i=i, bs_=bs_: e.activation(tmpE[i], bbc[:, bs_], AF.Exp, bias=LN_SQ), reads=[r_bbc], writes=[r_tmpE[i]])
                            yield
                            fw.op("dve", lambda e, i=i, bs_=bs_: e.tensor_tensor(Qp[:, bs_], QT[:, bs_], tmpE[i], ALU.mult), reads=[r_QT, r_tmpE[i]], writes=[r_Qp])
                            yield
                        for q4 in range(4):
                            ts_ = slice(q4 * 8, (q4 + 1) * 8)
                            fw.op("pool", lambda e, msk=msk, ts_=ts_: e.tensor_tensor(b3()[:, ts_, :], b3()[:, ts_, :], msk[:].unsqueeze(1).to_broadcast([128, 8, 128]), ALU.add),
                                  reads=[r_bbc, r_const], writes=[r_bbc])
                            fw.op("dve", lambda e, d=d, h=h, ts_=ts_: e.tensor_tensor(Kp[:, ts_, :], Ktok[:, ts_, :], colsT[:, 2 + d, ts_, h].unsqueeze(2).to_broadcast([128, 8, 128]), ALU.mult),
                                  reads=[r_Ktok, r_const], writes=[r_Kp])
                            yield
                        fw.op("dve", lambda e: e.memset(Cst, 0.0), writes=[r_C])
                        fw.op("pool", lambda e: e.memset(Cbf, 0.0), writes=[r_Cbf])
                        yield
                        order = list(range(NT)) if d == 0 else list(range(NT - 1, -1, -1))
                        for n_, c in enumerate(order):
                            cs_ = slice(c * 128, (c + 1) * 128)
                            j = n_ & 1
                            k = n_ & 1
                            m3 = n_ % 3
                            stt, r_stt = stm[k], r_stm[k]
                            fw.op("act", lambda e, j=j, cs_=cs_, d=d, c=c, h=h: e.activation(Dt[j], bbc[:, cs_], AF.Exp, bias=colsT[:, d, c, h:h + 1]),
                                  reads=[r_bbc, r_const], writes=[r_Dt[j]])
                            bS, rbS = mbank()
                            mm(bS[:, 0:128], [(KT[:, cs_], QT[:, cs_])], [r_KT, r_QT], rbS)
                            if d == 1:
                                fw.dma("sp", hfo[m3], hf_scr[cs_, :], reads=[r_hf[c]], writes=[r_hfo[m3]])
                            yield
                            fw.op("dve", lambda e, j=j, bS=bS: e.tensor_tensor(Sp[j], bS[:, 0:128], Dt[j], ALU.mult), reads=[rbS, r_Dt[j]], writes=[r_Sp[j]])
                            yield
                            bN, rbN = mbank()
                            mm(bN[:, 0:257], [(Sp[j], Vaug[:, c, 0:257]), (Qp[:, cs_], Cbf[:, 0:257])], [r_Sp[j], r_V, r_Qp, r_Cbf], rbN)
                            bU, rbU = mbank()
                            mm(bU[:, 0:257], [(Kp[:, c, :], Vaug[:, c, 0:257])], [r_Kp, r_V], rbU)
                            yield
                            fw.op("dve", lambda e, c=c, bU=bU: e.scalar_tensor_tensor(Cst[:, 0:257], Cst[:, 0:257], aC[:, c:c + 1], bU[:, 0:257], ALU.mult, ALU.add),
                                  reads=[r_C, r_aC, rbU], writes=[r_C])
                            fw.op("act", lambda e, bN=bN, stt=stt: e.activation(stt[:, 0:1], bN[:, 256:257], AF.Abs), reads=[rbN], writes=[r_stt])
                            yield
                            fw.op("pool", lambda e: e.tensor_copy(Cbf[:, 0:257], Cst[:, 0:257]), reads=[r_C], writes=[r_Cbf])
                            fw.op("dve", lambda e, stt=stt: e.tensor_scalar_max(stt[:, 0:1], stt[:, 0:1], 1.0), reads=[r_stt], writes=[r_stt])
                            fw.op("dve", lambda e, stt=stt: e.reciprocal(stt[:, 1:2], stt[:, 0:1]), reads=[r_stt], writes=[r_stt])
                            yield
                            if d == 0:
                                fw.op("dve", lambda e, m3=m3, bN=bN, stt=stt: e.tensor_scalar(hfo[m3], bN[:, 0:256], stt[:, 1:2], None, ALU.mult), reads=[rbN, r_stt], writes=[r_hfo[m3]])
                                fw.dma("pool", hf_scr[cs_, :], hfo[m3], reads=[r_hfo[m3]], writes=[r_hf[c]])
                                yield
                            else:
                                fw.op("dve", lambda e, m3=m3, k=k, bN=bN, stt=stt: e.scalar_tensor_tensor(hs[k], bN[:, 0:256], stt[:, 1:2], hfo[m3], ALU.mult, ALU.add),
                                      reads=[rbN, r_stt, r_hfo[m3]], writes=[r_hs[k]])
                                yield
                                fw.op("act", lambda e, k=k, stt=stt: e.activation(hn[k], hs[k], AF.Square, accum_out=stt[:, 2:3]), reads=[r_hs[k]], writes=[r_hn[k], r_stt])
                                yield
                                fw.op("act", lambda e, stt=stt: e.activation(stt[:, 3:4], stt[:, 2:3], AF.Sqrt, bias=EPS, scale=1.0 / 256), reads=[r_stt], writes=[r_stt])
                                yield
                                fw.op("dve", lambda e, stt=stt: e.reciprocal(stt[:, 2:3], stt[:, 3:4]), reads=[r_stt], writes=[r_stt])
                                yield
                                fw.op("dve", lambda e, k=k, stt=stt: e.tensor_scalar(hn[k], hs[k], stt[:, 2:3], None, ALU.mult), reads=[r_hs[k], r_stt], writes=[r_hn[k]])
                                yield
                                m_transpose(hT2[k], hn[k], 2, [r_hn[k]], r_hT2[k])
                                fw.dma("pool", hmT_scr[2 * h:2 * h + 2, :, cs_].rearrange("k p n -> p k n"), hT2[k], reads=[r_hT2[k]], writes=[r_hm])
                                yield

            gen = mlstm_gen()
            gst = {"done": False, "n": 0, "calls": 0}

            def advance():
                gst["calls"] += 1
                for _ in range(2 if gst["calls"] % 8 == 0 else 1):
                    if not gst["done"]:
                        try:
                            next(gen)
                            gst["n"] += 1
                        except StopIteration:
                            gst["done"] = True

            def wload0(name, W_ap, k0, nk, c0, ncols, off):
                view = wsl[0][:, off:off + nk * ncols].rearrange("p (k n) -> p k n", k=nk)
                src = W_ap[k0 * 128:(k0 + nk) * 128, c0:c0 + ncols].rearrange("(k p) n -> p k n", p=128)
                fw.dma("sp", view, src, reads=[r_W[name]], writes=[r_wsl[0]])
                return view, r_wsl[0]

            fw.dma("sp", cqnT, cqnT_scr.rearrange("k p n -> p k n"), writes=[r_cqn])
            fw.dma("sp", ckvnT, ckvnT_scr.rearrange("k p n -> p k n"), writes=[r_ckvn])
            fw.op("pool", lambda e: e.memset(krT[64:128, :], 0.0), writes=[r_krT])
            fw.op("pool", lambda e: e.memset(QrT[64:128, :], 0.0), writes=[r_QrT])
            fw.dma("sp", krT[0:64, :], krT_scr, reads=[r_krT], writes=[r_krT])
            abank = {"i": 0}

            def xbank():
                i = abank["i"]
                abank["i"] = (i + 1) % 5
                return banks[i], r_bank[i]

            for h in range(8):
                wk, rwk = wload0("w_ukv", Wb["w_ukv"], 0, 2, h * 256, 256, 0)
                wq, rwq = wload0("w_uq", Wb["w_uq"], 0, 2, h * 192, 192, 512)
                wqs, rwqs = wload0("w_uq_sw", w_uq_sw, 0, 2, h * 64, 64, 896)
                for b in range(NBLK):
                    bs_ = slice(b * 512, (b + 1) * 512)
                    bk, rb = xbank()
                    mm(bk[:, :], [(wk[:, kc, 0:128], ckvnT[:, kc, bs_]) for kc in range(2)], [rwk, r_ckvn], rb)
                    copy_op(evac_eng(), KhT[:, bs_], bk[:, :], [rb], [r_KhT])
                    bk, rb = xbank()
                    mm(bk[:, :], [(wq[:, kc, 0:128], cqnT[:, kc, bs_]) for kc in range(2)], [rwq, r_cqn], rb)
                    copy_op(evac_eng(), QhT[:, bs_], bk[:, :], [rb], [r_QhT])
                    bka, rba = xbank()
                    mm(bka[0:64, :], [(wq[:, kc, 128:192], cqnT[:, kc, bs_]) for kc in range(2)], [rwq, r_cqn], rba)
                    bkb, rbb = xbank()
                    mm(bkb[0:64, :], [(wqs[:, kc, :], cqnT[:, kc, bs_]) for kc in range(2)], [rwqs, r_cqn], rbb)
                    cq_, r_cq_ = cstq2[b & 1], r_cstq2[b & 1]
                    if b == 0:
                        fw.dma("sp", cq_, cs_scr[:, :, bs_].rearrange("c p n -> p c n"), writes=[r_cq_])
                    if b + 1 < NBLK:
                        nb_ = slice((b + 1) * 512, (b + 2) * 512)
                        fw.dma("sp", cstq2[(b + 1) & 1], cs_scr[:, :, nb_].rearrange("c p n -> p c n"), writes=[r_cstq2[(b + 1) & 1]])
                    fw.op("dve", lambda e, bka=bka, cq_=cq_: e.tensor_tensor(rt[0], bka[0:64, :], cq_[:, 0, :], ALU.mult), reads=[rba, r_cq_], writes=[r_rt])
                    fw.op("dve", lambda e, bkb=bkb, cq_=cq_: e.tensor_tensor(rt[1], bkb[0:64, :], cq_[:, 1, :], ALU.mult), reads=[rbb, r_cq_], writes=[r_rt])
                    fw.op("pool", lambda e, bs_=bs_: e.tensor_tensor(QrT[0:64, bs_], rt[0], rt[1], ALU.add), reads=[r_rt], writes=[r_QrT])
                    bk, rb = xbank()
                    for tt in range(4):
                        ts_ = slice(b * 512 + tt * 128, b * 512 + (tt + 1) * 128)
                        mm(bk[:, tt * 128:(tt + 1) * 128], [(ckvnT[:, kc, ts_], wk[:, kc, 128:256]) for kc in range(2)], [rwk, r_ckvn], rb)
                    copy_op(evac_eng(), Vh[:, bs_], bk[:, :], [rb], [r_Vh])
                    advance()
                for qb in range(NBLK):
                    qs_ = slice(qb * 512, (qb + 1) * 512)
                    bo, rbo = banks[3], r_bank[3]
                    bsm, rbsm = banks[4], r_bank[4]

                    def S_step(kc, qs_=qs_):
                        ks_ = slice(kc * 128, (kc + 1) * 128)
                        bS, rbS = banks[kc % 3], r_bank[kc % 3]
                        mm(bS[:, :], [(KhT[:, ks_], QhT[:, qs_]), (krT[:, ks_], QrT[:, qs_])], [r_KhT, r_QhT, r_krT, r_QrT], rbS)
                        fw.op("act", lambda e, bS=bS, kc=kc: e.activation(PT[kc % 3], bS[:, :], AF.Exp, scale=A_SCALE), reads=[rbS], writes=[r_PT[kc % 3]])

                    def O_step(kc, bo=bo, rbo=rbo, bsm=bsm, rbsm=rbsm):
                        ks_ = slice(kc * 128, (kc + 1) * 128)
                        fw.op("pe", lambda e, kc=kc, ks_=ks_, bo=bo: e.matmul(bo[:, :], Vh[:, ks_], PT[kc % 3], start=(kc == 0), stop=(kc == NT - 1)),
                              reads=[r_Vh, r_PT[kc % 3]], writes=[rbo], signal=False)
                        fw.op("pe", lambda e, kc=kc, bsm=bsm: e.matmul(bsm[:, :], onesb[:], PT[kc % 3], start=(kc == 0), stop=(kc == NT - 1)),
                              reads=[r_const, r_PT[kc % 3]], writes=[rbsm], signal=True)
                    S_step(0)
                    S_step(1)
                    for kc in range(NT):
                        if kc + 2 < NT:
                            S_step(kc + 2)
                        O_step(kc)
                        advance()
                    fw.op("dve", lambda e, bsm=bsm: e.reciprocal(rsum, bsm[:, :]), reads=[rbsm], writes=[r_rsum])
                    o_ = oTs[qb & 1]
                    fw.op("dve", lambda e, bo=bo, o_=o_: e.tensor_tensor(o_, bo[:, :], rsum, ALU.mult), reads=[rbo, r_rsum], writes=[r_oTs[qb & 1]])
                    fw.dma("pool", oT_scr[h, :, qs_], o_, reads=[r_oTs[qb & 1]], writes=[r_oT])
            n_inside = gst["n"]
            while not gst["done"]:
                advance()
            mix_stats.append((n_inside, gst["n"]))

        def phase_attn(sq, heads=range(8), main_loop=True, qbs=None):
            cqnT = VB(0, 8192).rearrange("p (k n) -> p k n", k=2); r_cqn = RES("cqnT")
            ckvnT = VB(16384, 8192).rearrange("p (k n) -> p k n", k=2); r_ckvn = RES("ckvnT")
            krT = VB(32768, 4096, 128); r_krT = RES("krT")
            KhT = VB(40960, 4096); r_KhT = RES("KhT")
            Vh = VB(49152, 4096); r_Vh = RES("Vh")
            QhT = VB(57344, 4096); r_QhT = RES("QhT")
            QrT = VB(65536, 4096, 128); r_QrT = RES("QrT")
            cstq = VF(73728, 1024, 64).rearrange("p (c n) -> p c n", c=2); r_cstq = RES("cstq")
            rt = [VF(77824 + i * 2048, 512, 64) for i in range(2)]; r_rt = RES("rt")
            PT = [VB(81920 + i * 1024, 512) for i in range(3)]; r_PT = [RES("PT%d" % i) for i in range(3)]
            rsum = VF(84992, 512); r_rsum = RES("rsum")
            oTs = [VB(87040 + i * 1024, 512) for i in range(2)]; r_oTs = [RES("oTs%d" % i) for i in range(2)]
            r_oT = RES("@oT")
            fw.dma("sp", cqnT, cqnT_scr.rearrange("k p n -> p k n"), writes=[r_cqn])
            fw.dma("sp", ckvnT, ckvnT_scr.rearrange("k p n -> p k n"), writes=[r_ckvn])
            fw.op("pool", lambda e: e.memset(krT[64:128, :], 0.0), writes=[r_krT])
            fw.op("pool", lambda e: e.memset(QrT[64:128, :], 0.0), writes=[r_QrT])
            fw.dma("sp", krT[0:64, :], krT_scr, reads=[r_krT], writes=[r_krT])
            for h in heads:
                wk, rwk = wload("w_ukv", Wb["w_ukv"], 0, 2, h * 256, 256)
                wq, rwq = wload("w_uq", Wb["w_uq"], 0, 2, h * 192, 192)
                wqs, rwqs = wload("w_uq_sw", w_uq_sw, 0, 2, h * 64, 64)
                for b in range(NBLK):
                    bs_ = slice(b * 512, (b + 1) * 512)
                    bk, rb = nbank()
                    mm(bk[:, :], [(wk[:, kc, 0:128], ckvnT[:, kc, bs_]) for kc in range(2)], [rwk, r_ckvn], rb)
                    copy_op(evac_eng(), KhT[:, bs_], bk[:, :], [rb], [r_KhT])
                    bk, rb = nbank()
                    mm(bk[:, :], [(wq[:, kc, 0:128], cqnT[:, kc, bs_]) for kc in range(2)], [rwq, r_cqn], rb)
                    copy_op(evac_eng(), QhT[:, bs_], bk[:, :], [rb], [r_QhT])
                    bka, rba = nbank()
                    mm(bka[0:64, :], [(wq[:, kc, 128:192], cqnT[:, kc, bs_]) for kc in range(2)], [rwq, r_cqn], rba)
                    bkb, rbb = nbank()
                    mm(bkb[0:64, :], [(wqs[:, kc, :], cqnT[:, kc, bs_]) for kc in range(2)], [rwqs, r_cqn], rbb)
                    fw.dma("sp", cstq, cs_scr[:, :, bs_].rearrange("c p n -> p c n"), writes=[r_cstq])
                    fw.op("dve", lambda e, bka=bka: e.tensor_tensor(rt[0], bka[0:64, :], cstq[:, 0, :], ALU.mult), reads=[rba, r_cstq], writes=[r_rt])
                    fw.op("dve", lambda e, bkb=bkb: e.tensor_tensor(rt[1], bkb[0:64, :], cstq[:, 1, :], ALU.mult), reads=[rbb, r_cstq], writes=[r_rt])
                    fw.op("pool", lambda e, bs_=bs_: e.tensor_tensor(QrT[0:64, bs_], rt[0], rt[1], ALU.add), reads=[r_rt], writes=[r_QrT])
                    bk, rb = nbank()
                    for tt in range(4):
                        ts_ = slice(b * 512 + tt * 128, b * 512 + (tt + 1) * 128)
                        mm(bk[:, tt * 128:(tt + 1) * 128], [(ckvnT[:, kc, ts_], wk[:, kc, 128:256]) for kc in range(2)], [rwk, r_ckvn], rb)
                    copy_op(evac_eng(), Vh[:, bs_], bk[:, :], [rb], [r_Vh])
                for qb in ((qbs if qbs is not None else range(NBLK)) if main_loop else ()):
                    qs_ = slice(qb * 512, (qb + 1) * 512)
                    bo, rbo = banks[4 + (qb & 1)], r_bank[4 + (qb & 1)]
                    bsm, rbsm = banks[6 + (qb & 1)], r_bank[6 + (qb & 1)]

                    def S_step(kc):
                        ks_ = slice(kc * 128, (kc + 1) * 128)
                        bS, rbS = banks[kc % 4], r_bank[kc % 4]
                        mm(bS[:, :], [(KhT[:, ks_], QhT[:, qs_]), (krT[:, ks_], QrT[:, qs_])], [r_KhT, r_QhT, r_krT, r_QrT], rbS)
                        fw.op("act", lambda e, bS=bS, kc=kc: e.activation(PT[kc % 3], bS[:, :], AF.Exp, scale=A_SCALE), reads=[rbS], writes=[r_PT[kc % 3]])

                    def O_step(kc):
                        ks_ = slice(kc * 128, (kc + 1) * 128)
                        fw.op("pe", lambda e, kc=kc, ks_=ks_, bo=bo: e.matmul(bo[:, :], Vh[:, ks_], PT[kc % 3], start=(kc == 0), stop=(kc == NT - 1)),
                              reads=[r_Vh, r_PT[kc % 3]], writes=[rbo], signal=False)
                        fw.op("pe", lambda e, kc=kc, bsm=bsm: e.matmul(bsm[:, :], onesb[:], PT[kc % 3], start=(kc == 0), stop=(kc == NT - 1)),
                              reads=[r_const, r_PT[kc % 3]], writes=[rbsm], signal=True)
                    S_step(0)
                    S_step(1)
                    for kc in range(NT):
                        if kc + 2 < NT:
                            S_step(kc + 2)
                        O_step(kc)
                    fw.op("dve", lambda e, bsm=bsm: e.reciprocal(rsum, bsm[:, :]), reads=[rbsm], writes=[r_rsum])
                    o_ = oTs[qb & 1]
                    fw.op("dve", lambda e, bo=bo, o_=o_: e.tensor_tensor(o_, bo[:, :], rsum, ALU.mult), reads=[rbo, r_rsum], writes=[r_oTs[qb & 1]])
                    fw.dma("pool", oT_scr[h, :, qs_], o_, reads=[r_oTs[qb & 1]], writes=[r_oT])
            if dbg == "M1":
                fw.dma("pool", dbg_out["d_oT"], oTs[0], reads=[r_oTs[0]], writes=[RES("@dbgM1")])
            if isinstance(dbg, str) and dbg.startswith("M:"):
                fw.dma("pool", dbg_out["d_oTs0"], oTs[0], reads=[r_oTs[0]], writes=[RES("@dbgMa")])
                fw.dma("pool", dbg_out["d_oTs1"], oTs[1], reads=[r_oTs[1]], writes=[RES("@dbgMb")])
            if dbg == "E":
                rd = RES("@dbgE")
                for n_, ap_, r_ in (("d_cqnT", cqnT, r_cqn), ("d_ckvnT", ckvnT, r_ckvn), ("d_krT", krT[0:64, :], r_krT),
                                    ("d_KhT", KhT, r_KhT), ("d_QhT", QhT, r_QhT), ("d_QrT", QrT[0:64, :], r_QrT), ("d_Vh", Vh, r_Vh)):
                    fw.dma("pool", dbg_out[n_], ap_, reads=[r_], writes=[rd])

        for sq in range(NSEQ):
            if 1 in phases:
                phase1(sq)
                fw.barrier()
            if 5 in phases:
                phase_mix(sq)
                fw.barrier()
            if 2 in phases:
                phase_mlstm(sq, heads=([0] if dbg == "L1" else range(4)))
                fw.barrier()
            if 3 in phases:
                if dbg == "E":
                    phase_attn(sq, heads=[0], main_loop=False)
                elif dbg == "M1":
                    phase_attn(sq, heads=[0], main_loop=True, qbs=[0])
                elif isinstance(dbg, str) and dbg.startswith("M:"):
                    _, nh_, nq_ = dbg.split(":")
                    phase_attn(sq, heads=list(range(int(nh_))), main_loop=True, qbs=list(range(int(nq_))))
                else:
                    phase_attn(sq)
                fw.barrier()
            if 4 in phases:
                phase3(sq)
                fw.barrier()
        fw.barrier()
        fw.emit_all()
    if mix_stats:
        print("[phase_mix] scan generator steps issued inside attention / total, per sequence:", mix_stats)
    return nc


NSEQ_PER_CORE = 3


def _consts():
    ident = np.eye(128, dtype=np.float32)
    s = np.arange(128)[:, None]
    t = np.arange(128)[None, :]
    maskf = np.where(t >= s, 0.0, -1000.0).astype(np.float32)
    maskb = np.where(t <= s, 0.0, -1000.0).astype(np.float32)
    return {"c_ident": ident, "c_maskf": maskf, "c_maskb": maskb}


def kernel(**inputs):
    X = np.concatenate([inputs["x_prompt"], inputs["x_sample"]], axis=0)
    P = np.concatenate([inputs["p_prompt"][0], inputs["p_sample"][0]], axis=0)
    nseq = X.shape[0]
    base = {}
    for k in WSHAPES:
        base[k] = np.ascontiguousarray(inputs[k][0])
    for k in GSHAPES:
        base[k] = np.ascontiguousarray(inputs[k].reshape(1, -1))
    base["g_final"] = np.ascontiguousarray(inputs["g_final"].reshape(1, -1))
    base["b_gate"] = np.ascontiguousarray(inputs["b_gate"].reshape(1, 16))
    base["conv_w"] = np.ascontiguousarray(inputs["conv_w"][0])
    base.update(_consts())
    slots = [[c, c + 8, c + 16 if c + 16 < nseq else c] for c in range(8)]
    maps = []
    for sl in slots:
        m = dict(base)
        m["xs"] = np.ascontiguousarray(X[sl])
        m["ps"] = np.ascontiguousarray(P[sl])
        maps.append(m)
    nc = build(NSEQ_PER_CORE, dbg=False, phases=(1, 5, 4))
    res = run_bass_kernel_spmd(nc, maps, core_ids=list(range(8)))
    Y = np.zeros((nseq, S, D), dtype=np.float32)
    for c, sl in enumerate(slots):
        yc = res.results[c]["y"]
        for j, sidx in enumerate(sl):
            if j == 2 and c + 16 >= nseq:
                continue
            Y[sidx] = yc[j]
    nb = inputs["x_prompt"].shape[0]
    return (np.ascontiguousarray(Y[:nb], dtype=np.float32), np.ascontiguousarray(Y[nb:], dtype=np.float32))
```

```python
import numpy as np
import concourse.bass as bass
import concourse.mybir as mybir
from concourse.bass_utils import run_bass_kernel_spmd

F32 = mybir.dt.float32
BF16 = mybir.dt.bfloat16
I32 = mybir.dt.int32
AF = mybir.ActivationFunctionType
ALU = mybir.AluOpType
AX = mybir.AxisListType


class Sem:
    def __init__(self, handle, name, dma=False, owner=None):
        self.h = handle
        self.name = name
        self.dma = dma
        self.owner = owner
        self.issued = 0


class Res:
    __slots__ = ("name", "w", "r", "dsem")

    def __init__(self, name):
        self.name = name
        self.w = None
        self.r = {}
        self.dsem = None


class Eng:
    def __init__(self, name, sem, is_pe=False):
        self.name = name
        self.sem = sem
        self.ops = []
        self.seen = {}
        self.is_pe = is_pe
        self.pending = False


class FW:
    def __init__(self, nc, stack):
        self.nc = nc
        self.stack = stack
        self.engs = {}
        self.nsem = 0
        self.all_sems = []
        for n in ("pe", "act", "dve", "pool", "sp"):
            self.engs[n] = Eng(n, None, is_pe=(n == "pe"))
            self.engs[n].sem = self._newsem("s_" + n, owner=self.engs[n])
        self.dma_res = []
        self.n_ins = 0

    SEM_LIMIT = 6000

    def _newsem(self, name, dma=False, owner=None):
        self.nsem += 1
        h = self.stack.enter_context(self.nc.semaphore("%s_%d" % (name, self.nsem)))
        sm = Sem(h, name, dma=dma, owner=owner)
        self.all_sems.append(sm)
        return sm

    def dsem(self, res, qclass):
        if res.dsem is None:
            res.dsem = {}
            self.dma_res.append(res)
        sm = res.dsem.get(qclass)
        if sm is None or sm.issued >= self.SEM_LIMIT:
            sm = res.dsem[qclass] = self._newsem("d%s_%s" % (qclass[0], res.name.replace("@", "h_")), dma=True)
        return sm

    def _collect(self, E, reads, writes):
        waits = {}

        def need(ev):
            if ev is None:
                return
            sem, val = ev
            if sem.owner is E:
                if E.is_pe:
                    return
            else:
                if sem.dma and sem.issued > val:
                    val = sem.issued
            if E.seen.get(sem, 0) >= val:
                return
            if waits.get(sem, 0) < val:
                waits[sem] = val

        for r in reads:
            need(r.w)
        for w in writes:
            need(w.w)
            for s, v in w.r.items():
                need((s, v))
        for s, v in waits.items():
            E.seen[s] = v
        return list(waits.items())

    def _commit(self, ev, reads, writes):
        for w in writes:
            w.w = ev
            w.r = {}
        for r in reads:
            if r in writes:
                continue
            s, v = ev
            if r.r.get(s, 0) < v:
                r.r[s] = v

    def op(self, eng, fn, reads=(), writes=(), signal=True):
        E = self.engs[eng]
        waits = self._collect(E, reads, writes)
        if signal:
            E.sem.issued += 1
        val = E.sem.issued if signal else E.sem.issued + 1
        sem_h = E.sem.h

        def emit(e, waits=waits, fn=fn, signal=signal):
            for s, v in waits:
                e.wait_ge(s.h, v)
            ins = fn(e)
            if signal:
                ins.then_inc(sem_h, 1)
        E.ops.append(emit)
        E.pending = not signal
        self._commit((E.sem, val), reads, writes)
        self.n_ins += 1
        if signal and E.sem.issued >= self.SEM_LIMIT:
            E.sem = self._newsem("s_" + E.name, owner=E)

    def dma(self, eng, out, in_, reads=(), writes=(), sem_res=None, **kw):
        E = self.engs[eng]
        waits = self._collect(E, reads, writes)
        if sem_res is None:
            sem_res = writes[0] if (writes and writes[0].name[0] != "@") else reads[0]
        ds = self.dsem(sem_res, "sw" if eng == "pool" else "hw")
        ds.issued += 16
        val = ds.issued

        def emit(e, waits=waits):
            for s, v in waits:
                e.wait_ge(s.h, v)
            e.dma_start(out=out, in_=in_, **kw).then_inc(ds.h, 16)
        E.ops.append(emit)
        self._commit((ds, val), reads, writes)
        self.n_ins += 1

    def barrier(self):
        for E in self.engs.values():
            assert not E.pending
        evs = [(sm, sm.issued) for sm in self.all_sems if sm.issued > 0]
        for E in self.engs.values():
            waits = []
            for s, v in evs:
                if s.owner is E and E.is_pe:
                    continue
                if E.seen.get(s, 0) < v:
                    waits.append((s, v))
                    E.seen[s] = v

            def emit(e, waits=waits):
                for s, v in waits:
                    e.wait_ge(s.h, v)
            E.ops.append(emit)

    def wait_all(self, eng, resources):
        E = self.engs[eng]
        waits = self._collect(E, resources, [])

        def emit(e, waits=waits):
            for s, v in waits:
                e.wait_ge(s.h, v)
        E.ops.append(emit)

    def emit_all(self):
        nc = self.nc
        with nc.Block() as block:
            @block.tensor
            def _(e):
                for f in self.engs["pe"].ops:
                    f(e)

            @block.scalar
            def _(e):
                for f in self.engs["act"].ops:
                    f(e)

            @block.vector
            def _(e):
                for f in self.engs["dve"].ops:
                    f(e)

            @block.gpsimd
            def _(e):
                for f in self.engs["pool"].ops:
                    f(e)

            @block.sync
            def _(e):
                for f in self.engs["sp"].ops:
                    f(e)


import math
from contextlib import ExitStack

S = 4096
D = 1024
NBLK = 8
NT = 32
DFF = 2816
V0, O0, G0, CQ0, CKV0, KR0, GM0, INW = 1024, 2048, 3072, 3088, 3344, 3600, 3664, 5712
EPS = 1e-6
LN_SQ = math.log(128 ** -0.5)
A_SCALE = 192 ** -0.5

WSHAPES = {
    "w_in": (1024, 5712), "w_mo": (1024, 1024), "w_uq": (256, 1536), "w_ukv": (256, 2048),
    "w_ao": (1024, 1024), "w_out": (1024, 1024), "w_gate": (1024, 2816), "w_up": (1024, 2816),
    "w_down": (2816, 1024), "w_ple": (256, 1024), "w_ple_gate": (1024, 1024),
}
GAINS = {"w_in": "g_mix", "w_mo": "g_mhead", "w_uq": "g_q", "w_ukv": "g_kv", "w_gate": "g_ffn", "w_up": "g_ffn"}
GSHAPES = {"g_mix": 1024, "g_mhead": 1024, "g_q": 256, "g_kv": 256, "g_ffn": 1024}


def build(NSEQ, dbg=False, phases=(1, 2, 3, 4)):
    nc = bass.Bass("TRN2", target_bir_lowering=False)
    din = lambda n, sh, dt=F32: nc.dram_tensor(n, list(sh), dt, kind="ExternalInput").ap()
    dscr = lambda n, sh, dt: nc.dram_tensor(n, list(sh), dt, kind="Internal").ap()
    xs = din("xs", [NSEQ, S, D])
    ps = din("ps", [NSEQ, S, 256])
    Wf = {k: din(k, sh) for k, sh in WSHAPES.items()}
    Gd = {k: din(k, [1, n]) for k, n in GSHAPES.items()}
    g_final = din("g_final", [1, D])
    b_gate = din("b_gate", [1, 16])
    conv_w = din("conv_w", [5, 1024])
    c_ident = din("c_ident", [128, 128])
    c_maskf = din("c_maskf", [128, 128])
    c_maskb = din("c_maskb", [128, 128])
    y = nc.dram_tensor("y", [NSEQ, S, D], F32, kind="ExternalOutput").ap()
    dbg_out = {}
    if dbg == "P1":
        dbg_out["d_colsT"] = nc.dram_tensor("d_colsT", [128, 512], F32, kind="ExternalOutput").ap()
    if dbg == "M1":
        dbg_out["d_oT"] = nc.dram_tensor("d_oT", [128, 512], BF16, kind="ExternalOutput").ap()
    if isinstance(dbg, str) and dbg.startswith("M:"):
        dbg_out["d_oTs0"] = nc.dram_tensor("d_oTs0", [128, 512], BF16, kind="ExternalOutput").ap()
        dbg_out["d_oTs1"] = nc.dram_tensor("d_oTs1", [128, 512], BF16, kind="ExternalOutput").ap()
    if dbg == "E":
        for n_, sh_ in (("d_cqnT", [128, 2, S]), ("d_ckvnT", [128, 2, S]), ("d_krT", [64, S]),
                        ("d_KhT", [128, S]), ("d_QhT", [128, S]), ("d_QrT", [64, S]), ("d_Vh", [128, S])):
            dbg_out[n_] = nc.dram_tensor(n_, sh_, BF16, kind="ExternalOutput").ap()
    Wb = {k: dscr(k + "_b", sh, BF16) for k, sh in WSHAPES.items()}
    w_kr_sw = dscr("w_kr_sw", [1024, 64], BF16)
    w_uq_sw = dscr("w_uq_sw", [256, 512], BF16)
    cs_scr = dscr("cs_scr", [2, 64, S], F32)
    qk_scr = (nc.dram_tensor("qk_scr", [8, 128, S], BF16, kind="ExternalOutput").ap() if dbg == "P1" else dscr("qk_scr", [8, 128, S], BF16))
    v_scr = (nc.dram_tensor("v_scr", [S, 1024], BF16, kind="ExternalOutput").ap() if dbg == "P1" else dscr("v_scr", [S, 1024], BF16))
    nb_scr = (nc.dram_tensor("nb_scr", [2, 4, S], F32, kind="ExternalOutput").ap() if dbg == "P1" else dscr("nb_scr", [2, 4, S], F32))
    hf_scr = dscr("hf_scr", [S, 256], F32)
    hmT_scr = (nc.dram_tensor("hmT_scr", [8, 128, S], BF16, kind="ExternalOutput").ap() if dbg == "L1" else dscr("hmT_scr", [8, 128, S], BF16))
    if isinstance(dbg, str) and dbg.startswith("M:"):
        oT_scr = nc.dram_tensor("oT_scr", [8, 128, S], BF16, kind="ExternalOutput").ap()
    else:
        oT_scr = dscr("oT_scr", [8, 128, S], BF16)
    cqnT_scr = dscr("cqnT_scr", [2, 128, S], BF16)
    ckvnT_scr = dscr("ckvnT_scr", [2, 128, S], BF16)
    krT_scr = dscr("krT_scr", [64, S], BF16)

    _res_registry = {}
    _ResClass = Res

    def RES(name):
        r = _res_registry.get(name)
        if r is None:
            r = _res_registry[name] = _ResClass(name)
        return r

    with ExitStack() as st:
        fw = FW(nc, st)
        T = lambda name, shape, dt: st.enter_context(nc.sbuf_tensor(name, shape, dt))
        identf = T("identf", [128, 128], F32)
        identb = T("identb", [128, 128], BF16)
        maskf = T("maskf", [128, 128], F32)
        maskb = T("maskb", [128, 128], F32)
        onesb = T("onesb", [128, 128], BF16)
        gfin = T("gfin", [128, D], F32)
        convw = T("convw", [128, 5, 8], F32)
        gcol = {k: T("gc_" + k, [128, n // 128], F32) for k, n in GSHAPES.items()}
        gneg = {k: T("gn_" + k, [128, GSHAPES[k] // 128], F32) for k in ("g_mix", "g_q")}
        bg = T("bg", [4, 4], F32)
        colsT = T("colsT", [128, 4, NT, 4], F32)
        wsl = [T("wsl%d" % i, [128, 4096], BF16) for i in range(6)]
        r_wsl = [RES("wsl%d" % i) for i in range(6)]
        AR = 34816
        arena = T("arena", [128, AR], F32)
        arena_b = arena.bitcast(BF16)
        arena_i = arena.bitcast(I32)
        banks = [st.enter_context(nc.psum_tensor("pb%d" % i, [128, 512], F32)) for i in range(8)]
        r_bank = [RES("pb%d" % i) for i in range(8)]
        r_const = RES("const")
        state = {"bank": 0, "wsl": 0, "ev": 0}

        def VF(off, n, parts=128):
            assert off % 4 == 0 and off // 4 + n <= AR
            return arena[0:parts, off // 4: off // 4 + n]

        def VB(off, n, parts=128):
            assert off % 2 == 0 and off // 2 + n <= 2 * AR
            return arena_b[0:parts, off // 2: off // 2 + n]

        def nbank():
            i = state["bank"]
            state["bank"] = (i + 1) % 8
            return banks[i], r_bank[i]

        def evac_eng():
            state["ev"] ^= 1
            return "act" if state["ev"] else "dve"

        def copy_op(eng, out, in_, reads, writes):
            if eng == "act":
                fw.op("act", lambda e: e.copy(out, in_), reads=reads, writes=writes)
            else:
                fw.op(eng, lambda e: e.tensor_copy(out, in_), reads=reads, writes=writes)

        def mm(out_ap, pairs, reads, wres):
            n = len(pairs)
            for i, (l, r) in enumerate(pairs):
                fw.op("pe", lambda e, l=l, r=r, i=i: e.matmul(out_ap, l, r, start=(i == 0), stop=(i == n - 1)),
                      reads=reads, writes=[wres], signal=(i == n - 1))

        def transp(out_ap, in_ap, ident, reads, wres, signal=True):
            fw.op("pe", lambda e: e.transpose(out_ap, in_ap, ident), reads=reads, writes=[wres], signal=signal)

        r_W = {k: RES("@" + k) for k in WSHAPES}
        r_W["w_kr_sw"] = RES("@w_kr_sw")
        r_W["w_uq_sw"] = RES("@w_uq_sw")

        def wload(name, W_ap, k0, nk, c0, ncols):
            i = state["wsl"]
            state["wsl"] = (i + 1) % 6
            assert nk * ncols <= 4096
            view = wsl[i][:, 0:nk * ncols].rearrange("p (k n) -> p k n", k=nk)
            src = W_ap[k0 * 128:(k0 + nk) * 128, c0:c0 + ncols].rearrange("(k p) n -> p k n", p=128)
            fw.dma("sp", view, src, reads=[r_W[name]], writes=[r_wsl[i]])
            return view, r_wsl[i]

        fw.dma("sp", identf[:], c_ident, writes=[r_const])
        fw.dma("sp", maskf[:], c_maskf, writes=[r_const])
        fw.dma("sp", maskb[:], c_maskb, writes=[r_const])
        fw.dma("sp", gfin[:], g_final.partition_broadcast(128).rearrange("p o n -> p (o n)"), writes=[r_const])
        for j in range(5):
            fw.dma("sp", convw[:, j, :], conv_w[j:j + 1, :].rearrange("o (c p) -> p (o c)", p=128), writes=[r_const], allow_slow_non_contiguous=True)
        for k in GSHAPES:
            fw.dma("sp", gcol[k][:], Gd[k].rearrange("o (k p) -> p (o k)", p=128), writes=[r_const], allow_slow_non_contiguous=True)
        fw.dma("sp", bg[:], b_gate.rearrange("o (g h) -> h (o g)", h=4), writes=[r_const], allow_slow_non_contiguous=True)
        fw.op("dve", lambda e: e.tensor_copy(identb[:], identf[:]), reads=[r_const], writes=[r_const])
        fw.op("dve", lambda e: e.memset(onesb[:], 1.0), writes=[r_const])
        for k in gneg:
            fw.op("dve", lambda e, k=k: e.tensor_scalar(gneg[k][:], gcol[k][:], -1.0, None, ALU.mult), reads=[r_const], writes=[r_const])
        fw.barrier()

        stg_in = [VF(i * 8192, 2048) for i in range(4)]
        stg_out = [VB(32768 + i * 4096, 2048) for i in range(4)]
        r_si = [RES("stg_in%d" % i) for i in range(4)]
        r_so = [RES("stg_out%d" % i) for i in range(4)]
        pw = {"i": 0}

        def prep(name):
            K, N = WSHAPES[name]
            g = gcol[GAINS[name]] if name in GAINS else None
            for kc in range(K // 128):
                for c0 in range(0, N, 2048):
                    n_ = min(2048, N - c0)
                    i = pw["i"] & 3
                    pw["i"] += 1
                    fw.dma("sp", stg_in[i][:, 0:n_], Wf[name][kc * 128:(kc + 1) * 128, c0:c0 + n_], writes=[r_si[i]])
                    eng = evac_eng()
                    if g is None:
                        copy_op(eng, stg_out[i][:, 0:n_], stg_in[i][:, 0:n_], [r_si[i]], [r_so[i]])
                    elif eng == "act":
                        fw.op("act", lambda e, i=i, kc=kc, n_=n_, g=g: e.activation(stg_out[i][:, 0:n_], stg_in[i][:, 0:n_], AF.Copy, scale=g[:, kc:kc + 1]),
                              reads=[r_si[i], r_const], writes=[r_so[i]])
                    else:
                        fw.op("dve", lambda e, i=i, kc=kc, n_=n_, g=g: e.tensor_scalar(stg_out[i][:, 0:n_], stg_in[i][:, 0:n_], g[:, kc:kc + 1], None, ALU.mult),
                              reads=[r_si[i], r_const], writes=[r_so[i]])
                    fw.dma("pool", Wb[name][kc * 128:(kc + 1) * 128, c0:c0 + n_], stg_out[i][:, 0:n_], reads=[r_so[i]], writes=[RES("@%s_st%d" % (name, pw["i"]))])

        def prep_sw(dst, dname, src, K, bases, g, gn):
            for kc in range(K // 128):
                i = pw["i"] & 3
                pw["i"] += 1
                nb_ = len(bases)
                for j, b in enumerate(bases):
                    fw.dma("sp", stg_in[i][:, j * 64:(j + 1) * 64], src[kc * 128:(kc + 1) * 128, b:b + 64], writes=[r_si[i]])
                for j in range(nb_):
                    fw.op("dve", lambda e, i=i, j=j, kc=kc: e.tensor_scalar(stg_out[i][:, j * 64:j * 64 + 32], stg_in[i][:, j * 64 + 32:j * 64 + 64], gn[:, kc:kc + 1], None, ALU.mult),
                          reads=[r_si[i], r_const], writes=[r_so[i]])
                    fw.op("dve", lambda e, i=i, j=j, kc=kc: e.tensor_scalar(stg_out[i][:, j * 64 + 32:j * 64 + 64], stg_in[i][:, j * 64:j * 64 + 32], g[:, kc:kc + 1], None, ALU.mult),
                          reads=[r_si[i], r_const], writes=[r_so[i]])
                fw.dma("pool", dst[kc * 128:(kc + 1) * 128, 0:64 * nb_], stg_out[i][:, 0:64 * nb_], reads=[r_so[i]], writes=[RES("@%s_st%d" % (dname, pw["i"]))])

        for name in WSHAPES:
            prep(name)
        prep_sw(w_kr_sw, "w_kr_sw", Wf["w_in"], 1024, [KR0], gcol["g_mix"], gneg["g_mix"])
        prep_sw(w_uq_sw, "w_uq_sw", Wf["w_uq"], 256, [h * 192 + 128 for h in range(8)], gcol["g_q"], gneg["g_q"])

        r_tr = RES("trig")
        TB0 = 49152
        posf = VF(TB0, S, 64)
        tq = VF(TB0 + 16384, S, 64)
        ti = arena_i[0:64, (TB0 + 32768) // 4:(TB0 + 32768) // 4 + S]
        tk = VF(TB0 + 49152, S, 64)
        tu = VF(TB0 + 65536, S, 64)
        pidx = T("pidx", [64, 1], F32)
        invf = T("invf", [64, 1], F32)
        pidx_i = T("pidx_i", [64, 1], I32)
        fw.op("pool", lambda e: e.iota(ti, [[1, S]], base=0, channel_multiplier=0), writes=[r_tr])
        fw.op("pool", lambda e: e.iota(pidx_i[0:32, :], [[0, 1]], base=0, channel_multiplier=1), writes=[r_tr])
        fw.op("pool", lambda e: e.iota(pidx_i[32:64, :], [[0, 1]], base=0, channel_multiplier=1), writes=[r_tr])
        fw.op("dve", lambda e: e.tensor_copy(posf, ti), reads=[r_tr], writes=[r_tr])
        fw.op("dve", lambda e: e.tensor_copy(pidx[:], pidx_i[:]), reads=[r_tr], writes=[r_tr])
        fw.op("act", lambda e: e.activation(invf[:], pidx[:], AF.Exp, scale=-math.log(10000.0) / 32.0), reads=[r_tr], writes=[r_tr])
        fw.op("dve", lambda e: e.tensor_scalar(tq, posf, invf[:, 0:1], 1.0 / (2 * math.pi), ALU.mult, ALU.mult), reads=[r_tr], writes=[r_tr])
        for which, shift in ((1, 0.0), (0, 0.25)):
            fw.op("dve", lambda e, shift=shift: e.tensor_scalar(tu, tq, shift, None, ALU.add), reads=[r_tr], writes=[r_tr])
            fw.op("dve", lambda e: e.tensor_copy(ti, tu), reads=[r_tr], writes=[r_tr])
            fw.op("dve", lambda e: e.tensor_copy(tk, ti), reads=[r_tr], writes=[r_tr])
            fw.op("dve", lambda e: e.tensor_tensor(tu, tu, tk, ALU.subtract), reads=[r_tr], writes=[r_tr])
            fw.op("dve", lambda e: e.tensor_scalar(tk, tu, 0.5, None, ALU.is_gt), reads=[r_tr], writes=[r_tr])
            fw.op("dve", lambda e: e.tensor_tensor(tu, tu, tk, ALU.subtract), reads=[r_tr], writes=[r_tr])
            fw.op("dve", lambda e: e.tensor_scalar(tk, tu, -0.5, None, ALU.is_lt), reads=[r_tr], writes=[r_tr])
            fw.op("dve", lambda e: e.tensor_tensor(tu, tu, tk, ALU.add), reads=[r_tr], writes=[r_tr])
            fw.op("act", lambda e: e.activation(tk, tu, AF.Sin, scale=2 * math.pi), reads=[r_tr], writes=[r_tr])
            fw.dma("pool", cs_scr[which], tk, reads=[r_tr], writes=[RES("@cs")], sem_res=r_tr)
        fw.barrier()

        _once = {}

        def T_once(name, shape, dt):
            if name not in _once:
                _once[name] = T(name, shape, dt)
            return _once[name]
        mix_stats = []
        st1 = [T("st1_%d" % i, [128, 8], F32) for i in range(4)]
        r_st1 = [RES("st1_%d" % i) for i in range(4)]
        cmask = T("cmask", [4, 512], F32)
        fw.op("pool", lambda e: e.memset(cmask[:], 1.0), writes=[r_const])
        for j in range(4):
            fw.op("pool", lambda e, j=j: e.memset(cmask[:, j * 128:j * 128 + 1], 0.0), writes=[r_const])
        fw.barrier()

        def rms_tile(x_ap, xn_ap, stt, r_x, r_xn, r_stt, n):
            fw.op("act", lambda e: e.activation(xn_ap, x_ap, AF.Square, accum_out=stt[:, 0:1]), reads=[r_x], writes=[r_xn, r_stt])
            fw.op("act", lambda e: e.activation(stt[:, 1:2], stt[:, 0:1], AF.Sqrt, bias=EPS, scale=1.0 / n), reads=[r_stt], writes=[r_stt])
            fw.op("dve", lambda e: e.reciprocal(stt[:, 2:3], stt[:, 1:2]), reads=[r_stt], writes=[r_stt])
            fw.op("dve", lambda e: e.tensor_scalar(xn_ap, x_ap, stt[:, 2:3], None, ALU.mult), reads=[r_x, r_stt], writes=[r_xn])

        def transpose_to(dst3, src_ap, nk, reads, r_dst):
            bk, rb = nbank()
            pbt = bk.bitcast(BF16)
            for k in range(nk):
                transp(pbt[:, k * 128:(k + 1) * 128], src_ap[:, k * 128:(k + 1) * 128], identb[:], reads + [r_const], rb, signal=(k == nk - 1))
            eng = evac_eng()
            copy_op(eng, dst3, pbt[:, 0:nk * 128].rearrange("p (k n) -> p k n", k=nk), [rb], [r_dst])

        def phase1(sq):
            hTb = [VB(i * 8192, 4096).rearrange("p (k n) -> p k n", k=8) for i in range(2)]
            r_hTb = [RES("hTb%d" % i) for i in range(2)]
            xa = [VF(16384 + i * 4096, 1024) for i in range(2)] + [VF(108544 + i * 4096, 1024) for i in range(2)]
            r_xa = [RES("xa%d" % i) for i in range(4)]
            xnb = [VB(24576 + i * 2048, 1024) for i in range(2)] + [VB(116736 + i * 2048, 1024) for i in range(2)]
            r_xnb = [RES("xnb%d" % i) for i in range(4)]
            U = [VF(28672 + ct * 2064, 516) for ct in range(8)]
            r_U = [RES("U%d" % ct) for ct in range(8)]
            acc = [VF(45184 + i * 2048, 512) for i in range(2)]
            r_acc = [RES("acc%d" % i) for i in range(2)]
            qkb = [VB(49280 + i * 1024, 512) for i in range(2)]
            r_qkb = [RES("qkb%d" % i) for i in range(2)]
            vst = [VB(51328 + i * 2048, 1024) for i in range(4)]
            r_vst = [RES("vst%d" % i) for i in range(4)]
            GB = 59520
            gt = [VF(GB + i * 2048, 512, 4) for i in range(16)]
            r_g = RES("gates")
            LB = GB + 32768
            lat = VB(LB, 2048).rearrange("p (k n) -> p k n", k=4)
            r_lat = RES("lat")
            cnb = [VB(LB + 4096 + i * 1024, 512) for i in range(2)]
            r_cnb = [RES("cnb%d" % i) for i in range(2)]
            cst = VF(LB + 6144, 1024, 64).rearrange("p (c n) -> p c n", c=2)
            r_cst = RES("cst")
            krt = [VF(LB + 10240 + i * 2048, 512, 64) for i in range(2)]
            r_krt = RES("krt")
            krb = VB(LB + 14336, 512, 64)
            r_krb = RES("krb")
            r_qk = RES("@qk_scr"); r_v = RES("@v_scr"); r_nb = RES("@nb_scr")
            r_cq = RES("@cq"); r_ckv = RES("@ckv"); r_kr = RES("@kr")
            xi = 0
            for b in range(NBLK):
                tok0 = b * 512
                j = b & 1
                for tt in range(4):
                    i = xi & 3
                    xi += 1
                    fw.dma("sp", xa[i], xs[sq, tok0 + tt * 128: tok0 + (tt + 1) * 128, :], writes=[r_xa[i]])
                    rms_tile(xa[i], xnb[i], st1[i], r_xa[i], r_xnb[i], r_st1[i], 1024)
                    transpose_to(hTb[j][:, :, tt * 128:(tt + 1) * 128], xnb[i], 8, [r_xnb[i]], r_hTb[j])
                for grp in range(2):
                    wv, rw = wload("w_in", Wb["w_in"], 0, 8, grp * 512, 512)
                    for cl in range(4):
                        ct = grp * 4 + cl
                        bk, rb = nbank()
                        mm(bk[:, :], [(wv[:, kc, cl * 128:(cl + 1) * 128], hTb[j][:, kc, :]) for kc in range(8)], [rw, r_hTb[j]], rb)
                        if b == 0:
                            fw.op("pool", lambda e, ct=ct: e.memset(U[ct][:, 0:4], 0.0), writes=[r_U[ct]])
                        fw.op("act", lambda e, ct=ct, bk=bk: e.copy(U[ct][:, 4:516], bk[:, :]), reads=[rb], writes=[r_U[ct]])
                        a = ct & 1

                        def conv(n, a=a, ct=ct):
                            fw.op("dve", lambda e: e.tensor_scalar(acc[a][:, 0:n], U[ct][:, 0:n], convw[:, 0, ct:ct + 1], None, ALU.mult),
                                  reads=[r_U[ct], r_const], writes=[r_acc[a]])
                            for jj in range(1, 5):
                                fw.op("dve", lambda e, jj=jj: e.scalar_tensor_tensor(acc[a][:, 0:n], U[ct][:, jj:jj + n], convw[:, jj, ct:ct + 1], acc[a][:, 0:n], ALU.mult, ALU.add),
                                      reads=[r_U[ct], r_const, r_acc[a]], writes=[r_acc[a]])
                            fw.op("act", lambda e: e.activation(qkb[a][:, 0:n], acc[a][:, 0:n], AF.Silu), reads=[r_acc[a]], writes=[r_qkb[a]])
                        conv(512)
                        if b == 0:
                            fw.dma("pool", qk_scr[ct, :, 0:510], qkb[a][:, 2:512], reads=[r_qkb[a]], writes=[r_qk])
                        else:
                            fw.dma("pool", qk_scr[ct, :, tok0 - 2:tok0 + 510], qkb[a][:, 0:512], reads=[r_qkb[a]], writes=[r_qk])
                        fw.op("pool", lambda e, ct=ct: e.tensor_copy(U[ct][:, 0:4], U[ct][:, 512:516]), reads=[r_U[ct]], writes=[r_U[ct]])
                        if b == NBLK - 1:
                            fw.op("pool", lambda e, ct=ct: e.memset(U[ct][:, 4:8], 0.0), writes=[r_U[ct]])
                            conv(2)
                            fw.dma("pool", qk_scr[ct, :, S - 2:S], qkb[a][:, 0:2], reads=[r_qkb[a]], writes=[r_qk])
                for half in range(2):
                    wv, rw = wload("w_in", Wb["w_in"], 0, 8, V0 + half * 512, 512)
                    for tt in range(4):
                        bk, rb = nbank()
                        mm(bk[:, :], [(hTb[j][:, kc, tt * 128:(tt + 1) * 128], wv[:, kc, :]) for kc in range(8)], [rw, r_hTb[j]], rb)
                        copy_op(evac_eng(), vst[tt][:, half * 512:(half + 1) * 512], bk[:, :], [rb], [r_vst[tt]])
                for tt in range(4):
                    fw.dma("pool", v_scr[tok0 + tt * 128: tok0 + (tt + 1) * 128, :], vst[tt], reads=[r_vst[tt]], writes=[r_v])
                wv, rw = wload("w_in", Wb["w_in"], 0, 8, G0, 16)
                for g in range(4):
                    bk, rb = nbank()
                    mm(bk[0:4, :], [(wv[:, kc, 4 * g:4 * g + 4], hTb[j][:, kc, :]) for kc in range(8)], [rw, r_hTb[j]], rb)
                    fw.op("act", lambda e, g=g, bk=bk: e.activation(gt[g], bk[0:4, :], AF.Identity, bias=bg[:, g:g + 1]), reads=[rb, r_const], writes=[r_g])
                IFt, IBt, FFt, FBt = gt[0], gt[1], gt[2], gt[3]
                SPf, SPb, csf, csb, NBf, NBb, Af, Ab, Gf, Gb, tmp = gt[4:15]
                G = lambda fn: fw.op(fn[0], fn[1], reads=[r_g, r_const], writes=[r_g])
                for FX, SP in ((FFt, SPf), (FBt, SPb)):
                    G(("act", lambda e, FX=FX: e.activation(tmp, FX, AF.Exp, scale=-1.0)))
                    G(("act", lambda e, SP=SP: e.activation(SP, tmp, AF.Ln, bias=1.0)))
                G(("dve", lambda e: e.tensor_tensor_scan(csf, cmask[:], SPf, 0.0, ALU.mult, ALU.add)))
                G(("dve", lambda e: e.tensor_tensor_scan(csb, cmask[:], SPb, 0.0, ALU.mult, ALU.add)))
                v3 = lambda t: t.rearrange("p (a n) -> p a n", a=4)
                tot = lambda t: t.rearrange("p (a n) -> p a n", a=4)[:, :, 127:128].to_broadcast([4, 4, 128])
                G(("dve", lambda e: e.tensor_scalar(NBf, csf, -1.0, None, ALU.mult)))
                G(("dve", lambda e: e.tensor_tensor(Af, IFt, csf, ALU.add)))
                G(("dve", lambda e: e.tensor_tensor(v3(tmp), v3(Af), tot(csf), ALU.subtract)))
                G(("act", lambda e: e.activation(Gf, tmp, AF.Exp)))
                G(("dve", lambda e: e.tensor_tensor(NBb, SPb, csb, ALU.subtract)))
                G(("dve", lambda e: e.tensor_tensor(v3(NBb), v3(NBb), tot(csb), ALU.add)))
                G(("dve", lambda e: e.tensor_tensor(Ab, IBt, NBb, ALU.add)))
                G(("dve", lambda e: e.tensor_tensor(v3(tmp), v3(Ab), tot(csb), ALU.subtract)))
                G(("act", lambda e: e.activation(Gb, tmp, AF.Exp)))
                G(("dve", lambda e: e.tensor_scalar(NBb, NBb, -1.0, None, ALU.mult)))
                fw.dma("pool", nb_scr[0, :, tok0:tok0 + 512], NBf, reads=[r_g], writes=[r_nb])
                fw.dma("pool", nb_scr[1, :, tok0:tok0 + 512], NBb, reads=[r_g], writes=[r_nb])
                wv, rw = wload("w_in", Wb["w_in"], 0, 8, CQ0, 512)
                cb = [None] * 4

                def cq_mm(tt, wv=wv, rw=rw, j=j, cb=cb):
                    bk, rb = nbank()
                    mm(bk[:, :], [(hTb[j][:, kc, tt * 128:(tt + 1) * 128], wv[:, kc, :]) for kc in range(8)], [rw, r_hTb[j]], rb)
                    cb[tt] = (bk, rb)

                def cq_chain(tt, cb=cb):
                    bk, rb = cb[tt]
                    i = tt & 1
                    stt = st1[i]
                    fw.op("act", lambda e, i=i, bk=bk, stt=stt: e.activation(cnb[i][:, 0:256], bk[:, 0:256], AF.Square, accum_out=stt[:, 4:5]), reads=[rb], writes=[r_cnb[i], r_st1[i]])
                    fw.op("act", lambda e, i=i, bk=bk, stt=stt: e.activation(cnb[i][:, 256:512], bk[:, 256:512], AF.Square, accum_out=stt[:, 5:6]), reads=[rb], writes=[r_cnb[i], r_st1[i]])
                    fw.op("act", lambda e, stt=stt: e.activation(stt[:, 6:8], stt[:, 4:6], AF.Sqrt, bias=EPS, scale=1.0 / 256), reads=[r_st1[i]], writes=[r_st1[i]])
                    fw.op("dve", lambda e, stt=stt: e.reciprocal(stt[:, 4:6], stt[:, 6:8]), reads=[r_st1[i]], writes=[r_st1[i]])
                    fw.op("dve", lambda e, i=i, bk=bk, stt=stt: e.tensor_scalar(cnb[i][:, 0:256], bk[:, 0:256], stt[:, 4:5], None, ALU.mult), reads=[rb, r_st1[i]], writes=[r_cnb[i]])
                    fw.op("act", lambda e, i=i, bk=bk, stt=stt: e.activation(cnb[i][:, 256:512], bk[:, 256:512], AF.Copy, scale=stt[:, 5:6]), reads=[rb, r_st1[i]], writes=[r_cnb[i]])

                def cq_tr(tt):
                    i = tt & 1
                    transpose_to(lat[:, :, tt * 128:(tt + 1) * 128], cnb[i], 4, [r_cnb[i]], r_lat)
                cq_mm(0)
                cq_mm(1)
                for tt in range(4):
                    cq_chain(tt)
                    if tt + 2 < 4:
                        cq_mm(tt + 2)
                    if tt >= 1:
                        cq_tr(tt - 1)
                cq_tr(3)
                fw.dma("pool", cqnT_scr[:, :, tok0:tok0 + 512].rearrange("k p n -> p k n"), lat[:, 0:2, :], reads=[r_lat], writes=[r_cq])
                fw.dma("pool", ckvnT_scr[:, :, tok0:tok0 + 512].rearrange("k p n -> p k n"), lat[:, 2:4, :], reads=[r_lat], writes=[r_ckv])
                wa, rwa = wload("w_in", Wb["w_in"], 0, 8, KR0, 64)
                wb_, rwb = wload("w_kr_sw", w_kr_sw, 0, 8, 0, 64)
                bka, rba = nbank()
                mm(bka[0:64, :], [(wa[:, kc, :], hTb[j][:, kc, :]) for kc in range(8)], [rwa, r_hTb[j]], rba)
                bkb, rbb = nbank()
                mm(bkb[0:64, :], [(wb_[:, kc, :], hTb[j][:, kc, :]) for kc in range(8)], [rwb, r_hTb[j]], rbb)
                fw.dma("sp", cst, cs_scr[:, :, tok0:tok0 + 512].rearrange("c p n -> p c n"), writes=[r_cst])
                fw.op("dve", lambda e, bka=bka: e.tensor_tensor(krt[0], bka[0:64, :], cst[:, 0, :], ALU.mult), reads=[rba, r_cst], writes=[r_krt])
                fw.op("dve", lambda e, bkb=bkb: e.tensor_tensor(krt[1], bkb[0:64, :], cst[:, 1, :], ALU.mult), reads=[rbb, r_cst], writes=[r_krt])
                fw.op("pool", lambda e: e.tensor_tensor(krb, krt[0], krt[1], ALU.add), reads=[r_krt], writes=[r_krb])
                fw.dma("pool", krT_scr[:, tok0:tok0 + 512], krb, reads=[r_krb], writes=[r_kr])
                bk, rb = nbank()
                for qi, Q in enumerate((Af, Ab, Gf, Gb)):
                    for tt in range(4):
                        c0 = (qi * 4 + tt) * 4
                        transp(bk[:, c0:c0 + 4], Q[:, tt * 128:(tt + 1) * 128], identf[0:4, 0:4], [r_g, r_const], rb, signal=(qi == 3 and tt == 3))
                fw.op("dve", lambda e, bk=bk, b=b: e.tensor_copy(colsT[:, :, b * 4:(b + 1) * 4, :], bk[:, 0:64].rearrange("p (q t h) -> p q t h", q=4, t=4)),
                      reads=[rb], writes=[r_const])
            fw.op("dve", lambda e: e.tensor_scalar(colsT[:, 0:2], colsT[:, 0:2], LN_SQ, None, ALU.add), reads=[r_const], writes=[r_const])
            if dbg == "P1":
                fw.dma("pool", dbg_out["d_colsT"], colsT[:].rearrange("p q t h -> p (q t h)"), reads=[r_const], writes=[RES("@dbgP1")], sem_res=r_st1[0])

        def phase3(sq):
            xa4 = VF(0, 4096).rearrange("p (t n) -> p t n", t=4)
            r_xa4 = [RES("xa4_%d" % i) for i in range(4)]
            xn4 = VB(16384, 4096).rearrange("p (t n) -> p t n", t=4)
            r_xn4 = [RES("xn4_%d" % i) for i in range(4)]
            k8 = lambda ap: ap.rearrange("p (k n) -> p k n", k=8)
            hTb = k8(VB(24576, 4096)); r_hTb = RES("p3hTb")
            sgo = k8(VB(32768, 4096)); r_sgo = RES("sgo")
            hmT = k8(VB(40960, 4096)); r_hmT = RES("hmTb")
            oTb = k8(VB(49152, 4096)); r_oTb = RES("oTb")
            mrg = k8(VB(57344, 4096)); r_mrg = RES("mrg")
            sg = [VF(65536 + i * 2048, 512) for i in range(2)]; r_sg = [RES("sg%d" % i) for i in range(2)]
            t12 = [VF(69632 + i * 2048, 512) for i in range(2)]; r_t12 = [RES("t12_%d" % i) for i in range(2)]
            actT = VB(73728, 22 * 512).rearrange("p (k n) -> p k n", k=22); r_actT = RES("actT")
            pa = VF(96256, 1024).rearrange("p (t n) -> p t n", t=4); r_pa = RES("pa")
            pn = VB(100352, 1024).rearrange("p (t n) -> p t n", t=4); r_pn = RES("pn")
            pT = VB(102400, 1024).rearrange("p (k n) -> p k n", k=2); r_pT = RES("pT")
            r_y = RES("@y")
            for b in range(NBLK):
                tok0 = b * 512
                tsl = lambda tt: slice(tt * 128, (tt + 1) * 128)
                for tt in range(4):
                    i = tt & 1
                    fw.dma("sp", xa4[:, tt, :], xs[sq, tok0 + tt * 128: tok0 + (tt + 1) * 128, :], writes=[r_xa4[tt]])
                    rms_tile(xa4[:, tt, :], xn4[:, tt, :], st1[i], r_xa4[tt], r_xn4[tt], r_st1[i], 1024)
                    transpose_to(hTb[:, :, tsl(tt)], xn4[:, tt, :], 8, [r_xn4[tt]], r_hTb)
                fw.dma("sp", pa, ps[sq, tok0:tok0 + 512, :].rearrange("(t p) n -> p t n", p=128), writes=[r_pa])
                if dbg:
                    fw.op("pool", lambda e: e.memset(hmT, 0.0), writes=[r_hmT])
                else:
                    fw.dma("sp", hmT, hmT_scr[:, :, tok0:tok0 + 512].rearrange("k p n -> p k n"), reads=[RES("@hm_dummy")], writes=[r_hmT])
                if dbg == 2:
                    fw.op("pool", lambda e: e.memset(oTb, 0.0), writes=[r_oTb])
                else:
                    fw.dma("sp", oTb, oT_scr[:, :, tok0:tok0 + 512].rearrange("k p n -> p k n"), reads=[RES("@o_dummy")], writes=[r_oTb])
                for grp in range(2):
                    wv, rw = wload("w_in", Wb["w_in"], 0, 8, O0 + grp * 512, 512)
                    for cl in range(4):
                        bk, rb = nbank()
                        mm(bk[:, :], [(wv[:, kc, cl * 128:(cl + 1) * 128], hTb[:, kc, :]) for kc in range(8)], [rw, r_hTb], rb)
                        fw.op("act", lambda e, bk=bk, c=grp * 4 + cl: e.activation(sgo[:, c, :], bk[:, :], AF.Sigmoid), reads=[rb], writes=[r_sgo])
                fw.op("dve", lambda e: e.tensor_tensor(sgo, sgo, hmT, ALU.mult), reads=[r_sgo, r_hmT], writes=[r_sgo])
                for grp in range(2):
                    wm, rwm = wload("w_mo", Wb["w_mo"], 0, 8, grp * 512, 512)
                    wa, rwa = wload("w_ao", Wb["w_ao"], 0, 8, grp * 512, 512)
                    wg1, rw1 = wload("w_in", Wb["w_in"], 0, 8, GM0 + grp * 512, 512)
                    wg2, rw2 = wload("w_in", Wb["w_in"], 0, 8, GM0 + 1024 + grp * 512, 512)
                    for cl in range(4):
                        c = grp * 4 + cl
                        cs_ = slice(cl * 128, (cl + 1) * 128)
                        bm, rbm = nbank(); mm(bm[:, :], [(wm[:, kc, cs_], sgo[:, kc, :]) for kc in range(8)], [rwm, r_sgo], rbm)
                        ba, rba = nbank(); mm(ba[:, :], [(wa[:, kc, cs_], oTb[:, kc, :]) for kc in range(8)], [rwa, r_oTb], rba)
                        b1, rb1 = nbank(); mm(b1[:, :], [(wg1[:, kc, cs_], hTb[:, kc, :]) for kc in range(8)], [rw1, r_hTb], rb1)
                        b2, rb2 = nbank(); mm(b2[:, :], [(wg2[:, kc, cs_], hTb[:, kc, :]) for kc in range(8)], [rw2, r_hTb], rb2)
                        fw.op("act", lambda e, b1=b1: e.activation(sg[0], b1[:, :], AF.Sigmoid), reads=[rb1], writes=[r_sg[0]])
                        fw.op("act", lambda e, b2=b2: e.activation(sg[1], b2[:, :], AF.Sigmoid), reads=[rb2], writes=[r_sg[1]])
                        fw.op("dve", lambda e, bm=bm: e.tensor_tensor(t12[0], bm[:, :], sg[0], ALU.mult), reads=[rbm, r_sg[0]], writes=[r_t12[0]])
                        fw.op("dve", lambda e, ba=ba: e.tensor_tensor(t12[1], ba[:, :], sg[1], ALU.mult), reads=[rba, r_sg[1]], writes=[r_t12[1]])
                        fw.op("pool", lambda e, c=c: e.tensor_tensor(mrg[:, c, :], t12[0], t12[1], ALU.add), reads=r_t12, writes=[r_mrg])
                for half in range(2):
                    wo, rwo = wload("w_out", Wb["w_out"], 0, 8, half * 512, 512)
                    for tt in range(4):
                        bk, rb = nbank()
                        mm(bk[:, :], [(mrg[:, kc, tsl(tt)], wo[:, kc, :]) for kc in range(8)], [rwo, r_mrg], rb)
                        xsl = xa4[:, tt, half * 512:(half + 1) * 512]
                        fw.op("dve", lambda e, bk=bk, xsl=xsl: e.tensor_tensor(xsl, xsl, bk[:, :], ALU.add), reads=[rb], writes=[r_xa4[tt]])
                for tt in range(4):
                    i = tt & 1
                    rms_tile(xa4[:, tt, :], xn4[:, tt, :], st1[i], r_xa4[tt], r_xn4[tt], r_st1[i], 1024)
                    transpose_to(hTb[:, :, tsl(tt)], xn4[:, tt, :], 8, [r_xn4[tt]], r_hTb)
                for g in range(6):
                    n_ = 512 if g < 5 else 256
                    wg, rwg = wload("w_gate", Wb["w_gate"], 0, 8, g * 512, n_)
                    wu, rwu = wload("w_up", Wb["w_up"], 0, 8, g * 512, n_)
                    for cl in range(n_ // 128):
                        f = g * 4 + cl
                        cs_ = slice(cl * 128, (cl + 1) * 128)
                        bg_, rbg = nbank(); mm(bg_[:, :], [(wg[:, kc, cs_], hTb[:, kc, :]) for kc in range(8)], [rwg, r_hTb], rbg)
                        bu, rbu = nbank(); mm(bu[:, :], [(wu[:, kc, cs_], hTb[:, kc, :]) for kc in range(8)], [rwu, r_hTb], rbu)
                        i = f & 1
                        fw.op("act", lambda e, bg_=bg_, i=i: e.activation(sg[i], bg_[:, :], AF.Silu), reads=[rbg], writes=[r_sg[i]])
                        fw.op("dve", lambda e, bu=bu, i=i, f=f: e.tensor_tensor(actT[:, f, :], bu[:, :], sg[i], ALU.mult), reads=[rbu, r_sg[i]], writes=[r_actT])
                for half in range(2):
                    parts = []
                    for k0, nk in ((0, 8), (8, 8), (16, 6)):
                        wd, rwd = wload("w_down", Wb["w_down"], k0, nk, half * 512, 512)
                        parts.append((k0, nk, wd, rwd))
                    for tt in range(4):
                        bk, rb = nbank()
                        pairs = []
                        for k0, nk, wd, rwd in parts:
                            pairs += [(actT[:, k0 + kk, tsl(tt)], wd[:, kk, :]) for kk in range(nk)]
                        mm(bk[:, :], pairs, [p_[3] for p_ in parts] + [r_actT], rb)
                        xsl = xa4[:, tt, half * 512:(half + 1) * 512]
                        fw.op("dve", lambda e, bk=bk, xsl=xsl: e.tensor_tensor(xsl, xsl, bk[:, :], ALU.add), reads=[rb], writes=[r_xa4[tt]])
                fw.op("act", lambda e: e.copy(pn, pa), reads=[r_pa], writes=[r_pn])
                for tt in range(4):
                    fw.op("act", lambda e, tt=tt: e.copy(xn4[:, tt, :], xa4[:, tt, :]), reads=[r_xa4[tt]], writes=[r_xn4[tt]])
                    transpose_to(hTb[:, :, tsl(tt)], xn4[:, tt, :], 8, [r_xn4[tt]], r_hTb)
                    transpose_to(pT[:, :, tsl(tt)], pn[:, tt, :], 2, [r_pn], r_pT)
                for half in range(2):
                    wpg, rwpg = wload("w_ple_gate", Wb["w_ple_gate"], 0, 8, half * 512, 512)
                    wp, rwp = wload("w_ple", Wb["w_ple"], 0, 2, half * 512, 512)
                    for tt in range(4):
                        bgt, rbgt = nbank(); mm(bgt[:, :], [(hTb[:, kc, tsl(tt)], wpg[:, kc, :]) for kc in range(8)], [rwpg, r_hTb], rbgt)
                        bp, rbp = nbank(); mm(bp[:, :], [(pT[:, kc, tsl(tt)], wp[:, kc, :]) for kc in range(2)], [rwp, r_pT], rbp)
                        i = tt & 1
                        fw.op("act", lambda e, bgt=bgt, i=i: e.activation(sg[i], bgt[:, :], AF.Sigmoid), reads=[rbgt], writes=[r_sg[i]])
                        fw.op("dve", lambda e, bp=bp, i=i: e.tensor_tensor(t12[i], bp[:, :], sg[i], ALU.mult), reads=[rbp, r_sg[i]], writes=[r_t12[i]])
                        xsl = xa4[:, tt, half * 512:(half + 1) * 512]
                        fw.op("dve", lambda e, xsl=xsl, i=i: e.tensor_tensor(xsl, xsl, t12[i], ALU.add), reads=[r_t12[i]], writes=[r_xa4[tt]])
                for tt in range(4):
                    i = tt & 1
                    stt = st1[i]
                    fw.op("act", lambda e, tt=tt, stt=stt: e.activation(xn4[:, tt, :], xa4[:, tt, :], AF.Square, accum_out=stt[:, 0:1]), reads=[r_xa4[tt]], writes=[r_xn4[tt], r_st1[i]])
                    fw.op("act", lambda e, stt=stt: e.activation(stt[:, 1:2], stt[:, 0:1], AF.Sqrt, bias=EPS, scale=1.0 / 1024), reads=[r_st1[i]], writes=[r_st1[i]])
                    fw.op("dve", lambda e, stt=stt: e.reciprocal(stt[:, 2:3], stt[:, 1:2]), reads=[r_st1[i]], writes=[r_st1[i]])
                    fw.op("dve", lambda e, tt=tt, stt=stt: e.scalar_tensor_tensor(xa4[:, tt, :], xa4[:, tt, :], stt[:, 2:3], gfin[:], ALU.mult, ALU.mult),
                          reads=[r_st1[i], r_const], writes=[r_xa4[tt]])
                    fw.dma("pool", y[sq, tok0 + tt * 128: tok0 + (tt + 1) * 128, :], xa4[:, tt, :], reads=[r_xa4[tt]], writes=[r_y])

        def phase_mlstm(sq, heads=range(4)):
            QT = VB(0, 4096); KT = VB(8192, 4096); Qp = VB(16384, 4096)
            Ktok = VB(24576, 4096).rearrange("p (t n) -> p t n", t=NT)
            Kp = VB(32768, 4096).rearrange("p (t n) -> p t n", t=NT)
            Vaug = VB(40960, NT * 258).rearrange("p (t n) -> p t n", t=NT)
            bbc = VF(57472, 4096)
            tmpE = [VF(73856 + i * 2048, 512) for i in range(2)]
            Cst = VF(77952, 258); Cbf = VB(78992, 258)
            Dt = [VF(79520 + i * 512, 128) for i in range(2)]
            Sp = [VB(80544 + i * 256, 128) for i in range(2)]
            aC = VF(81056, 32)
            hfo = [VF(81184 + i * 1024, 256) for i in range(3)]
            hs = [VF(84256 + i * 1024, 256) for i in range(2)]
            hn = [VB(86304 + i * 512, 256) for i in range(2)]
            hT2 = [VB(87328 + i * 512, 256).rearrange("p (k n) -> p k n", k=2) for i in range(2)]
            r_QT, r_KT, r_Qp, r_Ktok, r_Kp, r_V, r_bbc = (RES(n) for n in ("QT", "KT", "Qp", "Ktok", "Kp", "Vaug", "bbc"))
            r_tmpE = [RES("tmpE%d" % i) for i in range(2)]
            r_C, r_Cbf, r_aC = RES("Cst"), RES("Cbf"), RES("aC")
            r_Dt = [RES("Dt%d" % i) for i in range(2)]; r_Sp = [RES("Sp%d" % i) for i in range(2)]
            r_hfo = [RES("hfo%d" % i) for i in range(3)]; r_hs = [RES("hs%d" % i) for i in range(2)]
            r_hn = [RES("hn%d" % i) for i in range(2)]; r_hT2 = [RES("hT2_%d" % i) for i in range(2)]
            r_hf = [RES("@hf%d" % c) for c in range(NT)]
            r_hm = RES("@hmT")
            for h in heads:
                fw.dma("sp", QT, qk_scr[h], writes=[r_QT])
                fw.dma("sp", KT, qk_scr[4 + h], writes=[r_KT])
                fw.op("pool", lambda e: e.memset(Vaug[:, :, 256:258], 1.0), writes=[r_V])
                fw.dma("sp", Vaug[:, :, 0:256], v_scr[:, h * 256:(h + 1) * 256].rearrange("(t p) n -> p t n", p=128), reads=[r_V], writes=[r_V])
                for c4 in range(4):
                    transpose_to(Ktok[:, c4 * 8:(c4 + 1) * 8, :], KT[:, c4 * 1024:(c4 + 1) * 1024], 8, [r_KT], r_Ktok)
                for d in range(2):
                    msk = maskf if d == 0 else maskb
                    far = 127 if d == 0 else 0
                    fw.dma("sp", bbc, nb_scr[d, h:h + 1, :].partition_broadcast(128).rearrange("p o n -> p (o n)"), writes=[r_bbc])
                    fw.op("act", lambda e, far=far: e.activation(aC, bbc.rearrange("p (t n) -> p t n", t=NT)[:, :, far], AF.Exp), reads=[r_bbc], writes=[r_aC])
                    for blk in range(NBLK):
                        i = blk & 1
                        bs_ = slice(blk * 512, (blk + 1) * 512)
                        fw.op("act", lambda e, i=i, bs_=bs_: e.activation(tmpE[i], bbc[:, bs_], AF.Exp, bias=LN_SQ), reads=[r_bbc], writes=[r_tmpE[i]])
                        fw.op("dve", lambda e, i=i, bs_=bs_: e.tensor_tensor(Qp[:, bs_], QT[:, bs_], tmpE[i], ALU.mult), reads=[r_QT, r_tmpE[i]], writes=[r_Qp])
                    fw.op("pool", lambda e, msk=msk: e.tensor_tensor(bbc.rearrange("p (t n) -> p t n", t=NT), bbc.rearrange("p (t n) -> p t n", t=NT),
                                                                    msk[:].unsqueeze(1).to_broadcast([128, NT, 128]), ALU.add), reads=[r_bbc, r_const], writes=[r_bbc])
                    fw.op("dve", lambda e, d=d, h=h: e.tensor_tensor(Kp, Ktok, colsT[:, 2 + d, :, h].unsqueeze(2).to_broadcast([128, NT, 128]), ALU.mult),
                          reads=[r_Ktok, r_const], writes=[r_Kp])
                    fw.op("dve", lambda e: e.memset(Cst, 0.0), writes=[r_C])
                    fw.op("pool", lambda e: e.memset(Cbf, 0.0), writes=[r_Cbf])
                    order = list(range(NT)) if d == 0 else list(range(NT - 1, -1, -1))
                    for n_, c in enumerate(order):
                        cs_ = slice(c * 128, (c + 1) * 128)
                        j = n_ & 1
                        fw.op("act", lambda e, j=j, cs_=cs_, d=d, c=c, h=h: e.activation(Dt[j], bbc[:, cs_], AF.Exp, bias=colsT[:, d, c, h:h + 1]),
                              reads=[r_bbc, r_const], writes=[r_Dt[j]])
                        bS, rbS = nbank()
                        mm(bS[:, 0:128], [(KT[:, cs_], QT[:, cs_])], [r_KT, r_QT], rbS)
                        fw.op("dve", lambda e, j=j, bS=bS: e.tensor_tensor(Sp[j], bS[:, 0:128], Dt[j], ALU.mult), reads=[rbS, r_Dt[j]], writes=[r_Sp[j]])
                        bN, rbN = nbank()
                        mm(bN[:, 0:257], [(Sp[j], Vaug[:, c, 0:257]), (Qp[:, cs_], Cbf[:, 0:257])], [r_Sp[j], r_V, r_Qp, r_Cbf], rbN)
                        bU, rbU = nbank()
                        mm(bU[:, 0:257], [(Kp[:, c, :], Vaug[:, c, 0:257])], [r_Kp, r_V], rbU)
                        fw.op("dve", lambda e, c=c, bU=bU: e.scalar_tensor_tensor(Cst[:, 0:257], Cst[:, 0:257], aC[:, c:c + 1], bU[:, 0:257], ALU.mult, ALU.add),
                              reads=[r_C, r_aC, rbU], writes=[r_C])
                        fw.op("act", lambda e: e.copy(Cbf[:, 0:257], Cst[:, 0:257]), reads=[r_C], writes=[r_Cbf])
                        k = n_ & 1
                        stt, r_stt = st1[k], r_st1[k]
                        fw.op("act", lambda e, bN=bN, stt=stt: e.activation(stt[:, 0:1], bN[:, 256:257], AF.Abs), reads=[rbN], writes=[r_stt])
                        fw.op("dve", lambda e, stt=stt: e.tensor_scalar_max(stt[:, 0:1], stt[:, 0:1], 1.0), reads=[r_stt], writes=[r_stt])
                        fw.op("dve", lambda e, stt=stt: e.reciprocal(stt[:, 1:2], stt[:, 0:1]), reads=[r_stt], writes=[r_stt])
                        m3 = n_ % 3
                        if d == 0:
                            fw.op("dve", lambda e, m3=m3, bN=bN, stt=stt: e.tensor_scalar(hfo[m3], bN[:, 0:256], stt[:, 1:2], None, ALU.mult), reads=[rbN, r_stt], writes=[r_hfo[m3]])
                            fw.dma("pool", hf_scr[cs_, :], hfo[m3], reads=[r_hfo[m3]], writes=[r_hf[c]])
                        else:
                            fw.dma("sp", hfo[m3], hf_scr[cs_, :], reads=[r_hf[c]], writes=[r_hfo[m3]])
                            fw.op("dve", lambda e, m3=m3, k=k, bN=bN, stt=stt: e.scalar_tensor_tensor(hs[k], bN[:, 0:256], stt[:, 1:2], hfo[m3], ALU.mult, ALU.add),
                                  reads=[rbN, r_stt, r_hfo[m3]], writes=[r_hs[k]])
                            fw.op("act", lambda e, k=k, stt=stt: e.activation(hn[k], hs[k], AF.Square, accum_out=stt[:, 2:3]), reads=[r_hs[k]], writes=[r_hn[k], r_stt])
                            fw.op("act", lambda e, stt=stt: e.activation(stt[:, 3:4], stt[:, 2:3], AF.Sqrt, bias=EPS, scale=1.0 / 256), reads=[r_stt], writes=[r_stt])
                            fw.op("dve", lambda e, stt=stt: e.reciprocal(stt[:, 2:3], stt[:, 3:4]), reads=[r_stt], writes=[r_stt])
                            fw.op("dve", lambda e, k=k, stt=stt: e.tensor_scalar(hn[k], hs[k], stt[:, 2:3], None, ALU.mult), reads=[r_hs[k], r_stt], writes=[r_hn[k]])
                            transpose_to(hT2[k], hn[k], 2, [r_hn[k]], r_hT2[k])
                            fw.dma("pool", hmT_scr[2 * h:2 * h + 2, :, cs_].rearrange("k p n -> p k n"), hT2[k], reads=[r_hT2[k]], writes=[r_hm])

        def phase_mix(sq):
            cqnT = VB(0, 8192).rearrange("p (k n) -> p k n", k=2); r_cqn = RES("cqnT")
            ckvnT = VB(16384, 8192).rearrange("p (k n) -> p k n", k=2); r_ckvn = RES("ckvnT")
            krT = VB(32768, 4096, 128); r_krT = RES("krT")
            KhT = VB(40960, 4096); r_KhT = RES("KhT")
            Vh = VB(49152, 4096); r_Vh = RES("Vh")
            QhT = VB(57344, 4096); r_QhT = RES("QhT")
            QrT = VB(65536, 4096, 128); r_QrT = RES("QrT")
            cstq2 = [VF(73728, 1024, 64).rearrange("p (c n) -> p c n", c=2),
                     wsl[0].bitcast(F32)[0:64, 512:1536].rearrange("p (c n) -> p c n", c=2)]
            r_cstq2 = [RES("cstq"), RES("cstq_b")]
            rt = [VF(77824 + i * 2048, 512, 64) for i in range(2)]; r_rt = RES("rt")
            PT = [VB(81920 + i * 1024, 512) for i in range(3)]; r_PT = [RES("PT%d" % i) for i in range(3)]
            rsum = VF(84992, 512); r_rsum = RES("rsum")
            oTs = [VB(87040 + i * 1024, 512) for i in range(2)]; r_oTs = [RES("oTs%d" % i) for i in range(2)]
            r_oT = RES("@oT")
            QT, KT, Qp = wsl[1][:, :], wsl[2][:, :], wsl[3][:, :]
            Ktok = wsl[4][:, :].rearrange("p (t n) -> p t n", t=NT)
            Kp = wsl[5][:, :].rearrange("p (t n) -> p t n", t=NT)
            M0 = 89088
            Vaug = VB(M0, NT * 258).rearrange("p (t n) -> p t n", t=NT)
            bbc = VF(M0 + 16512, 4096)
            tmpE = [VF(M0 + 32896 + i * 2048, 512) for i in range(2)]
            Cst = VF(M0 + 36992, 258); Cbf = VB(M0 + 38032, 258)
            Dt = [VF(M0 + 38560 + i * 512, 128) for i in range(2)]
            Sp = [VB(M0 + 39584 + i * 256, 128) for i in range(2)]
            aC = VF(M0 + 40096, 32)
            hfo = [VF(M0 + 40224 + i * 1024, 256) for i in range(3)]
            hs = [VF(M0 + 43296 + i * 1024, 256) for i in range(2)]
            hn = [VB(M0 + 45344 + i * 512, 256) for i in range(2)]
            hT2 = [VB(M0 + 46368 + i * 512, 256).rearrange("p (k n) -> p k n", k=2) for i in range(2)]
            stm = [T_once("stm%d" % i, [128, 8], F32) for i in range(2)]; r_stm = [RES("stm%d" % i) for i in range(2)]
            assert M0 + 47392 <= AR * 4
            r_QT, r_KT, r_Qp, r_Ktok, r_Kp, r_V, r_bbc = (RES(n) for n in ("QT", "KT", "Qp", "Ktok", "Kp", "Vaug", "bbc"))
            r_tmpE = [RES("tmpE%d" % i) for i in range(2)]
            r_C, r_Cbf, r_aC = RES("Cst"), RES("Cbf"), RES("aC")
            r_Dt = [RES("Dt%d" % i) for i in range(2)]; r_Sp = [RES("Sp%d" % i) for i in range(2)]
            r_hfo = [RES("hfo%d" % i) for i in range(3)]; r_hs = [RES("hs%d" % i) for i in range(2)]
            r_hn = [RES("hn%d" % i) for i in range(2)]; r_hT2 = [RES("hT2_%d" % i) for i in range(2)]
            r_hf = [RES("@hf%d" % c) for c in range(NT)]
            r_hm = RES("@hmT")
            mst = {"b": 0}

            def mbank():
                i = 5 + mst["b"]
                mst["b"] = (mst["b"] + 1) % 3
                return banks[i], r_bank[i]

            def m_transpose(dst3, src_ap, nk, reads, r_dst):
                bk, rb = mbank()
                pbt = bk.bitcast(BF16)
                for k in range(nk):
                    transp(pbt[:, k * 128:(k + 1) * 128], src_ap[:, k * 128:(k + 1) * 128], identb[:], reads + [r_const], rb, signal=(k == nk - 1))
                copy_op(evac_eng(), dst3, pbt[:, 0:nk * 128].rearrange("p (k n) -> p k n", k=nk), [rb], [r_dst])

            b3 = lambda: bbc.rearrange("p (t n) -> p t n", t=NT)

            def mlstm_gen():
                for h in range(4):
                    fw.dma("sp", QT, qk_scr[h], writes=[r_QT])
                    fw.dma("sp", KT, qk_scr[4 + h], writes=[r_KT])
                    fw.op("pool", lambda e: e.memset(Vaug[:, :, 256:258], 1.0), writes=[r_V])
                    fw.dma("sp", Vaug[:, :, 0:256], v_scr[:, h * 256:(h + 1) * 256].rearrange("(t p) n -> p t n", p=128), reads=[r_V], writes=[r_V])
                    yield
                    for c8 in range(8):
                        m_transpose(Ktok[:, c8 * 4:(c8 + 1) * 4, :], KT[:, c8 * 512:(c8 + 1) * 512], 4, [r_KT], r_Ktok)
                        yield
                    for d in range(2):
                        msk = maskf if d == 0 else maskb
                        far = 127 if d == 0 else 0
                        fw.dma("sp", bbc, nb_scr[d, h:h + 1, :].partition_broadcast(128).rearrange("p o n -> p (o n)"), writes=[r_bbc])
                        yield
                        fw.op("act", lambda e, far=far: e.activation(aC, bbc.rearrange("p (t n) -> p t n", t=NT)[:, :, far], AF.Exp), reads=[r_bbc], writes=[r_aC])
                        for blk in range(NBLK):
                            i = blk & 1
                            bs_ = slice(blk * 512, (blk + 1) * 512)
                            fw.op("act", lambda e, i=i, bs_=bs_: e.activation(tmpE[i], bbc[:, bs_], AF.Exp, bias=LN_SQ), reads=[r_bbc], writes=[r_tmpE[i]])
                            yield
                            fw.op("dve", lambda e, i=i, bs_=bs_: e.tensor_tensor(Qp[:, bs_], QT[:, bs_], tmpE[i], ALU.mult), reads=[r_QT, r_tmpE[i]], writes=[r_Qp])
                            yield
                        for q4 in range(4):
                            ts_ = slice(q4 * 8, (q4 + 1) * 8)
                            fw.op("pool", lambda e, msk=msk, ts_=ts_: e.tensor_tensor(b3()[:, ts_, :], b3()[:, ts_, :], msk[:].unsqueeze(1).to_broadcast([128, 8, 128]), ALU.add),
                                  reads=[r_bbc, r_const], writes=[r_bbc])
                            fw.op("dve", lambda e, d=d, h=h, ts_=ts_: e.tensor_tensor(Kp[:, ts_, :], Ktok[:, ts_, :], colsT[:, 2 + d, ts_, h].unsqueeze(2).to_broadcast([128, 8, 128]), ALU.mult),
                                  reads=[r_Ktok, r_const], writes=[r_Kp])
                            yield
                        fw.op("dve", lambda e: e.memset(Cst, 0.0), writes=[r_C])
                        fw.op("pool", lambda e: e.memset(Cbf, 0.0), writes=[r_Cbf])
                        yield
                        order = list(range(NT)) if d == 0 else list(range(NT - 1, -1, -1))
                        for n_, c in enumerate(order):
                            cs_ = slice(c * 128, (c + 1) * 128)
                            j = n_ & 1
                            k = n_ & 1
                            m3 = n_ % 3
                            stt, r_stt = stm[k], r_stm[k]
                            fw.op("act", lambda e, j=j, cs_=cs_, d=d, c=c, h=h: e.activation(Dt[j], bbc[:, cs_], AF.Exp, bias=colsT[:, d, c, h:h + 1]),
                                  reads=[r_bbc, r_const], writes=[r_Dt[j]])
                            bS, rbS = mbank()
                            mm(bS[:, 0:128], [(KT[:, cs_], QT[:, cs_])], [r_KT, r_QT], rbS)
                            if d == 1:
                                fw.dma("sp", hfo[m3], hf_scr[cs_, :], reads=[r_hf[c]], writes=[r_hfo[m3]])
                            yield
                            fw.op("dve", lambda e, j=j, bS=bS: e.tensor_tensor(Sp[j], bS[:, 0:128], Dt[j], ALU.mult), reads=[rbS, r_Dt[j]], writes=[r_Sp[j]])
                            yield
                            bN, rbN = mbank()
                            mm(bN[:, 0:257], [(Sp[j], Vaug[:, c, 0:257]), (Qp[:, cs_], Cbf[:, 0:257])], [r_Sp[j], r_V, r_Qp, r_Cbf], rbN)
                            bU, rbU = mbank()
                            mm(bU[:, 0:257], [(Kp[:, c, :], Vaug[:, c, 0:257])], [r_Kp, r_V], rbU)
                            yield
                            fw.op("dve", lambda e, c=c, bU=bU: e.scalar_tensor_tensor(Cst[:, 0:257], Cst[:, 0:257], aC[:, c:c + 1], bU[:, 0:257], ALU.mult, ALU.add),
                                  reads=[r_C, r_aC, rbU], writes=[r_C])
                            fw.op("act", lambda e, bN=bN, stt=stt: e.activation(stt[:, 0:1], bN[:, 256:257], AF.Abs), reads=[rbN], writes=[r_stt])
                            yield
                            fw.op("pool", lambda e: e.tensor_copy(Cbf[:, 0:257], Cst[:, 0:257]), reads=[r_C], writes=[r_Cbf])
                            fw.op("dve", lambda e, stt=stt: e.tensor_scalar_max(stt[:, 0:1], stt[:, 0:1], 1.0), reads=[r_stt], writes=[r_stt])
                            fw.op("dve", lambda e, stt=stt: e.reciprocal(stt[:, 1:2], stt[:, 0:1]), reads=[r_stt], writes=[r_stt])
                            yield
                            if d == 0:
                                fw.op("dve", lambda e, m3=m3, bN=bN, stt=stt: e.tensor_scalar(hfo[m3], bN[:, 0:256], stt[:, 1:2], None, ALU.mult), reads=[rbN, r_stt], writes=[r_hfo[m3]])
                                fw.dma("pool", hf_scr[cs_, :], hfo[m3], reads=[r_hfo[m3]], writes=[r_hf[c]])
                                yield
                            else:
                                fw.op("dve", lambda e, m3=m3, k=k, bN=bN, stt=stt: e.scalar_tensor_tensor(hs[k], bN[:, 0:256], stt[:, 1:2], hfo[m3], ALU.mult, ALU.add),
                                      reads=[rbN, r_stt, r_hfo[m3]], writes=[r_hs[k]])
                                yield
                                fw.op("act", lambda e, k=k, stt=stt: e.activation(hn[k], hs[k], AF.Square, accum_out=stt[:, 2:3]), reads=[r_hs[k]], writes=[r_hn[k], r_stt])
                                yield
                                fw.op("act", lambda e, stt=stt: e.activation(stt[:, 3:4], stt[:, 2:3], AF.Sqrt, bias=EPS, scale=1.0 / 256), reads=[r_stt], writes=[r_stt])
                                yield
                                fw.op("dve", lambda e, stt=stt: e.reciprocal(stt[:, 2:3], stt[:, 3:4]), reads=[r_stt], writes=[r_stt])
                                yield
                                fw.op("dve", lambda e, k=k, stt=stt: e.tensor_scalar(hn[k], hs[k], stt[:, 2:3], None, ALU.mult), reads=[r_hs[k], r_stt], writes=[r_hn[k]])
                                yield
                                m_transpose(hT2[k], hn[k], 2, [r_hn[k]], r_hT2[k])
                                fw.dma("pool", hmT_scr[2 * h:2 * h + 2, :, cs_].rearrange("k p n -> p k n"), hT2[k], reads=[r_hT2[k]], writes=[r_hm])
                                yield

            gen = mlstm_gen()
            gst = {"done": False, "n": 0, "calls": 0}

            def advance():
                gst["calls"] += 1
                for _ in range(2 if gst["calls"] % 8 == 0 else 1):
                    if not gst["done"]:
                        try:
                            next(gen)
                            gst["n"] += 1
                        except StopIteration:
                            gst["done"] = True

            def wload0(name, W_ap, k0, nk, c0, ncols, off):
                view = wsl[0][:, off:off + nk * ncols].rearrange("p (k n) -> p k n", k=nk)
                src = W_ap[k0 * 128:(k0 + nk) * 128, c0:c0 + ncols].rearrange("(k p) n -> p k n", p=128)
                fw.dma("sp", view, src, reads=[r_W[name]], writes=[r_wsl[0]])
                return view, r_wsl[0]

            fw.dma("sp", cqnT, cqnT_scr.rearrange("k p n -> p k n"), writes=[r_cqn])
            fw.dma("sp", ckvnT, ckvnT_scr.rearrange("k p n -> p k n"), writes=[r_ckvn])
            fw.op("pool", lambda e: e.memset(krT[64:128, :], 0.0), writes=[r_krT])
            fw.op("pool", lambda e: e.memset(QrT[64:128, :], 0.0), writes=[r_QrT])
            fw.dma("sp", krT[0:64, :], krT_scr, reads=[r_krT], writes=[r_krT])
            abank = {"i": 0}

            def xbank():
                i = abank["i"]
                abank["i"] = (i + 1) % 5
                return banks[i], r_bank[i]

            for h in range(8):
                wk, rwk = wload0("w_ukv", Wb["w_ukv"], 0, 2, h * 256, 256, 0)
                wq, rwq = wload0("w_uq", Wb["w_uq"], 0, 2, h * 192, 192, 512)
                wqs, rwqs = wload0("w_uq_sw", w_uq_sw, 0, 2, h * 64, 64, 896)
                for b in range(NBLK):
                    bs_ = slice(b * 512, (b + 1) * 512)
                    bk, rb = xbank()
                    mm(bk[:, :], [(wk[:, kc, 0:128], ckvnT[:, kc, bs_]) for kc in range(2)], [rwk, r_ckvn], rb)
                    copy_op(evac_eng(), KhT[:, bs_], bk[:, :], [rb], [r_KhT])
                    bk, rb = xbank()
                    mm(bk[:, :], [(wq[:, kc, 0:128], cqnT[:, kc, bs_]) for kc in range(2)], [rwq, r_cqn], rb)
                    copy_op(evac_eng(), QhT[:, bs_], bk[:, :], [rb], [r_QhT])
                    bka, rba = xbank()
                    mm(bka[0:64, :], [(wq[:, kc, 128:192], cqnT[:, kc, bs_]) for kc in range(2)], [rwq, r_cqn], rba)
                    bkb, rbb = xbank()
                    mm(bkb[0:64, :], [(wqs[:, kc, :], cqnT[:, kc, bs_]) for kc in range(2)], [rwqs, r_cqn], rbb)
                    cq_, r_cq_ = cstq2[b & 1], r_cstq2[b & 1]
                    if b == 0:
                        fw.dma("sp", cq_, cs_scr[:, :, bs_].rearrange("c p n -> p c n"), writes=[r_cq_])
                    if b + 1 < NBLK:
                        nb_ = slice((b + 1) * 512, (b + 2) * 512)
                        fw.dma("sp", cstq2[(b + 1) & 1], cs_scr[:, :, nb_].rearrange("c p n -> p c n"), writes=[r_cstq2[(b + 1) & 1]])
                    fw.op("dve", lambda e, bka=bka, cq_=cq_: e.tensor_tensor(rt[0], bka[0:64, :], cq_[:, 0, :], ALU.mult), reads=[rba, r_cq_], writes=[r_rt])
                    fw.op("dve", lambda e, bkb=bkb, cq_=cq_: e.tensor_tensor(rt[1], bkb[0:64, :], cq_[:, 1, :], ALU.mult), reads=[rbb, r_cq_], writes=[r_rt])
                    fw.op("pool", lambda e, bs_=bs_: e.tensor_tensor(QrT[0:64, bs_], rt[0], rt[1], ALU.add), reads=[r_rt], writes=[r_QrT])
                    bk, rb = xbank()
                    for tt in range(4):
                        ts_ = slice(b * 512 + tt * 128, b * 512 + (tt + 1) * 128)
                        mm(bk[:, tt * 128:(tt + 1) * 128], [(ckvnT[:, kc, ts_], wk[:, kc, 128:256]) for kc in range(2)], [rwk, r_ckvn], rb)
                    copy_op(evac_eng(), Vh[:, bs_], bk[:, :], [rb], [r_Vh])
                    advance()
                for qb in range(NBLK):
                    qs_ = slice(qb * 512, (qb + 1) * 512)
                    bo, rbo = banks[3], r_bank[3]
                    bsm, rbsm = banks[4], r_bank[4]

                    def S_step(kc, qs_=qs_):
                        ks_ = slice(kc * 128, (kc + 1) * 128)
                        bS, rbS = banks[kc % 3], r_bank[kc % 3]
                        mm(bS[:, :], [(KhT[:, ks_], QhT[:, qs_]), (krT[:, ks_], QrT[:, qs_])], [r_KhT, r_QhT, r_krT, r_QrT], rbS)
                        fw.op("act", lambda e, bS=bS, kc=kc: e.activation(PT[kc % 3], bS[:, :], AF.Exp, scale=A_SCALE), reads=[rbS], writes=[r_PT[kc % 3]])

                    def O_step(kc, bo=bo, rbo=rbo, bsm=bsm, rbsm=rbsm):
                        ks_ = slice(kc * 128, (kc + 1) * 128)
                        fw.op("pe", lambda e, kc=kc, ks_=ks_, bo=bo: e.matmul(bo[:, :], Vh[:, ks_], PT[kc % 3], start=(kc == 0), stop=(kc == NT - 1)),
                              reads=[r_Vh, r_PT[kc % 3]], writes=[rbo], signal=False)
                        fw.op("pe", lambda e, kc=kc, bsm=bsm: e.matmul(bsm[:, :], onesb[:], PT[kc % 3], start=(kc == 0), stop=(kc == NT - 1)),
                              reads=[r_const, r_PT[kc % 3]], writes=[rbsm], signal=True)
                    S_step(0)
                    S_step(1)
                    for kc in range(NT):
                        if kc + 2 < NT:
                            S_step(kc + 2)
                        O_step(kc)
                        advance()
                    fw.op("dve", lambda e, bsm=bsm: e.reciprocal(rsum, bsm[:, :]), reads=[rbsm], writes=[r_rsum])
                    o_ = oTs[qb & 1]
                    fw.op("dve", lambda e, bo=bo, o_=o_: e.tensor_tensor(o_, bo[:, :], rsum, ALU.mult), reads=[rbo, r_rsum], writes=[r_oTs[qb & 1]])
                    fw.dma("pool", oT_scr[h, :, qs_], o_, reads=[r_oTs[qb & 1]], writes=[r_oT])
            n_inside = gst["n"]
            while not gst["done"]:
                advance()
            mix_stats.append((n_inside, gst["n"]))

        def phase_attn(sq, heads=range(8), main_loop=True, qbs=None):
            cqnT = VB(0, 8192).rearrange("p (k n) -> p k n", k=2); r_cqn = RES("cqnT")
            ckvnT = VB(16384, 8192).rearrange("p (k n) -> p k n", k=2); r_ckvn = RES("ckvnT")
            krT = VB(32768, 4096, 128); r_krT = RES("krT")
            KhT = VB(40960, 4096); r_KhT = RES("KhT")
            Vh = VB(49152, 4096); r_Vh = RES("Vh")
            QhT = VB(57344, 4096); r_QhT = RES("QhT")
            QrT = VB(65536, 4096, 128); r_QrT = RES("QrT")
            cstq = VF(73728, 1024, 64).rearrange("p (c n) -> p c n", c=2); r_cstq = RES("cstq")
            rt = [VF(77824 + i * 2048, 512, 64) for i in range(2)]; r_rt = RES("rt")
            PT = [VB(81920 + i * 1024, 512) for i in range(3)]; r_PT = [RES("PT%d" % i) for i in range(3)]
            rsum = VF(84992, 512); r_rsum = RES("rsum")
            oTs = [VB(87040 + i * 1024, 512) for i in range(2)]; r_oTs = [RES("oTs%d" % i) for i in range(2)]
            r_oT = RES("@oT")
            fw.dma("sp", cqnT, cqnT_scr.rearrange("k p n -> p k n"), writes=[r_cqn])
            fw.dma("sp", ckvnT, ckvnT_scr.rearrange("k p n -> p k n"), writes=[r_ckvn])
            fw.op("pool", lambda e: e.memset(krT[64:128, :], 0.0), writes=[r_krT])
            fw.op("pool", lambda e: e.memset(QrT[64:128, :], 0.0), writes=[r_QrT])
            fw.dma("sp", krT[0:64, :], krT_scr, reads=[r_krT], writes=[r_krT])
            for h in heads:
                wk, rwk = wload("w_ukv", Wb["w_ukv"], 0, 2, h * 256, 256)
                wq, rwq = wload("w_uq", Wb["w_uq"], 0, 2, h * 192, 192)
                wqs, rwqs = wload("w_uq_sw", w_uq_sw, 0, 2, h * 64, 64)
                for b in range(NBLK):
                    bs_ = slice(b * 512, (b + 1) * 512)
                    bk, rb = nbank()
                    mm(bk[:, :], [(wk[:, kc, 0:128], ckvnT[:, kc, bs_]) for kc in range(2)], [rwk, r_ckvn], rb)
                    copy_op(evac_eng(), KhT[:, bs_], bk[:, :], [rb], [r_KhT])
                    bk, rb = nbank()
                    mm(bk[:, :], [(wq[:, kc, 0:128], cqnT[:, kc, bs_]) for kc in range(2)], [rwq, r_cqn], rb)
                    copy_op(evac_eng(), QhT[:, bs_], bk[:, :], [rb], [r_QhT])
                    bka, rba = nbank()
                    mm(bka[0:64, :], [(wq[:, kc, 128:192], cqnT[:, kc, bs_]) for kc in range(2)], [rwq, r_cqn], rba)
                    bkb, rbb = nbank()
                    mm(bkb[0:64, :], [(wqs[:, kc, :], cqnT[:, kc, bs_]) for kc in range(2)], [rwqs, r_cqn], rbb)
                    fw.dma("sp", cstq, cs_scr[:, :, bs_].rearrange("c p n -> p c n"), writes=[r_cstq])
                    fw.op("dve", lambda e, bka=bka: e.tensor_tensor(rt[0], bka[0:64, :], cstq[:, 0, :], ALU.mult), reads=[rba, r_cstq], writes=[r_rt])
                    fw.op("dve", lambda e, bkb=bkb: e.tensor_tensor(rt[1], bkb[0:64, :], cstq[:, 1, :], ALU.mult), reads=[rbb, r_cstq], writes=[r_rt])
                    fw.op("pool", lambda e, bs_=bs_: e.tensor_tensor(QrT[0:64, bs_], rt[0], rt[1], ALU.add), reads=[r_rt], writes=[r_QrT])
                    bk, rb = nbank()
                    for tt in range(4):
                        ts_ = slice(b * 512 + tt * 128, b * 512 + (tt + 1) * 128)
                        mm(bk[:, tt * 128:(tt + 1) * 128], [(ckvnT[:, kc, ts_], wk[:, kc, 128:256]) for kc in range(2)], [rwk, r_ckvn], rb)
                    copy_op(evac_eng(), Vh[:, bs_], bk[:, :], [rb], [r_Vh])
                for qb in ((qbs if qbs is not None else range(NBLK)) if main_loop else ()):
                    qs_ = slice(qb * 512, (qb + 1) * 512)
                    bo, rbo = banks[4 + (qb & 1)], r_bank[4 + (qb & 1)]
                    bsm, rbsm = banks[6 + (qb & 1)], r_bank[6 + (qb & 1)]

                    def S_step(kc):
                        ks_ = slice(kc * 128, (kc + 1) * 128)
                        bS, rbS = banks[kc % 4], r_bank[kc % 4]
                        mm(bS[:, :], [(KhT[:, ks_], QhT[:, qs_]), (krT[:, ks_], QrT[:, qs_])], [r_KhT, r_QhT, r_krT, r_QrT], rbS)
                        fw.op("act", lambda e, bS=bS, kc=kc: e.activation(PT[kc % 3], bS[:, :], AF.Exp, scale=A_SCALE), reads=[rbS], writes=[r_PT[kc % 3]])

                    def O_step(kc):
                        ks_ = slice(kc * 128, (kc + 1) * 128)
                        fw.op("pe", lambda e, kc=kc, ks_=ks_, bo=bo: e.matmul(bo[:, :], Vh[:, ks_], PT[kc % 3], start=(kc == 0), stop=(kc == NT - 1)),
                              reads=[r_Vh, r_PT[kc % 3]], writes=[rbo], signal=False)
                        fw.op("pe", lambda e, kc=kc, bsm=bsm: e.matmul(bsm[:, :], onesb[:], PT[kc % 3], start=(kc == 0), stop=(kc == NT - 1)),
                              reads=[r_const, r_PT[kc % 3]], writes=[rbsm], signal=True)
                    S_step(0)
                    S_step(1)
                    for kc in range(NT):
                        if kc + 2 < NT:
                            S_step(kc + 2)
                        O_step(kc)
                    fw.op("dve", lambda e, bsm=bsm: e.reciprocal(rsum, bsm[:, :]), reads=[rbsm], writes=[r_rsum])
                    o_ = oTs[qb & 1]
                    fw.op("dve", lambda e, bo=bo, o_=o_: e.tensor_tensor(o_, bo[:, :], rsum, ALU.mult), reads=[rbo, r_rsum], writes=[r_oTs[qb & 1]])
                    fw.dma("pool", oT_scr[h, :, qs_], o_, reads=[r_oTs[qb & 1]], writes=[r_oT])
            if dbg == "M1":
                fw.dma("pool", dbg_out["d_oT"], oTs[0], reads=[r_oTs[0]], writes=[RES("@dbgM1")])
            if isinstance(dbg, str) and dbg.startswith("M:"):
                fw.dma("pool", dbg_out["d_oTs0"], oTs[0], reads=[r_oTs[0]], writes=[RES("@dbgMa")])
                fw.dma("pool", dbg_out["d_oTs1"], oTs[1], reads=[r_oTs[1]], writes=[RES("@dbgMb")])
            if dbg == "E":
                rd = RES("@dbgE")
                for n_, ap_, r_ in (("d_cqnT", cqnT, r_cqn), ("d_ckvnT", ckvnT, r_ckvn), ("d_krT", krT[0:64, :], r_krT),
                                    ("d_KhT", KhT, r_KhT), ("d_QhT", QhT, r_QhT), ("d_QrT", QrT[0:64, :], r_QrT), ("d_Vh", Vh, r_Vh)):
                    fw.dma("pool", dbg_out[n_], ap_, reads=[r_], writes=[rd])

        for sq in range(NSEQ):
            if 1 in phases:
                phase1(sq)
                fw.barrier()
            if 5 in phases:
                phase_mix(sq)
                fw.barrier()
            if 2 in phases:
                phase_mlstm(sq, heads=([0] if dbg == "L1" else range(4)))
                fw.barrier()
            if 3 in phases:
                if dbg == "E":
                    phase_attn(sq, heads=[0], main_loop=False)
                elif dbg == "M1":
                    phase_attn(sq, heads=[0], main_loop=True, qbs=[0])
                elif isinstance(dbg, str) and dbg.startswith("M:"):
                    _, nh_, nq_ = dbg.split(":")
                    phase_attn(sq, heads=list(range(int(nh_))), main_loop=True, qbs=list(range(int(nq_))))
                else:
                    phase_attn(sq)
                fw.barrier()
            if 4 in phases:
                phase3(sq)
                fw.barrier()
        fw.barrier()
        fw.emit_all()
    if mix_stats:
        print("[phase_mix] scan generator steps issued inside attention / total, per sequence:", mix_stats)
    return nc


NSEQ_PER_CORE = 3


def _consts():
    ident = np.eye(128, dtype=np.float32)
    s = np.arange(128)[:, None]
    t = np.arange(128)[None, :]
    maskf = np.where(t >= s, 0.0, -1000.0).astype(np.float32)
    maskb = np.where(t <= s, 0.0, -1000.0).astype(np.float32)
    return {"c_ident": ident, "c_maskf": maskf, "c_maskb": maskb}


def kernel(**inputs):
    X = np.concatenate([inputs["x_prompt"], inputs["x_sample"]], axis=0)
    P = np.concatenate([inputs["p_prompt"][0], inputs["p_sample"][0]], axis=0)
    nseq = X.shape[0]
    base = {}
    for k in WSHAPES:
        base[k] = np.ascontiguousarray(inputs[k][0])
    for k in GSHAPES:
        base[k] = np.ascontiguousarray(inputs[k].reshape(1, -1))
    base["g_final"] = np.ascontiguousarray(inputs["g_final"].reshape(1, -1))
    base["b_gate"] = np.ascontiguousarray(inputs["b_gate"].reshape(1, 16))
    base["conv_w"] = np.ascontiguousarray(inputs["conv_w"][0])
    base.update(_consts())
    slots = [[c, c + 8, c + 16 if c + 16 < nseq else c] for c in range(8)]
    maps = []
    for sl in slots:
        m = dict(base)
        m["xs"] = np.ascontiguousarray(X[sl])
        m["ps"] = np.ascontiguousarray(P[sl])
        maps.append(m)
    nc = build(NSEQ_PER_CORE, dbg=False, phases=(1, 5, 4))
    res = run_bass_kernel_spmd(nc, maps, core_ids=list(range(8)))
    Y = np.zeros((nseq, S, D), dtype=np.float32)
    for c, sl in enumerate(slots):
        yc = res.results[c]["y"]
        for j, sidx in enumerate(sl):
            if j == 2 and c + 16 >= nseq:
                continue
            Y[sidx] = yc[j]
    nb = inputs["x_prompt"].shape[0]
    return (np.ascontiguousarray(Y[:nb], dtype=np.float32), np.ascontiguousarray(Y[nb:], dtype=np.float32))
```

```python
import numpy as np
import concourse.bass as bass
import concourse.mybir as mybir
from concourse.bass_utils import run_bass_kernel_spmd

F32 = mybir.dt.float32
BF16 = mybir.dt.bfloat16
I32 = mybir.dt.int32
AF = mybir.ActivationFunctionType
ALU = mybir.AluOpType
AX = mybir.AxisListType


class Sem:
    def __init__(self, handle, name, dma=False, owner=None):
        self.h = handle
        self.name = name
        self.dma = dma
        self.owner = owner
        self.issued = 0


class Res:
    __slots__ = ("name", "w", "r", "dsem")

    def __init__(self, name):
        self.name = name
        self.w = None
        self.r = {}
        self.dsem = None


class Eng:
    def __init__(self, name, sem, is_pe=False):
        self.name = name
        self.sem = sem
        self.ops = []
        self.seen = {}
        self.is_pe = is_pe
        self.pending = False


class FW:
    def __init__(self, nc, stack):
        self.nc = nc
        self.stack = stack
        self.engs = {}
        self.nsem = 0
        self.all_sems = []
        for n in ("pe", "act", "dve", "pool", "sp"):
            self.engs[n] = Eng(n, None, is_pe=(n == "pe"))
            self.engs[n].sem = self._newsem("s_" + n, owner=self.engs[n])
        self.dma_res = []
        self.n_ins = 0

    SEM_LIMIT = 6000

    def _newsem(self, name, dma=False, owner=None):
        self.nsem += 1
        h = self.stack.enter_context(self.nc.semaphore("%s_%d" % (name, self.nsem)))
        sm = Sem(h, name, dma=dma, owner=owner)
        self.all_sems.append(sm)
        return sm

    def dsem(self, res, qclass):
        if res.dsem is None:
            res.dsem = {}
            self.dma_res.append(res)
        sm = res.dsem.get(qclass)
        if sm is None or sm.issued >= self.SEM_LIMIT:
            sm = res.dsem[qclass] = self._newsem("d%s_%s" % (qclass[0], res.name.replace("@", "h_")), dma=True)
        return sm

    def _collect(self, E, reads, writes):
        waits = {}

        def need(ev):
            if ev is None:
                return
            sem, val = ev
            if sem.owner is E:
                if E.is_pe:
                    return
            else:
                if sem.dma and sem.issued > val:
                    val = sem.issued
            if E.seen.get(sem, 0) >= val:
                return
            if waits.get(sem, 0) < val:
                waits[sem] = val

        for r in reads:
            need(r.w)
        for w in writes:
            need(w.w)
            for s, v in w.r.items():
                need((s, v))
        for s, v in waits.items():
            E.seen[s] = v
        return list(waits.items())

    def _commit(self, ev, reads, writes):
        for w in writes:
            w.w = ev
            w.r = {}
        for r in reads:
            if r in writes:
                continue
            s, v = ev
            if r.r.get(s, 0) < v:
                r.r[s] = v

    def op(self, eng, fn, reads=(), writes=(), signal=True):
        E = self.engs[eng]
        waits = self._collect(E, reads, writes)
        if signal:
            E.sem.issued += 1
        val = E.sem.issued if signal else E.sem.issued + 1
        sem_h = E.sem.h

        def emit(e, waits=waits, fn=fn, signal=signal):
            for s, v in waits:
                e.wait_ge(s.h, v)
            ins = fn(e)
            if signal:
                ins.then_inc(sem_h, 1)
        E.ops.append(emit)
        E.pending = not signal
        self._commit((E.sem, val), reads, writes)
        self.n_ins += 1
        if signal and E.sem.issued >= self.SEM_LIMIT:
            E.sem = self._newsem("s_" + E.name, owner=E)

    def dma(self, eng, out, in_, reads=(), writes=(), sem_res=None, **kw):
        E = self.engs[eng]
        waits = self._collect(E, reads, writes)
        if sem_res is None:
            sem_res = writes[0] if (writes and writes[0].name[0] != "@") else reads[0]
        ds = self.dsem(sem_res, "sw" if eng == "pool" else "hw")
        ds.issued += 16
        val = ds.issued

        def emit(e, waits=waits):
            for s, v in waits:
                e.wait_ge(s.h, v)
            e.dma_start(out=out, in_=in_, **kw).then_inc(ds.h, 16)
        E.ops.append(emit)
        self._commit((ds, val), reads, writes)
        self.n_ins += 1

    def barrier(self):
        for E in self.engs.values():
            assert not E.pending
        evs = [(sm, sm.issued) for sm in self.all_sems if sm.issued > 0]
        for E in self.engs.values():
            waits = []
            for s, v in evs:
                if s.owner is E and E.is_pe:
                    continue
                if E.seen.get(s, 0) < v:
                    waits.append((s, v))
                    E.seen[s] = v

            def emit(e, waits=waits):
                for s, v in waits:
                    e.wait_ge(s.h, v)
            E.ops.append(emit)

    def wait_all(self, eng, resources):
        E = self.engs[eng]
        waits = self._collect(E, resources, [])

        def emit(e, waits=waits):
            for s, v in waits:
                e.wait_ge(s.h, v)
        E.ops.append(emit)

    def emit_all(self):
        nc = self.nc
        with nc.Block() as block:
            @block.tensor
            def _(e):
                for f in self.engs["pe"].ops:
                    f(e)

            @block.scalar
            def _(e):
                for f in self.engs["act"].ops:
                    f(e)

            @block.vector
            def _(e):
                for f in self.engs["dve"].ops:
                    f(e)

            @block.gpsimd
            def _(e):
                for f in self.engs["pool"].ops:
                    f(e)

            @block.sync
            def _(e):
                for f in self.engs["sp"].ops:
                    f(e)


import math
from contextlib import ExitStack

S = 4096
D = 1024
NBLK = 8
NT = 32
DFF = 2816
V0, O0, G0, CQ0, CKV0, KR0, GM0, INW = 1024, 2048, 3072, 3088, 3344, 3600, 3664, 5712
EPS = 1e-6
LN_SQ = math.log(128 ** -0.5)
A_SCALE = 192 ** -0.5

WSHAPES = {
    "w_in": (1024, 5712), "w_mo": (1024, 1024), "w_uq": (256, 1536), "w_ukv": (256, 2048),
    "w_ao": (1024, 1024), "w_out": (1024, 1024), "w_gate": (1024, 2816), "w_up": (1024, 2816),
    "w_down": (2816, 1024), "w_ple": (256, 1024), "w_ple_gate": (1024, 1024),
}
GAINS = {"w_in": "g_mix", "w_mo": "g_mhead", "w_uq": "g_q", "w_ukv": "g_kv", "w_gate": "g_ffn", "w_up": "g_ffn"}
GSHAPES = {"g_mix": 1024, "g_mhead": 1024, "g_q": 256, "g_kv": 256, "g_ffn": 1024}


def build(NSEQ, dbg=False, phases=(1, 2, 3, 4)):
    nc = bass.Bass("TRN2", target_bir_lowering=False)
    din = lambda n, sh, dt=F32: nc.dram_tensor(n, list(sh), dt, kind="ExternalInput").ap()
    dscr = lambda n, sh, dt: nc.dram_tensor(n, list(sh), dt, kind="Internal").ap()
    xs = din("xs", [NSEQ, S, D])
    ps = din("ps", [NSEQ, S, 256])
    Wf = {k: din(k, sh) for k, sh in WSHAPES.items()}
    Gd = {k: din(k, [1, n]) for k, n in GSHAPES.items()}
    g_final = din("g_final", [1, D])
    b_gate = din("b_gate", [1, 16])
    conv_w = din("conv_w", [5, 1024])
    c_ident = din("c_ident", [128, 128])
    c_maskf = din("c_maskf", [128, 128])
    c_maskb = din("c_maskb", [128, 128])
    y = nc.dram_tensor("y", [NSEQ, S, D], F32, kind="ExternalOutput").ap()
    dbg_out = {}
    if dbg == "P1":
        dbg_out["d_colsT"] = nc.dram_tensor("d_colsT", [128, 512], F32, kind="ExternalOutput").ap()
    if dbg == "M1":
        dbg_out["d_oT"] = nc.dram_tensor("d_oT", [128, 512], BF16, kind="ExternalOutput").ap()
    if isinstance(dbg, str) and dbg.startswith("M:"):
        dbg_out["d_oTs0"] = nc.dram_tensor("d_oTs0", [128, 512], BF16, kind="ExternalOutput").ap()
        dbg_out["d_oTs1"] = nc.dram_tensor("d_oTs1", [128, 512], BF16, kind="ExternalOutput").ap()
    if dbg == "E":
        for n_, sh_ in (("d_cqnT", [128, 2, S]), ("d_ckvnT", [128, 2, S]), ("d_krT", [64, S]),
                        ("d_KhT", [128, S]), ("d_QhT", [128, S]), ("d_QrT", [64, S]), ("d_Vh", [128, S])):
            dbg_out[n_] = nc.dram_tensor(n_, sh_, BF16, kind="ExternalOutput").ap()
    Wb = {k: dscr(k + "_b", sh, BF16) for k, sh in WSHAPES.items()}
    w_kr_sw = dscr("w_kr_sw", [1024, 64], BF16)
    w_uq_sw = dscr("w_uq_sw", [256, 512], BF16)
    cs_scr = dscr("cs_scr", [2, 64, S], F32)
    qk_scr = (nc.dram_tensor("qk_scr", [8, 128, S], BF16, kind="ExternalOutput").ap() if dbg == "P1" else dscr("qk_scr", [8, 128, S], BF16))
    v_scr = (nc.dram_tensor("v_scr", [S, 1024], BF16, kind="ExternalOutput").ap() if dbg == "P1" else dscr("v_scr", [S, 1024], BF16))
    nb_scr = (nc.dram_tensor("nb_scr", [2, 4, S], F32, kind="ExternalOutput").ap() if dbg == "P1" else dscr("nb_scr", [2, 4, S], F32))
    hf_scr = dscr("hf_scr", [S, 256], F32)
    hmT_scr = (nc.dram_tensor("hmT_scr", [8, 128, S], BF16, kind="ExternalOutput").ap() if dbg == "L1" else dscr("hmT_scr", [8, 128, S], BF16))
    if isinstance(dbg, str) and dbg.startswith("M:"):
        oT_scr = nc.dram_tensor("oT_scr", [8, 128, S], BF16, kind="ExternalOutput").ap()
    else:
        oT_scr = dscr("oT_scr", [8, 128, S], BF16)
    cqnT_scr = dscr("cqnT_scr", [2, 128, S], BF16)
    ckvnT_scr = dscr("ckvnT_scr", [2, 128, S], BF16)
    krT_scr = dscr("krT_scr", [64, S], BF16)

    _res_registry = {}
    _ResClass = Res

    def RES(name):
        r = _res_registry.get(name)
        if r is None:
            r = _res_registry[name] = _ResClass(name)
        return r

    with ExitStack() as st:
        fw = FW(nc, st)
        T = lambda name, shape, dt: st.enter_context(nc.sbuf_tensor(name, shape, dt))
        identf = T("identf", [128, 128], F32)
        identb = T("identb", [128, 128], BF16)
        maskf = T("maskf", [128, 128], F32)
        maskb = T("maskb", [128, 128], F32)
        onesb = T("onesb", [128, 128], BF16)
        gfin = T("gfin", [128, D], F32)
        convw = T("convw", [128, 5, 8], F32)
        gcol = {k: T("gc_" + k, [128, n // 128], F32) for k, n in GSHAPES.items()}
        gneg = {k: T("gn_" + k, [128, GSHAPES[k] // 128], F32) for k in ("g_mix", "g_q")}
        bg = T("bg", [4, 4], F32)
        colsT = T("colsT", [128, 4, NT, 4], F32)
        wsl = [T("wsl%d" % i, [128, 4096], BF16) for i in range(6)]
        r_wsl = [RES("wsl%d" % i) for i in range(6)]
        AR = 34816
        arena = T("arena", [128, AR], F32)
        arena_b = arena.bitcast(BF16)
        arena_i = arena.bitcast(I32)
        banks = [st.enter_context(nc.psum_tensor("pb%d" % i, [128, 512], F32)) for i in range(8)]
        r_bank = [RES("pb%d" % i) for i in range(8)]
        r_const = RES("const")
        state = {"bank": 0, "wsl": 0, "ev": 0}

        def VF(off, n, parts=128):
            assert off % 4 == 0 and off // 4 + n <= AR
            return arena[0:parts, off // 4: off // 4 + n]

        def VB(off, n, parts=128):
            assert off % 2 == 0 and off // 2 + n <= 2 * AR
            return arena_b[0:parts, off // 2: off // 2 + n]

        def nbank():
            i = state["bank"]
            state["bank"] = (i + 1) % 8
            return banks[i], r_bank[i]

        def evac_eng():
            state["ev"] ^= 1
            return "act" if state["ev"] else "dve"

        def copy_op(eng, out, in_, reads, writes):
            if eng == "act":
                fw.op("act", lambda e: e.copy(out, in_), reads=reads, writes=writes)
            else:
                fw.op(eng, lambda e: e.tensor_copy(out, in_), reads=reads, writes=writes)

        def mm(out_ap, pairs, reads, wres):
            n = len(pairs)
            for i, (l, r) in enumerate(pairs):
                fw.op("pe", lambda e, l=l, r=r, i=i: e.matmul(out_ap, l, r, start=(i == 0), stop=(i == n - 1)),
                      reads=reads, writes=[wres], signal=(i == n - 1))

        def transp(out_ap, in_ap, ident, reads, wres, signal=True):
            fw.op("pe", lambda e: e.transpose(out_ap, in_ap, ident), reads=reads, writes=[wres], signal=signal)

        r_W = {k: RES("@" + k) for k in WSHAPES}
        r_W["w_kr_sw"] = RES("@w_kr_sw")
        r_W["w_uq_sw"] = RES("@w_uq_sw")

        def wload(name, W_ap, k0, nk, c0, ncols):
            i = state["wsl"]
            state["wsl"] = (i + 1) % 6
            assert nk * ncols <= 4096
            view = wsl[i][:, 0:nk * ncols].rearrange("p (k n) -> p k n", k=nk)
            src = W_ap[k0 * 128:(k0 + nk) * 128, c0:c0 + ncols].rearrange("(k p) n -> p k n", p=128)
            fw.dma("sp", view, src, reads=[r_W[name]], writes=[r_wsl[i]])
            return view, r_wsl[i]

        fw.dma("sp", identf[:], c_ident, writes=[r_const])
        fw.dma("sp", maskf[:], c_maskf, writes=[r_const])
        fw.dma("sp", maskb[:], c_maskb, writes=[r_const])
        fw.dma("sp", gfin[:], g_final.partition_broadcast(128).rearrange("p o n -> p (o n)"), writes=[r_const])
        for j in range(5):
            fw.dma("sp", convw[:, j, :], conv_w[j:j + 1, :].rearrange("o (c p) -> p (o c)", p=128), writes=[r_const], allow_slow_non_contiguous=True)
        for k in GSHAPES:
            fw.dma("sp", gcol[k][:], Gd[k].rearrange("o (k p) -> p (o k)", p=128), writes=[r_const], allow_slow_non_contiguous=True)
        fw.dma("sp", bg[:], b_gate.rearrange("o (g h) -> h (o g)", h=4), writes=[r_const], allow_slow_non_contiguous=True)
        fw.op("dve", lambda e: e.tensor_copy(identb[:], identf[:]), reads=[r_const], writes=[r_const])
        fw.op("dve", lambda e: e.memset(onesb[:], 1.0), writes=[r_const])
        for k in gneg:
            fw.op("dve", lambda e, k=k: e.tensor_scalar(gneg[k][:], gcol[k][:], -1.0, None, ALU.mult), reads=[r_const], writes=[r_const])
        fw.barrier()

        stg_in = [VF(i * 8192, 2048) for i in range(4)]
        stg_out = [VB(32768 + i * 4096, 2048) for i in range(4)]
        r_si = [RES("stg_in%d" % i) for i in range(4)]
        r_so = [RES("stg_out%d" % i) for i in range(4)]
        pw = {"i": 0}

        def prep(name):
            K, N = WSHAPES[name]
            g = gcol[GAINS[name]] if name in GAINS else None
            for kc in range(K // 128):
                for c0 in range(0, N, 2048):
                    n_ = min(2048, N - c0)
                    i = pw["i"] & 3
                    pw["i"] += 1
                    fw.dma("sp", stg_in[i][:, 0:n_], Wf[name][kc * 128:(kc + 1) * 128, c0:c0 + n_], writes=[r_si[i]])
                    eng = evac_eng()
                    if g is None:
                        copy_op(eng, stg_out[i][:, 0:n_], stg_in[i][:, 0:n_], [r_si[i]], [r_so[i]])
                    elif eng == "act":
                        fw.op("act", lambda e, i=i, kc=kc, n_=n_, g=g: e.activation(stg_out[i][:, 0:n_], stg_in[i][:, 0:n_], AF.Copy, scale=g[:, kc:kc + 1]),
                              reads=[r_si[i], r_const], writes=[r_so[i]])
                    else:
                        fw.op("dve", lambda e, i=i, kc=kc, n_=n_, g=g: e.tensor_scalar(stg_out[i][:, 0:n_], stg_in[i][:, 0:n_], g[:, kc:kc + 1], None, ALU.mult),
                              reads=[r_si[i], r_const], writes=[r_so[i]])
                    fw.dma("pool", Wb[name][kc * 128:(kc + 1) * 128, c0:c0 + n_], stg_out[i][:, 0:n_], reads=[r_so[i]], writes=[RES("@%s_st%d" % (name, pw["i"]))])

        def prep_sw(dst, dname, src, K, bases, g, gn):
            for kc in range(K // 128):
                i = pw["i"] & 3
                pw["i"] += 1
                nb_ = len(bases)
                for j, b in enumerate(bases):
                    fw.dma("sp", stg_in[i][:, j * 64:(j + 1) * 64], src[kc * 128:(kc + 1) * 128, b:b + 64], writes=[r_si[i]])
                for j in range(nb_):
                    fw.op("dve", lambda e, i=i, j=j, kc=kc: e.tensor_scalar(stg_out[i][:, j * 64:j * 64 + 32], stg_in[i][:, j * 64 + 32:j * 64 + 64], gn[:, kc:kc + 1], None, ALU.mult),
                          reads=[r_si[i], r_const], writes=[r_so[i]])
                    fw.op("dve", lambda e, i=i, j=j, kc=kc: e.tensor_scalar(stg_out[i][:, j * 64 + 32:j * 64 + 64], stg_in[i][:, j * 64:j * 64 + 32], g[:, kc:kc + 1], None, ALU.mult),
                          reads=[r_si[i], r_const], writes=[r_so[i]])
                fw.dma("pool", dst[kc * 128:(kc + 1) * 128, 0:64 * nb_], stg_out[i][:, 0:64 * nb_], reads=[r_so[i]], writes=[RES("@%s_st%d" % (dname, pw["i"]))])

        for name in WSHAPES:
            prep(name)
        prep_sw(w_kr_sw, "w_kr_sw", Wf["w_in"], 1024, [KR0], gcol["g_mix"], gneg["g_mix"])
        prep_sw(w_uq_sw, "w_uq_sw", Wf["w_uq"], 256, [h * 192 + 128 for h in range(8)], gcol["g_q"], gneg["g_q"])

        r_tr = RES("trig")
        TB0 = 49152
        posf = VF(TB0, S, 64)
        tq = VF(TB0 + 16384, S, 64)
        ti = arena_i[0:64, (TB0 + 32768) // 4:(TB0 + 32768) // 4 + S]
        tk = VF(TB0 + 49152, S, 64)
        tu = VF(TB0 + 65536, S, 64)
        pidx = T("pidx", [64, 1], F32)
        invf = T("invf", [64, 1], F32)
        pidx_i = T("pidx_i", [64, 1], I32)
        fw.op("pool", lambda e: e.iota(ti, [[1, S]], base=0, channel_multiplier=0), writes=[r_tr])
        fw.op("pool", lambda e: e.iota(pidx_i[0:32, :], [[0, 1]], base=0, channel_multiplier=1), writes=[r_tr])
        fw.op("pool", lambda e: e.iota(pidx_i[32:64, :], [[0, 1]], base=0, channel_multiplier=1), writes=[r_tr])
        fw.op("dve", lambda e: e.tensor_copy(posf, ti), reads=[r_tr], writes=[r_tr])
        fw.op("dve", lambda e: e.tensor_copy(pidx[:], pidx_i[:]), reads=[r_tr], writes=[r_tr])
        fw.op("act", lambda e: e.activation(invf[:], pidx[:], AF.Exp, scale=-math.log(10000.0) / 32.0), reads=[r_tr], writes=[r_tr])
        fw.op("dve", lambda e: e.tensor_scalar(tq, posf, invf[:, 0:1], 1.0 / (2 * math.pi), ALU.mult, ALU.mult), reads=[r_tr], writes=[r_tr])
        for which, shift in ((1, 0.0), (0, 0.25)):
            fw.op("dve", lambda e, shift=shift: e.tensor_scalar(tu, tq, shift, None, ALU.add), reads=[r_tr], writes=[r_tr])
            fw.op("dve", lambda e: e.tensor_copy(ti, tu), reads=[r_tr], writes=[r_tr])
            fw.op("dve", lambda e: e.tensor_copy(tk, ti), reads=[r_tr], writes=[r_tr])
            fw.op("dve", lambda e: e.tensor_tensor(tu, tu, tk, ALU.subtract), reads=[r_tr], writes=[r_tr])
            fw.op("dve", lambda e: e.tensor_scalar(tk, tu, 0.5, None, ALU.is_gt), reads=[r_tr], writes=[r_tr])
            fw.op("dve", lambda e: e.tensor_tensor(tu, tu, tk, ALU.subtract), reads=[r_tr], writes=[r_tr])
            fw.op("dve", lambda e: e.tensor_scalar(tk, tu, -0.5, None, ALU.is_lt), reads=[r_tr], writes=[r_tr])
            fw.op("dve", lambda e: e.tensor_tensor(tu, tu, tk, ALU.add), reads=[r_tr], writes=[r_tr])
            fw.op("act", lambda e: e.activation(tk, tu, AF.Sin, scale=2 * math.pi), reads=[r_tr], writes=[r_tr])
            fw.dma("pool", cs_scr[which], tk, reads=[r_tr], writes=[RES("@cs")], sem_res=r_tr)
        fw.barrier()

        _once = {}

        def T_once(name, shape, dt):
            if name not in _once:
                _once[name] = T(name, shape, dt)
            return _once[name]
        mix_stats = []
        st1 = [T("st1_%d" % i, [128, 8], F32) for i in range(4)]
        r_st1 = [RES("st1_%d" % i) for i in range(4)]
        cmask = T("cmask", [4, 512], F32)
        fw.op("pool", lambda e: e.memset(cmask[:], 1.0), writes=[r_const])
        for j in range(4):
            fw.op("pool", lambda e, j=j: e.memset(cmask[:, j * 128:j * 128 + 1], 0.0), writes=[r_const])
        fw.barrier()

        def rms_tile(x_ap, xn_ap, stt, r_x, r_xn, r_stt, n):
            fw.op("act", lambda e: e.activation(xn_ap, x_ap, AF.Square, accum_out=stt[:, 0:1]), reads=[r_x], writes=[r_xn, r_stt])
            fw.op("act", lambda e: e.activation(stt[:, 1:2], stt[:, 0:1], AF.Sqrt, bias=EPS, scale=1.0 / n), reads=[r_stt], writes=[r_stt])
            fw.op("dve", lambda e: e.reciprocal(stt[:, 2:3], stt[:, 1:2]), reads=[r_stt], writes=[r_stt])
            fw.op("dve", lambda e: e.tensor_scalar(xn_ap, x_ap, stt[:, 2:3], None, ALU.mult), reads=[r_x, r_stt], writes=[r_xn])

        def transpose_to(dst3, src_ap, nk, reads, r_dst):
            bk, rb = nbank()
            pbt = bk.bitcast(BF16)
            for k in range(nk):
                transp(pbt[:, k * 128:(k + 1) * 128], src_ap[:, k * 128:(k + 1) * 128], identb[:], reads + [r_const], rb, signal=(k == nk - 1))
            eng = evac_eng()
            copy_op(eng, dst3, pbt[:, 0:nk * 128].rearrange("p (k n) -> p k n", k=nk), [rb], [r_dst])

        def phase1(sq):
            hTb = [VB(i * 8192, 4096).rearrange("p (k n) -> p k n", k=8) for i in range(2)]
            r_hTb = [RES("hTb%d" % i) for i in range(2)]
            xa = [VF(16384 + i * 4096, 1024) for i in range(2)] + [VF(108544 + i * 4096, 1024) for i in range(2)]
            r_xa = [RES("xa%d" % i) for i in range(4)]
            xnb = [VB(24576 + i * 2048, 1024) for i in range(2)] + [VB(116736 + i * 2048, 1024) for i in range(2)]
            r_xnb = [RES("xnb%d" % i) for i in range(4)]
            U = [VF(28672 + ct * 2064, 516) for ct in range(8)]
            r_U = [RES("U%d" % ct) for ct in range(8)]
            acc = [VF(45184 + i * 2048, 512) for i in range(2)]
            r_acc = [RES("acc%d" % i) for i in range(2)]
            qkb = [VB(49280 + i * 1024, 512) for i in range(2)]
            r_qkb = [RES("qkb%d" % i) for i in range(2)]
            vst = [VB(51328 + i * 2048, 1024) for i in range(4)]
            r_vst = [RES("vst%d" % i) for i in range(4)]
            GB = 59520
            gt = [VF(GB + i * 2048, 512, 4) for i in range(16)]
            r_g = RES("gates")
            LB = GB + 32768
            lat = VB(LB, 2048).rearrange("p (k n) -> p k n", k=4)
            r_lat = RES("lat")
            cnb = [VB(LB + 4096 + i * 1024, 512) for i in range(2)]
            r_cnb = [RES("cnb%d" % i) for i in range(2)]
            cst = VF(LB + 6144, 1024, 64).rearrange("p (c n) -> p c n", c=2)
            r_cst = RES("cst")
            krt = [VF(LB + 10240 + i * 2048, 512, 64) for i in range(2)]
            r_krt = RES("krt")
            krb = VB(LB + 14336, 512, 64)
            r_krb = RES("krb")
            r_qk = RES("@qk_scr"); r_v = RES("@v_scr"); r_nb = RES("@nb_scr")
            r_cq = RES("@cq"); r_ckv = RES("@ckv"); r_kr = RES("@kr")
            xi = 0
            for b in range(NBLK):
                tok0 = b * 512
                j = b & 1
                for tt in range(4):
                    i = xi & 3
                    xi += 1
                    fw.dma("sp", xa[i], xs[sq, tok0 + tt * 128: tok0 + (tt + 1) * 128, :], writes=[r_xa[i]])
                    rms_tile(xa[i], xnb[i], st1[i], r_xa[i], r_xnb[i], r_st1[i], 1024)
                    transpose_to(hTb[j][:, :, tt * 128:(tt + 1) * 128], xnb[i], 8, [r_xnb[i]], r_hTb[j])
                for grp in range(2):
                    wv, rw = wload("w_in", Wb["w_in"], 0, 8, grp * 512, 512)
                    for cl in range(4):
                        ct = grp * 4 + cl
                        bk, rb = nbank()
                        mm(bk[:, :], [(wv[:, kc, cl * 128:(cl + 1) * 128], hTb[j][:, kc, :]) for kc in range(8)], [rw, r_hTb[j]], rb)
                        if b == 0:
                            fw.op("pool", lambda e, ct=ct: e.memset(U[ct][:, 0:4], 0.0), writes=[r_U[ct]])
                        fw.op("act", lambda e, ct=ct, bk=bk: e.copy(U[ct][:, 4:516], bk[:, :]), reads=[rb], writes=[r_U[ct]])
                        a = ct & 1

                        def conv(n, a=a, ct=ct):
                            fw.op("act", lambda e: e.activation(acc[a][:, 0:n], U[ct][:, 0:n], AF.Copy, scale=convw[:, 0, ct:ct + 1]),
                                  reads=[r_U[ct], r_const], writes=[r_acc[a]])
                            for jj in range(1, 5):
                                fw.op("dve", lambda e, jj=jj: e.scalar_tensor_tensor(acc[a][:, 0:n], U[ct][:, jj:jj + n], convw[:, jj, ct:ct + 1], acc[a][:, 0:n], ALU.mult, ALU.add),
                                      reads=[r_U[ct], r_const, r_acc[a]], writes=[r_acc[a]])
                            fw.op("act", lambda e: e.activation(qkb[a][:, 0:n], acc[a][:, 0:n], AF.Silu), reads=[r_acc[a]], writes=[r_qkb[a]])
                        conv(512)
                        if b == 0:
                            fw.dma("pool", qk_scr[ct, :, 0:510], qkb[a][:, 2:512], reads=[r_qkb[a]], writes=[r_qk])
                        else:
                            fw.dma("pool", qk_scr[ct, :, tok0 - 2:tok0 + 510], qkb[a][:, 0:512], reads=[r_qkb[a]], writes=[r_qk])
                        fw.op("pool", lambda e, ct=ct: e.tensor_copy(U[ct][:, 0:4], U[ct][:, 512:516]), reads=[r_U[ct]], writes=[r_U[ct]])
                        if b == NBLK - 1:
                            fw.op("pool", lambda e, ct=ct: e.memset(U[ct][:, 4:8], 0.0), writes=[r_U[ct]])
                            conv(2)
                            fw.dma("pool", qk_scr[ct, :, S - 2:S], qkb[a][:, 0:2], reads=[r_qkb[a]], writes=[r_qk])
                for half in range(2):
                    wv, rw = wload("w_in", Wb["w_in"], 0, 8, V0 + half * 512, 512)
                    for tt in range(4):
                        bk, rb = nbank()
                        mm(bk[:, :], [(hTb[j][:, kc, tt * 128:(tt + 1) * 128], wv[:, kc, :]) for kc in range(8)], [rw, r_hTb[j]], rb)
                        copy_op(evac_eng(), vst[tt][:, half * 512:(half + 1) * 512], bk[:, :], [rb], [r_vst[tt]])
                for tt in range(4):
                    fw.dma("pool", v_scr[tok0 + tt * 128: tok0 + (tt + 1) * 128, :], vst[tt], reads=[r_vst[tt]], writes=[r_v])
                wv, rw = wload("w_in", Wb["w_in"], 0, 8, G0, 16)
                for g in range(4):
                    bk, rb = nbank()
                    mm(bk[0:4, :], [(wv[:, kc, 4 * g:4 * g + 4], hTb[j][:, kc, :]) for kc in range(8)], [rw, r_hTb[j]], rb)
                    fw.op("act", lambda e, g=g, bk=bk: e.activation(gt[g], bk[0:4, :], AF.Identity, bias=bg[:, g:g + 1]), reads=[rb, r_const], writes=[r_g])
                IFt, IBt, FFt, FBt = gt[0], gt[1], gt[2], gt[3]
                SPf, SPb, csf, csb, NBf, NBb, Af, Ab, Gf, Gb, tmp = gt[4:15]
                G = lambda fn: fw.op(fn[0], fn[1], reads=[r_g, r_const], writes=[r_g])
                for FX, SP in ((FFt, SPf), (FBt, SPb)):
                    G(("act", lambda e, FX=FX: e.activation(tmp, FX, AF.Exp, scale=-1.0)))
                    G(("act", lambda e, SP=SP: e.activation(SP, tmp, AF.Ln, bias=1.0)))
                G(("dve", lambda e: e.tensor_tensor_scan(csf, cmask[:], SPf, 0.0, ALU.mult, ALU.add)))
                G(("dve", lambda e: e.tensor_tensor_scan(csb, cmask[:], SPb, 0.0, ALU.mult, ALU.add)))
                v3 = lambda t: t.rearrange("p (a n) -> p a n", a=4)
                tot = lambda t: t.rearrange("p (a n) -> p a n", a=4)[:, :, 127:128].to_broadcast([4, 4, 128])
                G(("dve", lambda e: e.tensor_scalar(NBf, csf, -1.0, None, ALU.mult)))
                G(("dve", lambda e: e.tensor_tensor(Af, IFt, csf, ALU.add)))
                G(("dve", lambda e: e.tensor_tensor(v3(tmp), v3(Af), tot(csf), ALU.subtract)))
                G(("act", lambda e: e.activation(Gf, tmp, AF.Exp)))
                G(("dve", lambda e: e.tensor_tensor(NBb, SPb, csb, ALU.subtract)))
                G(("dve", lambda e: e.tensor_tensor(v3(NBb), v3(NBb), tot(csb), ALU.add)))
                G(("dve", lambda e: e.tensor_tensor(Ab, IBt, NBb, ALU.add)))
                G(("dve", lambda e: e.tensor_tensor(v3(tmp), v3(Ab), tot(csb), ALU.subtract)))
                G(("act", lambda e: e.activation(Gb, tmp, AF.Exp)))
                G(("dve", lambda e: e.tensor_scalar(NBb, NBb, -1.0, None, ALU.mult)))
                fw.dma("pool", nb_scr[0, :, tok0:tok0 + 512], NBf, reads=[r_g], writes=[r_nb])
                fw.dma("pool", nb_scr[1, :, tok0:tok0 + 512], NBb, reads=[r_g], writes=[r_nb])
                wv, rw = wload("w_in", Wb["w_in"], 0, 8, CQ0, 512)
                cb = [None] * 4

                def cq_mm(tt, wv=wv, rw=rw, j=j, cb=cb):
                    bk, rb = nbank()
                    mm(bk[:, :], [(hTb[j][:, kc, tt * 128:(tt + 1) * 128], wv[:, kc, :]) for kc in range(8)], [rw, r_hTb[j]], rb)
                    cb[tt] = (bk, rb)

                def cq_chain(tt, cb=cb):
                    bk, rb = cb[tt]
                    i = tt & 1
                    stt = st1[i]
                    fw.op("act", lambda e, i=i, bk=bk, stt=stt: e.activation(cnb[i][:, 0:256], bk[:, 0:256], AF.Square, accum_out=stt[:, 4:5]), reads=[rb], writes=[r_cnb[i], r_st1[i]])
                    fw.op("act", lambda e, i=i, bk=bk, stt=stt: e.activation(cnb[i][:, 256:512], bk[:, 256:512], AF.Square, accum_out=stt[:, 5:6]), reads=[rb], writes=[r_cnb[i], r_st1[i]])
                    fw.op("act", lambda e, stt=stt: e.activation(stt[:, 6:8], stt[:, 4:6], AF.Sqrt, bias=EPS, scale=1.0 / 256), reads=[r_st1[i]], writes=[r_st1[i]])
                    fw.op("dve", lambda e, stt=stt: e.reciprocal(stt[:, 4:6], stt[:, 6:8]), reads=[r_st1[i]], writes=[r_st1[i]])
                    fw.op("dve", lambda e, i=i, bk=bk, stt=stt: e.tensor_scalar(cnb[i][:, 0:256], bk[:, 0:256], stt[:, 4:5], None, ALU.mult), reads=[rb, r_st1[i]], writes=[r_cnb[i]])
                    fw.op("act", lambda e, i=i, bk=bk, stt=stt: e.activation(cnb[i][:, 256:512], bk[:, 256:512], AF.Copy, scale=stt[:, 5:6]), reads=[rb, r_st1[i]], writes=[r_cnb[i]])

                def cq_tr(tt):
                    i = tt & 1
                    transpose_to(lat[:, :, tt * 128:(tt + 1) * 128], cnb[i], 4, [r_cnb[i]], r_lat)
                cq_mm(0)
                cq_mm(1)
                for tt in range(4):
                    cq_chain(tt)
                    if tt + 2 < 4:
                        cq_mm(tt + 2)
                    if tt >= 1:
                        cq_tr(tt - 1)
                cq_tr(3)
                fw.dma("pool", cqnT_scr[:, :, tok0:tok0 + 512].rearrange("k p n -> p k n"), lat[:, 0:2, :], reads=[r_lat], writes=[r_cq])
                fw.dma("pool", ckvnT_scr[:, :, tok0:tok0 + 512].rearrange("k p n -> p k n"), lat[:, 2:4, :], reads=[r_lat], writes=[r_ckv])
                wa, rwa = wload("w_in", Wb["w_in"], 0, 8, KR0, 64)
                wb_, rwb = wload("w_kr_sw", w_kr_sw, 0, 8, 0, 64)
                bka, rba = nbank()
                mm(bka[0:64, :], [(wa[:, kc, :], hTb[j][:, kc, :]) for kc in range(8)], [rwa, r_hTb[j]], rba)
                bkb, rbb = nbank()
                mm(bkb[0:64, :], [(wb_[:, kc, :], hTb[j][:, kc, :]) for kc in range(8)], [rwb, r_hTb[j]], rbb)
                fw.dma("sp", cst, cs_scr[:, :, tok0:tok0 + 512].rearrange("c p n -> p c n"), writes=[r_cst])
                fw.op("dve", lambda e, bka=bka: e.tensor_tensor(krt[0], bka[0:64, :], cst[:, 0, :], ALU.mult), reads=[rba, r_cst], writes=[r_krt])
                fw.op("dve", lambda e, bkb=bkb: e.tensor_tensor(krt[1], bkb[0:64, :], cst[:, 1, :], ALU.mult), reads=[rbb, r_cst], writes=[r_krt])
                fw.op("pool", lambda e: e.tensor_tensor(krb, krt[0], krt[1], ALU.add), reads=[r_krt], writes=[r_krb])
                fw.dma("pool", krT_scr[:, tok0:tok0 + 512], krb, reads=[r_krb], writes=[r_kr])
                bk, rb = nbank()
                for qi, Q in enumerate((Af, Ab, Gf, Gb)):
                    for tt in range(4):
                        c0 = (qi * 4 + tt) * 4
                        transp(bk[:, c0:c0 + 4], Q[:, tt * 128:(tt + 1) * 128], identf[0:4, 0:4], [r_g, r_const], rb, signal=(qi == 3 and tt == 3))
                fw.op("dve", lambda e, bk=bk, b=b: e.tensor_copy(colsT[:, :, b * 4:(b + 1) * 4, :], bk[:, 0:64].rearrange("p (q t h) -> p q t h", q=4, t=4)),
                      reads=[rb], writes=[r_const])
            fw.op("dve", lambda e: e.tensor_scalar(colsT[:, 0:2], colsT[:, 0:2], LN_SQ, None, ALU.add), reads=[r_const], writes=[r_const])
            if dbg == "P1":
                fw.dma("pool", dbg_out["d_colsT"], colsT[:].rearrange("p q t h -> p (q t h)"), reads=[r_const], writes=[RES("@dbgP1")], sem_res=r_st1[0])

        def phase3(sq):
            xa4 = VF(0, 4096).rearrange("p (t n) -> p t n", t=4)
            r_xa4 = [RES("xa4_%d" % i) for i in range(4)]
            xn4 = VB(16384, 4096).rearrange("p (t n) -> p t n", t=4)
            r_xn4 = [RES("xn4_%d" % i) for i in range(4)]
            k8 = lambda ap: ap.rearrange("p (k n) -> p k n", k=8)
            hTb = k8(VB(24576, 4096)); r_hTb = RES("p3hTb")
            sgo = k8(VB(32768, 4096)); r_sgo = RES("sgo")
            hmT = k8(VB(40960, 4096)); r_hmT = RES("hmTb")
            oTb = k8(VB(49152, 4096)); r_oTb = RES("oTb")
            mrg = k8(VB(57344, 4096)); r_mrg = RES("mrg")
            sg = [VF(65536 + i * 2048, 512) for i in range(2)]; r_sg = [RES("sg%d" % i) for i in range(2)]
            t12 = [VF(69632 + i * 2048, 512) for i in range(2)]; r_t12 = [RES("t12_%d" % i) for i in range(2)]
            actT = VB(73728, 22 * 512).rearrange("p (k n) -> p k n", k=22); r_actT = RES("actT")
            pa = VF(96256, 1024).rearrange("p (t n) -> p t n", t=4); r_pa = RES("pa")
            pn = VB(100352, 1024).rearrange("p (t n) -> p t n", t=4); r_pn = RES("pn")
            pT = VB(102400, 1024).rearrange("p (k n) -> p k n", k=2); r_pT = RES("pT")
            r_y = RES("@y")
            for b in range(NBLK):
                tok0 = b * 512
                tsl = lambda tt: slice(tt * 128, (tt + 1) * 128)
                for tt in range(4):
                    i = tt & 1
                    fw.dma("sp", xa4[:, tt, :], xs[sq, tok0 + tt * 128: tok0 + (tt + 1) * 128, :], writes=[r_xa4[tt]])
                    rms_tile(xa4[:, tt, :], xn4[:, tt, :], st1[i], r_xa4[tt], r_xn4[tt], r_st1[i], 1024)
                    transpose_to(hTb[:, :, tsl(tt)], xn4[:, tt, :], 8, [r_xn4[tt]], r_hTb)
                fw.dma("sp", pa, ps[sq, tok0:tok0 + 512, :].rearrange("(t p) n -> p t n", p=128), writes=[r_pa])
                if dbg:
                    fw.op("pool", lambda e: e.memset(hmT, 0.0), writes=[r_hmT])
                else:
                    fw.dma("sp", hmT, hmT_scr[:, :, tok0:tok0 + 512].rearrange("k p n -> p k n"), reads=[RES("@hm_dummy")], writes=[r_hmT])
                if dbg == 2:
                    fw.op("pool", lambda e: e.memset(oTb, 0.0), writes=[r_oTb])
                else:
                    fw.dma("sp", oTb, oT_scr[:, :, tok0:tok0 + 512].rearrange("k p n -> p k n"), reads=[RES("@o_dummy")], writes=[r_oTb])
                for grp in range(2):
                    wv, rw = wload("w_in", Wb["w_in"], 0, 8, O0 + grp * 512, 512)
                    for cl in range(4):
                        bk, rb = nbank()
                        mm(bk[:, :], [(wv[:, kc, cl * 128:(cl + 1) * 128], hTb[:, kc, :]) for kc in range(8)], [rw, r_hTb], rb)
                        fw.op("act", lambda e, bk=bk, c=grp * 4 + cl: e.activation(sgo[:, c, :], bk[:, :], AF.Sigmoid), reads=[rb], writes=[r_sgo])
                fw.op("dve", lambda e: e.tensor_tensor(sgo, sgo, hmT, ALU.mult), reads=[r_sgo, r_hmT], writes=[r_sgo])
                for grp in range(2):
                    wm, rwm = wload("w_mo", Wb["w_mo"], 0, 8, grp * 512, 512)
                    wa, rwa = wload("w_ao", Wb["w_ao"], 0, 8, grp * 512, 512)
                    wg1, rw1 = wload("w_in", Wb["w_in"], 0, 8, GM0 + grp * 512, 512)
                    wg2, rw2 = wload("w_in", Wb["w_in"], 0, 8, GM0 + 1024 + grp * 512, 512)
                    for cl in range(4):
                        c = grp * 4 + cl
                        cs_ = slice(cl * 128, (cl + 1) * 128)
                        bm, rbm = nbank(); mm(bm[:, :], [(wm[:, kc, cs_], sgo[:, kc, :]) for kc in range(8)], [rwm, r_sgo], rbm)
                        ba, rba = nbank(); mm(ba[:, :], [(wa[:, kc, cs_], oTb[:, kc, :]) for kc in range(8)], [rwa, r_oTb], rba)
                        b1, rb1 = nbank(); mm(b1[:, :], [(wg1[:, kc, cs_], hTb[:, kc, :]) for kc in range(8)], [rw1, r_hTb], rb1)
                        b2, rb2 = nbank(); mm(b2[:, :], [(wg2[:, kc, cs_], hTb[:, kc, :]) for kc in range(8)], [rw2, r_hTb], rb2)
                        fw.op("act", lambda e, b1=b1: e.activation(sg[0], b1[:, :], AF.Sigmoid), reads=[rb1], writes=[r_sg[0]])
                        fw.op("act", lambda e, b2=b2: e.activation(sg[1], b2[:, :], AF.Sigmoid), reads=[rb2], writes=[r_sg[1]])
                        fw.op("dve", lambda e, bm=bm: e.tensor_tensor(t12[0], bm[:, :], sg[0], ALU.mult), reads=[rbm, r_sg[0]], writes=[r_t12[0]])
                        fw.op("dve", lambda e, ba=ba: e.tensor_tensor(t12[1], ba[:, :], sg[1], ALU.mult), reads=[rba, r_sg[1]], writes=[r_t12[1]])
                        fw.op("pool", lambda e, c=c: e.tensor_tensor(mrg[:, c, :], t12[0], t12[1], ALU.add), reads=r_t12, writes=[r_mrg])
                for half in range(2):
                    wo, rwo = wload("w_out", Wb["w_out"], 0, 8, half * 512, 512)
                    for tt in range(4):
                        bk, rb = nbank()
                        mm(bk[:, :], [(mrg[:, kc, tsl(tt)], wo[:, kc, :]) for kc in range(8)], [rwo, r_mrg], rb)
                        xsl = xa4[:, tt, half * 512:(half + 1) * 512]
                        fw.op("dve", lambda e, bk=bk, xsl=xsl: e.tensor_tensor(xsl, xsl, bk[:, :], ALU.add), reads=[rb], writes=[r_xa4[tt]])
                for tt in range(4):
                    i = tt & 1
                    rms_tile(xa4[:, tt, :], xn4[:, tt, :], st1[i], r_xa4[tt], r_xn4[tt], r_st1[i], 1024)
                    transpose_to(hTb[:, :, tsl(tt)], xn4[:, tt, :], 8, [r_xn4[tt]], r_hTb)
                for g in range(6):
                    n_ = 512 if g < 5 else 256
                    wg, rwg = wload("w_gate", Wb["w_gate"], 0, 8, g * 512, n_)
                    wu, rwu = wload("w_up", Wb["w_up"], 0, 8, g * 512, n_)
                    for cl in range(n_ // 128):
                        f = g * 4 + cl
                        cs_ = slice(cl * 128, (cl + 1) * 128)
                        bg_, rbg = nbank(); mm(bg_[:, :], [(wg[:, kc, cs_], hTb[:, kc, :]) for kc in range(8)], [rwg, r_hTb], rbg)
                        bu, rbu = nbank(); mm(bu[:, :], [(wu[:, kc, cs_], hTb[:, kc, :]) for kc in range(8)], [rwu, r_hTb], rbu)
                        i = f & 1
                        fw.op("act", lambda e, bg_=bg_, i=i: e.activation(sg[i], bg_[:, :], AF.Silu), reads=[rbg], writes=[r_sg[i]])
                        fw.op("dve", lambda e, bu=bu, i=i, f=f: e.tensor_tensor(actT[:, f, :], bu[:, :], sg[i], ALU.mult), reads=[rbu, r_sg[i]], writes=[r_actT])
                for half in range(2):
                    parts = []
                    for k0, nk in ((0, 8), (8, 8), (16, 6)):
                        wd, rwd = wload("w_down", Wb["w_down"], k0, nk, half * 512, 512)
                        parts.append((k0, nk, wd, rwd))
                    for tt in range(4):
                        bk, rb = nbank()
                        pairs = []
                        for k0, nk, wd, rwd in parts:
                            pairs += [(actT[:, k0 + kk, tsl(tt)], wd[:, kk, :]) for kk in range(nk)]
                        mm(bk[:, :], pairs, [p_[3] for p_ in parts] + [r_actT], rb)
                        xsl = xa4[:, tt, half * 512:(half + 1) * 512]
                        fw.op("dve", lambda e, bk=bk, xsl=xsl: e.tensor_tensor(xsl, xsl, bk[:, :], ALU.add), reads=[rb], writes=[r_xa4[tt]])
                fw.op("act", lambda e: e.copy(pn, pa), reads=[r_pa], writes=[r_pn])
                for tt in range(4):
                    fw.op("act", lambda e, tt=tt: e.copy(xn4[:, tt, :], xa4[:, tt, :]), reads=[r_xa4[tt]], writes=[r_xn4[tt]])
                    transpose_to(hTb[:, :, tsl(tt)], xn4[:, tt, :], 8, [r_xn4[tt]], r_hTb)
                    transpose_to(pT[:, :, tsl(tt)], pn[:, tt, :], 2, [r_pn], r_pT)
                for half in range(2):
                    wpg, rwpg = wload("w_ple_gate", Wb["w_ple_gate"], 0, 8, half * 512, 512)
                    wp, rwp = wload("w_ple", Wb["w_ple"], 0, 2, half * 512, 512)
                    for tt in range(4):
                        bgt, rbgt = nbank(); mm(bgt[:, :], [(hTb[:, kc, tsl(tt)], wpg[:, kc, :]) for kc in range(8)], [rwpg, r_hTb], rbgt)
                        bp, rbp = nbank(); mm(bp[:, :], [(pT[:, kc, tsl(tt)], wp[:, kc, :]) for kc in range(2)], [rwp, r_pT], rbp)
                        i = tt & 1
                        fw.op("act", lambda e, bgt=bgt, i=i: e.activation(sg[i], bgt[:, :], AF.Sigmoid), reads=[rbgt], writes=[r_sg[i]])
                        fw.op("dve", lambda e, bp=bp, i=i: e.tensor_tensor(t12[i], bp[:, :], sg[i], ALU.mult), reads=[rbp, r_sg[i]], writes=[r_t12[i]])
                        xsl = xa4[:, tt, half * 512:(half + 1) * 512]
                        fw.op("dve", lambda e, xsl=xsl, i=i: e.tensor_tensor(xsl, xsl, t12[i], ALU.add), reads=[r_t12[i]], writes=[r_xa4[tt]])
                for tt in range(4):
                    i = tt & 1
                    stt = st1[i]
                    fw.op("act", lambda e, tt=tt, stt=stt: e.activation(xn4[:, tt, :], xa4[:, tt, :], AF.Square, accum_out=stt[:, 0:1]), reads=[r_xa4[tt]], writes=[r_xn4[tt], r_st1[i]])
                    fw.op("act", lambda e, stt=stt: e.activation(stt[:, 1:2], stt[:, 0:1], AF.Sqrt, bias=EPS, scale=1.0 / 1024), reads=[r_st1[i]], writes=[r_st1[i]])
                    fw.op("dve", lambda e, stt=stt: e.reciprocal(stt[:, 2:3], stt[:, 1:2]), reads=[r_st1[i]], writes=[r_st1[i]])
                    fw.op("dve", lambda e, tt=tt, stt=stt: e.scalar_tensor_tensor(xa4[:, tt, :], xa4[:, tt, :], stt[:, 2:3], gfin[:], ALU.mult, ALU.mult),
                          reads=[r_st1[i], r_const], writes=[r_xa4[tt]])
                    fw.dma("pool", y[sq, tok0 + tt * 128: tok0 + (tt + 1) * 128, :], xa4[:, tt, :], reads=[r_xa4[tt]], writes=[r_y])

        def phase_mlstm(sq, heads=range(4)):
            QT = VB(0, 4096); KT = VB(8192, 4096); Qp = VB(16384, 4096)
            Ktok = VB(24576, 4096).rearrange("p (t n) -> p t n", t=NT)
            Kp = VB(32768, 4096).rearrange("p (t n) -> p t n", t=NT)
            Vaug = VB(40960, NT * 258).rearrange("p (t n) -> p t n", t=NT)
            bbc = VF(57472, 4096)
            tmpE = [VF(73856 + i * 2048, 512) for i in range(2)]
            Cst = VF(77952, 258); Cbf = VB(78992, 258)
            Dt = [VF(79520 + i * 512, 128) for i in range(2)]
            Sp = [VB(80544 + i * 256, 128) for i in range(2)]
            aC = VF(81056, 32)
            hfo = [VF(81184 + i * 1024, 256) for i in range(3)]
            hs = [VF(84256 + i * 1024, 256) for i in range(2)]
            hn = [VB(86304 + i * 512, 256) for i in range(2)]
            hT2 = [VB(87328 + i * 512, 256).rearrange("p (k n) -> p k n", k=2) for i in range(2)]
            r_QT, r_KT, r_Qp, r_Ktok, r_Kp, r_V, r_bbc = (RES(n) for n in ("QT", "KT", "Qp", "Ktok", "Kp", "Vaug", "bbc"))
            r_tmpE = [RES("tmpE%d" % i) for i in range(2)]
            r_C, r_Cbf, r_aC = RES("Cst"), RES("Cbf"), RES("aC")
            r_Dt = [RES("Dt%d" % i) for i in range(2)]; r_Sp = [RES("Sp%d" % i) for i in range(2)]
            r_hfo = [RES("hfo%d" % i) for i in range(3)]; r_hs = [RES("hs%d" % i) for i in range(2)]
            r_hn = [RES("hn%d" % i) for i in range(2)]; r_hT2 = [RES("hT2_%d" % i) for i in range(2)]
            r_hf = [RES("@hf%d" % c) for c in range(NT)]
            r_hm = RES("@hmT")
            for h in heads:
                fw.dma("sp", QT, qk_scr[h], writes=[r_QT])
                fw.dma("sp", KT, qk_scr[4 + h], writes=[r_KT])
                fw.op("pool", lambda e: e.memset(Vaug[:, :, 256:258], 1.0), writes=[r_V])
                fw.dma("sp", Vaug[:, :, 0:256], v_scr[:, h * 256:(h + 1) * 256].rearrange("(t p) n -> p t n", p=128), reads=[r_V], writes=[r_V])
                for c4 in range(4):
                    transpose_to(Ktok[:, c4 * 8:(c4 + 1) * 8, :], KT[:, c4 * 1024:(c4 + 1) * 1024], 8, [r_KT], r_Ktok)
                for d in range(2):
                    msk = maskf if d == 0 else maskb
                    far = 127 if d == 0 else 0
                    fw.dma("sp", bbc, nb_scr[d, h:h + 1, :].partition_broadcast(128).rearrange("p o n -> p (o n)"), writes=[r_bbc])
                    fw.op("act", lambda e, far=far: e.activation(aC, bbc.rearrange("p (t n) -> p t n", t=NT)[:, :, far], AF.Exp), reads=[r_bbc], writes=[r_aC])
                    for blk in range(NBLK):
                        i = blk & 1
                        bs_ = slice(blk * 512, (blk + 1) * 512)
                        fw.op("act", lambda e, i=i, bs_=bs_: e.activation(tmpE[i], bbc[:, bs_], AF.Exp, bias=LN_SQ), reads=[r_bbc], writes=[r_tmpE[i]])
                        fw.op("dve", lambda e, i=i, bs_=bs_: e.tensor_tensor(Qp[:, bs_], QT[:, bs_], tmpE[i], ALU.mult), reads=[r_QT, r_tmpE[i]], writes=[r_Qp])
                    fw.op("pool", lambda e, msk=msk: e.tensor_tensor(bbc.rearrange("p (t n) -> p t n", t=NT), bbc.rearrange("p (t n) -> p t n", t=NT),
                                                                    msk[:].unsqueeze(1).to_broadcast([128, NT, 128]), ALU.add), reads=[r_bbc, r_const], writes=[r_bbc])
                    fw.op("dve", lambda e, d=d, h=h: e.tensor_tensor(Kp, Ktok, colsT[:, 2 + d, :, h].unsqueeze(2).to_broadcast([128, NT, 128]), ALU.mult),
                          reads=[r_Ktok, r_const], writes=[r_Kp])
                    fw.op("dve", lambda e: e.memset(Cst, 0.0), writes=[r_C])
                    fw.op("pool", lambda e: e.memset(Cbf, 0.0), writes=[r_Cbf])
                    order = list(range(NT)) if d == 0 else list(range(NT - 1, -1, -1))
                    for n_, c in enumerate(order):
                        cs_ = slice(c * 128, (c + 1) * 128)
                        j = n_ & 1
                        fw.op("act", lambda e, j=j, cs_=cs_, d=d, c=c, h=h: e.activation(Dt[j], bbc[:, cs_], AF.Exp, bias=colsT[:, d, c, h:h + 1]),
                              reads=[r_bbc, r_const], writes=[r_Dt[j]])
                        bS, rbS = nbank()
                        mm(bS[:, 0:128], [(KT[:, cs_], QT[:, cs_])], [r_KT, r_QT], rbS)
                        fw.op("dve", lambda e, j=j, bS=bS: e.tensor_tensor(Sp[j], bS[:, 0:128], Dt[j], ALU.mult), reads=[rbS, r_Dt[j]], writes=[r_Sp[j]])
                        bN, rbN = nbank()
                        mm(bN[:, 0:257], [(Sp[j], Vaug[:, c, 0:257]), (Qp[:, cs_], Cbf[:, 0:257])], [r_Sp[j], r_V, r_Qp, r_Cbf], rbN)
                        bU, rbU = nbank()
                        mm(bU[:, 0:257], [(Kp[:, c, :], Vaug[:, c, 0:257])], [r_Kp, r_V], rbU)
                        fw.op("dve", lambda e, c=c, bU=bU: e.scalar_tensor_tensor(Cst[:, 0:257], Cst[:, 0:257], aC[:, c:c + 1], bU[:, 0:257], ALU.mult, ALU.add),
                              reads=[r_C, r_aC, rbU], writes=[r_C])
                        fw.op("act", lambda e: e.copy(Cbf[:, 0:257], Cst[:, 0:257]), reads=[r_C], writes=[r_Cbf])
                        k = n_ & 1
                        stt, r_stt = st1[k], r_st1[k]
                        fw.op("act", lambda e, bN=bN, stt=stt: e.activation(stt[:, 0:1], bN[:, 256:257], AF.Abs), reads=[rbN], writes=[r_stt])
                        fw.op("dve", lambda e, stt=stt: e.tensor_scalar_max(stt[:, 0:1], stt[:, 0:1], 1.0), reads=[r_stt], writes=[r_stt])
                        fw.op("dve", lambda e, stt=stt: e.reciprocal(stt[:, 1:2], stt[:, 0:1]), reads=[r_stt], writes=[r_stt])
                        m3 = n_ % 3
                        if d == 0:
                            fw.op("dve", lambda e, m3=m3, bN=bN, stt=stt: e.tensor_scalar(hfo[m3], bN[:, 0:256], stt[:, 1:2], None, ALU.mult), reads=[rbN, r_stt], writes=[r_hfo[m3]])
                            fw.dma("pool", hf_scr[cs_, :], hfo[m3], reads=[r_hfo[m3]], writes=[r_hf[c]])
                        else:
                            fw.dma("sp", hfo[m3], hf_scr[cs_, :], reads=[r_hf[c]], writes=[r_hfo[m3]])
                            fw.op("dve", lambda e, m3=m3, k=k, bN=bN, stt=stt: e.scalar_tensor_tensor(hs[k], bN[:, 0:256], stt[:, 1:2], hfo[m3], ALU.mult, ALU.add),
                                  reads=[rbN, r_stt, r_hfo[m3]], writes=[r_hs[k]])
                            fw.op("act", lambda e, k=k, stt=stt: e.activation(hn[k], hs[k], AF.Square, accum_out=stt[:, 2:3]), reads=[r_hs[k]], writes=[r_hn[k], r_stt])
                            fw.op("act", lambda e, stt=stt: e.activation(stt[:, 3:4], stt[:, 2:3], AF.Sqrt, bias=EPS, scale=1.0 / 256), reads=[r_stt], writes=[r_stt])
                            fw.op("dve", lambda e, stt=stt: e.reciprocal(stt[:, 2:3], stt[:, 3:4]), reads=[r_stt], writes=[r_stt])
                            fw.op("dve", lambda e, k=k, stt=stt: e.tensor_scalar(hn[k], hs[k], stt[:, 2:3], None, ALU.mult), reads=[r_hs[k], r_stt], writes=[r_hn[k]])
                            transpose_to(hT2[k], hn[k], 2, [r_hn[k]], r_hT2[k])
                            fw.dma("pool", hmT_scr[2 * h:2 * h + 2, :, cs_].rearrange("k p n -> p k n"), hT2[k], reads=[r_hT2[k]], writes=[r_hm])

        def phase_mix(sq):
            cqnT = VB(0, 8192).rearrange("p (k n) -> p k n", k=2); r_cqn = RES("cqnT")
            ckvnT = VB(16384, 8192).rearrange("p (k n) -> p k n", k=2); r_ckvn = RES("ckvnT")
            krT = VB(32768, 4096, 128); r_krT = RES("krT")
            KhT = VB(40960, 4096); r_KhT = RES("KhT")
            Vh = VB(49152, 4096); r_Vh = RES("Vh")
            QhT = VB(57344, 4096); r_QhT = RES("QhT")
            QrT = VB(65536, 4096, 128); r_QrT = RES("QrT")
            cstq2 = [VF(73728, 1024, 64).rearrange("p (c n) -> p c n", c=2),
                     wsl[0].bitcast(F32)[0:64, 512:1536].rearrange("p (c n) -> p c n", c=2)]
            r_cstq2 = [RES("cstq"), RES("cstq_b")]
            rt = [VF(77824 + i * 2048, 512, 64) for i in range(2)]; r_rt = RES("rt")
            PT = [VB(81920 + i * 1024, 512) for i in range(3)]; r_PT = [RES("PT%d" % i) for i in range(3)]
            rsum = VF(84992, 512); r_rsum = RES("rsum")
            oTs = [VB(87040 + i * 1024, 512) for i in range(2)]; r_oTs = [RES("oTs%d" % i) for i in range(2)]
            r_oT = RES("@oT")
            QT, KT, Qp = wsl[1][:, :], wsl[2][:, :], wsl[3][:, :]
            Ktok = wsl[4][:, :].rearrange("p (t n) -> p t n", t=NT)
            Kp = wsl[5][:, :].rearrange("p (t n) -> p t n", t=NT)
            M0 = 89088
            Vaug = VB(M0, NT * 258).rearrange("p (t n) -> p t n", t=NT)
            bbc = VF(M0 + 16512, 4096)
            tmpE = [VF(M0 + 32896 + i * 2048, 512) for i in range(2)]
            Cst = VF(M0 + 36992, 258); Cbf = VB(M0 + 38032, 258)
            Dt = [VF(M0 + 38560 + i * 512, 128) for i in range(2)]
            Sp = [VB(M0 + 39584 + i * 256, 128) for i in range(2)]
            aC = VF(M0 + 40096, 32)
            hfo = [VF(M0 + 40224 + i * 1024, 256) for i in range(3)]
            hs = [VF(M0 + 43296 + i * 1024, 256) for i in range(2)]
            hn = [VB(M0 + 45344 + i * 512, 256) for i in range(2)]
            hT2 = [VB(M0 + 46368 + i * 512, 256).rearrange("p (k n) -> p k n", k=2) for i in range(2)]
            stm = [T_once("stm%d" % i, [128, 8], F32) for i in range(2)]; r_stm = [RES("stm%d" % i) for i in range(2)]
            assert M0 + 47392 <= AR * 4
            r_QT, r_KT, r_Qp, r_Ktok, r_Kp, r_V, r_bbc = (RES(n) for n in ("QT", "KT", "Qp", "Ktok", "Kp", "Vaug", "bbc"))
            r_tmpE = [RES("tmpE%d" % i) for i in range(2)]
            r_C, r_Cbf, r_aC = RES("Cst"), RES("Cbf"), RES("aC")
            r_Dt = [RES("Dt%d" % i) for i in range(2)]; r_Sp = [RES("Sp%d" % i) for i in range(2)]
            r_hfo = [RES("hfo%d" % i) for i in range(3)]; r_hs = [RES("hs%d" % i) for i in range(2)]
            r_hn = [RES("hn%d" % i) for i in range(2)]; r_hT2 = [RES("hT2_%d" % i) for i in range(2)]
            r_hf = [RES("@hf%d" % c) for c in range(NT)]
            r_hm = RES("@hmT")
            mst = {"b": 0}

            def mbank():
                i = 5 + mst["b"]
                mst["b"] = (mst["b"] + 1) % 3
                return banks[i], r_bank[i]

            def m_transpose(dst3, src_ap, nk, reads, r_dst):
                bk, rb = mbank()
                pbt = bk.bitcast(BF16)
                for k in range(nk):
                    transp(pbt[:, k * 128:(k + 1) * 128], src_ap[:, k * 128:(k + 1) * 128], identb[:], reads + [r_const], rb, signal=(k == nk - 1))
                copy_op(evac_eng(), dst3, pbt[:, 0:nk * 128].rearrange("p (k n) -> p k n", k=nk), [rb], [r_dst])

            b3 = lambda: bbc.rearrange("p (t n) -> p t n", t=NT)

            def mlstm_gen():
                for h in range(4):
                    fw.dma("sp", QT, qk_scr[h], writes=[r_QT])
                    fw.dma("sp", KT, qk_scr[4 + h], writes=[r_KT])
                    fw.op("pool", lambda e: e.memset(Vaug[:, :, 256:258], 1.0), writes=[r_V])
                    fw.dma("sp", Vaug[:, :, 0:256], v_scr[:, h * 256:(h + 1) * 256].rearrange("(t p) n -> p t n", p=128), reads=[r_V], writes=[r_V])
                    yield
                    for c8 in range(8):
                        m_transpose(Ktok[:, c8 * 4:(c8 + 1) * 4, :], KT[:, c8 * 512:(c8 + 1) * 512], 4, [r_KT], r_Ktok)
                        yield
                    for d in range(2):
                        msk = maskf if d == 0 else maskb
                        far = 127 if d == 0 else 0
                        fw.dma("sp", bbc, nb_scr[d, h:h + 1, :].partition_broadcast(128).rearrange("p o n -> p (o n)"), writes=[r_bbc])
                        yield
                        fw.op("act", lambda e, far=far: e.activation(aC, bbc.rearrange("p (t n) -> p t n", t=NT)[:, :, far], AF.Exp), reads=[r_bbc], writes=[r_aC])
                        for blk in range(NBLK):
                            i = blk & 1
                            bs_ = slice(blk * 512, (blk + 1) * 512)
                            fw.op("act", lambda e, i=i, bs_=bs_: e.activation(tmpE[i], bbc[:, bs_], AF.Exp, bias=LN_SQ), reads=[r_bbc], writes=[r_tmpE[i]])
                            yield
                            fw.op("dve", lambda e, i=i, bs_=bs_: e.tensor_tensor(Qp[:, bs_], QT[:, bs_], tmpE[i], ALU.mult), reads=[r_QT, r_tmpE[i]], writes=[r_Qp])
                            yield
                        for q4 in range(4):
                            ts_ = slice(q4 * 8, (q4 + 1) * 8)
                            fw.op("pool", lambda e, msk=msk, ts_=ts_: e.tensor_tensor(b3()[:, ts_, :], b3()[:, ts_, :], msk[:].unsqueeze(1).to_broadcast([128, 8, 128]), ALU.add),
                                  reads=[r_bbc, r_const], writes=[r_bbc])
                            fw.op("dve", lambda e, d=d, h=h, ts_=ts_: e.tensor_tensor(Kp[:, ts_, :], Ktok[:, ts_, :], colsT[:, 2 + d, ts_, h].unsqueeze(2).to_broadcast([128, 8, 128]), ALU.mult),
                                  reads=[r_Ktok, r_const], writes=[r_Kp])
                            yield
                        fw.op("dve", lambda e: e.memset(Cst, 0.0), writes=[r_C])
                        fw.op("pool", lambda e: e.memset(Cbf, 0.0), writes=[r_Cbf])
                        yield
                        order = list(range(NT)) if d == 0 else list(range(NT - 1, -1, -1))
                        for n_, c in enumerate(order):
                            cs_ = slice(c * 128, (c + 1) * 128)
                            j = n_ & 1
                            k = n_ & 1
                            m3 = n_ % 3
                            stt, r_stt = stm[k], r_stm[k]
                            fw.op("act", lambda e, j=j, cs_=cs_, d=d, c=c, h=h: e.activation(Dt[j], bbc[:, cs_], AF.Exp, bias=colsT[:, d, c, h:h + 1]),
                                  reads=[r_bbc, r_const], writes=[r_Dt[j]])
                            bS, rbS = mbank()
                            mm(bS[:, 0:128], [(KT[:, cs_], QT[:, cs_])], [r_KT, r_QT], rbS)
                            if d == 1:
                                fw.dma("sp", hfo[m3], hf_scr[cs_, :], reads=[r_hf[c]], writes=[r_hfo[m3]])
                            yield
                            fw.op("dve", lambda e, j=j, bS=bS: e.tensor_tensor(Sp[j], bS[:, 0:128], Dt[j], ALU.mult), reads=[rbS, r_Dt[j]], writes=[r_Sp[j]])
                            yield
                            bN, rbN = mbank()
                            mm(bN[:, 0:257], [(Sp[j], Vaug[:, c, 0:257]), (Qp[:, cs_], Cbf[:, 0:257])], [r_Sp[j], r_V, r_Qp, r_Cbf], rbN)
                            bU, rbU = mbank()
                            mm(bU[:, 0:257], [(Kp[:, c, :], Vaug[:, c, 0:257])], [r_Kp, r_V], rbU)
                            yield
                            fw.op("dve", lambda e, c=c, bU=bU: e.scalar_tensor_tensor(Cst[:, 0:257], Cst[:, 0:257], aC[:, c:c + 1], bU[:, 0:257], ALU.mult, ALU.add),
                                  reads=[r_C, r_aC, rbU], writes=[r_C])
                            fw.op("act", lambda e, bN=bN, stt=stt: e.activation(stt[:, 0:1], bN[:, 256:257], AF.Abs), reads=[rbN], writes=[r_stt])
                            yield
                            fw.op("pool", lambda e: e.tensor_copy(Cbf[:, 0:257], Cst[:, 0:257]), reads=[r_C], writes=[r_Cbf])
                            fw.op("dve", lambda e, stt=stt: e.tensor_scalar_max(stt[:, 0:1], stt[:, 0:1], 1.0), reads=[r_stt], writes=[r_stt])
                            fw.op("dve", lambda e, stt=stt: e.reciprocal(stt[:, 1:2], stt[:, 0:1]), reads=[r_stt], writes=[r_stt])
                            yield
                            if d == 0:
                                fw.op("dve", lambda e, m3=m3, bN=bN, stt=stt: e.tensor_scalar(hfo[m3], bN[:, 0:256], stt[:, 1:2], None, ALU.mult), reads=[rbN, r_stt], writes=[r_hfo[m3]])
                                fw.dma("pool", hf_scr[cs_, :], hfo[m3], reads=[r_hfo[m3]], writes=[r_hf[c]])
                                yield
                            else:
                                fw.op("dve", lambda e, m3=m3, k=k, bN=bN, stt=stt: e.scalar_tensor_tensor(hs[k], bN[:, 0:256], stt[:, 1:2], hfo[m3], ALU.mult, ALU.add),
                                      reads=[rbN, r_stt, r_hfo[m3]], writes=[r_hs[k]])
                                yield
                                fw.op("act", lambda e, k=k, stt=stt: e.activation(hn[k], hs[k], AF.Square, accum_out=stt[:, 2:3]), reads=[r_hs[k]], writes=[r_hn[k], r_stt])
                                yield
                                fw.op("act", lambda e, stt=stt: e.activation(stt[:, 3:4], stt[:, 2:3], AF.Sqrt, bias=EPS, scale=1.0 / 256), reads=[r_stt], writes=[r_stt])
                                yield
                                fw.op("dve", lambda e, stt=stt: e.reciprocal(stt[:, 2:3], stt[:, 3:4]), reads=[r_stt], writes=[r_stt])
                                yield
                                fw.op("dve", lambda e, k=k, stt=stt: e.tensor_scalar(hn[k], hs[k], stt[:, 2:3], None, ALU.mult), reads=[r_hs[k], r_stt], writes=[r_hn[k]])
                                yield
                                m_transpose(hT2[k], hn[k], 2, [r_hn[k]], r_hT2[k])
                                fw.dma("pool", hmT_scr[2 * h:2 * h + 2, :, cs_].rearrange("k p n -> p k n"), hT2[k], reads=[r_hT2[k]], writes=[r_hm])
                                yield

            gen = mlstm_gen()
            gst = {"done": False, "n": 0, "calls": 0}

            def advance():
                gst["calls"] += 1
                for _ in range(2 if gst["calls"] % 8 == 0 else 1):
                    if not gst["done"]:
                        try:
                            next(gen)
                            gst["n"] += 1
                        except StopIteration:
                            gst["done"] = True

            def wload0(name, W_ap, k0, nk, c0, ncols, off):
                view = wsl[0][:, off:off + nk * ncols].rearrange("p (k n) -> p k n", k=nk)
                src = W_ap[k0 * 128:(k0 + nk) * 128, c0:c0 + ncols].rearrange("(k p) n -> p k n", p=128)
                fw.dma("sp", view, src, reads=[r_W[name]], writes=[r_wsl[0]])
                return view, r_wsl[0]

            fw.dma("sp", cqnT, cqnT_scr.rearrange("k p n -> p k n"), writes=[r_cqn])
            fw.dma("sp", ckvnT, ckvnT_scr.rearrange("k p n -> p k n"), writes=[r_ckvn])
            fw.op("pool", lambda e: e.memset(krT[64:128, :], 0.0), writes=[r_krT])
            fw.op("pool", lambda e: e.memset(QrT[64:128, :], 0.0), writes=[r_QrT])
            fw.dma("sp", krT[0:64, :], krT_scr, reads=[r_krT], writes=[r_krT])
            abank = {"i": 0}

            def xbank():
                i = abank["i"]
                abank["i"] = (i + 1) % 5
                return banks[i], r_bank[i]

            for h in range(8):
                wk, rwk = wload0("w_ukv", Wb["w_ukv"], 0, 2, h * 256, 256, 0)
                wq, rwq = wload0("w_uq", Wb["w_uq"], 0, 2, h * 192, 192, 512)
                wqs, rwqs = wload0("w_uq_sw", w_uq_sw, 0, 2, h * 64, 64, 896)
                for b in range(NBLK):
                    bs_ = slice(b * 512, (b + 1) * 512)
                    bk, rb = xbank()
                    mm(bk[:, :], [(wk[:, kc, 0:128], ckvnT[:, kc, bs_]) for kc in range(2)], [rwk, r_ckvn], rb)
                    copy_op(evac_eng(), KhT[:, bs_], bk[:, :], [rb], [r_KhT])
                    bk, rb = xbank()
                    mm(bk[:, :], [(wq[:, kc, 0:128], cqnT[:, kc, bs_]) for kc in range(2)], [rwq, r_cqn], rb)
                    copy_op(evac_eng(), QhT[:, bs_], bk[:, :], [rb], [r_QhT])
                    bka, rba = xbank()
                    mm(bka[0:64, :], [(wq[:, kc, 128:192], cqnT[:, kc, bs_]) for kc in range(2)], [rwq, r_cqn], rba)
                    bkb, rbb = xbank()
                    mm(bkb[0:64, :], [(wqs[:, kc, :], cqnT[:, kc, bs_]) for kc in range(2)], [rwqs, r_cqn], rbb)
                    cq_, r_cq_ = cstq2[b & 1], r_cstq2[b & 1]
                    if b == 0:
                        fw.dma("sp", cq_, cs_scr[:, :, bs_].rearrange("c p n -> p c n"), writes=[r_cq_])
                    if b + 1 < NBLK:
                        nb_ = slice((b + 1) * 512, (b + 2) * 512)
                        fw.dma("sp", cstq2[(b + 1) & 1], cs_scr[:, :, nb_].rearrange("c p n -> p c n"), writes=[r_cstq2[(b + 1) & 1]])
                    fw.op("dve", lambda e, bka=bka, cq_=cq_: e.tensor_tensor(rt[0], bka[0:64, :], cq_[:, 0, :], ALU.mult), reads=[rba, r_cq_], writes=[r_rt])
                    fw.op("dve", lambda e, bkb=bkb, cq_=cq_: e.tensor_tensor(rt[1], bkb[0:64, :], cq_[:, 1, :], ALU.mult), reads=[rbb, r_cq_], writes=[r_rt])
                    fw.op("pool", lambda e, bs_=bs_: e.tensor_tensor(QrT[0:64, bs_], rt[0], rt[1], ALU.add), reads=[r_rt], writes=[r_QrT])
                    bk, rb = xbank()
                    for tt in range(4):
                        ts_ = slice(b * 512 + tt * 128, b * 512 + (tt + 1) * 128)
                        mm(bk[:, tt * 128:(tt + 1) * 128], [(ckvnT[:, kc, ts_], wk[:, kc, 128:256]) for kc in range(2)], [rwk, r_ckvn], rb)
                    copy_op(evac_eng(), Vh[:, bs_], bk[:, :], [rb], [r_Vh])
                    advance()
                for qb in range(NBLK):
                    qs_ = slice(qb * 512, (qb + 1) * 512)
                    bo, rbo = banks[3], r_bank[3]
                    bsm, rbsm = banks[4], r_bank[4]

                    def S_step(kc, qs_=qs_):
                        ks_ = slice(kc * 128, (kc + 1) * 128)
                        bS, rbS = banks[kc % 3], r_bank[kc % 3]
                        mm(bS[:, :], [(KhT[:, ks_], QhT[:, qs_]), (krT[:, ks_], QrT[:, qs_])], [r_KhT, r_QhT, r_krT, r_QrT], rbS)
                        fw.op("act", lambda e, bS=bS, kc=kc: e.activation(PT[kc % 3], bS[:, :], AF.Exp, scale=A_SCALE), reads=[rbS], writes=[r_PT[kc % 3]])

                    def O_step(kc, bo=bo, rbo=rbo, bsm=bsm, rbsm=rbsm):
                        ks_ = slice(kc * 128, (kc + 1) * 128)
                        fw.op("pe", lambda e, kc=kc, ks_=ks_, bo=bo: e.matmul(bo[:, :], Vh[:, ks_], PT[kc % 3], start=(kc == 0), stop=(kc == NT - 1)),
                              reads=[r_Vh, r_PT[kc % 3]], writes=[rbo], signal=False)
                        fw.op("pe", lambda e, kc=kc, bsm=bsm: e.matmul(bsm[:, :], onesb[:], PT[kc % 3], start=(kc == 0), stop=(kc == NT - 1)),
                              reads=[r_const, r_PT[kc % 3]], writes=[rbsm], signal=True)
                    S_step(0)
                    S_step(1)
                    for kc in range(NT):
                        if kc + 2 < NT:
                            S_step(kc + 2)
                        O_step(kc)
                        advance()
                    fw.op("dve", lambda e, bsm=bsm: e.reciprocal(rsum, bsm[:, :]), reads=[rbsm], writes=[r_rsum])
                    o_ = oTs[qb & 1]
                    fw.op("dve", lambda e, bo=bo, o_=o_: e.tensor_tensor(o_, bo[:, :], rsum, ALU.mult), reads=[rbo, r_rsum], writes=[r_oTs[qb & 1]])
                    fw.dma("pool", oT_scr[h, :, qs_], o_, reads=[r_oTs[qb & 1]], writes=[r_oT])
            n_inside = gst["n"]
            while not gst["done"]:
                advance()
            mix_stats.append((n_inside, gst["n"]))

        def phase_attn(sq, heads=range(8), main_loop=True, qbs=None):
            cqnT = VB(0, 8192).rearrange("p (k n) -> p k n", k=2); r_cqn = RES("cqnT")
            ckvnT = VB(16384, 8192).rearrange("p (k n) -> p k n", k=2); r_ckvn = RES("ckvnT")
            krT = VB(32768, 4096, 128); r_krT = RES("krT")
            KhT = VB(40960, 4096); r_KhT = RES("KhT")
            Vh = VB(49152, 4096); r_Vh = RES("Vh")
            QhT = VB(57344, 4096); r_QhT = RES("QhT")
            QrT = VB(65536, 4096, 128); r_QrT = RES("QrT")
            cstq = VF(73728, 1024, 64).rearrange("p (c n) -> p c n", c=2); r_cstq = RES("cstq")
            rt = [VF(77824 + i * 2048, 512, 64) for i in range(2)]; r_rt = RES("rt")
            PT = [VB(81920 + i * 1024, 512) for i in range(3)]; r_PT = [RES("PT%d" % i) for i in range(3)]
            rsum = VF(84992, 512); r_rsum = RES("rsum")
            oTs = [VB(87040 + i * 1024, 512) for i in range(2)]; r_oTs = [RES("oTs%d" % i) for i in range(2)]
            r_oT = RES("@oT")
            fw.dma("sp", cqnT, cqnT_scr.rearrange("k p n -> p k n"), writes=[r_cqn])
            fw.dma("sp", ckvnT, ckvnT_scr.rearrange("k p n -> p k n"), writes=[r_ckvn])
            fw.op("pool", lambda e: e.memset(krT[64:128, :], 0.0), writes=[r_krT])
            fw.op("pool", lambda e: e.memset(QrT[64:128, :], 0.0), writes=[r_QrT])
            fw.dma("sp", krT[0:64, :], krT_scr, reads=[r_krT], writes=[r_krT])
            for h in heads:
                wk, rwk = wload("w_ukv", Wb["w_ukv"], 0, 2, h * 256, 256)
                wq, rwq = wload("w_uq", Wb["w_uq"], 0, 2, h * 192, 192)
                wqs, rwqs = wload("w_uq_sw", w_uq_sw, 0, 2, h * 64, 64)
                for b in range(NBLK):
                    bs_ = slice(b * 512, (b + 1) * 512)
                    bk, rb = nbank()
                    mm(bk[:, :], [(wk[:, kc, 0:128], ckvnT[:, kc, bs_]) for kc in range(2)], [rwk, r_ckvn], rb)
                    copy_op(evac_eng(), KhT[:, bs_], bk[:, :], [rb], [r_KhT])
                    bk, rb = nbank()
                    mm(bk[:, :], [(wq[:, kc, 0:128], cqnT[:, kc, bs_]) for kc in range(2)], [rwq, r_cqn], rb)
                    copy_op(evac_eng(), QhT[:, bs_], bk[:, :], [rb], [r_QhT])
                    bka, rba = nbank()
                    mm(bka[0:64, :], [(wq[:, kc, 128:192], cqnT[:, kc, bs_]) for kc in range(2)], [rwq, r_cqn], rba)
                    bkb, rbb = nbank()
                    mm(bkb[0:64, :], [(wqs[:, kc, :], cqnT[:, kc, bs_]) for kc in range(2)], [rwqs, r_cqn], rbb)
                    fw.dma("sp", cstq, cs_scr[:, :, bs_].rearrange("c p n -> p c n"), writes=[r_cstq])
                    fw.op("dve", lambda e, bka=bka: e.tensor_tensor(rt[0], bka[0:64, :], cstq[:, 0, :], ALU.mult), reads=[rba, r_cstq], writes=[r_rt])
                    fw.op("dve", lambda e, bkb=bkb: e.tensor_tensor(rt[1], bkb[0:64, :], cstq[:, 1, :], ALU.mult), reads=[rbb, r_cstq], writes=[r_rt])
                    fw.op("pool", lambda e, bs_=bs_: e.tensor_tensor(QrT[0:64, bs_], rt[0], rt[1], ALU.add), reads=[r_rt], writes=[r_QrT])
                    bk, rb = nbank()
                    for tt in range(4):
                        ts_ = slice(b * 512 + tt * 128, b * 512 + (tt + 1) * 128)
                        mm(bk[:, tt * 128:(tt + 1) * 128], [(ckvnT[:, kc, ts_], wk[:, kc, 128:256]) for kc in range(2)], [rwk, r_ckvn], rb)
                    copy_op(evac_eng(), Vh[:, bs_], bk[:, :], [rb], [r_Vh])
                for qb in ((qbs if qbs is not None else range(NBLK)) if main_loop else ()):
                    qs_ = slice(qb * 512, (qb + 1) * 512)
                    bo, rbo = banks[4 + (qb & 1)], r_bank[4 + (qb & 1)]
                    bsm, rbsm = banks[6 + (qb & 1)], r_bank[6 + (qb & 1)]

                    def S_step(kc):
                        ks_ = slice(kc * 128, (kc + 1) * 128)
                        bS, rbS = banks[kc % 4], r_bank[kc % 4]
                        mm(bS[:, :], [(KhT[:, ks_], QhT[:, qs_]), (krT[:, ks_], QrT[:, qs_])], [r_KhT, r_QhT, r_krT, r_QrT], rbS)
                        fw.op("act", lambda e, bS=bS, kc=kc: e.activation(PT[kc % 3], bS[:, :], AF.Exp, scale=A_SCALE), reads=[rbS], writes=[r_PT[kc % 3]])

                    def O_step(kc):
                        ks_ = slice(kc * 128, (kc + 1) * 128)
                        fw.op("pe", lambda e, kc=kc, ks_=ks_, bo=bo: e.matmul(bo[:, :], Vh[:, ks_], PT[kc % 3], start=(kc == 0), stop=(kc == NT - 1)),
                              reads=[r_Vh, r_PT[kc % 3]], writes=[rbo], signal=False)
                        fw.op("pe", lambda e, kc=kc, bsm=bsm: e.matmul(bsm[:, :], onesb[:], PT[kc % 3], start=(kc == 0), stop=(kc == NT - 1)),
                              reads=[r_const, r_PT[kc % 3]], writes=[rbsm], signal=True)
                    S_step(0)
                    S_step(1)
                    for kc in range(NT):
                        if kc + 2 < NT:
                            S_step(kc + 2)
                        O_step(kc)
                    fw.op("dve", lambda e, bsm=bsm: e.reciprocal(rsum, bsm[:, :]), reads=[rbsm], writes=[r_rsum])
                    o_ = oTs[qb & 1]
                    fw.op("dve", lambda e, bo=bo, o_=o_: e.tensor_tensor(o_, bo[:, :], rsum, ALU.mult), reads=[rbo, r_rsum], writes=[r_oTs[qb & 1]])
                    fw.dma("pool", oT_scr[h, :, qs_], o_, reads=[r_oTs[qb & 1]], writes=[r_oT])
            if dbg == "M1":
                fw.dma("pool", dbg_out["d_oT"], oTs[0], reads=[r_oTs[0]], writes=[RES("@dbgM1")])
            if isinstance(dbg, str) and dbg.startswith("M:"):
                fw.dma("pool", dbg_out["d_oTs0"], oTs[0], reads=[r_oTs[0]], writes=[RES("@dbgMa")])
                fw.dma("pool", dbg_out["d_oTs1"], oTs[1], reads=[r_oTs[1]], writes=[RES("@dbgMb")])
            if dbg == "E":
                rd = RES("@dbgE")
                for n_, ap_, r_ in (("d_cqnT", cqnT, r_cqn), ("d_ckvnT", ckvnT, r_ckvn), ("d_krT", krT[0:64, :], r_krT),
                                    ("d_KhT", KhT, r_KhT), ("d_QhT", QhT, r_QhT), ("d_QrT", QrT[0:64, :], r_QrT), ("d_Vh", Vh, r_Vh)):
                    fw.dma("pool", dbg_out[n_], ap_, reads=[r_], writes=[rd])

        for sq in range(NSEQ):
            if 1 in phases:
                phase1(sq)
                fw.barrier()
            if 5 in phases:
                phase_mix(sq)
                fw.barrier()
            if 2 in phases:
                phase_mlstm(sq, heads=([0] if dbg == "L1" else range(4)))
                fw.barrier()
            if 3 in phases:
                if dbg == "E":
                    phase_attn(sq, heads=[0], main_loop=False)
                elif dbg == "M1":
                    phase_attn(sq, heads=[0], main_loop=True, qbs=[0])
                elif isinstance(dbg, str) and dbg.startswith("M:"):
                    _, nh_, nq_ = dbg.split(":")
                    phase_attn(sq, heads=list(range(int(nh_))), main_loop=True, qbs=list(range(int(nq_))))
                else:
                    phase_attn(sq)
                fw.barrier()
            if 4 in phases:
                phase3(sq)
                fw.barrier()
        fw.barrier()
        fw.emit_all()
    if mix_stats:
        print("[phase_mix] scan generator steps issued inside attention / total, per sequence:", mix_stats)
    return nc


NSEQ_PER_CORE = 3


def _consts():
    ident = np.eye(128, dtype=np.float32)
    s = np.arange(128)[:, None]
    t = np.arange(128)[None, :]
    maskf = np.where(t >= s, 0.0, -1000.0).astype(np.float32)
    maskb = np.where(t <= s, 0.0, -1000.0).astype(np.float32)
    return {"c_ident": ident, "c_maskf": maskf, "c_maskb": maskb}


def kernel(**inputs):
    X = np.concatenate([inputs["x_prompt"], inputs["x_sample"]], axis=0)
    P = np.concatenate([inputs["p_prompt"][0], inputs["p_sample"][0]], axis=0)
    nseq = X.shape[0]
    base = {}
    for k in WSHAPES:
        base[k] = np.ascontiguousarray(inputs[k][0])
    for k in GSHAPES:
        base[k] = np.ascontiguousarray(inputs[k].reshape(1, -1))
    base["g_final"] = np.ascontiguousarray(inputs["g_final"].reshape(1, -1))
    base["b_gate"] = np.ascontiguousarray(inputs["b_gate"].reshape(1, 16))
    base["conv_w"] = np.ascontiguousarray(inputs["conv_w"][0])
    base.update(_consts())
    slots = [[c, c + 8, c + 16 if c + 16 < nseq else c] for c in range(8)]
    maps = []
    for sl in slots:
        m = dict(base)
        m["xs"] = np.ascontiguousarray(X[sl])
        m["ps"] = np.ascontiguousarray(P[sl])
        maps.append(m)
    nc = build(NSEQ_PER_CORE, dbg=False, phases=(1, 5, 4))
    res = run_bass_kernel_spmd(nc, maps, core_ids=list(range(8)))
    Y = np.zeros((nseq, S, D), dtype=np.float32)
    for c, sl in enumerate(slots):
        yc = res.results[c]["y"]
        for j, sidx in enumerate(sl):
            if j == 2 and c + 16 >= nseq:
                continue
            Y[sidx] = yc[j]
    nb = inputs["x_prompt"].shape[0]
    return (np.ascontiguousarray(Y[:nb], dtype=np.float32), np.ascontiguousarray(Y[nb:], dtype=np.float32))
```
